# Optimizing a Trainium2 kernel written in Bass

```python
import jax, jax.numpy as jnp
from jax import lax
import numpy as np

D_MODEL = 1024
BATCH = 32
SEQ = 256
DEPTH = 2
DEC_BATCH = 4
DEC_SEQ = 4096
PAST_LEN = 256

GRID_W = 64
N_BRANCH = 4
BR_WIDTH = 512
EPS = 1e-6
Q_BLOCK = 128
CONV_K = 4

RW_HEADS = 8
RW_HEAD = 64
RW_DECAY_LORA = 64
RW_A_LORA = 64
RW_GN_EPS = 64e-5
RW_PRE = 3 * BR_WIDTH + RW_DECAY_LORA + RW_A_LORA

MLA_HEADS = 8
MLA_NOPE = 64
MLA_ROPE = 32
MLA_QK = MLA_NOPE + MLA_ROPE
MLA_V = 64
MLA_Q_RANK = 384
MLA_KV_RANK = 256
ROPE_THETA = 10000.0

SSD_HEADS = 8
SSD_HEAD_DIM = 64
SSD_GROUPS = 2
SSD_HPG = SSD_HEADS // SSD_GROUPS
SSD_STATE = 64
SSD_CHUNK = 128
SSD_CONV_DIM = BR_WIDTH + 2 * SSD_GROUPS * SSD_STATE

LRU_WIDTH = 512
LRU_BLOCKS = 8
LRU_BLOCK = LRU_WIDTH // LRU_BLOCKS
LRU_C = 8.0

IN_WIDTHS = (RW_PRE, BR_WIDTH, MLA_Q_RANK, MLA_KV_RANK + MLA_ROPE, BR_WIDTH, BR_WIDTH, SSD_CONV_DIM, 2 * SSD_HEADS, LRU_WIDTH, LRU_WIDTH, N_BRANCH * D_MODEL)
N_IN = RW_PRE + MLA_Q_RANK + MLA_KV_RANK + MLA_ROPE + 3 * BR_WIDTH + SSD_CONV_DIM + 2 * SSD_HEADS + 2 * LRU_WIDTH + N_BRANCH * D_MODEL

F32 = jnp.float32

kernel_name = 'hybrid_rwkv_mla_ssd_lru_diffusion_step'


def split_cols(z, widths):
    idx = []
    acc = 0
    for w in widths[:-1]:
        acc += w
        idx.append(acc)
    return jnp.split(z, idx, axis=-1)


def rmsnorm(x, g):
    x32 = x.astype(F32)
    y = x32 * lax.rsqrt(jnp.mean(x32 * x32, axis=-1, keepdims=True) + EPS)
    return (y * g).astype(x.dtype)


def centred_dwconv(x, w, bias):
    y = lax.conv_general_dilated(x, w[:, None, :].astype(x.dtype), window_strides=(1,),
                                 padding=[(CONV_K // 2, CONV_K - 1 - CONV_K // 2)],
                                 dimension_numbers=('NWC', 'WIO', 'NWC'),
                                 feature_group_count=x.shape[-1])
    return y + bias


def centred_shift(u):
    up = jnp.pad(u, ((0, 0), (1, 1), (0, 0)))
    return 0.5 * (up[:, :-2] + up[:, 2:])


def segsum(x):
    t = x.shape[-1]
    xe = jnp.broadcast_to(x[..., None], x.shape + (t,))
    xe = jnp.where(jnp.tril(jnp.ones((t, t), bool), -1), xe, 0.0)
    xs = jnp.cumsum(xe, axis=-2)
    return jnp.where(jnp.tril(jnp.ones((t, t), bool), 0), xs, -jnp.inf)


def axial_rope(x):
    n = x.shape[1]
    rows = n // GRID_W
    row = jnp.broadcast_to(jnp.arange(rows)[:, None], (rows, GRID_W)).reshape(n).astype(F32)
    col = jnp.broadcast_to(jnp.arange(GRID_W)[None, :], (rows, GRID_W)).reshape(n).astype(F32)
    half = MLA_ROPE // 2
    nf = half // 2
    inv = ROPE_THETA ** (-jnp.arange(nf, dtype=F32) / nf)
    out = []
    for pos, xa in ((row, x[..., :half]), (col, x[..., half:])):
        ang = pos[:, None] * inv[None, :]
        cos = jnp.cos(ang)[:, None, :]
        sin = jnp.sin(ang)[:, None, :]
        x1, x2 = xa[..., :nf], xa[..., nf:]
        out += [x1 * cos - x2 * sin, x1 * sin + x2 * cos]
    return jnp.concatenate(out, axis=-1).astype(x.dtype)


def rotate_latent(t):
    return jnp.concatenate([t[..., :MLA_NOPE], axial_rope(t[..., MLA_NOPE:])], axis=-1)


def block_attention(q, k, v):
    b, nq, h, dk = q.shape
    qb = jnp.moveaxis(q.reshape(b, nq // Q_BLOCK, Q_BLOCK, h, dk), 1, 0)
    scale = dk ** -0.5

    def one(qblk):
        s = jnp.einsum('bqhd,bkhd->bhqk', qblk, k).astype(F32) * scale
        p = jax.nn.softmax(s, axis=-1).astype(v.dtype)
        return jnp.einsum('bhqk,bkhd->bqhd', p, v)

    o = lax.map(one, qb)
    return jnp.moveaxis(o, 0, 1).reshape(b, nq, h, v.shape[-1])


def rwkv7_scan(r, decay, k, v, kk, a, s0, reverse):
    def step(s, inp):
        r_t, w_t, k_t, v_t, kk_t, a_t = inp
        s_kk = jnp.einsum('bhij,bhj->bhi', s, kk_t)
        s = (s * w_t[:, :, None, :] - s_kk[..., None] * (kk_t * a_t)[:, :, None, :]
             + v_t[..., None] * k_t[:, :, None, :])
        return s, jnp.einsum('bhij,bhj->bhi', s, r_t)
    xs = tuple(jnp.moveaxis(t, 1, 0) for t in (r, decay, k, v, kk, a))
    s_fin, out = lax.scan(step, s0.astype(F32), xs, reverse=reverse)
    return jnp.moveaxis(out, 0, 1), s_fin


def rwkv_branch(u, P, l, s0):
    b, n, _ = u.shape
    dtype = u.dtype
    u = (u + P['rw_mu'][l] * (centred_shift(u) - u)).astype(F32)
    r, k, v, w_lo, a_lo = split_cols(u, (BR_WIDTH, BR_WIDTH, BR_WIDTH, RW_DECAY_LORA, RW_A_LORA))
    heads = lambda t: t.reshape(b, n, RW_HEADS, RW_HEAD)
    r, k, v = heads(r), heads(k), heads(v)
    kk = k * P['rw_kk'][l].reshape(RW_HEADS, RW_HEAD)
    kk = kk * lax.rsqrt(jnp.sum(kk * kk, axis=-1, keepdims=True) + 1e-12)
    k_a = P['rw_ka'][l].reshape(RW_HEADS, RW_HEAD)
    r_k = P['rw_rk'][l]
    w_lo = jnp.tanh(w_lo)
    wkv = 0.0
    bonus = 0.0
    finals = []
    for d in range(2):
        w_raw = P['rw_w0'][l, d] + jnp.matmul(w_lo, P['rw_w2'][l, d])
        decay = heads(jnp.exp(-jnp.exp(-jax.nn.softplus(-w_raw) - 0.5)))
        a = heads(jax.nn.sigmoid(P['rw_a0'][l, d] + jnp.matmul(a_lo, P['rw_a2'][l, d])))
        k_d = k * (1.0 + (a - 1.0) * k_a)
        o_d, s_d = rwkv7_scan(r, decay, k_d, v, kk, a, s0[:, d], reverse=(d == 1))
        wkv = wkv + o_d
        bonus = bonus + jnp.sum(r * k_d * r_k, axis=-1, keepdims=True) * v
        finals.append(s_d)
    mu = jnp.mean(wkv, axis=-1, keepdims=True)
    var = jnp.mean(jnp.square(wkv - mu), axis=-1, keepdims=True)
    o = ((wkv - mu) * lax.rsqrt(var + RW_GN_EPS) * P['rw_ln_g'][l].reshape(RW_HEADS, RW_HEAD)
         + P['rw_ln_b'][l].reshape(RW_HEADS, RW_HEAD) + bonus)
    return o.reshape(b, n, BR_WIDTH).astype(dtype), jnp.stack(finals, axis=1)


def mla_keys_values(c_kv, k_rope, kv_up, kn_g):
    b, n, _ = c_kv.shape
    kv = jnp.matmul(c_kv, kv_up).reshape(b, n, MLA_HEADS, MLA_NOPE + MLA_V)
    k_nope, v = kv[..., :MLA_NOPE], kv[..., MLA_NOPE:]
    k = jnp.concatenate([k_nope, jnp.broadcast_to(k_rope[:, :, None, :], (b, n, MLA_HEADS, MLA_ROPE)).astype(k_nope.dtype)], axis=-1)
    return rmsnorm(k, kn_g), v


def mla_branch(q_lat, kv_lat, P, l, ctx_kv):
    b, n, _ = q_lat.shape
    q = jnp.matmul(rmsnorm(q_lat, P['mla_qa_g'][l]), P['mla_q_up'][l]).reshape(b, n, MLA_HEADS, MLA_QK)
    q = rmsnorm(q, P['mla_qn_g'][l])
    c_kv, k_rope = split_cols(kv_lat, (MLA_KV_RANK, MLA_ROPE))
    c_kv = rmsnorm(c_kv, P['mla_kva_g'][l])
    k, v = mla_keys_values(c_kv, k_rope, P['mla_kv_up'][l], P['mla_kn_g'][l])
    if ctx_kv is None:
        o = block_attention(q, k, v)
    else:
        q, k = rotate_latent(q), rotate_latent(k)
        k_ctx, v_ctx = mla_keys_values(ctx_kv[0], ctx_kv[1], P['mla_kv_up'][l], P['mla_kn_g'][l])
        k_all = jnp.concatenate([k_ctx.astype(k.dtype), k], axis=1)
        v_all = jnp.concatenate([v_ctx.astype(v.dtype), v], axis=1)
        o = block_attention(q, k_all, v_all)
    return o.reshape(b, n, MLA_HEADS * MLA_V), c_kv, k_rope


def ssd_chunked(x, dtA, bm, cm, h0):
    b, n, g, r, p = x.shape
    c, ln = n // SSD_CHUNK, SSD_CHUNK
    x = x.reshape(b, c, ln, g, r, p)
    bm = bm.reshape(b, c, ln, g, SSD_STATE)
    cm = cm.reshape(b, c, ln, g, SSD_STATE)
    a = jnp.moveaxis(dtA.reshape(b, c, ln, g, r), 2, -1)
    a_cs = jnp.cumsum(a, axis=-1)
    lmat = jnp.exp(segsum(a))
    y_diag = jnp.einsum('bclgn,bcsgn,bcgrls,bcsgrp->bclgrp', cm, bm, lmat, x)
    decay_states = jnp.exp(a_cs[..., -1:] - a_cs)
    states = jnp.einsum('bclgn,bcgrl,bclgrp->bcgrpn', bm, decay_states, x)
    states = jnp.concatenate([h0[:, None], states], axis=1)
    chunk_a = jnp.pad(jnp.moveaxis(a_cs[..., -1], 1, -1), ((0, 0), (0, 0), (0, 0), (1, 0)))
    decay_chunk = jnp.exp(segsum(chunk_a))
    new_states = jnp.einsum('bgrzc,bcgrpn->bzgrpn', decay_chunk, states)
    states, final = new_states[:, :-1], new_states[:, -1]
    y_off = jnp.einsum('bclgn,bcgrpn,bcgrl->bclgrp', cm, states, jnp.exp(a_cs))
    return (y_diag + y_off).reshape(b, n, g, r, p), final


def ssd_branch(zg, xbc, dt_raw, P, l, h0):
    b, n, _ = xbc.shape
    dtype = xbc.dtype
    xbc = jax.nn.silu(centred_dwconv(xbc, P['ssd_conv_w'][l], P['ssd_conv_b'][l])).astype(F32)
    xs, bm, cm = split_cols(xbc, (BR_WIDTH, SSD_GROUPS * SSD_STATE, SSD_GROUPS * SSD_STATE))
    xs = xs.reshape(b, n, SSD_GROUPS, SSD_HPG, SSD_HEAD_DIM)
    bm = bm.reshape(b, n, SSD_GROUPS, SSD_STATE)
    cm = cm.reshape(b, n, SSD_GROUPS, SSD_STATE)
    dt_raw = dt_raw.astype(F32).reshape(b, n, 2, SSD_HEADS)
    y = xs * P['ssd_d'][l].reshape(SSD_GROUPS, SSD_HPG)[..., None]
    finals = []
    for d in range(2):
        dt = jax.nn.softplus(dt_raw[:, :, d] + P['ssd_dt_bias'][l, d]).reshape(b, n, SSD_GROUPS, SSD_HPG)
        dtA = dt * (-jnp.exp(P['ssd_a_log'][l, d])).reshape(SSD_GROUPS, SSD_HPG)
        h0_d = h0[:, d].astype(F32).reshape(b, SSD_GROUPS, SSD_HPG, SSD_HEAD_DIM, SSD_STATE)
        args = (xs * dt[..., None], dtA, bm, cm)
        if d == 1:
            args = tuple(jnp.flip(t, axis=1) for t in args)
        y_d, h_d = ssd_chunked(*args, h0_d)
        if d == 1:
            y_d = jnp.flip(y_d, axis=1)
        y = y + y_d
        finals.append(h_d.reshape(b, SSD_HEADS, SSD_HEAD_DIM, SSD_STATE))
    y = y.reshape(b, n, BR_WIDTH) * jax.nn.silu(zg.astype(F32))
    return rmsnorm(y, P['ssd_norm_g'][l]).astype(dtype), jnp.stack(finals, axis=1)


def linear_scan(a, u, h0, reverse):
    def combine(e1, e2):
        a1, b1 = e1
        a2, b2 = e2
        return a1 * a2, a2 * b1 + b2
    a_cum, b_cum = lax.associative_scan(combine, (a, u), axis=1, reverse=reverse)
    h = a_cum * h0[:, None, :] + b_cum
    final = h[:, 0] if reverse else h[:, -1]
    return h, final


def lru_branch(xl, P, l, h0):
    b, n, _ = xl.shape
    dtype = xl.dtype
    xc = centred_dwconv(xl, P['lru_conv_w'][l], P['lru_conv_b'][l]).astype(F32)
    xb = xc.reshape(b, n, LRU_BLOCKS, LRU_BLOCK)
    y = 0.0
    finals = []
    for d in range(2):
        g_r = jax.nn.sigmoid(jnp.einsum('bnki,kij->bnkj', xb, P['lru_wa'][l, d]).reshape(b, n, LRU_WIDTH) + P['lru_ba'][l, d])
        g_i = jax.nn.sigmoid(jnp.einsum('bnki,kij->bnkj', xb, P['lru_wx'][l, d]).reshape(b, n, LRU_WIDTH) + P['lru_bx'][l, d])
        log_a = -LRU_C * g_r * jax.nn.softplus(-P['lru_lambda'][l, d])
        a = jnp.exp(log_a)
        u = jnp.sqrt(-jnp.expm1(2.0 * log_a)) * (g_i * xc)
        h, h_fin = linear_scan(a, u, h0[:, d].astype(F32), reverse=(d == 1))
        y = y + h
        finals.append(h_fin)
    return y.astype(dtype), jnp.stack(finals, axis=1)


def trunk_layer(x, cond, P, l, ctx):
    b, n, _ = x.shape
    mod = jnp.matmul(jax.nn.silu(cond), P['ada_w'][l]) + P['ada_b'][l]
    shift, scale, gate = jnp.split(mod[:, None, :], 3, axis=-1)
    h = rmsnorm(x, P['norm_g'][l]) * (1 + scale) + shift
    z = jnp.matmul(h, P['w_in'][l])
    (rw_pre, rw_gate, mla_q, mla_kv, mla_gate, ssd_gate, ssd_xbc, ssd_dt,
     lru_x, lru_gate, merge_logits) = split_cols(z, IN_WIDTHS)
    if ctx is None:
        ctx_kv = None
        rw0 = jnp.zeros((b, 2, RW_HEADS, RW_HEAD, RW_HEAD), F32)
        ssd0 = jnp.zeros((b, 2, SSD_HEADS, SSD_HEAD_DIM, SSD_STATE), F32)
        lru0 = jnp.zeros((b, 2, LRU_WIDTH), F32)
    else:
        ckv0, krope0, rw0, ssd0, lru0 = ctx
        ctx_kv = (ckv0, krope0)
    o_rw, rw_fin = rwkv_branch(rw_pre, P, l, rw0)
    o_mla, c_kv, k_rope = mla_branch(mla_q, mla_kv, P, l, ctx_kv)
    o_ssd, ssd_fin = ssd_branch(ssd_gate, ssd_xbc, ssd_dt, P, l, ssd0)
    o_lru, lru_fin = lru_branch(lru_x, P, l, lru0)
    branches = jnp.stack([o_rw * jax.nn.silu(rw_gate), o_mla * jax.nn.silu(mla_gate),
                          o_ssd, o_lru * jax.nn.silu(lru_gate)], axis=2)
    proj = jnp.einsum('bnmc,mcd->bnmd', branches, P['w_branch'][l])
    gates = jax.nn.sigmoid(merge_logits.reshape(b, n, N_BRANCH, D_MODEL))
    merged = jnp.sum(gates * proj, axis=2)
    y = x + gate * jnp.matmul(merged, P['w_out'][l])
    return y, (c_kv, k_rope, rw_fin, ssd_fin, lru_fin)


def setup_inputs(seed: int = 0) -> dict:
    key = jax.random.key(seed)
    ks = iter(jax.random.split(key, 64))
    L = DEPTH

    def nrm(shape, scale=1.0):
        return scale * jax.random.normal(next(ks), shape, F32)

    def gain(shape):
        return 1.0 + 0.05 * jax.random.normal(next(ks), shape, F32)

    def unif(shape, lo, hi):
        return jax.random.uniform(next(ks), shape, F32, lo, hi)

    dt0 = jnp.exp(unif((L, 2, SSD_HEADS), np.log(1e-3).item(), np.log(1e-1).item()))
    ssd_dt_bias = dt0 + jnp.log(-jnp.expm1(-dt0))
    a_lru = unif((L, 2, LRU_WIDTH), 0.9, 0.999) ** (1.0 / LRU_C)
    lru_lambda = jnp.log(a_lru) - jnp.log1p(-a_lru)
    return {
        'x_prompt': nrm((BATCH, SEQ, D_MODEL)),
        'x_sample': nrm((DEC_BATCH, DEC_SEQ, D_MODEL)),
        'cache_mla_ckv': nrm((DEC_BATCH, DEPTH, PAST_LEN, MLA_KV_RANK)),
        'cache_mla_krope': nrm((DEC_BATCH, DEPTH, PAST_LEN, MLA_ROPE)),
        'state_rwkv': nrm((DEC_BATCH, DEPTH, 2, RW_HEADS, RW_HEAD, RW_HEAD), 0.1),
        'state_ssd': nrm((DEC_BATCH, DEPTH, 2, SSD_HEADS, SSD_HEAD_DIM, SSD_STATE), 0.1),
        'state_lru': nrm((DEC_BATCH, DEPTH, 2, LRU_WIDTH), 0.5),
        'c': nrm((DEC_BATCH, D_MODEL)),
        'c_ctx': nrm((D_MODEL,)),
        'ada_w': nrm((L, D_MODEL, 3 * D_MODEL), D_MODEL ** -0.5),
        'ada_b': nrm((L, 3 * D_MODEL), 0.02),
        'norm_g': gain((L, D_MODEL)),
        'w_in': nrm((L, D_MODEL, N_IN), D_MODEL ** -0.5),
        'rw_mu': unif((L, RW_PRE), 0.0, 1.0),
        'rw_w0': unif((L, 2, BR_WIDTH), -5.0, 1.0),
        'rw_w2': nrm((L, 2, RW_DECAY_LORA, BR_WIDTH), 0.5 * RW_DECAY_LORA ** -0.5),
        'rw_a0': nrm((L, 2, BR_WIDTH), 0.5),
        'rw_a2': nrm((L, 2, RW_A_LORA, BR_WIDTH), 0.5 * RW_A_LORA ** -0.5),
        'rw_kk': gain((L, BR_WIDTH)),
        'rw_ka': gain((L, BR_WIDTH)),
        'rw_rk': nrm((L, RW_HEADS, RW_HEAD), 0.1),
        'rw_ln_g': gain((L, BR_WIDTH)),
        'rw_ln_b': nrm((L, BR_WIDTH), 0.02),
        'mla_qa_g': gain((L, MLA_Q_RANK)),
        'mla_q_up': nrm((L, MLA_Q_RANK, MLA_HEADS * MLA_QK), MLA_Q_RANK ** -0.5),
        'mla_kva_g': gain((L, MLA_KV_RANK)),
        'mla_kv_up': nrm((L, MLA_KV_RANK, MLA_HEADS * (MLA_NOPE + MLA_V)), MLA_KV_RANK ** -0.5),
        'mla_qn_g': gain((L, MLA_QK)),
        'mla_kn_g': gain((L, MLA_QK)),
        'ssd_conv_w': nrm((L, CONV_K, SSD_CONV_DIM), CONV_K ** -0.5),
        'ssd_conv_b': nrm((L, SSD_CONV_DIM), 0.02),
        'ssd_dt_bias': ssd_dt_bias,
        'ssd_a_log': jnp.log(unif((L, 2, SSD_HEADS), 1.0, 16.0)),
        'ssd_d': gain((L, SSD_HEADS)),
        'ssd_norm_g': gain((L, BR_WIDTH)),
        'lru_conv_w': nrm((L, CONV_K, LRU_WIDTH), CONV_K ** -0.5),
        'lru_conv_b': nrm((L, LRU_WIDTH), 0.02),
        'lru_wa': nrm((L, 2, LRU_BLOCKS, LRU_BLOCK, LRU_BLOCK), LRU_BLOCK ** -0.5),
        'lru_ba': nrm((L, 2, LRU_WIDTH), 0.02),
        'lru_wx': nrm((L, 2, LRU_BLOCKS, LRU_BLOCK, LRU_BLOCK), LRU_BLOCK ** -0.5),
        'lru_bx': nrm((L, 2, LRU_WIDTH), 0.02),
        'lru_lambda': lru_lambda,
        'w_branch': nrm((L, N_BRANCH, BR_WIDTH, D_MODEL), BR_WIDTH ** -0.5),
        'w_out': nrm((L, D_MODEL, D_MODEL), D_MODEL ** -0.5),
    }


def reference(x_prompt, x_sample, cache_mla_ckv, cache_mla_krope, state_rwkv, state_ssd, state_lru,
              c, c_ctx, ada_w, ada_b, norm_g, w_in,
              rw_mu, rw_w0, rw_w2, rw_a0, rw_a2, rw_kk, rw_ka, rw_rk, rw_ln_g, rw_ln_b,
              mla_qa_g, mla_q_up, mla_kva_g, mla_kv_up, mla_qn_g, mla_kn_g,
              ssd_conv_w, ssd_conv_b, ssd_dt_bias, ssd_a_log, ssd_d, ssd_norm_g,
              lru_conv_w, lru_conv_b, lru_wa, lru_ba, lru_wx, lru_bx, lru_lambda,
              w_branch, w_out):
    P = dict(ada_w=ada_w, ada_b=ada_b, norm_g=norm_g, w_in=w_in,
             rw_mu=rw_mu, rw_w0=rw_w0, rw_w2=rw_w2, rw_a0=rw_a0, rw_a2=rw_a2, rw_kk=rw_kk,
             rw_ka=rw_ka, rw_rk=rw_rk, rw_ln_g=rw_ln_g, rw_ln_b=rw_ln_b,
             mla_qa_g=mla_qa_g, mla_q_up=mla_q_up, mla_kva_g=mla_kva_g, mla_kv_up=mla_kv_up,
             mla_qn_g=mla_qn_g, mla_kn_g=mla_kn_g,
             ssd_conv_w=ssd_conv_w, ssd_conv_b=ssd_conv_b, ssd_dt_bias=ssd_dt_bias,
             ssd_a_log=ssd_a_log, ssd_d=ssd_d, ssd_norm_g=ssd_norm_g,
             lru_conv_w=lru_conv_w, lru_conv_b=lru_conv_b, lru_wa=lru_wa, lru_ba=lru_ba,
             lru_wx=lru_wx, lru_bx=lru_bx, lru_lambda=lru_lambda,
             w_branch=w_branch, w_out=w_out)

    y_prompt = x_prompt
    cond_ctx = c_ctx[None, :]
    ctx_states = []
    for l in range(DEPTH):
        y_prompt, st = trunk_layer(y_prompt, cond_ctx, P, l, None)
        ctx_states.append(st)
    new_mla_ckv = jnp.stack([s[0] for s in ctx_states], axis=1)
    new_mla_krope = jnp.stack([s[1] for s in ctx_states], axis=1)
    new_rwkv = jnp.stack([s[2] for s in ctx_states], axis=1)
    new_ssd = jnp.stack([s[3] for s in ctx_states], axis=1)
    new_lru = jnp.stack([s[4] for s in ctx_states], axis=1)

    y_sample = x_sample
    for l in range(DEPTH):
        ctx = (cache_mla_ckv[:, l], cache_mla_krope[:, l], state_rwkv[:, l], state_ssd[:, l], state_lru[:, l])
        y_sample, _ = trunk_layer(y_sample, c, P, l, ctx)

    return (y_prompt, y_sample, new_mla_ckv, new_mla_krope, new_rwkv, new_ssd, new_lru)
```

```python
import contextlib
import numpy as np
import ml_dtypes
import concourse.bass as bass
import concourse.mybir as mybir
from concourse.bass_utils import run_bass_kernel_spmd

F32 = mybir.dt.float32
BF16 = mybir.dt.bfloat16
AF = mybir.ActivationFunctionType
ALU = mybir.AluOpType
AX = mybir.AxisListType

D = 1024
NL = 2
NDMA_SLOTS = 6
DEBUG_FLAGS = ()
SAME_ENGINE_SYNC = ("act", "dve", "pool")


class Res:
    __slots__ = ("w", "r", "name")

    def __init__(self, name=""):
        self.w = []
        self.r = {}
        self.name = name


class Tl:
    def __init__(self, h, name, psum=False):
        self.h = h
        self.res = Res(name)
        self.name = name
        self.psum = psum

    def __getitem__(self, k):
        return self.h[k]


class Eng:
    def __init__(self, name, h, sem):
        self.name = name
        self.h = h
        self.sem = sem
        self.count = 0
        self.waited = {}
        self.dma_sems = []
        self.dma_vals = []
        self.dma_n = 0


def _res(x):
    return x.res if isinstance(x, Tl) else x


class KB:
    def __init__(self, nc):
        self.nc = nc
        self.es = contextlib.ExitStack()
        self.stacks = [self.es]
        self.eng = {}
        self.uid = 0
        for name, h in (("pe", nc.tensor), ("dve", nc.vector), ("act", nc.scalar),
                        ("pool", nc.gpsimd), ("sp", nc.sync)):
            sem = self.es.enter_context(nc.semaphore("s_" + name))
            self.eng[name] = Eng(name, h, sem)
        for qn in ("sp", "pool", "act"):
            E = self.eng[qn]
            for i in range(NDMA_SLOTS):
                E.dma_sems.append(self.es.enter_context(nc.semaphore(f"d_{qn}{i}")))
                E.dma_vals.append(0)

    def sb(self, name, shape, dtype=F32):
        self.uid += 1
        nm = f"{name}_{self.uid}"
        h = self.stacks[-1].enter_context(self.nc.sbuf_tensor(nm, list(shape), dtype))
        return Tl(h, nm)

    def ps(self, name, shape, dtype=F32):
        self.uid += 1
        nm = f"{name}_{self.uid}"
        h = self.stacks[-1].enter_context(self.nc.psum_tensor(nm, list(shape), dtype))
        return Tl(h, nm, psum=True)

    def dram(self, name, shape, dtype=F32, kind="Internal"):
        h = self.nc.dram_tensor(name, list(shape), dtype, kind=kind)
        return Tl(h.ap(), name)

    @contextlib.contextmanager
    def phase(self):
        self.barrier()
        st = contextlib.ExitStack()
        self.stacks.append(st)
        try:
            with st:
                yield
                self.barrier()
        finally:
            self.stacks.pop()

    def _wait(self, E, ev):
        sem, val, src = ev
        if src is E and E.name not in SAME_ENGINE_SYNC:
            return
        k = id(sem)
        if E.waited.get(k, 0) >= val:
            return
        E.h.wait_ge(sem, val)
        E.waited[k] = val

    def _deps(self, E, reads, writes):
        for r in reads:
            for ev in _res(r).w:
                self._wait(E, ev)
        for w in writes:
            rs = _res(w)
            for ev in rs.w:
                self._wait(E, ev)
            for ev in rs.r.values():
                self._wait(E, ev)

    def _commit(self, ev, key, reads, writes):
        for r in reads:
            _res(r).r[key] = ev
        for w in writes:
            rs = _res(w)
            rs.w = [ev]
            rs.r = {}

    def op(self, en, fn, reads=(), writes=()):
        E = self.eng[en]
        pr = [r for r in reads if isinstance(r, Tl) and r.psum]
        if pr:
            reads = [r for r in reads if not (isinstance(r, Tl) and r.psum)]
            writes = list(writes) + [r for r in pr if r not in writes]
        self._deps(E, reads, writes)
        ins = fn(E.h)
        E.count += 1
        ins.then_inc(E.sem, 1)
        ev = (E.sem, E.count, E)
        self._commit(ev, en, reads, writes)
        return ins

    def dma(self, qn, out, in_, reads=(), writes=(), **kw):
        E = self.eng[qn]
        self._deps(E, reads, writes)
        slot = E.dma_n % NDMA_SLOTS
        E.dma_n += 1
        sem = E.dma_sems[slot]
        pv = E.dma_vals[slot]
        if pv > 0:
            self._wait(E, (sem, pv, None))
        E.h.dma_start(out=out, in_=in_, **kw).then_inc(sem, 16)
        E.dma_vals[slot] = pv + 16
        ev = (sem, pv + 16, None)
        self._commit(ev, ("dma", qn, slot), reads, writes)

    def barrier(self):
        evs = []
        for E in self.eng.values():
            if E.count:
                evs.append((E.sem, E.count, E))
            for s, v in zip(E.dma_sems, E.dma_vals):
                if v:
                    evs.append((s, v, None))
        for E in self.eng.values():
            for ev in evs:
                self._wait(E, ev)

    def mm(self, out, lhsT, rhs, start, stop, reads, writes):
        return self.op("pe", lambda e: e.matmul(out, lhsT=lhsT, rhs=rhs, start=start, stop=stop),
                       reads, writes)

    def tr(self, out, in_, ident, reads, writes):
        return self.op("pe", lambda e: e.transpose(out, in_, ident), reads, writes)

    def act(self, out, in_, func, reads, writes, bias=None, scale=None, accum_out=None, en="act"):
        kw = {}
        if bias is not None:
            kw["bias"] = bias
        if scale is not None:
            kw["scale"] = scale
        if accum_out is not None:
            kw["accum_out"] = accum_out
        return self.op(en, lambda e: e.activation(out=out, in_=in_, func=func, **kw), reads, writes)

    def tt(self, out, in0, in1, op, reads, writes, en="dve"):
        return self.op(en, lambda e: e.tensor_tensor(out=out, in0=in0, in1=in1, op=op), reads, writes)

    def ts(self, out, in0, s1, s2, op0, op1, reads, writes, en="dve", accum_out=None):
        kw = {}
        if accum_out is not None:
            kw["accum_out"] = accum_out
        if op1 is None:
            return self.op(en, lambda e: e.tensor_scalar(out=out, in0=in0, scalar1=s1, scalar2=None,
                                                         op0=op0, **kw), reads, writes)
        return self.op(en, lambda e: e.tensor_scalar(out=out, in0=in0, scalar1=s1, scalar2=s2,
                                                     op0=op0, op1=op1, **kw), reads, writes)

    def stt(self, out, in0, scalar, in1, op0, op1, reads, writes):
        return self.op("dve", lambda e: e.scalar_tensor_tensor(out=out, in0=in0, scalar=scalar, in1=in1,
                                                               op0=op0, op1=op1), reads, writes)

    def cp(self, out, in_, reads, writes, en="dve"):
        if en == "act":
            return self.op("act", lambda e: e.copy(out=out, in_=in_), reads, writes)
        return self.op(en, lambda e: e.tensor_copy(out=out, in_=in_), reads, writes)

    def scan(self, out, d0, d1, init, reads, writes, op0=ALU.mult, op1=ALU.add):
        return self.op("dve", lambda e: e.tensor_tensor_scan(out=out, data0=d0, data1=d1, initial=init,
                                                             op0=op0, op1=op1), reads, writes)

    def memset(self, ap, val, writes, en="pool"):
        return self.op(en, lambda e: e.memset(ap, val), (), writes)

    def finish(self):
        self.barrier()


ZROWS = {}
_o = 0
for _n, _w in (("rw_r", 512), ("rw_k", 512), ("rw_v", 512), ("rw_lora", 128), ("rw_gate", 512),
               ("mla_q", 384), ("mla_ckv", 256), ("misc", 128), ("mla_gate", 512),
               ("ssd_xbc", 768), ("lru_x", 512), ("lru_gate", 512)):
    ZROWS[_n] = (_o, _w)
    _o += _w
NZ_FM = _o
NZ_ALL = NZ_FM + 512
_IW = dict(rw_pre=0, rw_gate=1664, mla_q=2176, mla_ckv=2560, mla_krope=2816, mla_gate=2848,
           ssd_gate=3360, ssd_xbc=3872, ssd_dt=4640, lru_x=4656, lru_gate=5168, merge=5680)


class VecPack:
    def __init__(self):
        self.cols = {}
        self.n = 0
        self.data = []

    def add(self, name, arr2d):
        k = arr2d.shape[-1]
        self.cols[name] = (self.n, k)
        self.n += k
        self.data.append(np.asarray(arr2d, np.float32))

    def fm(self, name, v):
        v = np.asarray(v, np.float32)
        n = v.shape[-1]
        if n % 128:
            pad = 128 - n % 128
            v = np.concatenate([v, np.zeros(v.shape[:-1] + (pad,), np.float32)], -1)
        k = v.shape[-1] // 128
        self.add(name, v.reshape(v.shape[0], k, 128).transpose(0, 2, 1))

    def pack(self):
        return np.ascontiguousarray(np.concatenate(self.data, -1))


def vec_layout():
    vp = VecPack()
    z = lambda *s: np.zeros(s, np.float32)
    vp.fm("ada_b_ss", z(NL, 2048))
    vp.fm("norm_g", z(NL, 1024))
    vp.fm("mla_qa_g", z(NL, 384))
    vp.fm("mla_kva_g", z(NL, 256))
    vp.fm("mla_qn_g", z(NL, 96))
    vp.fm("mla_kn_g", z(NL, 96))
    vp.fm("rw_mu", z(NL, 1664))
    vp.fm("rw_w0", z(NL, 1024))
    vp.fm("rw_a0", z(NL, 1024))
    vp.fm("rw_kk", z(NL, 512))
    vp.fm("rw_ka", z(NL, 512))
    vp.fm("rw_rk", z(NL, 512))
    vp.fm("rw_ln_g", z(NL, 512))
    vp.fm("rw_ln_b", z(NL, 512))
    vp.fm("ssd_cw", z(NL, 4 * 768))
    vp.fm("ssd_cb", z(NL, 768))
    vp.fm("ssd_dtb", z(NL, 64))
    vp.fm("ssd_alog", z(NL, 64))
    vp.fm("lru_cw", z(NL, 4 * 512))
    vp.fm("lru_cb", z(NL, 512))
    vp.fm("lru_ba", z(NL, 2 * 512))
    vp.fm("lru_bx", z(NL, 2 * 512))
    vp.fm("lru_lam", z(NL, 2 * 512))
    return vp


def host_vecs(inp):
    vp = VecPack()
    vp.fm("ada_b_ss", inp["ada_b"][:, :2048])
    vp.fm("norm_g", inp["norm_g"])
    r2 = lambda a: np.asarray(a, np.float32).reshape(NL, -1)
    vp.fm("mla_qa_g", inp["mla_qa_g"])
    vp.fm("mla_kva_g", inp["mla_kva_g"])
    vp.fm("mla_qn_g", inp["mla_qn_g"])
    vp.fm("mla_kn_g", inp["mla_kn_g"])
    vp.fm("rw_mu", inp["rw_mu"])
    vp.fm("rw_w0", r2(inp["rw_w0"]))
    vp.fm("rw_a0", r2(inp["rw_a0"]))
    vp.fm("rw_kk", inp["rw_kk"])
    vp.fm("rw_ka", inp["rw_ka"])
    vp.fm("rw_rk", r2(inp["rw_rk"]))
    vp.fm("rw_ln_g", inp["rw_ln_g"])
    vp.fm("rw_ln_b", inp["rw_ln_b"])
    vp.fm("ssd_cw", r2(inp["ssd_conv_w"]))
    vp.fm("ssd_cb", inp["ssd_conv_b"])
    def d64(a):
        a = np.asarray(a, np.float32)
        o_ = np.zeros((NL, 64), np.float32)
        o_[:, 0:8] = a[:, 0]
        o_[:, 32:40] = a[:, 1]
        return o_
    vp.fm("ssd_dtb", d64(inp["ssd_dt_bias"]))
    vp.fm("ssd_alog", d64(inp["ssd_a_log"]))
    vp.fm("lru_cw", r2(inp["lru_conv_w"]))
    vp.fm("lru_cb", inp["lru_conv_b"])
    vp.fm("lru_ba", r2(inp["lru_ba"]))
    vp.fm("lru_bx", r2(inp["lru_bx"]))
    vp.fm("lru_lam", r2(inp["lru_lambda"]))
    return vp


class Cfg:
    def __init__(self, n_prompt=4, lp=256, ls=4096, debug=()):
        self.n_prompt = n_prompt
        self.lp = lp
        self.ls = ls
        self.TP = n_prompt * lp
        self.T = self.TP + ls
        self.seqs = [(i * lp, lp, False) for i in range(n_prompt)] + [(self.TP, ls, True)]
        self.debug = set(debug)
        assert self.T % 512 == 0 and lp % 128 == 0 and ls % 512 == 0 and self.TP % 512 == 0


class Prog:
    def __init__(self, cfg):
        self.cfg = cfg
        self.nc = bass.Bass("TRN2", target_bir_lowering=False)
        self.k = KB(self.nc)
        self.inputs = {}
        self.outputs = {}
        self.vl = vec_layout()

    def din(self, name, shape, dtype=F32):
        t = self.k.dram(name, shape, dtype, kind="ExternalInput")
        self.inputs[name] = t
        return t

    def dout(self, name, shape, dtype=F32):
        t = self.k.dram(name, shape, dtype, kind="ExternalOutput")
        self.outputs[name] = t
        return t

    def vcol(self, name, j=0, n=1):
        o, k = self.vl.cols[name]
        assert j + n <= k
        return self.vecs[:, o + j:o + j + n]

    def build(self):
        cfg, k = self.cfg, self.k
        T = cfg.T
        with k.es:
            self.x_in = self.din("x_all", [T, D])
            self.condT = self.din("condT", [128, 8, 2])
            self.ada_w = self.din("ada_w", [NL, D, 3 * D])
            self.ada_bg = self.din("ada_bg", [NL, 1, D])
            self.w_in_a = self.din("w_in_a", [NL, D, NZ_ALL])
            self.vecs_d = self.din("vecs", [NL, 128, self.vl.n])
            self.ident_d = self.din("ident", [128, 128])
            self.sel2_d = self.din("sel2", [2, 2, 128])
            self.lru_w = self.din("lru_w", [NL, 128, 16, 128])
            self.lru_h0 = self.din("lru_h0", [NL, 128, 2, 4])
            self.o_lru_fin = self.dout("o_lru_fin", [NL, 128, cfg.n_prompt, 2, 4])
            self.q_up = self.din("q_up", [NL, 384, 768])
            self.kv_up_k = self.din("kv_up_k", [NL, 256, 512])
            self.kv_up_v = self.din("kv_up_v", [NL, 256, 512])
            self.cache_ckv = self.din("cache_ckv", [NL, 256, 256])
            self.cache_kr = self.din("cache_kr", [NL, 256, 32])
            self.rope_cs = self.din("rope_cs", [2, 32, cfg.ls])
            self.pm96_d = self.din("pm96", [32, 96])
            self.o_ckv = self.dout("o_ckv", [NL, 256, cfg.TP])
            self.o_kr = self.dout("o_kr", [NL, 32, cfg.TP])
            self.ssd_bc = self.din("ssd_bc", [NL, 128, 8 + 512])
            self.ssd_h0 = self.din("ssd_h0", [NL, 2, 64, 8, 64])
            self.selm_d = self.din("selm", [64, 16, 128])
            self.seld_d = self.din("seld", [64, 2, 8])
            self.selg_d = self.din("selg", [64, 128])
            self.maskneg_d = self.din("maskneg", [128, 2, 128])
            self.cmask_d = self.din("cmask", [64, T + 1])
            self.o_ssd_fin = self.dout("o_ssd_fin", [NL, cfg.n_prompt, 2, 64, 8, 64])
            self.ssd_y = k.dram("ssd_y_scr", [T, 512])
            self.rw_lw = self.din("rw_lw", [NL, 128, 2, 512])
            self.rw_h0 = self.din("rw_h0", [NL, 2, 64, 8, 64])
            self.rw_mask_d = self.din("rw_mask", [128, 2, 3, 128])
            self.blk64_d = self.din("blk64", [128, 128])
            self.pmask_d = self.din("pmask", [128, 2])
            self.cmask64_d = self.din("cmask64", [128, 513])
            self.o_rw_fin = self.dout("o_rw_fin", [cfg.n_prompt, NL, 2, 8, 64, 64])
            self.rw_ops = k.dram("rw_ops_scr", [2, 6, 512, T])
            self.rw_v = k.dram("rw_v_scr", [512, T])
            self.rw_bonus = k.dram("rw_bonus_scr", [512, T])
            self.rw_o = k.dram("rw_o_scr", [2, 512, T])
            self.w_in_m = self.din("w_in_m", [NL, D, 4 * D])
            self.w_branch = self.din("w_branch", [NL, 4, 512, D])
            self.w_out = self.din("w_out", [NL, D, D])
            self.y_all = self.dout("y_all", [T, D])
            self.y1 = k.dram("y1_scr", [T, D])
            self.mT_scr = k.dram("mT_scr", [D, T], BF16)
            self.o_scr = k.dram("o_scr", [4, 512, T], BF16)
            if "o" in cfg.debug:
                self.dbg_o = self.dout("dbg_o", [4, 512, T], BF16)
            self.zT = k.dram("zT_scr", [NZ_FM, T])
            self.zg_tm = k.dram("zg_tm_scr", [T, 512])
            self.hT_scr = k.dram("hT_scr", [D, T], BF16)
            if "z" in cfg.debug:
                self.dbg_zT = self.dout("dbg_zT", [NZ_FM, T])
                self.dbg_zg = self.dout("dbg_zg", [T, 512])
            self.vecs = k.sb("vecs", [128, self.vl.n])
            self.ident_f = k.sb("ident_f", [128, 128])
            self.ident_b = k.sb("ident_b", [128, 128], BF16)
            self.sel2 = k.sb("sel2", [2, 2, 128])
            self.sc = k.sb("sc", [128, 8, 2])
            self.gmod = k.sb("gmod", [128, 8, 2])
            self.shiftc = k.sb("shiftc", [128, 8, 2])
            self.gate_bc = [k.sb(f"gate_bc{g}", [128, D]) for g in range(2)]
            self.ones_f = k.sb("ones_f", [128, 128])
            k.memset(self.ones_f[:], 1.0, [self.ones_f])
            self.pf = [k.ps(f"pf{i}", [128, 512]) for i in range(6)]
            self.pb = [k.ps(f"pb{i}", [128, 1024], BF16) for i in range(2)]
            self.pfi = 0
            k.dma("sp", self.ident_f[:], self.ident_d[:, :], (), [self.ident_f])
            k.dma("sp", self.sel2[:], self.sel2_d[:, :, :], (), [self.sel2])
            k.cp(self.ident_b[:], self.ident_f[:], [self.ident_f], [self.ident_b])
            k.dma("sp", self.sc[:], self.condT[:, :, :], (), [self.sc])
            k.act(self.sc[:], self.sc[:], AF.Silu, [self.sc], [self.sc])

            for l in range(NL):
                self.layer(l)
                if l == 0 and "stop0" in cfg.debug:
                    break
            k.finish()
        return self.nc

    def dump(self, name, tl, ap, shape, dtype=F32):
        if name not in self.cfg.debug:
            return
        t = self.dout("dump_" + name, shape, dtype)
        self.k.dma("sp", t[tuple(slice(None) for _ in shape)], ap, [tl], [t])

    def next_pf(self):
        p = self.pf[self.pfi % 4]
        self.pfi += 1
        return p

    def layer(self, l):
        cfg, k = self.cfg, self.k
        with k.phase():
            k.dma("sp", self.vecs[:], self.vecs_d[l], (), [self.vecs])
            self.phase_mod(l)
        if "zero_o" in cfg.debug:
            with k.phase():
                zt = k.sb("zt", [128, cfg.T], BF16)
                k.memset(zt[:], 0.0, [zt])
                for m in range(4):
                    for ft in range(4):
                        k.dma("sp", self.o_scr[m, ft * 128:(ft + 1) * 128, :], zt[:], [zt], [self.o_scr])
        with k.phase():
            self.phase_front(l)
        with k.phase():
            self.phase_lru(l)
        if "norw" not in cfg.debug:
            with k.phase():
                self.phase_rwkv(l)
        if "nossd" not in cfg.debug:
            with k.phase():
                self.phase_ssd(l)
        if "nomla" not in cfg.debug:
            with k.phase():
                self.phase_mla(l)
        with k.phase():
            self.phase_merge(l)
        with k.phase():
            self.phase_out(l)
        if "y1" in cfg.debug and l == 0:
            k.barrier()
            d_ = self.dout("dbg_y1", [cfg.T, D])
            k.dma("sp", d_[:, :], self.y1[:, :], [self.y1], [d_])
            k.barrier()
        if "o" in cfg.debug and l == 0:
            k.barrier()
            k.dma("sp", self.dbg_o[:, :, :], self.o_scr[:, :, :], [self.o_scr], [self.dbg_o])
            k.barrier()

    def phase_mod(self, l):
        k = self.k
        wst = [k.sb(f"adaw{i}", [128, 8, 512]) for i in range(2)]
        modc = k.sb("modc", [128, 16, 2])
        grow = k.sb("grow", [2, D])
        gb = k.sb("gb", [2, D])
        for g in range(2):
            k.dma("pool", gb[g:g + 1, :], self.ada_bg[l], (), [gb])
        for blk in range(6):
            w = wst[blk % 2]
            k.dma("sp", w[:], self.ada_w[l][:, blk * 512:(blk + 1) * 512].rearrange("(c p) n -> p c n", p=128),
                  (), [w])
            if blk < 4:
                for jt in range(4):
                    ps = self.next_pf()
                    for c in range(8):
                        k.mm(ps[:, 0:2], w[:, c, jt * 128:(jt + 1) * 128], self.sc[:, c, :], c == 0, c == 7,
                             [w, self.sc], [ps])
                    k.cp(modc[:, blk * 4 + jt, :], ps[:, 0:2], [ps], [modc])
            else:
                ps = self.next_pf()
                for c in range(8):
                    k.mm(ps[0:2, :], self.sc[:, c, :], w[:, c, :], c == 0, c == 7, [w, self.sc], [ps])
                hs = slice((blk - 4) * 512, (blk - 3) * 512)
                k.tt(grow[:, hs], ps[0:2, :], gb[:, hs], ALU.add, [ps, gb], [grow])
        ab = self.vcol("ada_b_ss", 0, 16)
        for g in range(2):
            k.tt(modc[:, :, g], modc[:, :, g], ab, ALU.add, [modc, self.vecs], [modc])
            k.cp(self.shiftc[:, :, g], modc[:, 0:8, g], [modc], [self.shiftc])
            k.stt(self.gmod[:, :, g], modc[:, 8:16, g], 1.0, self.vcol("norm_g", 0, 8), ALU.add, ALU.mult,
                  [modc, self.vecs], [self.gmod])
            for half in range(2):
                ps = self.next_pf()
                k.mm(ps[:, :], self.sel2[:, g, :], grow[:, half * 512:(half + 1) * 512], True, True,
                     [self.sel2, grow], [ps])
                k.cp(self.gate_bc[g][:, half * 512:(half + 1) * 512], ps[:, :], [ps], [self.gate_bc[g]])

    def phase_front(self, l):
        cfg, k = self.cfg, self.k
        T = cfg.T
        x_src = self.x_in if l == 0 else self.y1
        hT = k.sb("hT", [128, 8, T], BF16)
        with k.phase():
            xt = [k.sb(f"xt{i}", [128, D]) for i in range(3)]
            xn = [k.sb(f"xn{i}", [128, D], BF16) for i in range(2)]
            junk = k.sb("junk", [128, D], BF16)
            ss = [k.sb(f"ss{i}", [128, 1]) for i in range(2)]
            for st in range(T // 128):
                g = 0 if st * 128 < cfg.TP else 1
                x = xt[st % 3]
                xb = xn[st % 2]
                s = ss[st % 2]
                pb = self.pb[st % 2]
                k.dma("sp", x[:], x_src[st * 128:(st + 1) * 128, :], [x_src], [x])
                k.act(junk[:], x[:], AF.Square, [x], [junk, s], accum_out=s[:])
                k.ts(s[:], s[:], 1.0 / D, 1e-6, ALU.mult, ALU.add, [s], [s])
                k.act(s[:], s[:], AF.Sqrt, [s], [s])
                k.op("dve", lambda e: e.reciprocal(out=s[:], in_=s[:]), [s], [s])
                k.act(xb[:], x[:], AF.Copy, [x, s], [xb], scale=s[:])
                for c in range(8):
                    k.tr(pb[:, c * 128:(c + 1) * 128], xb[:, c * 128:(c + 1) * 128], self.ident_b[:],
                         [xb, self.ident_b], [pb])
                ho = hT[:, :, st * 128:(st + 1) * 128]
                pv = pb[:].rearrange("p (c t) -> p c t", c=8)
                k.tt(ho, pv, self.gmod[:, :, g:g + 1].to_broadcast([128, 8, 128]), ALU.mult,
                     [pb, self.gmod], [hT])
                k.tt(ho, ho, self.shiftc[:, :, g:g + 1].to_broadcast([128, 8, 128]), ALU.add,
                     [hT, self.shiftc], [hT])
        for c in range(8):
            k.dma("pool", self.hT_scr[c * 128:(c + 1) * 128, :], hT[:, c, :], [hT], [self.hT_scr])
        wst = [k.sb(f"wst{i}", [128, 8, 512]) for i in range(2)]
        wbf = [k.sb(f"wbf{i}", [128, 8, 512], BF16) for i in range(2)]
        zst = [k.sb(f"zst{i}", [128, 512]) for i in range(4)]
        blocks = [(c0, min(512, NZ_FM - c0), False) for c0 in range(0, NZ_FM, 512)] + [(NZ_FM, 512, True)]
        zi = 0
        for blk, (c0, ncol, tm_block) in enumerate(blocks):
            ws, wb = wst[blk % 2], wbf[blk % 2]
            k.dma("sp", ws[:, :, :ncol], self.w_in_a[l][:, c0:c0 + ncol].rearrange("(c p) n -> p c n", p=128),
                  (), [ws])
            k.cp(wb[:, :, :ncol], ws[:, :, :ncol], [ws], [wb], en="pool")
            for tt in range(T // 512):
                ts_ = slice(tt * 512, (tt + 1) * 512)
                if not tm_block:
                    for jt in range(ncol // 128):
                        ps = self.next_pf()
                        for c in range(8):
                            k.mm(ps[:, :], wb[:, c, jt * 128:(jt + 1) * 128], hT[:, c, ts_], c == 0, c == 7,
                                 [wb, hT], [ps])
                        z = zst[zi % 4]
                        if zi % 2 == 0:
                            k.cp(z[:], ps[:, :], [ps], [z])
                        else:
                            k.cp(z[:], ps[:, :], [ps], [z], en="act")
                        zi += 1
                        r0 = c0 + jt * 128
                        k.dma("pool", self.zT[r0:r0 + 128, ts_], z[:], [z], [self.zT])
                else:
                    for sub in range(4):
                        ps = self.next_pf()
                        t0 = tt * 512 + sub * 128
                        for c in range(8):
                            k.mm(ps[:, :], hT[:, c, t0:t0 + 128], wb[:, c, :], c == 0, c == 7, [wb, hT], [ps])
                        z = zst[zi % 4]
                        if zi % 2 == 0:
                            k.cp(z[:], ps[:, :], [ps], [z])
                        else:
                            k.cp(z[:], ps[:, :], [ps], [z], en="act")
                        zi += 1
                        k.dma("pool", self.zg_tm[t0:t0 + 128, :], z[:], [z], [self.zg_tm])
        if "z" in cfg.debug and l == 0:
            k.barrier()
            k.dma("sp", self.dbg_zT[:, :], self.zT[:, :], [self.zT], [self.dbg_zT])
            k.dma("sp", self.dbg_zg[:, :], self.zg_tm[:, :], [self.zg_tm], [self.dbg_zg])


    def groups(self):
        cfg = self.cfg
        return [(0, cfg.n_prompt, cfg.lp), (cfg.TP, 1, cfg.ls)]

    def gview(self, ap2d, grp, lo, hi):
        s0, ns, ln = grp
        return ap2d[:, s0:s0 + ns * ln].rearrange("p (s t) -> p s t", s=ns)[:, :, lo:hi]

    def dwconv(self, xc, xl, wcol, bcol, reads):
        k = self.k
        T = self.cfg.T
        k.ts(xc[:, :T], xl[:, :T], wcol(2), bcol, ALU.mult, ALU.add, [xl] + reads, [xc])
        for grp in self.groups():
            ln = grp[2]
            for kk, off in ((0, -2), (1, -1), (3, 1)):
                if off < 0:
                    src = self.gview(xl[:, :T], grp, 0, ln + off)
                    dst = self.gview(xc[:, :T], grp, -off, ln)
                else:
                    src = self.gview(xl[:, :T], grp, off, ln)
                    dst = self.gview(xc[:, :T], grp, 0, ln - off)
                k.stt(dst, src, wcol(kk), dst, ALU.mult, ALU.add, [xl, xc] + reads, [xc])

    def phase_lru(self, l):
        cfg, k = self.cfg, self.k
        T = cfg.T
        zo, _ = ZROWS["lru_x"]
        go, _ = ZROWS["lru_gate"]
        wts = k.sb("lru_wts", [128, 16, 128])
        h0 = k.sb("lru_h0", [128, 2, 4])
        clam = k.sb("lru_clam", [128, 8])
        fin = k.sb("lru_fin", [128, cfg.n_prompt, 2, 4])
        k.dma("sp", wts[:], self.lru_w[l], (), [wts])
        k.dma("sp", h0[:], self.lru_h0[l], (), [h0])
        k.act(clam[:], self.vcol("lru_lam", 0, 8), AF.Exp, [self.vecs], [clam], scale=-1.0)
        k.act(clam[:], clam[:], AF.Ln, [clam], [clam], bias=1.0)
        k.ts(clam[:], clam[:], -8.0, None, ALU.mult, None, [clam], [clam])
        xl = k.sb("lru_xl", [128, T])
        gt = k.sb("lru_gt", [128, T])
        xc = k.sb("lru_xc", [128, T])
        ta = k.sb("lru_ta", [128, T])
        tb = k.sb("lru_tb", [128, T])
        hh = [k.sb(f"lru_h{d}", [128, T]) for d in range(2)]
        ob = k.sb("lru_ob", [128, T], BF16)
        V = [self.vecs]
        for ft in range(4):
            k.dma("sp", xl[:], self.zT[zo + ft * 128:zo + (ft + 1) * 128, :], [self.zT], [xl])
            k.dma("sp", gt[:], self.zT[go + ft * 128:go + (ft + 1) * 128, :], [self.zT], [gt])
            self.dwconv(xc, xl, lambda kk: self.vcol("lru_cw", kk * 4 + ft), self.vcol("lru_cb", ft), V)
            k.act(gt[:], gt[:], AF.Silu, [gt], [gt])
            for d in range(2):
                for tt in range(T // 512):
                    sl = slice(tt * 512, (tt + 1) * 512)
                    pa = self.next_pf()
                    k.mm(pa[:, :], wts[:, (d * 2 + 0) * 4 + ft, :], xc[:, sl], True, True, [wts, xc], [pa])
                    px = self.next_pf()
                    k.mm(px[:, :], wts[:, (d * 2 + 1) * 4 + ft, :], xc[:, sl], True, True, [wts, xc], [px])
                    k.act(ta[:, sl], pa[:, :], AF.Sigmoid, [pa] + V, [ta], bias=self.vcol("lru_ba", d * 4 + ft))
                    k.act(tb[:, sl], px[:, :], AF.Sigmoid, [px] + V, [tb], bias=self.vcol("lru_bx", d * 4 + ft))
                k.act(ta[:], ta[:], AF.Exp, [ta, clam], [ta], scale=clam[:, d * 4 + ft:d * 4 + ft + 1])
                if ft == 0 and d == 0:
                    self.dump("lru_clam", clam, clam[:], [128, 8])
                    self.dump("lru_a", ta, ta[:], [128, T])
                    self.dump("lru_gi", tb, tb[:], [128, T])
                    self.dump("lru_xc", xc, xc[:], [128, T])
                k.tt(tb[:], tb[:], xc[:], ALU.mult, [tb, xc], [tb])
                h = hh[d]
                k.tt(h[:], ta[:], ta[:], ALU.mult, [ta], [h])
                k.act(h[:], h[:], AF.Sqrt, [h], [h], scale=-1.0, bias=1.0)
                k.tt(tb[:], tb[:], h[:], ALU.mult, [tb, h], [tb])
                if ft == 0 and d == 0:
                    self.dump("lru_sq", h, h[:], [128, T])
                    self.dump("lru_u", tb, tb[:], [128, T])
                for (s0, ln, is_s) in cfg.seqs:
                    sl = slice(s0, s0 + ln)
                    init = h0[:, d, ft:ft + 1] if is_s else 0.0
                    rd = [ta, tb] + ([h0] if is_s else [])
                    if d == 0:
                        k.scan(h[:, sl], ta[:, sl], tb[:, sl], init, rd, [h])
                    else:
                        rv = lambda t: t[:, s0:s0 + ln][:, ::-1]
                        k.scan(rv(h), rv(ta), rv(tb), init, rd, [h])
                    if not is_s:
                        si = s0 // ln
                        e = s0 + ln - 1 if d == 0 else s0
                        k.cp(fin[:, si, d, ft:ft + 1], h[:, e:e + 1], [h], [fin], en="pool")
            k.tt(hh[0][:], hh[0][:], hh[1][:], ALU.add, [hh[0], hh[1]], [hh[0]])
            k.tt(ob[:], hh[0][:], gt[:], ALU.mult, [hh[0], gt], [ob])
            k.dma("pool", self.o_scr[3, ft * 128:(ft + 1) * 128, :], ob[:], [ob], [self.o_scr])
        k.dma("pool", self.o_lru_fin[l], fin[:], [fin], [self.o_lru_fin])


    def phase_merge(self, l):
        cfg, k = self.cfg, self.k
        T = cfg.T
        wm = k.sb("wm", [128, 8, 4 * D], BF16)
        wbr = k.sb("wbr", [128, 16, D], BF16)
        with k.phase():
            stg = [k.sb(f"mstg{i}", [128, 4096]) for i in range(2)]
            si = 0
            for blk in range(8):
                st = stg[si % 2]
                si += 1
                k.dma("sp", st[:].rearrange("p (c n) -> p c n", c=8),
                      self.w_in_m[l][:, blk * 512:(blk + 1) * 512].rearrange("(c p) n -> p c n", p=128), (), [st])
                k.cp(wm[:, :, blk * 512:(blk + 1) * 512], st[:].rearrange("p (c n) -> p c n", c=8), [st], [wm], en="pool")
            for m in range(4):
                st = stg[si % 2]
                si += 1
                k.dma("sp", st[:].rearrange("p (c n) -> p c n", c=4),
                      self.w_branch[l, m].rearrange("(c p) n -> p c n", p=128), (), [st])
                k.cp(wbr[:, m * 4:(m + 1) * 4, :], st[:].rearrange("p (c n) -> p c n", c=4), [st], [wbr], en="pool")
        hts = [k.sb(f"mh{i}", [128, 8, 512], BF16) for i in range(2)]
        ots = [k.sb(f"mo{i}", [128, 16, 512], BF16) for i in range(2)]
        sgs = [k.sb(f"msg{i}", [128, 512]) for i in range(2)]
        tmp = [k.sb(f"mtmp{i}", [128, 512]) for i in range(2)]
        acc = [k.sb(f"macc{i}", [128, 512]) for i in range(2)]
        mts = [k.sb(f"mt{i}", [128, 8, 512], BF16) for i in range(2)]
        n = 0
        for tt in range(T // 512):
            tsl = slice(tt * 512, (tt + 1) * 512)
            ht, ot, mt = hts[tt % 2], ots[tt % 2], mts[tt % 2]
            k.dma("sp", ht[:], self.hT_scr[:, tsl].rearrange("(c p) t -> p c t", p=128), [self.hT_scr], [ht])
            for m in range(4):
                k.dma("sp", ot[:, m * 4:(m + 1) * 4, :], self.o_scr[m, :, tsl].rearrange("(c p) t -> p c t", p=128),
                      [self.o_scr], [ot])
            for dt in range(8):
                a = acc[dt % 2]
                for m in range(4):
                    pl = self.next_pf()
                    for c in range(8):
                        k.mm(pl[:, :], wm[:, c, m * D + dt * 128:m * D + (dt + 1) * 128], ht[:, c, :], c == 0, c == 7,
                             [wm, ht], [pl])
                    pp = self.next_pf()
                    for cc in range(4):
                        k.mm(pp[:, :], wbr[:, m * 4 + cc, dt * 128:(dt + 1) * 128], ot[:, m * 4 + cc, :], cc == 0, cc == 3,
                             [wbr, ot], [pp])
                    sg = sgs[n % 2]
                    n += 1
                    k.act(sg[:], pl[:, :], AF.Sigmoid, [pl], [sg])
                    if m == 0:
                        k.tt(a[:], sg[:], pp[:, :], ALU.mult, [sg, pp], [a])
                    else:
                        t_ = tmp[n % 2]
                        k.tt(t_[:], sg[:], pp[:, :], ALU.mult, [sg, pp], [t_])
                        if m < 3:
                            k.tt(a[:], a[:], t_[:], ALU.add, [a, t_], [a], en="pool")
                        else:
                            k.tt(mt[:, dt, :], a[:], t_[:], ALU.add, [a, t_], [mt], en="pool")
            k.dma("pool", self.mT_scr[:, tsl].rearrange("(c p) t -> p c t", p=128), mt[:], [mt], [self.mT_scr])

    def phase_out(self, l):
        cfg, k = self.cfg, self.k
        T = cfg.T
        x_src = self.x_in if l == 0 else self.y1
        y_dst = self.y1 if l < NL - 1 else self.y_all
        wo = k.sb("wo", [128, 8, D], BF16)
        stg = [k.sb(f"ostg{i}", [128, 4096]) for i in range(2)]
        for hb in range(2):
            st = stg[hb]
            k.dma("sp", st[:].rearrange("p (c n) -> p c n", c=8),
                  self.w_out[l][:, hb * 512:(hb + 1) * 512].rearrange("(c p) n -> p c n", p=128), (), [st])
            k.cp(wo[:, :, hb * 512:(hb + 1) * 512], st[:].rearrange("p (c n) -> p c n", c=8), [st], [wo], en="pool")
        mts = [k.sb(f"omt{i}", [128, 8, 128], BF16) for i in range(3)]
        xts = [k.sb(f"oxt{i}", [128, D]) for i in range(3)]
        yts = [k.sb(f"oyt{i}", [128, D]) for i in range(3)]
        for st_ in range(T // 128):
            g = 0 if st_ * 128 < cfg.TP else 1
            tsl = slice(st_ * 128, (st_ + 1) * 128)
            mt, xt, yt = mts[st_ % 3], xts[st_ % 3], yts[st_ % 3]
            k.dma("sp", mt[:], self.mT_scr[:, tsl].rearrange("(c p) t -> p c t", p=128), [self.mT_scr], [mt])
            k.dma("sp", xt[:], x_src[tsl, :], [x_src], [xt])
            for hb in range(2):
                hs = slice(hb * 512, (hb + 1) * 512)
                ps = self.next_pf()
                for c in range(8):
                    k.mm(ps[:, :], mt[:, c, :], wo[:, c, hs], c == 0, c == 7, [mt, wo], [ps])
                k.tt(yt[:, hs], ps[:, :], self.gate_bc[g][:, hs], ALU.mult, [ps, self.gate_bc[g]], [yt])
                k.tt(yt[:, hs], yt[:, hs], xt[:, hs], ALU.add, [yt, xt], [yt], en="pool")
            k.dma("pool", y_dst[tsl, :], yt[:], [yt], [y_dst])


    def rstd_from_sum(self, out, ps, n, reads):
        k = self.k
        tl, pt = reads
        k.ts(out, ps, 1.0 / n, 1e-6, ALU.mult, ALU.add, [pt], [tl])
        k.act(out, out, AF.Sqrt, [tl], [tl])
        k.op("dve", lambda e: e.reciprocal(out=out, in_=out), [tl], [tl])

    def phase_mla(self, l):
        cfg, k = self.cfg, self.k
        T, TP, ls = cfg.T, cfg.TP, cfg.ls
        TK = T + 256
        kidx = lambda t: t if t < TP else t + 256
        V = [self.vecs]
        qup = k.sb("qup", [128, 3, 768], BF16)
        kvk = k.sb("kvk", [128, 2, 512], BF16)
        kvv = k.sb("kvv", [128, 2, 512], BF16)
        pm96 = k.sb("pm96", [96, 96])
        cs = k.sb("ropecs", [96, 2, ls], BF16)
        ckv_all = k.sb("ckv_all", [128, 2, TK], BF16)
        krot = k.sb("krot", [96, TK], BF16)
        ssr = k.sb("ssr", [128, TK // 128])
        qn = k.sb("qn", [128, 3, T], BF16)
        vall = k.sb("vall", [128, TK // 128, 8, 65], BF16)
        with k.phase():
            kr_all = k.sb("kr_all", [96, TK])
            with k.phase():
                stg = k.sb("mlastg", [128, 4096])
                k.dma("sp", stg[:, :3 * 768].rearrange("p (c n) -> p c n", c=3),
                      self.q_up[l].rearrange("(c p) n -> p c n", p=128), (), [stg])
                k.cp(qup[:], stg[:, :3 * 768].rearrange("p (c n) -> p c n", c=3), [stg], [qup])
                k.dma("sp", stg[:, :1024].rearrange("p (c n) -> p c n", c=2),
                      self.kv_up_k[l].rearrange("(c p) n -> p c n", p=128), (), [stg])
                k.cp(kvk[:], stg[:, :1024].rearrange("p (c n) -> p c n", c=2), [stg], [kvk])
                k.dma("sp", stg[:, :1024].rearrange("p (c n) -> p c n", c=2),
                      self.kv_up_v[l].rearrange("(c p) n -> p c n", p=128), (), [stg])
                k.cp(kvv[:], stg[:, :1024].rearrange("p (c n) -> p c n", c=2), [stg], [kvv])
                k.dma("sp", pm96[64:96, :], self.pm96_d[:, :], (), [pm96])
                for j in range(2):
                    k.dma("sp", stg[64:96, :ls], self.rope_cs[j], (), [stg])
                    k.cp(cs[64:96, j, :], stg[64:96, :ls], [stg], [cs])
            k.memset(vall[:, :, :, 64:65], 1.0, [vall])
            if "mla_s0" in cfg.debug:
                return
            ctm = k.sb("ctm", [128, 2, 256])
            krtm = k.sb("krtm", [128, 2, 32])
            k.dma("sp", ctm[:], self.cache_ckv[l].rearrange("(a p) f -> p a f", p=128), (), [ctm])
            k.dma("sp", krtm[:], self.cache_kr[l].rearrange("(a p) f -> p a f", p=128), (), [krtm])
            for a in range(2):
                for c in range(2):
                    ps = self.next_pf()
                    k.tr(ps[:, 0:128], ctm[:, a, c * 128:(c + 1) * 128], self.ident_f[:], [ctm, self.ident_f], [ps])
                    k.cp(ckv_all[:, c, TP + a * 128:TP + (a + 1) * 128], ps[:, 0:128], [ps], [ckv_all])
                ps = self.next_pf()
                kpad = k.sb(f"kpad{a}", [128, 96])
                k.memset(kpad[:], 0.0, [kpad])
                k.cp(kpad[:, 64:96], krtm[:, a, :], [krtm, kpad], [kpad])
                k.tr(ps[0:96, 0:128], kpad[:, :], self.ident_f[:], [kpad, self.ident_f], [ps])
                k.cp(kr_all[64:96, TP + a * 128:TP + (a + 1) * 128], ps[64:96, 0:128], [ps], [kr_all])
            if "mla_s1" in cfg.debug:
                return
            zo_c, _ = ZROWS["mla_ckv"]
            zo_q, _ = ZROWS["mla_q"]
            zo_m, _ = ZROWS["misc"]
            xs = [k.sb(f"mx{i}", [128, 3, 512]) for i in range(2)]
            sq = [k.sb(f"msq{i}", [128, 3, 512]) for i in range(2)]
            rs = [k.sb(f"mrs{i}", [128, 512]) for i in range(2)]
            cn = [k.sb(f"mcn{i}", [128, 2, 512]) for i in range(2)]
            for tt in range(T // 512):
                tsl = slice(tt * 512, (tt + 1) * 512)
                ksl = slice(kidx(tt * 512), kidx(tt * 512) + 512)
                x, q2, r_, c_ = xs[tt % 2], sq[tt % 2], rs[tt % 2], cn[tt % 2]
                for (zo, nch, gname, is_q) in ((zo_c, 2, "mla_kva_g", False), (zo_q, 3, "mla_qa_g", True)):
                    k.dma("sp", x[:, :nch, :], self.zT[zo:zo + nch * 128, tsl].rearrange("(c p) t -> p c t", p=128),
                          [self.zT], [x])
                    k.act(q2[:, :nch, :], x[:, :nch, :], AF.Square, [x], [q2])
                    ps = self.next_pf()
                    for c in range(nch):
                        k.mm(ps[:, :], self.ones_f[:, :], q2[:, c, :], c == 0, c == nch - 1, [self.ones_f, q2], [ps])
                    self.rstd_from_sum(r_[:], ps[:, :], nch * 128, (r_, ps))
                    for c in range(nch):
                        if is_q:
                            k.stt(qn[:, c, tsl], x[:, c, :], self.vcol(gname, c), r_[:], ALU.mult, ALU.mult,
                                  [x, r_] + V, [qn])
                        else:
                            k.stt(c_[:, c, :], x[:, c, :], self.vcol(gname, c), r_[:], ALU.mult, ALU.mult,
                                  [x, r_] + V, [c_])
                    if not is_q:
                        k.cp(ckv_all[:, :, ksl], c_[:, :, :], [c_], [ckv_all], en="pool")
                        if tt * 512 < TP:
                            k.dma("pool", self.o_ckv[l][:, tsl].rearrange("(c p) t -> p c t", p=128), c_[:, :, :],
                                  [c_], [self.o_ckv])
            if "mla_s2" in cfg.debug:
                return
            k.dma("sp", kr_all[64:96, 0:TP], self.zT[zo_m:zo_m + 32, 0:TP], [self.zT], [kr_all])
            k.dma("sp", kr_all[64:96, TP + 256:TK], self.zT[zo_m:zo_m + 32, TP:T], [self.zT], [kr_all])
            k.dma("pool", self.o_kr[l], kr_all[64:96, 0:TP], [kr_all], [self.o_kr])
            krs = k.sb("krs", [96, 512])
            krg = k.sb("krg", [96, 512])
            t1 = k.sb("krt1", [96, 512])
            pss = self.pf[5]
            lat0 = TP + 256
            segs = [(ks, min(512, lat0 - ks)) for ks in range(0, lat0, 512)] + \
                   [(ks, min(512, TK - ks)) for ks in range(lat0, TK, 512)]
            for ks, w in segs:
                k.act(krs[64:96, :w], kr_all[64:96, ks:ks + w], AF.Square, [kr_all], [krs])
                for j in range(w // 128):
                    kt = ks // 128 + j
                    k.mm(pss[:, kt:kt + 1], krs[64:96, j * 128:(j + 1) * 128], self.ones_f[64:96, 0:1], True, True,
                         [krs, self.ones_f], [pss])
                k.ts(krg[64:96, :w], kr_all[64:96, ks:ks + w], self.vecs[64:96, self.vl.cols["mla_kn_g"][0]:self.vl.cols["mla_kn_g"][0] + 1],
                     None, ALU.mult, None, [kr_all] + V, [krg])
                if ks >= lat0:
                    pr = self.next_pf()
                    k.mm(pr[0:96, :w], pm96[64:96, :], krg[64:96, :w], True, True, [pm96, krg], [pr])
                    po = ks - lat0
                    k.tt(t1[64:96, :w], pr[64:96, :w], cs[64:96, 1, po:po + w], ALU.mult, [pr, cs], [t1])
                    k.tt(krg[64:96, :w], krg[64:96, :w], cs[64:96, 0, po:po + w], ALU.mult, [krg, cs], [krg])
                    k.tt(krot[64:96, ks:ks + w], krg[64:96, :w], t1[64:96, :w], ALU.add, [krg, t1], [krot])
                else:
                    k.cp(krot[64:96, ks:ks + w], krg[64:96, :w], [krg], [krot])
            k.cp(ssr[:], pss[:, 0:TK // 128], [pss], [ssr])
            if "mla_s3" in cfg.debug:
                return
            for kt in range(TK // 128):
                ps = self.next_pf()
                for c in range(2):
                    k.mm(ps[:, :], ckv_all[:, c, kt * 128:(kt + 1) * 128], kvv[:, c, :], c == 0, c == 1,
                         [ckv_all, kvv], [ps])
                k.cp(vall[:, kt, :, 0:64], ps[:, :].rearrange("p (h d) -> p h d", h=8), [ps], [vall],
                     en=("act" if kt % 2 else "dve"))
        if "mla_s4" in cfg.debug:
            return
        zo_g, _ = ZROWS["mla_gate"]
        gcol = self.vl.cols["mla_qn_g"][0]
        kcol = self.vl.cols["mla_kn_g"][0]
        kth = k.sb("kth", [96, TK], BF16)
        qth = k.sb("qth", [96, T], BF16)
        rk = k.sb("rk", [128, TK // 128])
        sqt = [k.sb(f"hsq{i}", [96, 512]) for i in range(2)]
        qf = [k.sb(f"hqf{i}", [96, 512]) for i in range(2)]
        rq = [k.sb(f"hrq{i}", [96, 512]) for i in range(2)]
        t1 = k.sb("ht1", [96, 512])
        t2 = k.sb("ht2", [96, 512])
        pts = [k.sb(f"hpt{i}", [128, 512], BF16) for i in range(3)]
        oa = [k.sb(f"hoa{i}", [65, 512]) for i in range(2)]
        gts = [k.sb(f"hgt{i}", [64, 512]) for i in range(2)]
        obs = [k.sb(f"hob{i}", [64, 512], BF16) for i in range(2)]
        npt = 0
        npo = 0
        for h in range(8):
            prk = self.pf[4]
            for ks in range(0, TK, 512):
                w = min(512, TK - ks)
                ps = self.next_pf()
                for c in range(2):
                    k.mm(ps[0:64, :w], kvk[:, c, h * 64:(h + 1) * 64], ckv_all[:, c, ks:ks + w], c == 0, c == 1,
                         [kvk, ckv_all], [ps])
                s_ = sqt[(ks // 512) % 2]
                if "k_noact" not in cfg.debug:
                    k.act(s_[0:64, :w], ps[0:64, :w], AF.Square, [ps], [s_])
                for j in range(w // 128):
                    kt = ks // 128 + j
                    if "mla_k1" in cfg.debug:
                        continue
                    k.mm(prk[:, kt:kt + 1], s_[0:64, j * 128:(j + 1) * 128], self.ones_f[0:64, 0:1], True, True,
                         [s_, self.ones_f], [prk])
                if "k_nots" not in cfg.debug:
                    k.ts(kth[0:64, ks:ks + w], ps[0:64, :w], self.vecs[0:64, kcol:kcol + 1], None, ALU.mult, None,
                         [ps] + V, [kth])
            k.cp(kth[64:96, :], krot[64:96, :], [krot], [kth], en=("dve" if "mla_k2" in cfg.debug else "pool"))
            if "k_nork" in cfg.debug:
                return
            k.tt(rk[:], prk[:, 0:TK // 128], ssr[:], ALU.add, [prk, ssr], [rk])
            k.ts(rk[:], rk[:], 1.0, 96e-6, ALU.mult, ALU.add, [rk], [rk])
            k.act(rk[:], rk[:], AF.Sqrt, [rk], [rk])
            k.op("dve", lambda e: e.reciprocal(out=rk[:], in_=rk[:]), [rk], [rk])
            if "mla_s5" in cfg.debug:
                return
            for tt in range(T // 512):
                tsl = slice(tt * 512, (tt + 1) * 512)
                ps = self.next_pf()
                for c in range(3):
                    k.mm(ps[0:96, :], qup[:, c, h * 96:(h + 1) * 96], qn[:, c, tsl], c == 0, c == 2, [qup, qn], [ps])
                s_, f_, r_ = sqt[tt % 2], qf[tt % 2], rq[tt % 2]
                k.act(s_[:, :], ps[0:96, :], AF.Square, [ps], [s_])
                p2 = self.next_pf()
                k.mm(p2[0:96, :], self.ones_f[0:96, 0:96], s_[:, :], True, True, [self.ones_f, s_], [p2])
                self.rstd_from_sum(r_[:, :], p2[0:96, :], 96, (r_, p2))
                k.stt(f_[:, :], ps[0:96, :], self.vecs[0:96, gcol:gcol + 1], r_[:, :], ALU.mult, ALU.mult,
                      [ps, r_] + V, [f_])
                k.cp(qth[0:64, tsl], f_[0:64, :], [f_], [qth], en="pool")
                if tt * 512 >= TP:
                    po = tt * 512 - TP
                    pr = self.next_pf()
                    k.mm(pr[0:96, :], pm96[64:96, :], f_[64:96, :], True, True, [pm96, f_], [pr])
                    k.tt(t1[64:96, :], pr[64:96, :], cs[64:96, 1, po:po + 512], ALU.mult, [pr, cs], [t1])
                    k.tt(t2[64:96, :], f_[64:96, :], cs[64:96, 0, po:po + 512], ALU.mult, [f_, cs], [t2])
                    k.tt(qth[64:96, tsl], t1[64:96, :], t2[64:96, :], ALU.add, [t1, t2], [qth], en="pool")
                else:
                    k.cp(qth[64:96, tsl], f_[64:96, :], [f_], [qth], en="pool")
            if "mla_s6" in cfg.debug:
                return
            if h == 0:
                self.dump("mla_kth", kth, kth[:, :], [96, TK], BF16)
                self.dump("mla_qth", qth, qth[:, :], [96, T], BF16)
                self.dump("mla_rk", rk, rk[:, :], [128, TK // 128])
            for (s0, ln, is_s) in cfg.seqs:
                k0 = TP if is_s else s0
                nk = ln + 256 if is_s else ln
                qw = min(512, ln)
                for qg in range(ln // qw):
                    q0 = s0 + qg * qw
                    po = self.pf[4 + (npo % 2)]
                    npo += 1
                    nkt = nk // 128
                    for j in range(nkt):
                        kt = k0 // 128 + j
                        pS = self.next_pf()
                        k.mm(pS[:, :qw], kth[0:96, kt * 128:(kt + 1) * 128], qth[0:96, q0:q0 + qw], True, True,
                             [kth, qth], [pS])
                        pt = pts[npt % 3]
                        npt += 1
                        k.act(pt[:, :qw], pS[:, :qw], AF.Exp, [pS, rk], [pt], scale=rk[:, kt:kt + 1])
                        k.mm(po[0:65, :qw], vall[:, kt, h, :], pt[:, :qw], j == 0, j == nkt - 1, [vall, pt], [po])
                    o_ = oa[qg % 2]
                    k.cp(o_[:, :qw], po[0:65, :qw], [po], [o_])
                    k.op("dve", lambda e: e.reciprocal(out=o_[64:65, :qw], in_=o_[64:65, :qw]), [o_], [o_])
                    pbc = self.next_pf()
                    k.mm(pbc[0:64, :qw], self.ones_f[64:65, 0:64], o_[64:65, :qw], True, True, [self.ones_f, o_], [pbc])
                    g_ = gts[qg % 2]
                    k.dma("sp", g_[:, :qw], self.zT[zo_g + h * 64:zo_g + (h + 1) * 64, q0:q0 + qw], [self.zT], [g_])
                    k.act(g_[:, :qw], g_[:, :qw], AF.Silu, [g_], [g_])
                    k.tt(o_[0:64, :qw], o_[0:64, :qw], pbc[0:64, :qw], ALU.mult, [o_, pbc], [o_])
                    ob = obs[qg % 2]
                    k.tt(ob[:, :qw], o_[0:64, :qw], g_[:, :qw], ALU.mult, [o_, g_], [ob])
                    k.dma("pool", self.o_scr[1, h * 64:(h + 1) * 64, q0:q0 + qw], ob[:, :qw], [ob], [self.o_scr])


    def phase_ssd(self, l):
        cfg, k = self.cfg, self.k
        T, TP = cfg.T, cfg.TP
        NCH = T // 128
        V = [self.vecs]
        zo_x, _ = ZROWS["ssd_xbc"]
        zo_m, _ = ZROWS["misc"]
        x_tm = k.sb("sx_tm", [128, NCH, 512], BF16)
        b_tm = k.sb("sb_tm", [128, NCH, 128], BF16)
        bT = k.sb("s_bT", [128, T], BF16)
        cT = k.sb("s_cT", [128, T], BF16)
        selm = k.sb("s_selm", [64, 16, 128])
        seld = k.sb("s_seld", [64, 2, 8])
        maskneg = k.sb("s_mask", [128, 2, 128])
        bcv = k.sb("s_bcv", [128, 8 + 512])
        k.dma("sp", selm[:], self.selm_d[:, :, :], (), [selm])
        k.dma("sp", seld[:], self.seld_d[:, :, :], (), [seld])
        k.dma("sp", maskneg[:], self.maskneg_d[:, :, :], (), [maskneg])
        k.dma("sp", bcv[:], self.ssd_bc[l], (), [bcv])
        with k.phase():
            xl = [k.sb(f"s_xl{i}", [128, T]) for i in range(2)]
            xc = [k.sb(f"s_xc{i}", [128, T]) for i in range(2)]
            for ft in range(6):
                a, c_ = xl[ft % 2], xc[ft % 2]
                k.dma("sp", a[:], self.zT[zo_x + ft * 128:zo_x + (ft + 1) * 128, :], [self.zT], [a])
                self.dwconv(c_, a, lambda kk: self.vcol("ssd_cw", kk * 6 + ft), self.vcol("ssd_cb", ft), V)
                k.act(c_[:], c_[:], AF.Silu, [c_], [c_])
                if ft == 4:
                    k.cp(bT[:], c_[:], [c_], [bT], en="pool")
                if ft == 5:
                    k.cp(cT[:], c_[:], [c_], [cT], en="pool")
                if ft <= 4:
                    for ch in range(NCH):
                        ps = self.next_pf()
                        k.tr(ps[:, 0:128], c_[:, ch * 128:(ch + 1) * 128], self.ident_f[:], [c_, self.ident_f], [ps])
                        dst = x_tm[:, ch, ft * 128:(ft + 1) * 128] if ft < 4 else b_tm[:, ch, :]
                        k.cp(dst, ps[:, 0:128], [ps], [x_tm if ft < 4 else b_tm], en=("act" if ch % 2 else "dve"))
        cs = k.sb("s_cs", [64, T])
        sc3t = k.sb("s_sc3", [64, 3, T])
        nega = k.sb("s_nega", [64, 1])
        _cm_stack = contextlib.ExitStack()
        k.barrier()
        k.stacks.append(_cm_stack)
        cmask = k.sb("s_cmask", [64, T + 1])

        class _V:
            def __init__(self, tl, j):
                self.tl, self.j, self.res, self.psum = tl, j, tl.res, False

            def __getitem__(self, key):
                if not isinstance(key, tuple):
                    key = (key,)
                return self.tl[(key[0], self.j) + tuple(key[1:])]
        sc3 = sc3t
        dtt = sc3t
        D0 = lambda *a: sc3t[(a[0], 0) + tuple(a[1:])] if a else sc3t[:, 0, :]
        k.memset(sc3t[:, 0, :], 0.0, [sc3t])
        k.dma("sp", sc3t[0:8, 0, :], self.zT[zo_m + 32:zo_m + 40, :], [self.zT], [sc3t])
        k.dma("sp", sc3t[32:40, 0, :], self.zT[zo_m + 40:zo_m + 48, :], [self.zT], [sc3t])
        k.dma("sp", cmask[:], self.cmask_d[:, :], (), [cmask])
        k.act(sc3t[:, 0, :], sc3t[:, 0, :], AF.Exp, [sc3t] + V, [sc3t], bias=self.vecs[0:64, self.vl.cols["ssd_dtb"][0]:self.vl.cols["ssd_dtb"][0] + 1])
        k.act(sc3t[:, 0, :], sc3t[:, 0, :], AF.Ln, [sc3t], [sc3t], bias=1.0)
        k.act(nega[:], self.vecs[0:64, self.vl.cols["ssd_alog"][0]:self.vl.cols["ssd_alog"][0] + 1], AF.Exp, V, [nega])
        k.ts(nega[:], nega[:], -1.0, None, ALU.mult, None, [nega], [nega])
        k.ts(sc3t[:, 2, :], sc3t[:, 0, :], nega[:, 0:1], None, ALU.mult, None, [sc3t, nega], [sc3t])
        k.scan(cs[0:32, :], cmask[0:32, 0:T], sc3t[0:32, 2, :], 0.0, [cmask, sc3t], [cs])
        k.scan(cs[32:64, :][:, ::-1], cmask[32:64, 1:T + 1][:, ::-1], sc3t[32:64, 2, :][:, ::-1], 0.0, [cmask, sc3t], [cs])
        k.barrier()
        k.stacks.pop()
        _cm_stack.close()
        c3 = lambda ap: ap.rearrange("p (c t) -> p c t", t=128)
        k.tt(c3(sc3[0:32, 1, :]), c3(cs[0:32, :])[:, :, 127:128].to_broadcast([32, NCH, 128]), c3(cs[0:32, :]),
             ALU.subtract, [cs], [sc3])
        k.tt(c3(sc3[32:64, 1, :]), c3(cs[32:64, :])[:, :, 0:1].to_broadcast([32, NCH, 128]), c3(cs[32:64, :]),
             ALU.subtract, [cs], [sc3])
        k.act(sc3[:, 1, :], sc3[:, 1, :], AF.Exp, [sc3], [sc3])
        k.tt(sc3[:, 1, :], sc3[:, 1, :], sc3[:, 0, :], ALU.mult, [sc3], [sc3])
        k.act(sc3[:, 2, :], cs[:], AF.Exp, [cs], [sc3])
        self.dump("ssd_cs", cs, cs[:, :], [64, T])
        self.dump("ssd_sc3", sc3t, sc3t[:, :, :], [64, 3, T])
        self.dump("ssd_xtm", x_tm, x_tm[:, :, :], [128, NCH, 512], BF16)
        self.dump("ssd_bT", bT, bT[:, :], [128, T], BF16)
        Hf = k.sb("s_H", [128, 4, 64])
        Hb = k.sb("s_Hb", [128, 4, 64], BF16)
        selg = k.sb("s_selg", [64, 128])
        k.dma("sp", selg[:], self.selg_d[:, :], (), [selg])
        sctm = [k.sb(f"s_sctm{i}", [128, 3, 64]) for i in range(2)]
        cstm = [k.sb(f"s_cstm{i}", [128, 64]) for i in range(2)]
        xd = [k.sb(f"s_xd{i}", [128, 8, 64], BF16) for i in range(2)]
        xdd = [k.sb(f"s_xdd{i}", [128, 8, 64], BF16) for i in range(2)]
        lt = [k.sb(f"s_lt{i}", [128, 4, 128]) for i in range(2)]
        mT = [k.sb(f"s_mT{i}", [128, 4, 128], BF16) for i in range(2)]
        yo = [k.sb(f"s_yo{i}", [128, 512]) for i in range(2)]
        yf = [k.sb("s_yf0", [128, 512])] * 2
        zg = [k.sb("s_zg0", [128, 512])] * 2
        et = k.sb("s_et", [128, 4])
        etr = k.sb("s_etr", [64, 4])
        h0t = k.sb("s_h0t", [64, 8, 64])
        fint = k.sb("s_fint", [64, 8, 64])
        junk = yf[0]
        ssq = k.sb("s_ssq", [128, 1])
        ob = [k.sb(f"s_ob{i}", [128, 4, 128], BF16) for i in range(2)]
        it = 0
        for (s0, ln, is_s) in cfg.seqs:
            si = s0 // ln
            chs = list(range(s0 // 128, (s0 + ln) // 128))
            for d in range(2):
                if is_s:
                    k.dma("sp", h0t[:], self.ssd_h0[l, d], (), [h0t])
                    for h in range(8):
                        ps = self.next_pf()
                        if h < 4:
                            k.tr(ps[0:64, 0:64], h0t[:, h, :], self.ident_f[0:64, 0:64], [h0t, self.ident_f], [ps])
                            k.cp(Hf[0:64, h, :], ps[0:64, 0:64], [ps], [Hf])
                        else:
                            k.tr(ps[:, 0:64], h0t[:, h - 1:h + 1, :].rearrange("p a n -> p (a n)"),
                                 self.ident_f[0:64, 0:64], [h0t, self.ident_f], [ps])
                            k.cp(Hf[64:128, h - 4, :], ps[64:128, 0:64], [ps], [Hf])
                else:
                    k.memset(Hf[:], 0.0, [Hf], en="dve")
                k.cp(Hb[:], Hf[:], [Hf], [Hb])
                for ch in (chs if d == 0 else chs[::-1]):
                    it += 1
                    tsl = slice(ch * 128, (ch + 1) * 128)
                    st_, ct_, xd_, xdd_ = sctm[it % 2], cstm[it % 2], xd[it % 2], xdd[it % 2]
                    pt = self.next_pf()
                    for j in range(3):
                        k.tr(pt[:, j * 64:(j + 1) * 64], sc3[:, j, tsl], self.ident_f[0:64, 0:64], [sc3, self.ident_f], [pt])
                    k.tr(pt[:, 192:256], cs[:, tsl], self.ident_f[0:64, 0:64], [cs, self.ident_f], [pt])
                    k.cp(st_[:], pt[:, 0:192].rearrange("p (j r) -> p j r", j=3), [pt], [st_])
                    k.cp(ct_[:], pt[:, 192:256], [pt], [ct_])
                    r0 = d * 32
                    xv = x_tm[:, ch, :].rearrange("p (h q) -> p h q", h=8)
                    k.tt(xd_[:], xv, st_[:, 0, r0:r0 + 8].unsqueeze(2).to_broadcast([128, 8, 64]), ALU.mult,
                         [x_tm, st_], [xd_])
                    k.tt(xdd_[:], xv, st_[:, 1, r0:r0 + 8].unsqueeze(2).to_broadcast([128, 8, 64]), ALU.mult,
                         [x_tm, st_], [xdd_], en="pool")
                    py = self.pf[4]
                    pyo = self.pf[5]
                    for g in range(2):
                        lt_, m_ = lt[g], mT[g]
                        pb_ = self.next_pf()
                        for hh in range(4):
                            h = g * 4 + hh
                            k.mm(pb_[:, hh * 128:(hh + 1) * 128], selm[:, d * 8 + h, :], cs[:, tsl], True, True,
                                 [selm, cs], [pb_])
                        k.tt(lt_[:], pb_[:, :].rearrange("p (h t) -> p h t", h=4),
                             ct_[:, r0 + g * 4:r0 + g * 4 + 4].unsqueeze(2).to_broadcast([128, 4, 128]), ALU.subtract,
                             [pb_, ct_], [lt_])
                        k.tt(lt_[:], lt_[:], maskneg[:, d:d + 1, :].to_broadcast([128, 4, 128]), ALU.add,
                             [lt_, maskneg], [lt_])
                        k.act(lt_[:], lt_[:], AF.Exp, [lt_], [lt_])
                        pg = self.next_pf()
                        k.mm(pg[:, 0:128], bT[g * 64:(g + 1) * 64, tsl], cT[g * 64:(g + 1) * 64, tsl], True, True,
                             [bT, cT], [pg])
                        k.tt(m_[:], lt_[:], pg[:, 0:128].unsqueeze(1).to_broadcast([128, 4, 128]), ALU.mult,
                             [lt_, pg], [m_])
                        for hh in range(4):
                            h = g * 4 + hh
                            k.mm(py[:, h * 64:(h + 1) * 64], m_[:, hh, :], xd_[:, h, :], True, True, [m_, xd_], [py])
                        k.mm(pyo[:, g * 256:(g + 1) * 256], cT[g * 64:(g + 1) * 64, tsl],
                             Hb[g * 64:(g + 1) * 64, :, :], True, True, [cT, Hb], [pyo])
                    y_ = yo[it % 2]
                    k.tt(y_[:].rearrange("p (h q) -> p h q", h=8), pyo[:, :].rearrange("p (h q) -> p h q", h=8),
                         st_[:, 2, r0:r0 + 8].unsqueeze(2).to_broadcast([128, 8, 64]), ALU.mult, [pyo, st_], [y_])
                    k.tt(y_[:], y_[:], py[:, :], ALU.add, [y_, py], [y_])
                    ps_ = self.next_pf()
                    for g in range(2):
                        k.mm(ps_[g * 64:(g + 1) * 64, 0:256], b_tm[:, ch, g * 64:(g + 1) * 64],
                             xdd_[:, g * 4:(g + 1) * 4, :], True, True, [b_tm, xdd_], [ps_])
                    e_tok = ch * 128 + (127 if d == 0 else 0)
                    k.ts(etr[:], seld[:, d, 0:4], cs[:, e_tok:e_tok + 1], None, ALU.mult, None, [seld, cs], [etr])
                    pe_ = self.next_pf()
                    k.mm(pe_[:, 0:4], selg[:, :], etr[:], True, True, [selg, etr], [pe_])
                    k.act(et[:], pe_[:, 0:4], AF.Exp, [pe_], [et])
                    k.tt(Hf[:], Hf[:], et[:].unsqueeze(2).to_broadcast([128, 4, 64]), ALU.mult, [Hf, et], [Hf])
                    k.tt(Hf[:], Hf[:], ps_[:, 0:256].rearrange("p (h q) -> p h q", h=4), ALU.add, [Hf, ps_], [Hf])
                    k.cp(Hb[:], Hf[:], [Hf], [Hb], en="pool")
                    if it == 1:
                        self.dump("ssd_lt", lt[1], lt[1][:, :, :], [128, 4, 128])
                        self.dump("ssd_mT", mT[1], mT[1][:, :, :], [128, 4, 128], BF16)
                        self.dump("ssd_y0", y_, y_[:, :], [128, 512])
                        self.dump("ssd_H1", Hf, Hf[:, :, :], [128, 4, 64])
                        self.dump("ssd_sctm", st_, st_[:, :, :], [128, 3, 64])
                    if d == 0:
                        k.dma("pool", self.ssd_y[tsl, :], y_[:], [y_], [self.ssd_y])
                    else:
                        f_, z_ = yf[it % 2], zg[it % 2]
                        k.dma("sp", f_[:], self.ssd_y[tsl, :], [self.ssd_y], [f_])
                        k.dma("sp", z_[:], self.zg_tm[tsl, :], [self.zg_tm], [z_])
                        k.tt(y_[:], y_[:], f_[:], ALU.add, [y_, f_], [y_])
                        k.tt(f_[:].rearrange("p (h q) -> p h q", h=8), xv,
                             bcv[:, 0:8].unsqueeze(2).to_broadcast([128, 8, 64]), ALU.mult, [x_tm, bcv], [f_])
                        k.tt(y_[:], y_[:], f_[:], ALU.add, [y_, f_], [y_])
                        k.act(z_[:], z_[:], AF.Silu, [z_], [z_])
                        k.tt(y_[:], y_[:], z_[:], ALU.mult, [y_, z_], [y_])
                        k.act(junk[:], y_[:], AF.Square, [y_], [junk, ssq], accum_out=ssq[:])
                        self.rstd_from_sum(ssq[:], ssq[:], 512, (ssq, ssq))
                        k.stt(y_[:], y_[:], ssq[:, 0:1], bcv[:, 8:8 + 512], ALU.mult, ALU.mult, [y_, ssq, bcv], [y_])
                        o_ = ob[it % 2]
                        for ft in range(4):
                            pt2 = self.next_pf()
                            k.tr(pt2[:, 0:128], y_[:, ft * 128:(ft + 1) * 128], self.ident_f[:], [y_, self.ident_f], [pt2])
                            k.cp(o_[:, ft, :], pt2[:, 0:128], [pt2], [o_], en=("act" if ft % 2 else "dve"))
                        k.dma("pool", self.o_scr[2, :, tsl].rearrange("(c p) t -> p c t", p=128), o_[:], [o_], [self.o_scr])
                if not is_s:
                    for h in range(8):
                        ps = self.next_pf()
                        g_ = h // 4
                        k.tr(ps[0:64, 0:64], Hf[g_ * 64:(g_ + 1) * 64, h % 4, :],
                             self.ident_f[g_ * 64:(g_ + 1) * 64, g_ * 64:(g_ + 1) * 64], [Hf, self.ident_f], [ps])
                        k.cp(fint[:, h, :], ps[0:64, 0:64], [ps], [fint])
                    k.dma("pool", self.o_ssd_fin[l, si, d], fint[:], [fint], [self.o_ssd_fin])


    def phase_rwkv(self, l):
        cfg, k = self.cfg, self.k
        T, TP = cfg.T, cfg.TP
        TS = 512
        V = [self.vecs]
        NC64 = T // 64
        zr = {n: ZROWS[n][0] for n in ("rw_r", "rw_k", "rw_v", "rw_lora", "rw_gate")}
        pc_all = k.sb("rw_pc", [128, 4, 2, NC64])
        blk = k.sb("rw_blk", [128, 128])
        k.dma("sp", blk[:], self.blk64_d[:, :], (), [blk])
        vc = lambda name, j: self.vcol(name, j)
        with k.phase():
            lw = k.sb("rw_lw", [128, 2, 512])
            cm = k.sb("rw_cm", [128, TS + 1])
            omk = k.sb("rw_omk", [128, 4])
            k.dma("sp", lw[:], self.rw_lw[l], (), [lw])
            k.dma("sp", cm[:], self.cmask64_d[:, :], (), [cm])
            k.ts(omk[:], self.vcol("rw_ka", 0, 4), -1.0, 1.0, ALU.mult, ALU.add, V, [omk])
            xe = k.sb("rw_xe", [128, TS + 2])
            sh = k.sb("rw_sh", [128, TS])
            lora = k.sb("rw_lora", [128, TS])
            names = ("r", "k", "v", "kk", "a", "kd", "be", "lw_", "cl", "e1", "e2", "t1", "t2", "bon")
            tl = {n: k.sb("rw_" + n, [128, TS]) for n in names}

            def mix(dst, zrow, mucol, a, is_s, s0, ln):
                b_ = a + TS
                k.dma("sp", xe[:, 1:TS + 1], self.zT[zrow:zrow + 128, a:b_], [self.zT], [xe])
                if is_s:
                    if a > s0:
                        k.dma("sp", xe[:, 0:1], self.zT[zrow:zrow + 128, a - 1:a], [self.zT], [xe], allow_slow_non_contiguous=True)
                    else:
                        k.memset(xe[:, 0:1], 0.0, [xe])
                    if b_ < s0 + ln:
                        k.dma("sp", xe[:, TS + 1:TS + 2], self.zT[zrow:zrow + 128, b_:b_ + 1], [self.zT], [xe], allow_slow_non_contiguous=True)
                    else:
                        k.memset(xe[:, TS + 1:TS + 2], 0.0, [xe])
                    k.tt(sh[:], xe[:, 0:TS], xe[:, 2:TS + 2], ALU.add, [xe], [sh])
                else:
                    ns = TS // ln
                    k.memset(sh[:], 0.0, [sh], en="dve")
                    v3 = lambda ap: ap.rearrange("p (s t) -> p s t", s=ns)
                    xv = v3(xe[:, 1:TS + 1])
                    k.tt(v3(sh[:])[:, :, 1:ln], v3(sh[:])[:, :, 1:ln], xv[:, :, 0:ln - 1], ALU.add, [sh, xe], [sh])
                    k.tt(v3(sh[:])[:, :, 0:ln - 1], v3(sh[:])[:, :, 0:ln - 1], xv[:, :, 1:ln], ALU.add, [sh, xe], [sh])
                k.stt(sh[:], sh[:], 0.5, xe[:, 1:TS + 1], ALU.mult, ALU.subtract, [sh, xe], [sh])
                k.stt(dst, sh[:], mucol, xe[:, 1:TS + 1], ALU.mult, ALU.add, [sh, xe] + V, [dst_tl[0]])

            dst_tl = [None]
            for a in range(0, T, TS):
                is_s = a >= TP
                s0, ln = (TP, cfg.ls) if is_s else (0, cfg.lp)
                c64 = a // 64
                dst_tl[0] = lora
                mix(lora[:], zr["rw_lora"], vc("rw_mu", 12), a, is_s, s0, ln)
                k.act(lora[0:64, :], lora[0:64, :], AF.Tanh, [lora], [lora])
                for ft in range(4):
                    R, K_, Vv, KK = tl["r"], tl["k"], tl["v"], tl["kk"]
                    for (dst, nm, mi) in ((R, "rw_r", 0), (K_, "rw_k", 4), (Vv, "rw_v", 8)):
                        dst_tl[0] = dst
                        mix(dst[:], zr[nm] + ft * 128, vc("rw_mu", mi + ft), a, is_s, s0, ln)
                    k.dma("pool", self.rw_v[ft * 128:(ft + 1) * 128, a:a + TS], Vv[:], [Vv], [self.rw_v])
                    t1, t2 = tl["t1"], tl["t2"]
                    k.ts(KK[:], K_[:], vc("rw_kk", ft), None, ALU.mult, None, [K_] + V, [KK])
                    k.tt(t1[:], KK[:], KK[:], ALU.mult, [KK], [t1])
                    ps = self.next_pf()
                    k.mm(ps[:, :], blk[:, :], t1[:], True, True, [blk, t1], [ps])
                    k.ts(t2[:], ps[:, :], 1e-12, None, ALU.add, None, [ps], [t2])
                    k.act(t2[:], t2[:], AF.Sqrt, [t2], [t2])
                    k.op("dve", lambda e: e.reciprocal(out=t2[:], in_=t2[:]), [t2], [t2])
                    k.tt(KK[:], KK[:], t2[:], ALU.mult, [KK, t2], [KK])
                    for d in range(2):
                        A_, KD, BE, LW, CL, E1, E2, BON = (tl[n] for n in ("a", "kd", "be", "lw_", "cl", "e1", "e2", "bon"))
                        ps = self.next_pf()
                        k.mm(ps[:, :], lw[0:64, d, ft * 128:(ft + 1) * 128], lora[0:64, :], True, True, [lw, lora], [ps])
                        k.act(LW[:], ps[:, :], AF.Sigmoid, [ps] + V, [LW], bias=vc("rw_w0", d * 4 + ft))
                        k.ts(LW[:], LW[:], -0.6065306597126334, None, ALU.mult, None, [LW], [LW])
                        ps2 = self.next_pf()
                        k.mm(ps2[:, :], lw[64:128, d, ft * 128:(ft + 1) * 128], lora[64:128, :], True, True, [lw, lora], [ps2])
                        k.act(A_[:], ps2[:, :], AF.Sigmoid, [ps2] + V, [A_], bias=vc("rw_a0", d * 4 + ft))
                        k.ts(KD[:], A_[:], vc("rw_ka", ft), omk[:, ft:ft + 1], ALU.mult, ALU.add, [A_, omk] + V, [KD])
                        k.tt(KD[:], KD[:], K_[:], ALU.mult, [KD, K_], [KD])
                        k.tt(BE[:], KK[:], A_[:], ALU.mult, [KK, A_], [BE])
                        k.stt(t1[:], R[:], vc("rw_rk", ft), KD[:], ALU.mult, ALU.mult, [R, KD] + V, [t1])
                        ps3 = self.next_pf()
                        k.mm(ps3[:, :], blk[:, :], t1[:], True, True, [blk, t1], [ps3])
                        if d == 0:
                            k.tt(BON[:], ps3[:, :], Vv[:], ALU.mult, [ps3, Vv], [BON])
                        else:
                            k.tt(t1[:], ps3[:, :], Vv[:], ALU.mult, [ps3, Vv], [t1])
                            k.tt(BON[:], BON[:], t1[:], ALU.add, [BON, t1], [BON])
                            k.dma("pool", self.rw_bonus[ft * 128:(ft + 1) * 128, a:a + TS], BON[:], [BON], [self.rw_bonus])
                        if d == 0:
                            k.scan(CL[:], cm[:, 0:TS], LW[:], 0.0, [cm, LW], [CL])
                        else:
                            k.scan(CL[:, ::-1], cm[:, 1:TS + 1][:, ::-1], LW[:, ::-1], 0.0, [cm, LW], [CL])
                        c3 = lambda ap: ap.rearrange("p (c t) -> p c t", t=64)
                        e_ = 63 if d == 0 else 0
                        ce = c3(CL[:])[:, :, e_:e_ + 1]
                        k.act(pc_all[:, ft, d, c64:c64 + TS // 64], ce.rearrange("p c o -> p (c o)"), AF.Exp, [CL], [pc_all])
                        O = lambda j: self.rw_ops[d, j, ft * 128:(ft + 1) * 128, a:a + TS]
                        k.act(E1[:], CL[:], AF.Exp, [CL], [E1])
                        k.tt(t1[:], R[:], E1[:], ALU.mult, [R, E1], [t1])
                        k.dma("pool", O(0), t1[:], [t1], [self.rw_ops])
                        k.act(E2[:], CL[:], AF.Exp, [CL], [E2], scale=-1.0)
                        k.tt(t2[:], KD[:], E2[:], ALU.mult, [KD, E2], [t2])
                        k.dma("pool", O(1), t2[:], [t2], [self.rw_ops])
                        k.tt(t1[:], BE[:], E2[:], ALU.mult, [BE, E2], [t1])
                        k.dma("pool", O(2), t1[:], [t1], [self.rw_ops])
                        k.tt(E1[:], CL[:], LW[:], ALU.subtract, [CL, LW], [E1])
                        k.act(E1[:], E1[:], AF.Exp, [E1], [E1])
                        k.tt(t2[:], KK[:], E1[:], ALU.mult, [KK, E1], [t2])
                        k.dma("pool", O(3), t2[:], [t2], [self.rw_ops])
                        k.tt(c3(E2[:]), ce.to_broadcast([128, TS // 64, 64]), c3(CL[:]), ALU.subtract, [CL], [E2])
                        k.act(E2[:], E2[:], AF.Exp, [E2], [E2])
                        k.tt(t1[:], KD[:], E2[:], ALU.mult, [KD, E2], [t1])
                        k.dma("pool", O(4), t1[:], [t1], [self.rw_ops])
                        k.tt(t2[:], BE[:], E2[:], ALU.mult, [BE, E2], [t2])
                        k.dma("pool", O(5), t2[:], [t2], [self.rw_ops])
        if "rw_prep_only" in cfg.debug:
            return
        with k.phase():
            msk = k.sb("rw_msk", [128, 2, 3, 128])
            k.dma("sp", msk[:], self.rw_mask_d[:, :, :, :], (), [msk])
            ops = [[k.sb(f"rw_op{j}_{i}", [128, 4, 128]) for j in range(7)] for i in range(2)]
            pmask = k.sb("rw_pmask", [128, 2])
            k.dma("sp", pmask[:], self.pmask_d[:, :], (), [pmask])
            mk = {nm: [k.sb(f"rw_mk{nm}{p_}", [128, 4, 128]) for p_ in range(2)] for nm in ("A", "B", "R")}
            tms = [k.sb(f"rw_tm{j}", [128, 512]) for j in range(3)]
            NT = [k.sb(f"rw_NT{i}", [128, 8, 128]) for i in range(2)]
            NN = [k.sb(f"rw_NN{i}", [128, 8, 128]) for i in range(2)]
            TT = k.sb("rw_TT", [128, 8, 128])
            AakT = k.sb("rw_AakT", [128, 8, 128])
            ArkT = k.sb("rw_ArkT", [128, 8, 128])
            ArbT = k.sb("rw_ArbT", [128, 8, 128])
            W1 = k.sb("rw_W1", [128, 512])
            Wt = k.sb("rw_Wt", [128, 512])
            Ut = k.sb("rw_Ut", [128, 512])
            ST = k.sb("rw_ST", [128, 4, 64])
            h0t = k.sb("rw_h0t", [64, 8, 64])
            fint = k.sb("rw_fint", [64, 8, 64])
            osb = [k.sb(f"rw_osb{i}", [128, 4, 128]) for i in range(2)]
            it = 0
            for (s0, ln, is_s) in cfg.seqs:
                si = s0 // ln
                wins = list(range(s0 // 128, (s0 + ln) // 128))
                for d in range(2):
                    if is_s:
                        k.dma("sp", h0t[:], self.rw_h0[l, d], (), [h0t])
                        for h in range(8):
                            ps = self.next_pf()
                            if h % 2 == 0:
                                k.tr(ps[0:64, 0:64], h0t[:, h, :], self.ident_f[0:64, 0:64], [h0t, self.ident_f], [ps])
                                k.cp(ST[0:64, h // 2, :], ps[0:64, 0:64], [ps], [ST])
                            else:
                                k.tr(ps[:, 0:64], h0t[:, h - 1:h + 1, :].rearrange("p a n -> p (a n)"),
                                     self.ident_f[0:64, 0:64], [h0t, self.ident_f], [ps])
                                k.cp(ST[64:128, h // 2, :], ps[64:128, 0:64], [ps], [ST])
                    else:
                        k.memset(ST[:], 0.0, [ST], en="dve")
                    for w in (wins if d == 0 else wins[::-1]):
                        it += 1
                        w0 = w * 128
                        op = ops[it % 2]
                        for j in range(6):
                            k.dma("sp", op[j][:], self.rw_ops[d, j, :, w0:w0 + 128].rearrange("(f p) t -> p f t", p=128),
                                  [self.rw_ops], [op[j]])
                        k.dma("sp", op[6][:], self.rw_v[:, w0:w0 + 128].rearrange("(f p) t -> p f t", p=128),
                              [self.rw_v], [op[6]])
                        Rt, Kt, Bt, At, Kh, Bh, Vf = op
                        for (src, dst, neg) in ((Vf, tms[0], False), (Kh, tms[1], False), (Bh, tms[2], True)):
                            ps = self.next_pf()
                            for ft in range(4):
                                k.tr(ps[:, ft * 128:(ft + 1) * 128], src[:, ft, :], self.ident_f[:], [src, self.ident_f], [ps])
                            if neg:
                                k.ts(dst[:], ps[:, :], -1.0, None, ALU.mult, None, [ps], [dst])
                            else:
                                k.cp(dst[:], ps[:, :], [ps], [dst], en="act")
                        Vtm, Khtm, Bhtm = tms
                        hb = lambda h: (h % 2) * 64
                        for nm, src in (("A", At), ("B", Bt), ("R", Rt)):
                            for p_ in range(2):
                                k.ts(mk[nm][p_][:], src[:], pmask[:, p_:p_ + 1], None, ALU.mult, None, [src, pmask], [mk[nm][p_]],
                                     en=("pool" if p_ else "dve"))
                        specs = ((Bt, "A", NT[0], 0, -1.0), (At, "B", NN[0], 1, -1.0), (Kt, "A", AakT, 0, 1.0),
                                 (Kt, "R", ArkT, 2, 1.0), (Bt, "R", ArbT, 2, -1.0))
                        for (L_, rn, dst, mi, sg) in specs:
                            for half in range(2):
                                ps = self.next_pf()
                                for hh in range(4):
                                    h = half * 4 + hh
                                    R_ = mk[rn][h % 2]
                                    k.mm(ps[:, hh * 128:(hh + 1) * 128], L_[:, h // 2, :], R_[:, h // 2, :],
                                         True, True, [L_, R_], [ps])
                                k.stt(dst[:, half * 4:(half + 1) * 4, :], ps[:, :].rearrange("p (h t) -> p h t", h=4), sg,
                                      msk[:, d, mi:mi + 1, :].to_broadcast([128, 4, 128]), ALU.mult, ALU.mult, [ps, msk], [dst])
                        k.tt(TT[:], NT[0][:], self.ident_f[:, :].unsqueeze(1).to_broadcast([128, 8, 128]), ALU.add,
                             [NT[0], self.ident_f], [TT])
                        cur = 0
                        for lev in range(1, 6):
                            nxt = 1 - cur
                            for half in range(2):
                                hs = slice(half * 4, (half + 1) * 4)
                                pn = self.next_pf()
                                for hh in range(4):
                                    h = half * 4 + hh
                                    k.mm(pn[:, hh * 128:(hh + 1) * 128], NT[cur][:, h, :], NN[cur][:, h, :], True, True,
                                         [NT[cur], NN[cur]], [pn])
                                k.cp(NN[nxt][:, hs, :], pn[:, :].rearrange("p (h t) -> p h t", h=4), [pn], [NN[nxt]], en="act")
                                if lev < 5:
                                    pt_ = self.next_pf()
                                    for hh in range(4):
                                        h = half * 4 + hh
                                        k.mm(pt_[:, hh * 128:(hh + 1) * 128], NN[cur][:, h, :], NT[cur][:, h, :], True, True,
                                             [NT[cur], NN[cur]], [pt_])
                                    k.cp(NT[nxt][:, hs, :], pt_[:, :].rearrange("p (h t) -> p h t", h=4), [pt_], [NT[nxt]], en="act")
                                pp = self.next_pf()
                                for hh in range(4):
                                    h = half * 4 + hh
                                    k.mm(pp[:, hh * 128:(hh + 1) * 128], NN[nxt][:, h, :], TT[:, h, :], True, True,
                                         [NN[nxt], TT], [pp])
                                k.tt(TT[:, hs, :], TT[:, hs, :], pp[:, :].rearrange("p (h t) -> p h t", h=4), ALU.add,
                                     [TT, pp], [TT])
                            cur = nxt
                        pw = self.next_pf()
                        for h in range(8):
                            k.mm(pw[:, h * 64:(h + 1) * 64], AakT[:, h, :], Vtm[:, h * 64:(h + 1) * 64], True, True,
                                 [AakT, Vtm], [pw])
                        k.cp(W1[:], pw[:, :], [pw], [W1])
                        po = self.pf[4 + it % 2]
                        o_ = osb[it % 2]
                        for cb in ((0, 64) if d == 0 else (64, 0)):
                            cs_ = slice(cb, cb + 64)
                            c64i = (w0 + cb) // 64
                            px = self.next_pf()
                            for h in range(8):
                                b0 = hb(h)
                                k.mm(px[cs_, h * 64:(h + 1) * 64], mk["A"][h % 2][:, h // 2, cs_], ST[:, h // 2, :],
                                     True, True, [mk["A"][h % 2], ST], [px])
                            k.tt(Wt[cs_, :], W1[cs_, :], px[cs_, :], ALU.add, [W1, px], [Wt])
                            pu = self.next_pf()
                            for h in range(8):
                                k.mm(pu[cs_, h * 64:(h + 1) * 64], TT[cs_, h, cs_], Wt[cs_, h * 64:(h + 1) * 64], True, True,
                                     [TT, Wt], [pu])
                            k.cp(Ut[cs_, :], pu[cs_, :], [pu], [Ut])
                            for h in range(8):
                                b0 = hb(h)
                                ft = h // 2
                                oreg = po[b0:b0 + 64, ft * 128 + cb:ft * 128 + cb + 64]
                                k.mm(oreg, ST[:, ft, :], mk["R"][h % 2][:, ft, cs_], True, False, [ST, mk["R"][h % 2]], [po])
                                k.mm(oreg, Vtm[cs_, h * 64:(h + 1) * 64], ArkT[cs_, h, cs_], False, False, [Vtm, ArkT], [po])
                                k.mm(oreg, Ut[cs_, h * 64:(h + 1) * 64], ArbT[cs_, h, cs_], False, True, [Ut, ArbT], [po])
                            pS = self.next_pf()
                            for h in range(8):
                                b0 = hb(h)
                                ft = h // 2
                                sreg = pS[b0:b0 + 64, ft * 64:(ft + 1) * 64]
                                k.mm(sreg, Khtm[cs_, h * 64:(h + 1) * 64], Vtm[cs_, h * 64:(h + 1) * 64], True, False,
                                     [Khtm, Vtm], [pS])
                                k.mm(sreg, Bhtm[cs_, h * 64:(h + 1) * 64], Ut[cs_, h * 64:(h + 1) * 64], False, True,
                                     [Bhtm, Ut], [pS])
                            k.tt(ST[:], ST[:], pc_all[:, :, d, c64i:c64i + 1].to_broadcast([128, 4, 64]), ALU.mult,
                                 [ST, pc_all], [ST])
                            k.tt(ST[:], ST[:], pS[:, 0:256].rearrange("p (f i) -> p f i", f=4), ALU.add, [ST, pS], [ST])
                        k.cp(o_[:], po[:, :].rearrange("p (f t) -> p f t", f=4), [po], [o_], en="act")
                        k.dma("pool", self.rw_o[d, :, w0:w0 + 128].rearrange("(f p) t -> p f t", p=128), o_[:], [o_], [self.rw_o])
                    if not is_s:
                        for h in range(8):
                            b0 = (h % 2) * 64
                            ps = self.next_pf()
                            k.tr(ps[0:64, 0:64], ST[b0:b0 + 64, h // 2, :], self.ident_f[b0:b0 + 64, b0:b0 + 64],
                                 [ST, self.ident_f], [ps])
                            k.cp(fint[:, h, :], ps[0:64, 0:64], [ps], [fint])
                        k.dma("pool", self.o_rw_fin[si, l, d].rearrange("h i j -> i h j"), fint[:], [fint], [self.o_rw_fin])
        with k.phase():
            of = [k.sb(f"rw_of{i}", [128, TS]) for i in range(2)]
            ob_ = [k.sb(f"rw_obb{i}", [128, TS]) for i in range(2)]
            bo = [k.sb(f"rw_bo{i}", [128, TS]) for i in range(2)]
            gt = [k.sb(f"rw_gt{i}", [128, TS]) for i in range(2)]
            xc = [k.sb(f"rw_xc{i}", [128, TS]) for i in range(2)]
            sq = [k.sb(f"rw_sq{i}", [128, TS]) for i in range(2)]
            oo = [k.sb(f"rw_oo{i}", [128, TS], BF16) for i in range(2)]
            n = 0
            for a in range(0, T, TS):
                for ft in range(4):
                    n += 1
                    i = n % 2
                    rs_ = slice(ft * 128, (ft + 1) * 128)
                    k.dma("sp", of[i][:], self.rw_o[0, rs_, a:a + TS], [self.rw_o], [of[i]])
                    k.dma("sp", ob_[i][:], self.rw_o[1, rs_, a:a + TS], [self.rw_o], [ob_[i]])
                    k.dma("sp", bo[i][:], self.rw_bonus[rs_, a:a + TS], [self.rw_bonus], [bo[i]])
                    k.dma("sp", gt[i][:], self.zT[zr["rw_gate"] + ft * 128:zr["rw_gate"] + (ft + 1) * 128, a:a + TS],
                          [self.zT], [gt[i]])
                    k.tt(of[i][:], of[i][:], ob_[i][:], ALU.add, [of[i], ob_[i]], [of[i]])
                    pm = self.next_pf()
                    k.mm(pm[:, :], blk[:, :], of[i][:], True, True, [blk, of[i]], [pm])
                    k.stt(xc[i][:], pm[:, :], -1.0 / 64, of[i][:], ALU.mult, ALU.add, [pm, of[i]], [xc[i]])
                    k.tt(sq[i][:], xc[i][:], xc[i][:], ALU.mult, [xc[i]], [sq[i]])
                    pv = self.next_pf()
                    k.mm(pv[:, :], blk[:, :], sq[i][:], True, True, [blk, sq[i]], [pv])
                    k.ts(sq[i][:], pv[:, :], 1.0 / 64, 64e-5, ALU.mult, ALU.add, [pv], [sq[i]])
                    k.act(sq[i][:], sq[i][:], AF.Sqrt, [sq[i]], [sq[i]])
                    k.op("dve", lambda e: e.reciprocal(out=sq[i][:], in_=sq[i][:]), [sq[i]], [sq[i]])
                    k.tt(xc[i][:], xc[i][:], sq[i][:], ALU.mult, [xc[i], sq[i]], [xc[i]])
                    k.ts(xc[i][:], xc[i][:], vc("rw_ln_g", ft), vc("rw_ln_b", ft), ALU.mult, ALU.add, [xc[i]] + V, [xc[i]])
                    k.tt(xc[i][:], xc[i][:], bo[i][:], ALU.add, [xc[i], bo[i]], [xc[i]])
                    k.act(gt[i][:], gt[i][:], AF.Silu, [gt[i]], [gt[i]])
                    k.tt(oo[i][:], xc[i][:], gt[i][:], ALU.mult, [xc[i], gt[i]], [oo[i]])
                    k.dma("pool", self.o_scr[0, rs_, a:a + TS], oo[i][:], [oo[i]], [self.o_scr])


def host_weights(inp):
    w_in = np.asarray(inp["w_in"], np.float32)
    cols = []
    o = _IW
    cols.append(w_in[:, :, o["rw_pre"]:o["rw_pre"] + 1664])
    cols.append(w_in[:, :, o["rw_gate"]:o["rw_gate"] + 512])
    cols.append(w_in[:, :, o["mla_q"]:o["mla_q"] + 384])
    cols.append(w_in[:, :, o["mla_ckv"]:o["mla_ckv"] + 256])
    misc = np.zeros((NL, D, 128), np.float32)
    misc[:, :, 0:32] = w_in[:, :, o["mla_krope"]:o["mla_krope"] + 32]
    misc[:, :, 32:48] = w_in[:, :, o["ssd_dt"]:o["ssd_dt"] + 16]
    cols.append(misc)
    cols.append(w_in[:, :, o["mla_gate"]:o["mla_gate"] + 512])
    cols.append(w_in[:, :, o["ssd_xbc"]:o["ssd_xbc"] + 768])
    cols.append(w_in[:, :, o["lru_x"]:o["lru_x"] + 512])
    cols.append(w_in[:, :, o["lru_gate"]:o["lru_gate"] + 512])
    cols.append(w_in[:, :, o["ssd_gate"]:o["ssd_gate"] + 512])
    w_in_a = np.ascontiguousarray(np.concatenate(cols, -1))
    assert w_in_a.shape[-1] == NZ_ALL
    sel2 = np.zeros((2, 2, 128), np.float32)
    sel2[0, 0] = 1.0
    sel2[1, 1] = 1.0
    lw = np.zeros((NL, 2, 2, 4, 128, 128), np.float32)
    for d in range(2):
        for gi, nm in enumerate(("lru_wa", "lru_wx")):
            w = np.asarray(inp[nm], np.float32)
            for ft in range(4):
                for kb in range(2):
                    lw[:, d, gi, ft, kb * 64:(kb + 1) * 64, kb * 64:(kb + 1) * 64] = w[:, d, ft * 2 + kb]
    lru_w = np.ascontiguousarray(lw.reshape(NL, 16, 128, 128).transpose(0, 2, 1, 3))
    kvu = np.asarray(inp["mla_kv_up"], np.float32).reshape(NL, 256, 8, 128)
    inv = 10000.0 ** (-np.arange(8, dtype=np.float32) / 8)
    W_pm = np.zeros((32, 96), np.float32)
    for m_ in range(32):
        sw = m_ + 8 if (m_ % 16) < 8 else m_ - 8
        W_pm[sw, 64 + m_] = 1.0
    selm = np.zeros((64, 16, 128), np.float32)
    seld = np.zeros((64, 2, 8), np.float32)
    selg = np.zeros((64, 128), np.float32)
    for d_ in range(2):
        for h_ in range(8):
            selm[d_ * 32 + h_, d_ * 8 + h_, :] = 1.0
            seld[d_ * 32 + h_, d_, h_ % 4] = 1.0
            selg[d_ * 32 + h_, (h_ // 4) * 64:(h_ // 4 + 1) * 64] = 1.0
    ii = np.arange(128)
    maskneg = np.zeros((128, 2, 128), np.float32)
    maskneg[:, 0, :] = np.where(ii[:, None] <= ii[None, :], 0.0, -30000.0)
    maskneg[:, 1, :] = np.where(ii[:, None] >= ii[None, :], 0.0, -30000.0)
    ssd_bc = np.concatenate([np.broadcast_to(np.asarray(inp["ssd_d"], np.float32)[:, None, :], (NL, 128, 8)),
                             np.broadcast_to(np.asarray(inp["ssd_norm_g"], np.float32)[:, None, :], (NL, 128, 512))], -1)
    rw_lw = np.concatenate([np.asarray(inp["rw_w2"], np.float32), np.asarray(inp["rw_a2"], np.float32)], 2)
    rw_lw = np.ascontiguousarray(rw_lw.transpose(0, 2, 1, 3))
    same = (ii[:, None] // 64) == (ii[None, :] // 64)
    rw_mask = np.zeros((128, 2, 3, 128), np.float32)
    rw_mask[:, 0, 0, :] = same & (ii[:, None] < ii[None, :])
    rw_mask[:, 0, 1, :] = same & (ii[:, None] > ii[None, :])
    rw_mask[:, 0, 2, :] = same & (ii[:, None] <= ii[None, :])
    rw_mask[:, 1, 0, :] = same & (ii[:, None] > ii[None, :])
    rw_mask[:, 1, 1, :] = same & (ii[:, None] < ii[None, :])
    rw_mask[:, 1, 2, :] = same & (ii[:, None] >= ii[None, :])
    cm64 = np.ones((128, 513), np.float32)
    cm64[:, ::64] = 0.0
    W = dict(
        pmask=np.ascontiguousarray(np.stack([(ii < 64), (ii >= 64)], 1).astype(np.float32)),
        rw_lw=rw_lw, rw_mask=rw_mask, blk64=same.astype(np.float32), cmask64=cm64,
        selm=selm, seld=seld, selg=selg, maskneg=maskneg, ssd_bc=np.ascontiguousarray(ssd_bc),
        q_up=np.ascontiguousarray(inp["mla_q_up"], np.float32),
        kv_up_k=np.ascontiguousarray(kvu[:, :, :, :64].reshape(NL, 256, 512)),
        kv_up_v=np.ascontiguousarray(kvu[:, :, :, 64:].reshape(NL, 256, 512)),
        pm96=W_pm,
        _inv=inv,
        w_in_m=np.ascontiguousarray(w_in[:, :, o["merge"]:]),
        w_branch=np.ascontiguousarray(inp["w_branch"], np.float32),
        w_out=np.ascontiguousarray(inp["w_out"], np.float32),
        lru_w=lru_w,
        ada_w=np.ascontiguousarray(inp["ada_w"], np.float32),
        ada_bg=np.ascontiguousarray(np.asarray(inp["ada_b"], np.float32)[:, None, 2048:]),
        w_in_a=w_in_a,
        vecs=host_vecs(inp).pack(),
        ident=np.eye(128, dtype=np.float32),
        sel2=sel2,
    )
    return W


def core_inputs(inp, W, core, cfg):
    b = core // 2
    xp = np.asarray(inp["x_prompt"], np.float32)[core * cfg.n_prompt:(core + 1) * cfg.n_prompt, :cfg.lp]
    xs = np.asarray(inp["x_sample"], np.float32)[b, :cfg.ls]
    x_all = np.ascontiguousarray(np.concatenate([xp.reshape(-1, D), xs], 0))
    cond = np.stack([np.asarray(inp["c_ctx"], np.float32), np.asarray(inp["c"], np.float32)[b]], 0)
    condT = np.ascontiguousarray(cond.reshape(2, 8, 128).transpose(2, 1, 0))
    m = dict(W)
    inv = m.pop("_inv")
    t = np.arange(cfg.ls)
    row = (t // 64).astype(np.float32)
    col = (t % 64).astype(np.float32)
    ar, ac = row[None, :] * inv[:, None], col[None, :] * inv[:, None]
    cosT = np.concatenate([np.cos(ar), np.cos(ar), np.cos(ac), np.cos(ac)], 0)
    sinT = np.concatenate([-np.sin(ar), np.sin(ar), -np.sin(ac), np.sin(ac)], 0)
    cm = np.ones((64, cfg.T + 1), np.float32)
    cm[:, ::128] = 0.0
    m["cmask"] = cm
    m["ssd_h0"] = np.ascontiguousarray(np.asarray(inp["state_ssd"], np.float32)[b].transpose(0, 1, 3, 2, 4))
    m["rw_h0"] = np.ascontiguousarray(np.asarray(inp["state_rwkv"], np.float32)[b].transpose(0, 1, 3, 2, 4))
    m["rope_cs"] = np.ascontiguousarray(np.stack([cosT, sinT], 0).astype(np.float32))
    m["cache_ckv"] = np.ascontiguousarray(np.asarray(inp["cache_mla_ckv"], np.float32)[b])
    m["cache_kr"] = np.ascontiguousarray(np.asarray(inp["cache_mla_krope"], np.float32)[b])
    sl = np.asarray(inp["state_lru"], np.float32)[b]
    lru_h0 = np.ascontiguousarray(sl.reshape(NL, 2, 4, 128).transpose(0, 3, 1, 2))
    m.update(x_all=x_all, condT=condT, lru_h0=lru_h0)
    return m


_PROG = {}


def kernel(**inp):
    cfg = Cfg(n_prompt=4, lp=256, ls=4096, debug=DEBUG_FLAGS)
    if "p" not in _PROG:
        prog = Prog(cfg)
        prog.build()
        _PROG["p"] = prog
    prog = _PROG["p"]
    W = host_weights(inp)
    in_maps = []
    for core in range(8):
        m = core_inputs(inp, W, core, cfg)
        in_maps.append({k_: np.ascontiguousarray(v) for k_, v in m.items() if k_ in prog.inputs})
    res = run_bass_kernel_spmd(prog.nc, in_maps, core_ids=list(range(8)))
    R = res.results
    npq = cfg.n_prompt
    y_prompt = np.concatenate([R[c]["y_all"][:cfg.TP].reshape(npq, cfg.lp, D) for c in range(8)], 0)
    y_sample = np.stack([R[2 * b]["y_all"][cfg.TP:] for b in range(4)], 0)
    ckv = np.concatenate([R[c]["o_ckv"].reshape(NL, 256, npq, cfg.lp).transpose(2, 0, 3, 1) for c in range(8)], 0)
    kr = np.concatenate([R[c]["o_kr"].reshape(NL, 32, npq, cfg.lp).transpose(2, 0, 3, 1) for c in range(8)], 0)
    if "o_rw_fin" in R[0]:
        rw = np.concatenate([R[c]["o_rw_fin"] for c in range(8)], 0)
    else:
        rw = np.zeros((32, NL, 2, 8, 64, 64), np.float32)
    if "o_ssd_fin" in R[0]:
        ssd = np.concatenate([R[c]["o_ssd_fin"].transpose(1, 0, 2, 4, 3, 5) for c in range(8)], 0)
    else:
        ssd = np.zeros((32, NL, 2, 8, 64, 64), np.float32)
    lru = np.concatenate([R[c]["o_lru_fin"].transpose(2, 0, 3, 4, 1).reshape(npq, NL, 2, 512) for c in range(8)], 0)
    f = lambda a: np.ascontiguousarray(a, dtype=np.float32)
    return (f(y_prompt), f(y_sample), f(ckv), f(kr), f(rw), f(ssd), f(lru))
```

```python
import contextlib
import numpy as np
import ml_dtypes
import concourse.bass as bass
import concourse.mybir as mybir
from concourse.bass_utils import run_bass_kernel_spmd

F32 = mybir.dt.float32
BF16 = mybir.dt.bfloat16
AF = mybir.ActivationFunctionType
ALU = mybir.AluOpType
AX = mybir.AxisListType

D = 1024
NL = 2
NDMA_SLOTS = 6
RW_BF16 = False
DEBUG_FLAGS = ()
SAME_ENGINE_SYNC = ("act", "dve", "pool")


class Res:
    __slots__ = ("w", "r", "name")

    def __init__(self, name=""):
        self.w = []
        self.r = {}
        self.name = name


class Tl:
    def __init__(self, h, name, psum=False):
        self.h = h
        self.res = Res(name)
        self.name = name
        self.psum = psum

    def __getitem__(self, k):
        return self.h[k]


class Eng:
    def __init__(self, name, h, sem):
        self.name = name
        self.h = h
        self.sem = sem
        self.count = 0
        self.waited = {}
        self.dma_sems = []
        self.dma_vals = []
        self.dma_n = 0


def _res(x):
    return x.res if isinstance(x, Tl) else x


class KB:
    def __init__(self, nc):
        self.nc = nc
        self.es = contextlib.ExitStack()
        self.stacks = [self.es]
        self.eng = {}
        self.uid = 0
        for name, h in (("pe", nc.tensor), ("dve", nc.vector), ("act", nc.scalar),
                        ("pool", nc.gpsimd), ("sp", nc.sync)):
            sem = self.es.enter_context(nc.semaphore("s_" + name))
            self.eng[name] = Eng(name, h, sem)
        for qn in ("sp", "pool", "act"):
            E = self.eng[qn]
            for i in range(NDMA_SLOTS):
                E.dma_sems.append(self.es.enter_context(nc.semaphore(f"d_{qn}{i}")))
                E.dma_vals.append(0)

    def sb(self, name, shape, dtype=F32):
        self.uid += 1
        nm = f"{name}_{self.uid}"
        h = self.stacks[-1].enter_context(self.nc.sbuf_tensor(nm, list(shape), dtype))
        return Tl(h, nm)

    def ps(self, name, shape, dtype=F32):
        self.uid += 1
        nm = f"{name}_{self.uid}"
        h = self.stacks[-1].enter_context(self.nc.psum_tensor(nm, list(shape), dtype))
        return Tl(h, nm, psum=True)

    def dram(self, name, shape, dtype=F32, kind="Internal"):
        h = self.nc.dram_tensor(name, list(shape), dtype, kind=kind)
        return Tl(h.ap(), name)

    @contextlib.contextmanager
    def phase(self):
        self.barrier()
        st = contextlib.ExitStack()
        self.stacks.append(st)
        try:
            with st:
                yield
                self.barrier()
        finally:
            self.stacks.pop()

    def _wait(self, E, ev):
        sem, val, src = ev
        if src is E and E.name not in SAME_ENGINE_SYNC:
            return
        k = id(sem)
        if E.waited.get(k, 0) >= val:
            return
        E.h.wait_ge(sem, val)
        E.waited[k] = val

    def _deps(self, E, reads, writes):
        for r in reads:
            for ev in _res(r).w:
                self._wait(E, ev)
        for w in writes:
            rs = _res(w)
            for ev in rs.w:
                self._wait(E, ev)
            for ev in rs.r.values():
                self._wait(E, ev)

    def _commit(self, ev, key, reads, writes):
        for r in reads:
            _res(r).r[key] = ev
        for w in writes:
            rs = _res(w)
            rs.w = [ev]
            rs.r = {}

    def op(self, en, fn, reads=(), writes=()):
        E = self.eng[en]
        pr = [r for r in reads if isinstance(r, Tl) and r.psum]
        if pr:
            reads = [r for r in reads if not (isinstance(r, Tl) and r.psum)]
            writes = list(writes) + [r for r in pr if r not in writes]
        self._deps(E, reads, writes)
        ins = fn(E.h)
        E.count += 1
        ins.then_inc(E.sem, 1)
        ev = (E.sem, E.count, E)
        self._commit(ev, en, reads, writes)
        return ins

    def dma(self, qn, out, in_, reads=(), writes=(), **kw):
        E = self.eng[qn]
        self._deps(E, reads, writes)
        slot = E.dma_n % NDMA_SLOTS
        E.dma_n += 1
        sem = E.dma_sems[slot]
        pv = E.dma_vals[slot]
        if pv > 0:
            self._wait(E, (sem, pv, None))
        E.h.dma_start(out=out, in_=in_, **kw).then_inc(sem, 16)
        E.dma_vals[slot] = pv + 16
        ev = (sem, pv + 16, None)
        self._commit(ev, ("dma", qn, slot), reads, writes)

    def barrier(self):
        evs = []
        for E in self.eng.values():
            if E.count:
                evs.append((E.sem, E.count, E))
            for s, v in zip(E.dma_sems, E.dma_vals):
                if v:
                    evs.append((s, v, None))
        for E in self.eng.values():
            for ev in evs:
                self._wait(E, ev)

    def mm(self, out, lhsT, rhs, start, stop, reads, writes):
        return self.op("pe", lambda e: e.matmul(out, lhsT=lhsT, rhs=rhs, start=start, stop=stop),
                       reads, writes)

    def tr(self, out, in_, ident, reads, writes):
        return self.op("pe", lambda e: e.transpose(out, in_, ident), reads, writes)

    def act(self, out, in_, func, reads, writes, bias=None, scale=None, accum_out=None, en="act"):
        kw = {}
        if bias is not None:
            kw["bias"] = bias
        if scale is not None:
            kw["scale"] = scale
        if accum_out is not None:
            kw["accum_out"] = accum_out
        return self.op(en, lambda e: e.activation(out=out, in_=in_, func=func, **kw), reads, writes)

    def tt(self, out, in0, in1, op, reads, writes, en="dve"):
        return self.op(en, lambda e: e.tensor_tensor(out=out, in0=in0, in1=in1, op=op), reads, writes)

    def ts(self, out, in0, s1, s2, op0, op1, reads, writes, en="dve", accum_out=None):
        kw = {}
        if accum_out is not None:
            kw["accum_out"] = accum_out
        if op1 is None:
            return self.op(en, lambda e: e.tensor_scalar(out=out, in0=in0, scalar1=s1, scalar2=None,
                                                         op0=op0, **kw), reads, writes)
        return self.op(en, lambda e: e.tensor_scalar(out=out, in0=in0, scalar1=s1, scalar2=s2,
                                                     op0=op0, op1=op1, **kw), reads, writes)

    def stt(self, out, in0, scalar, in1, op0, op1, reads, writes):
        return self.op("dve", lambda e: e.scalar_tensor_tensor(out=out, in0=in0, scalar=scalar, in1=in1,
                                                               op0=op0, op1=op1), reads, writes)

    def cp(self, out, in_, reads, writes, en="dve"):
        if en == "act":
            return self.op("act", lambda e: e.copy(out=out, in_=in_), reads, writes)
        return self.op(en, lambda e: e.tensor_copy(out=out, in_=in_), reads, writes)

    def scan(self, out, d0, d1, init, reads, writes, op0=ALU.mult, op1=ALU.add):
        return self.op("dve", lambda e: e.tensor_tensor_scan(out=out, data0=d0, data1=d1, initial=init,
                                                             op0=op0, op1=op1), reads, writes)

    def memset(self, ap, val, writes, en="pool"):
        return self.op(en, lambda e: e.memset(ap, val), (), writes)

    def finish(self):
        self.barrier()


ZROWS = {}
_o = 0
for _n, _w in (("rw_r", 512), ("rw_k", 512), ("rw_v", 512), ("rw_lora", 128), ("rw_gate", 512),
               ("mla_q", 384), ("mla_ckv", 256), ("misc", 128), ("mla_gate", 512),
               ("ssd_xbc", 768), ("lru_x", 512), ("lru_gate", 512)):
    ZROWS[_n] = (_o, _w)
    _o += _w
NZ_FM = _o
NZ_ALL = NZ_FM + 512
_IW = dict(rw_pre=0, rw_gate=1664, mla_q=2176, mla_ckv=2560, mla_krope=2816, mla_gate=2848,
           ssd_gate=3360, ssd_xbc=3872, ssd_dt=4640, lru_x=4656, lru_gate=5168, merge=5680)


class VecPack:
    def __init__(self):
        self.cols = {}
        self.n = 0
        self.data = []

    def add(self, name, arr2d):
        k = arr2d.shape[-1]
        self.cols[name] = (self.n, k)
        self.n += k
        self.data.append(np.asarray(arr2d, np.float32))

    def fm(self, name, v):
        v = np.asarray(v, np.float32)
        n = v.shape[-1]
        if n % 128:
            pad = 128 - n % 128
            v = np.concatenate([v, np.zeros(v.shape[:-1] + (pad,), np.float32)], -1)
        k = v.shape[-1] // 128
        self.add(name, v.reshape(v.shape[0], k, 128).transpose(0, 2, 1))

    def pack(self):
        return np.ascontiguousarray(np.concatenate(self.data, -1))


def vec_layout():
    vp = VecPack()
    z = lambda *s: np.zeros(s, np.float32)
    vp.fm("ada_b_ss", z(NL, 2048))
    vp.fm("norm_g", z(NL, 1024))
    vp.fm("mla_qa_g", z(NL, 384))
    vp.fm("mla_kva_g", z(NL, 256))
    vp.fm("mla_qn_g", z(NL, 96))
    vp.fm("mla_kn_g", z(NL, 96))
    vp.fm("rw_mu", z(NL, 1664))
    vp.fm("rw_w0", z(NL, 1024))
    vp.fm("rw_a0", z(NL, 1024))
    vp.fm("rw_kk", z(NL, 512))
    vp.fm("rw_ka", z(NL, 512))
    vp.fm("rw_rk", z(NL, 512))
    vp.fm("rw_ln_g", z(NL, 512))
    vp.fm("rw_ln_b", z(NL, 512))
    vp.fm("ssd_cw", z(NL, 4 * 768))
    vp.fm("ssd_cb", z(NL, 768))
    vp.fm("ssd_dtb", z(NL, 64))
    vp.fm("ssd_alog", z(NL, 64))
    vp.fm("lru_cw", z(NL, 4 * 512))
    vp.fm("lru_cb", z(NL, 512))
    vp.fm("lru_ba", z(NL, 2 * 512))
    vp.fm("lru_bx", z(NL, 2 * 512))
    vp.fm("lru_lam", z(NL, 2 * 512))
    return vp


def host_vecs(inp):
    vp = VecPack()
    vp.fm("ada_b_ss", inp["ada_b"][:, :2048])
    vp.fm("norm_g", inp["norm_g"])
    r2 = lambda a: np.asarray(a, np.float32).reshape(NL, -1)
    vp.fm("mla_qa_g", inp["mla_qa_g"])
    vp.fm("mla_kva_g", inp["mla_kva_g"])
    vp.fm("mla_qn_g", inp["mla_qn_g"])
    vp.fm("mla_kn_g", inp["mla_kn_g"])
    vp.fm("rw_mu", inp["rw_mu"])
    vp.fm("rw_w0", r2(inp["rw_w0"]))
    vp.fm("rw_a0", r2(inp["rw_a0"]))
    vp.fm("rw_kk", inp["rw_kk"])
    vp.fm("rw_ka", inp["rw_ka"])
    vp.fm("rw_rk", r2(inp["rw_rk"]))
    vp.fm("rw_ln_g", inp["rw_ln_g"])
    vp.fm("rw_ln_b", inp["rw_ln_b"])
    vp.fm("ssd_cw", r2(inp["ssd_conv_w"]))
    vp.fm("ssd_cb", inp["ssd_conv_b"])
    def d64(a):
        a = np.asarray(a, np.float32)
        o_ = np.zeros((NL, 64), np.float32)
        o_[:, 0:8] = a[:, 0]
        o_[:, 32:40] = a[:, 1]
        return o_
    vp.fm("ssd_dtb", d64(inp["ssd_dt_bias"]))
    vp.fm("ssd_alog", d64(inp["ssd_a_log"]))
    vp.fm("lru_cw", r2(inp["lru_conv_w"]))
    vp.fm("lru_cb", inp["lru_conv_b"])
    vp.fm("lru_ba", r2(inp["lru_ba"]))
    vp.fm("lru_bx", r2(inp["lru_bx"]))
    vp.fm("lru_lam", r2(inp["lru_lambda"]))
    return vp


class Cfg:
    def __init__(self, n_prompt=4, lp=256, ls=4096, debug=()):
        self.n_prompt = n_prompt
        self.lp = lp
        self.ls = ls
        self.TP = n_prompt * lp
        self.T = self.TP + ls
        self.seqs = [(i * lp, lp, False) for i in range(n_prompt)] + [(self.TP, ls, True)]
        self.debug = set(debug)
        assert self.T % 512 == 0 and lp % 128 == 0 and ls % 512 == 0 and self.TP % 512 == 0


class Prog:
    def __init__(self, cfg):
        self.cfg = cfg
        self.nc = bass.Bass("TRN2", target_bir_lowering=False)
        self.k = KB(self.nc)
        self.inputs = {}
        self.outputs = {}
        self.vl = vec_layout()

    def din(self, name, shape, dtype=F32):
        t = self.k.dram(name, shape, dtype, kind="ExternalInput")
        self.inputs[name] = t
        return t

    def dout(self, name, shape, dtype=F32):
        t = self.k.dram(name, shape, dtype, kind="ExternalOutput")
        self.outputs[name] = t
        return t

    def vcol(self, name, j=0, n=1):
        o, k = self.vl.cols[name]
        assert j + n <= k
        return self.vecs[:, o + j:o + j + n]

    def build(self):
        cfg, k = self.cfg, self.k
        T = cfg.T
        with k.es:
            self.x_in = self.din("x_all", [T, D])
            self.condT = self.din("condT", [128, 8, 2])
            self.ada_w = self.din("ada_w", [NL, D, 3 * D])
            self.ada_bg = self.din("ada_bg", [NL, 1, D])
            self.w_in_a = self.din("w_in_a", [NL, D, NZ_ALL])
            self.vecs_d = self.din("vecs", [NL, 128, self.vl.n])
            self.ident_d = self.din("ident", [128, 128])
            self.sel2_d = self.din("sel2", [2, 2, 128])
            self.lru_w = self.din("lru_w", [NL, 128, 16, 128])
            self.lru_h0 = self.din("lru_h0", [NL, 128, 2, 4])
            self.o_lru_fin = self.dout("o_lru_fin", [NL, 128, cfg.n_prompt, 2, 4])
            self.q_up = self.din("q_up", [NL, 384, 768])
            self.kv_up_k = self.din("kv_up_k", [NL, 256, 512])
            self.kv_up_v = self.din("kv_up_v", [NL, 256, 512])
            self.cache_ckv = self.din("cache_ckv", [NL, 256, 256])
            self.cache_kr = self.din("cache_kr", [NL, 256, 32])
            self.rope_cs = self.din("rope_cs", [2, 32, cfg.ls])
            self.pm96_d = self.din("pm96", [32, 96])
            self.o_ckv = self.dout("o_ckv", [NL, 256, cfg.TP])
            self.o_kr = self.dout("o_kr", [NL, 32, cfg.TP])
            self.ssd_bc = self.din("ssd_bc", [NL, 128, 8 + 512])
            self.ssd_h0 = self.din("ssd_h0", [NL, 2, 64, 8, 64])
            self.selm_d = self.din("selm", [64, 16, 128])
            self.seld_d = self.din("seld", [64, 2, 8])
            self.selg_d = self.din("selg", [64, 128])
            self.maskneg_d = self.din("maskneg", [128, 2, 128])
            self.cmask_d = self.din("cmask", [64, T + 1])
            self.o_ssd_fin = self.dout("o_ssd_fin", [NL, cfg.n_prompt, 2, 64, 8, 64])
            self.ssd_y = k.dram("ssd_y_scr", [T, 512])
            self.rw_lw = self.din("rw_lw", [NL, 128, 2, 512])
            self.rw_h0 = self.din("rw_h0", [NL, 2, 64, 8, 64])
            self.rw_mask_d = self.din("rw_mask", [128, 2, 3, 128])
            self.blk64_d = self.din("blk64", [128, 128])
            self.pmask_d = self.din("pmask", [128, 2])
            self.cmask64_d = self.din("cmask64", [128, 513])
            self.o_rw_fin = self.dout("o_rw_fin", [cfg.n_prompt, NL, 2, 8, 64, 64])
            self.rw_ops = k.dram("rw_ops_scr", [2, 6, 512, T])
            self.rw_v = k.dram("rw_v_scr", [512, T])
            self.rw_bonus = k.dram("rw_bonus_scr", [512, T])
            self.rw_o = k.dram("rw_o_scr", [2, 512, T])
            self.w_in_m = self.din("w_in_m", [NL, D, 4 * D])
            self.w_branch = self.din("w_branch", [NL, 4, 512, D])
            self.w_out = self.din("w_out", [NL, D, D])
            self.y_all = self.dout("y_all", [T, D])
            self.y1 = k.dram("y1_scr", [T, D])
            self.mT_scr = k.dram("mT_scr", [D, T], BF16)
            self.o_scr = k.dram("o_scr", [4, 512, T], BF16)
            if "o" in cfg.debug:
                self.dbg_o = self.dout("dbg_o", [4, 512, T], BF16)
            self.zT = k.dram("zT_scr", [NZ_FM, T])
            self.zg_tm = k.dram("zg_tm_scr", [T, 512])
            self.hT_scr = k.dram("hT_scr", [D, T], BF16)
            if "z" in cfg.debug:
                self.dbg_zT = self.dout("dbg_zT", [NZ_FM, T])
                self.dbg_zg = self.dout("dbg_zg", [T, 512])
            self.vecs = k.sb("vecs", [128, self.vl.n])
            self.ident_f = k.sb("ident_f", [128, 128])
            self.ident_b = k.sb("ident_b", [128, 128], BF16)
            self.sel2 = k.sb("sel2", [2, 2, 128])
            self.sc = k.sb("sc", [128, 8, 2])
            self.gmod = k.sb("gmod", [128, 8, 2])
            self.shiftc = k.sb("shiftc", [128, 8, 2])
            self.gate_bc = [k.sb(f"gate_bc{g}", [128, D]) for g in range(2)]
            self.ones_f = k.sb("ones_f", [128, 128])
            k.memset(self.ones_f[:], 1.0, [self.ones_f])
            self.pf = [k.ps(f"pf{i}", [128, 512]) for i in range(6)]
            self.pb = [k.ps(f"pb{i}", [128, 1024], BF16) for i in range(2)]
            self.pfi = 0
            k.dma("sp", self.ident_f[:], self.ident_d[:, :], (), [self.ident_f])
            k.dma("sp", self.sel2[:], self.sel2_d[:, :, :], (), [self.sel2])
            k.cp(self.ident_b[:], self.ident_f[:], [self.ident_f], [self.ident_b])
            k.dma("sp", self.sc[:], self.condT[:, :, :], (), [self.sc])
            k.act(self.sc[:], self.sc[:], AF.Silu, [self.sc], [self.sc])

            for l in range(NL):
                self.layer(l)
                if l == 0 and "stop0" in cfg.debug:
                    break
            k.finish()
        return self.nc

    def dump(self, name, tl, ap, shape, dtype=F32):
        if name not in self.cfg.debug:
            return
        t = self.dout("dump_" + name, shape, dtype)
        self.k.dma("sp", t[tuple(slice(None) for _ in shape)], ap, [tl], [t])

    def next_pf(self):
        p = self.pf[self.pfi % 4]
        self.pfi += 1
        return p

    def layer(self, l):
        cfg, k = self.cfg, self.k
        with k.phase():
            k.dma("sp", self.vecs[:], self.vecs_d[l], (), [self.vecs])
            self.phase_mod(l)
        if "zero_o" in cfg.debug:
            with k.phase():
                zt = k.sb("zt", [128, cfg.T], BF16)
                k.memset(zt[:], 0.0, [zt])
                for m in range(4):
                    for ft in range(4):
                        k.dma("sp", self.o_scr[m, ft * 128:(ft + 1) * 128, :], zt[:], [zt], [self.o_scr])
        with k.phase():
            self.phase_front(l)
        with k.phase():
            self.phase_lru(l)
        if "norw" not in cfg.debug:
            with k.phase():
                self.phase_rwkv(l)
        if "nossd" not in cfg.debug:
            with k.phase():
                self.phase_ssd(l)
        if "nomla" not in cfg.debug:
            with k.phase():
                self.phase_mla(l)
        with k.phase():
            self.phase_merge(l)
        with k.phase():
            self.phase_out(l)
        if "y1" in cfg.debug and l == 0:
            k.barrier()
            d_ = self.dout("dbg_y1", [cfg.T, D])
            k.dma("sp", d_[:, :], self.y1[:, :], [self.y1], [d_])
            k.barrier()
        if "o" in cfg.debug and l == 0:
            k.barrier()
            k.dma("sp", self.dbg_o[:, :, :], self.o_scr[:, :, :], [self.o_scr], [self.dbg_o])
            k.barrier()

    def phase_mod(self, l):
        k = self.k
        wst = [k.sb(f"adaw{i}", [128, 8, 512]) for i in range(2)]
        modc = k.sb("modc", [128, 16, 2])
        grow = k.sb("grow", [2, D])
        gb = k.sb("gb", [2, D])
        for g in range(2):
            k.dma("pool", gb[g:g + 1, :], self.ada_bg[l], (), [gb])
        for blk in range(6):
            w = wst[blk % 2]
            k.dma("sp", w[:], self.ada_w[l][:, blk * 512:(blk + 1) * 512].rearrange("(c p) n -> p c n", p=128),
                  (), [w])
            if blk < 4:
                for jt in range(4):
                    ps = self.next_pf()
                    for c in range(8):
                        k.mm(ps[:, 0:2], w[:, c, jt * 128:(jt + 1) * 128], self.sc[:, c, :], c == 0, c == 7,
                             [w, self.sc], [ps])
                    k.cp(modc[:, blk * 4 + jt, :], ps[:, 0:2], [ps], [modc])
            else:
                ps = self.next_pf()
                for c in range(8):
                    k.mm(ps[0:2, :], self.sc[:, c, :], w[:, c, :], c == 0, c == 7, [w, self.sc], [ps])
                hs = slice((blk - 4) * 512, (blk - 3) * 512)
                k.tt(grow[:, hs], ps[0:2, :], gb[:, hs], ALU.add, [ps, gb], [grow])
        ab = self.vcol("ada_b_ss", 0, 16)
        for g in range(2):
            k.tt(modc[:, :, g], modc[:, :, g], ab, ALU.add, [modc, self.vecs], [modc])
            k.cp(self.shiftc[:, :, g], modc[:, 0:8, g], [modc], [self.shiftc])
            k.stt(self.gmod[:, :, g], modc[:, 8:16, g], 1.0, self.vcol("norm_g", 0, 8), ALU.add, ALU.mult,
                  [modc, self.vecs], [self.gmod])
            for half in range(2):
                ps = self.next_pf()
                k.mm(ps[:, :], self.sel2[:, g, :], grow[:, half * 512:(half + 1) * 512], True, True,
                     [self.sel2, grow], [ps])
                k.cp(self.gate_bc[g][:, half * 512:(half + 1) * 512], ps[:, :], [ps], [self.gate_bc[g]])

    def phase_front(self, l):
        cfg, k = self.cfg, self.k
        T = cfg.T
        x_src = self.x_in if l == 0 else self.y1
        hT = k.sb("hT", [128, 8, T], BF16)
        with k.phase():
            xt = [k.sb(f"xt{i}", [128, D]) for i in range(3)]
            xn = [k.sb(f"xn{i}", [128, D], BF16) for i in range(2)]
            junk = k.sb("junk", [128, D], BF16)
            ss = [k.sb(f"ss{i}", [128, 1]) for i in range(2)]
            for st in range(T // 128):
                g = 0 if st * 128 < cfg.TP else 1
                x = xt[st % 3]
                xb = xn[st % 2]
                s = ss[st % 2]
                pb = self.pb[st % 2]
                k.dma("sp", x[:], x_src[st * 128:(st + 1) * 128, :], [x_src], [x])
                k.act(junk[:], x[:], AF.Square, [x], [junk, s], accum_out=s[:])
                k.ts(s[:], s[:], 1.0 / D, 1e-6, ALU.mult, ALU.add, [s], [s])
                k.act(s[:], s[:], AF.Sqrt, [s], [s])
                k.op("dve", lambda e: e.reciprocal(out=s[:], in_=s[:]), [s], [s])
                k.act(xb[:], x[:], AF.Copy, [x, s], [xb], scale=s[:])
                for c in range(8):
                    k.tr(pb[:, c * 128:(c + 1) * 128], xb[:, c * 128:(c + 1) * 128], self.ident_b[:],
                         [xb, self.ident_b], [pb])
                ho = hT[:, :, st * 128:(st + 1) * 128]
                pv = pb[:].rearrange("p (c t) -> p c t", c=8)
                k.tt(ho, pv, self.gmod[:, :, g:g + 1].to_broadcast([128, 8, 128]), ALU.mult,
                     [pb, self.gmod], [hT])
                k.tt(ho, ho, self.shiftc[:, :, g:g + 1].to_broadcast([128, 8, 128]), ALU.add,
                     [hT, self.shiftc], [hT])
        for c in range(8):
            k.dma("pool", self.hT_scr[c * 128:(c + 1) * 128, :], hT[:, c, :], [hT], [self.hT_scr])
        wst = [k.sb(f"wst{i}", [128, 8, 512]) for i in range(2)]
        wbf = [k.sb(f"wbf{i}", [128, 8, 512], BF16) for i in range(2)]
        zst = [k.sb(f"zst{i}", [128, 512]) for i in range(4)]
        blocks = [(c0, min(512, NZ_FM - c0), False) for c0 in range(0, NZ_FM, 512)] + [(NZ_FM, 512, True)]
        zi = 0
        for blk, (c0, ncol, tm_block) in enumerate(blocks):
            ws, wb = wst[blk % 2], wbf[blk % 2]
            k.dma("sp", ws[:, :, :ncol], self.w_in_a[l][:, c0:c0 + ncol].rearrange("(c p) n -> p c n", p=128),
                  (), [ws])
            k.cp(wb[:, :, :ncol], ws[:, :, :ncol], [ws], [wb], en="pool")
            for tt in range(T // 512):
                ts_ = slice(tt * 512, (tt + 1) * 512)
                if not tm_block:
                    for jt in range(ncol // 128):
                        ps = self.next_pf()
                        for c in range(8):
                            k.mm(ps[:, :], wb[:, c, jt * 128:(jt + 1) * 128], hT[:, c, ts_], c == 0, c == 7,
                                 [wb, hT], [ps])
                        z = zst[zi % 4]
                        if zi % 2 == 0:
                            k.cp(z[:], ps[:, :], [ps], [z])
                        else:
                            k.cp(z[:], ps[:, :], [ps], [z], en="act")
                        zi += 1
                        r0 = c0 + jt * 128
                        k.dma("pool", self.zT[r0:r0 + 128, ts_], z[:], [z], [self.zT])
                else:
                    for sub in range(4):
                        ps = self.next_pf()
                        t0 = tt * 512 + sub * 128
                        for c in range(8):
                            k.mm(ps[:, :], hT[:, c, t0:t0 + 128], wb[:, c, :], c == 0, c == 7, [wb, hT], [ps])
                        z = zst[zi % 4]
                        if zi % 2 == 0:
                            k.cp(z[:], ps[:, :], [ps], [z])
                        else:
                            k.cp(z[:], ps[:, :], [ps], [z], en="act")
                        zi += 1
                        k.dma("pool", self.zg_tm[t0:t0 + 128, :], z[:], [z], [self.zg_tm])
        if "z" in cfg.debug and l == 0:
            k.barrier()
            k.dma("sp", self.dbg_zT[:, :], self.zT[:, :], [self.zT], [self.dbg_zT])
            k.dma("sp", self.dbg_zg[:, :], self.zg_tm[:, :], [self.zg_tm], [self.dbg_zg])


    def groups(self):
        cfg = self.cfg
        return [(0, cfg.n_prompt, cfg.lp), (cfg.TP, 1, cfg.ls)]

    def gview(self, ap2d, grp, lo, hi):
        s0, ns, ln = grp
        return ap2d[:, s0:s0 + ns * ln].rearrange("p (s t) -> p s t", s=ns)[:, :, lo:hi]

    def dwconv(self, xc, xl, wcol, bcol, reads):
        k = self.k
        T = self.cfg.T
        k.ts(xc[:, :T], xl[:, :T], wcol(2), bcol, ALU.mult, ALU.add, [xl] + reads, [xc])
        for grp in self.groups():
            ln = grp[2]
            for kk, off in ((0, -2), (1, -1), (3, 1)):
                if off < 0:
                    src = self.gview(xl[:, :T], grp, 0, ln + off)
                    dst = self.gview(xc[:, :T], grp, -off, ln)
                else:
                    src = self.gview(xl[:, :T], grp, off, ln)
                    dst = self.gview(xc[:, :T], grp, 0, ln - off)
                k.stt(dst, src, wcol(kk), dst, ALU.mult, ALU.add, [xl, xc] + reads, [xc])

    def phase_lru(self, l):
        cfg, k = self.cfg, self.k
        T = cfg.T
        zo, _ = ZROWS["lru_x"]
        go, _ = ZROWS["lru_gate"]
        wts = k.sb("lru_wts", [128, 16, 128])
        h0 = k.sb("lru_h0", [128, 2, 4])
        clam = k.sb("lru_clam", [128, 8])
        fin = k.sb("lru_fin", [128, cfg.n_prompt, 2, 4])
        k.dma("sp", wts[:], self.lru_w[l], (), [wts])
        k.dma("sp", h0[:], self.lru_h0[l], (), [h0])
        k.act(clam[:], self.vcol("lru_lam", 0, 8), AF.Exp, [self.vecs], [clam], scale=-1.0)
        k.act(clam[:], clam[:], AF.Ln, [clam], [clam], bias=1.0)
        k.ts(clam[:], clam[:], -8.0, None, ALU.mult, None, [clam], [clam])
        xl = k.sb("lru_xl", [128, T])
        gt = k.sb("lru_gt", [128, T])
        xc = k.sb("lru_xc", [128, T])
        ta = k.sb("lru_ta", [128, T])
        tb = k.sb("lru_tb", [128, T])
        hh = [k.sb(f"lru_h{d}", [128, T]) for d in range(2)]
        ob = k.sb("lru_ob", [128, T], BF16)
        V = [self.vecs]
        for ft in range(4):
            k.dma("sp", xl[:], self.zT[zo + ft * 128:zo + (ft + 1) * 128, :], [self.zT], [xl])
            k.dma("sp", gt[:], self.zT[go + ft * 128:go + (ft + 1) * 128, :], [self.zT], [gt])
            self.dwconv(xc, xl, lambda kk: self.vcol("lru_cw", kk * 4 + ft), self.vcol("lru_cb", ft), V)
            k.act(gt[:], gt[:], AF.Silu, [gt], [gt])
            for d in range(2):
                for tt in range(T // 512):
                    sl = slice(tt * 512, (tt + 1) * 512)
                    pa = self.next_pf()
                    k.mm(pa[:, :], wts[:, (d * 2 + 0) * 4 + ft, :], xc[:, sl], True, True, [wts, xc], [pa])
                    px = self.next_pf()
                    k.mm(px[:, :], wts[:, (d * 2 + 1) * 4 + ft, :], xc[:, sl], True, True, [wts, xc], [px])
                    k.act(ta[:, sl], pa[:, :], AF.Sigmoid, [pa] + V, [ta], bias=self.vcol("lru_ba", d * 4 + ft))
                    k.act(tb[:, sl], px[:, :], AF.Sigmoid, [px] + V, [tb], bias=self.vcol("lru_bx", d * 4 + ft))
                k.act(ta[:], ta[:], AF.Exp, [ta, clam], [ta], scale=clam[:, d * 4 + ft:d * 4 + ft + 1])
                if ft == 0 and d == 0:
                    self.dump("lru_clam", clam, clam[:], [128, 8])
                    self.dump("lru_a", ta, ta[:], [128, T])
                    self.dump("lru_gi", tb, tb[:], [128, T])
                    self.dump("lru_xc", xc, xc[:], [128, T])
                k.tt(tb[:], tb[:], xc[:], ALU.mult, [tb, xc], [tb])
                h = hh[d]
                k.tt(h[:], ta[:], ta[:], ALU.mult, [ta], [h])
                k.act(h[:], h[:], AF.Sqrt, [h], [h], scale=-1.0, bias=1.0)
                k.tt(tb[:], tb[:], h[:], ALU.mult, [tb, h], [tb])
                if ft == 0 and d == 0:
                    self.dump("lru_sq", h, h[:], [128, T])
                    self.dump("lru_u", tb, tb[:], [128, T])
                for (s0, ln, is_s) in cfg.seqs:
                    sl = slice(s0, s0 + ln)
                    init = h0[:, d, ft:ft + 1] if is_s else 0.0
                    rd = [ta, tb] + ([h0] if is_s else [])
                    if d == 0:
                        k.scan(h[:, sl], ta[:, sl], tb[:, sl], init, rd, [h])
                    else:
                        rv = lambda t: t[:, s0:s0 + ln][:, ::-1]
                        k.scan(rv(h), rv(ta), rv(tb), init, rd, [h])
                    if not is_s:
                        si = s0 // ln
                        e = s0 + ln - 1 if d == 0 else s0
                        k.cp(fin[:, si, d, ft:ft + 1], h[:, e:e + 1], [h], [fin], en="pool")
            k.tt(hh[0][:], hh[0][:], hh[1][:], ALU.add, [hh[0], hh[1]], [hh[0]])
            k.tt(ob[:], hh[0][:], gt[:], ALU.mult, [hh[0], gt], [ob])
            k.dma("pool", self.o_scr[3, ft * 128:(ft + 1) * 128, :], ob[:], [ob], [self.o_scr])
        k.dma("pool", self.o_lru_fin[l], fin[:], [fin], [self.o_lru_fin])


    def phase_merge(self, l):
        cfg, k = self.cfg, self.k
        T = cfg.T
        wm = k.sb("wm", [128, 8, 4 * D], BF16)
        wbr = k.sb("wbr", [128, 16, D], BF16)
        with k.phase():
            stg = [k.sb(f"mstg{i}", [128, 4096]) for i in range(2)]
            si = 0
            for blk in range(8):
                st = stg[si % 2]
                si += 1
                k.dma("sp", st[:].rearrange("p (c n) -> p c n", c=8),
                      self.w_in_m[l][:, blk * 512:(blk + 1) * 512].rearrange("(c p) n -> p c n", p=128), (), [st])
                k.cp(wm[:, :, blk * 512:(blk + 1) * 512], st[:].rearrange("p (c n) -> p c n", c=8), [st], [wm], en="pool")
            for m in range(4):
                st = stg[si % 2]
                si += 1
                k.dma("sp", st[:].rearrange("p (c n) -> p c n", c=4),
                      self.w_branch[l, m].rearrange("(c p) n -> p c n", p=128), (), [st])
                k.cp(wbr[:, m * 4:(m + 1) * 4, :], st[:].rearrange("p (c n) -> p c n", c=4), [st], [wbr], en="pool")
        hts = [k.sb(f"mh{i}", [128, 8, 512], BF16) for i in range(2)]
        ots = [k.sb(f"mo{i}", [128, 16, 512], BF16) for i in range(2)]
        sgs = [k.sb(f"msg{i}", [128, 512]) for i in range(2)]
        tmp = [k.sb(f"mtmp{i}", [128, 512]) for i in range(2)]
        acc = [k.sb(f"macc{i}", [128, 512]) for i in range(2)]
        mts = [k.sb(f"mt{i}", [128, 8, 512], BF16) for i in range(2)]
        n = 0
        for tt in range(T // 512):
            tsl = slice(tt * 512, (tt + 1) * 512)
            ht, ot, mt = hts[tt % 2], ots[tt % 2], mts[tt % 2]
            k.dma("sp", ht[:], self.hT_scr[:, tsl].rearrange("(c p) t -> p c t", p=128), [self.hT_scr], [ht])
            for m in range(4):
                k.dma("sp", ot[:, m * 4:(m + 1) * 4, :], self.o_scr[m, :, tsl].rearrange("(c p) t -> p c t", p=128),
                      [self.o_scr], [ot])
            for dt in range(8):
                a = acc[dt % 2]
                for m in range(4):
                    pl = self.next_pf()
                    for c in range(8):
                        k.mm(pl[:, :], wm[:, c, m * D + dt * 128:m * D + (dt + 1) * 128], ht[:, c, :], c == 0, c == 7,
                             [wm, ht], [pl])
                    pp = self.next_pf()
                    for cc in range(4):
                        k.mm(pp[:, :], wbr[:, m * 4 + cc, dt * 128:(dt + 1) * 128], ot[:, m * 4 + cc, :], cc == 0, cc == 3,
                             [wbr, ot], [pp])
                    sg = sgs[n % 2]
                    n += 1
                    k.act(sg[:], pl[:, :], AF.Sigmoid, [pl], [sg])
                    if m == 0:
                        k.tt(a[:], sg[:], pp[:, :], ALU.mult, [sg, pp], [a])
                    else:
                        t_ = tmp[n % 2]
                        k.tt(t_[:], sg[:], pp[:, :], ALU.mult, [sg, pp], [t_])
                        if m < 3:
                            k.tt(a[:], a[:], t_[:], ALU.add, [a, t_], [a], en="pool")
                        else:
                            k.tt(mt[:, dt, :], a[:], t_[:], ALU.add, [a, t_], [mt], en="pool")
            k.dma("pool", self.mT_scr[:, tsl].rearrange("(c p) t -> p c t", p=128), mt[:], [mt], [self.mT_scr])

    def phase_out(self, l):
        cfg, k = self.cfg, self.k
        T = cfg.T
        x_src = self.x_in if l == 0 else self.y1
        y_dst = self.y1 if l < NL - 1 else self.y_all
        wo = k.sb("wo", [128, 8, D], BF16)
        stg = [k.sb(f"ostg{i}", [128, 4096]) for i in range(2)]
        for hb in range(2):
            st = stg[hb]
            k.dma("sp", st[:].rearrange("p (c n) -> p c n", c=8),
                  self.w_out[l][:, hb * 512:(hb + 1) * 512].rearrange("(c p) n -> p c n", p=128), (), [st])
            k.cp(wo[:, :, hb * 512:(hb + 1) * 512], st[:].rearrange("p (c n) -> p c n", c=8), [st], [wo], en="pool")
        mts = [k.sb(f"omt{i}", [128, 8, 128], BF16) for i in range(3)]
        xts = [k.sb(f"oxt{i}", [128, D]) for i in range(3)]
        yts = [k.sb(f"oyt{i}", [128, D]) for i in range(3)]
        for st_ in range(T // 128):
            g = 0 if st_ * 128 < cfg.TP else 1
            tsl = slice(st_ * 128, (st_ + 1) * 128)
            mt, xt, yt = mts[st_ % 3], xts[st_ % 3], yts[st_ % 3]
            k.dma("sp", mt[:], self.mT_scr[:, tsl].rearrange("(c p) t -> p c t", p=128), [self.mT_scr], [mt])
            k.dma("sp", xt[:], x_src[tsl, :], [x_src], [xt])
            for hb in range(2):
                hs = slice(hb * 512, (hb + 1) * 512)
                ps = self.next_pf()
                for c in range(8):
                    k.mm(ps[:, :], mt[:, c, :], wo[:, c, hs], c == 0, c == 7, [mt, wo], [ps])
                k.tt(yt[:, hs], ps[:, :], self.gate_bc[g][:, hs], ALU.mult, [ps, self.gate_bc[g]], [yt])
                k.tt(yt[:, hs], yt[:, hs], xt[:, hs], ALU.add, [yt, xt], [yt], en="pool")
            k.dma("pool", y_dst[tsl, :], yt[:], [yt], [y_dst])


    def rstd_from_sum(self, out, ps, n, reads):
        k = self.k
        tl, pt = reads
        k.ts(out, ps, 1.0 / n, 1e-6, ALU.mult, ALU.add, [pt], [tl])
        k.act(out, out, AF.Sqrt, [tl], [tl])
        k.op("dve", lambda e: e.reciprocal(out=out, in_=out), [tl], [tl])

    def phase_mla(self, l):
        cfg, k = self.cfg, self.k
        T, TP, ls = cfg.T, cfg.TP, cfg.ls
        TK = T + 256
        kidx = lambda t: t if t < TP else t + 256
        V = [self.vecs]
        qup = k.sb("qup", [128, 3, 768], BF16)
        kvk = k.sb("kvk", [128, 2, 512], BF16)
        kvv = k.sb("kvv", [128, 2, 512], BF16)
        pm96 = k.sb("pm96", [96, 96])
        cs = k.sb("ropecs", [96, 2, ls], BF16)
        ckv_all = k.sb("ckv_all", [128, 2, TK], BF16)
        krot = k.sb("krot", [96, TK], BF16)
        ssr = k.sb("ssr", [128, TK // 128])
        qn = k.sb("qn", [128, 3, T], BF16)
        vall = k.sb("vall", [128, TK // 128, 8, 65], BF16)
        with k.phase():
            kr_all = k.sb("kr_all", [96, TK])
            with k.phase():
                stg = k.sb("mlastg", [128, 4096])
                k.dma("sp", stg[:, :3 * 768].rearrange("p (c n) -> p c n", c=3),
                      self.q_up[l].rearrange("(c p) n -> p c n", p=128), (), [stg])
                k.cp(qup[:], stg[:, :3 * 768].rearrange("p (c n) -> p c n", c=3), [stg], [qup])
                k.dma("sp", stg[:, :1024].rearrange("p (c n) -> p c n", c=2),
                      self.kv_up_k[l].rearrange("(c p) n -> p c n", p=128), (), [stg])
                k.cp(kvk[:], stg[:, :1024].rearrange("p (c n) -> p c n", c=2), [stg], [kvk])
                k.dma("sp", stg[:, :1024].rearrange("p (c n) -> p c n", c=2),
                      self.kv_up_v[l].rearrange("(c p) n -> p c n", p=128), (), [stg])
                k.cp(kvv[:], stg[:, :1024].rearrange("p (c n) -> p c n", c=2), [stg], [kvv])
                k.dma("sp", pm96[64:96, :], self.pm96_d[:, :], (), [pm96])
                for j in range(2):
                    k.dma("sp", stg[64:96, :ls], self.rope_cs[j], (), [stg])
                    k.cp(cs[64:96, j, :], stg[64:96, :ls], [stg], [cs])
            k.memset(vall[:, :, :, 64:65], 1.0, [vall])
            if "mla_s0" in cfg.debug:
                return
            ctm = k.sb("ctm", [128, 2, 256])
            krtm = k.sb("krtm", [128, 2, 32])
            k.dma("sp", ctm[:], self.cache_ckv[l].rearrange("(a p) f -> p a f", p=128), (), [ctm])
            k.dma("sp", krtm[:], self.cache_kr[l].rearrange("(a p) f -> p a f", p=128), (), [krtm])
            for a in range(2):
                for c in range(2):
                    ps = self.next_pf()
                    k.tr(ps[:, 0:128], ctm[:, a, c * 128:(c + 1) * 128], self.ident_f[:], [ctm, self.ident_f], [ps])
                    k.cp(ckv_all[:, c, TP + a * 128:TP + (a + 1) * 128], ps[:, 0:128], [ps], [ckv_all])
                ps = self.next_pf()
                kpad = k.sb(f"kpad{a}", [128, 96])
                k.memset(kpad[:], 0.0, [kpad])
                k.cp(kpad[:, 64:96], krtm[:, a, :], [krtm, kpad], [kpad])
                k.tr(ps[0:96, 0:128], kpad[:, :], self.ident_f[:], [kpad, self.ident_f], [ps])
                k.cp(kr_all[64:96, TP + a * 128:TP + (a + 1) * 128], ps[64:96, 0:128], [ps], [kr_all])
            if "mla_s1" in cfg.debug:
                return
            zo_c, _ = ZROWS["mla_ckv"]
            zo_q, _ = ZROWS["mla_q"]
            zo_m, _ = ZROWS["misc"]
            xs = [k.sb(f"mx{i}", [128, 3, 512]) for i in range(2)]
            sq = [k.sb(f"msq{i}", [128, 3, 512]) for i in range(2)]
            rs = [k.sb(f"mrs{i}", [128, 512]) for i in range(2)]
            cn = [k.sb(f"mcn{i}", [128, 2, 512]) for i in range(2)]
            for tt in range(T // 512):
                tsl = slice(tt * 512, (tt + 1) * 512)
                ksl = slice(kidx(tt * 512), kidx(tt * 512) + 512)
                x, q2, r_, c_ = xs[tt % 2], sq[tt % 2], rs[tt % 2], cn[tt % 2]
                for (zo, nch, gname, is_q) in ((zo_c, 2, "mla_kva_g", False), (zo_q, 3, "mla_qa_g", True)):
                    k.dma("sp", x[:, :nch, :], self.zT[zo:zo + nch * 128, tsl].rearrange("(c p) t -> p c t", p=128),
                          [self.zT], [x])
                    k.act(q2[:, :nch, :], x[:, :nch, :], AF.Square, [x], [q2])
                    ps = self.next_pf()
                    for c in range(nch):
                        k.mm(ps[:, :], self.ones_f[:, :], q2[:, c, :], c == 0, c == nch - 1, [self.ones_f, q2], [ps])
                    self.rstd_from_sum(r_[:], ps[:, :], nch * 128, (r_, ps))
                    for c in range(nch):
                        if is_q:
                            k.stt(qn[:, c, tsl], x[:, c, :], self.vcol(gname, c), r_[:], ALU.mult, ALU.mult,
                                  [x, r_] + V, [qn])
                        else:
                            k.stt(c_[:, c, :], x[:, c, :], self.vcol(gname, c), r_[:], ALU.mult, ALU.mult,
                                  [x, r_] + V, [c_])
                    if not is_q:
                        k.cp(ckv_all[:, :, ksl], c_[:, :, :], [c_], [ckv_all], en="pool")
                        if tt * 512 < TP:
                            k.dma("pool", self.o_ckv[l][:, tsl].rearrange("(c p) t -> p c t", p=128), c_[:, :, :],
                                  [c_], [self.o_ckv])
            if "mla_s2" in cfg.debug:
                return
            k.dma("sp", kr_all[64:96, 0:TP], self.zT[zo_m:zo_m + 32, 0:TP], [self.zT], [kr_all])
            k.dma("sp", kr_all[64:96, TP + 256:TK], self.zT[zo_m:zo_m + 32, TP:T], [self.zT], [kr_all])
            k.dma("pool", self.o_kr[l], kr_all[64:96, 0:TP], [kr_all], [self.o_kr])
            krs = k.sb("krs", [96, 512])
            krg = k.sb("krg", [96, 512])
            t1 = k.sb("krt1", [96, 512])
            pss = self.pf[5]
            lat0 = TP + 256
            segs = [(ks, min(512, lat0 - ks)) for ks in range(0, lat0, 512)] + \
                   [(ks, min(512, TK - ks)) for ks in range(lat0, TK, 512)]
            for ks, w in segs:
                k.act(krs[64:96, :w], kr_all[64:96, ks:ks + w], AF.Square, [kr_all], [krs])
                for j in range(w // 128):
                    kt = ks // 128 + j
                    k.mm(pss[:, kt:kt + 1], krs[64:96, j * 128:(j + 1) * 128], self.ones_f[64:96, 0:1], True, True,
                         [krs, self.ones_f], [pss])
                k.ts(krg[64:96, :w], kr_all[64:96, ks:ks + w], self.vecs[64:96, self.vl.cols["mla_kn_g"][0]:self.vl.cols["mla_kn_g"][0] + 1],
                     None, ALU.mult, None, [kr_all] + V, [krg])
                if ks >= lat0:
                    pr = self.next_pf()
                    k.mm(pr[0:96, :w], pm96[64:96, :], krg[64:96, :w], True, True, [pm96, krg], [pr])
                    po = ks - lat0
                    k.tt(t1[64:96, :w], pr[64:96, :w], cs[64:96, 1, po:po + w], ALU.mult, [pr, cs], [t1])
                    k.tt(krg[64:96, :w], krg[64:96, :w], cs[64:96, 0, po:po + w], ALU.mult, [krg, cs], [krg])
                    k.tt(krot[64:96, ks:ks + w], krg[64:96, :w], t1[64:96, :w], ALU.add, [krg, t1], [krot])
                else:
                    k.cp(krot[64:96, ks:ks + w], krg[64:96, :w], [krg], [krot])
            k.cp(ssr[:], pss[:, 0:TK // 128], [pss], [ssr])
            if "mla_s3" in cfg.debug:
                return
            for kt in range(TK // 128):
                ps = self.next_pf()
                for c in range(2):
                    k.mm(ps[:, :], ckv_all[:, c, kt * 128:(kt + 1) * 128], kvv[:, c, :], c == 0, c == 1,
                         [ckv_all, kvv], [ps])
                k.cp(vall[:, kt, :, 0:64], ps[:, :].rearrange("p (h d) -> p h d", h=8), [ps], [vall],
                     en=("act" if kt % 2 else "dve"))
        if "mla_s4" in cfg.debug:
            return
        zo_g, _ = ZROWS["mla_gate"]
        gcol = self.vl.cols["mla_qn_g"][0]
        kcol = self.vl.cols["mla_kn_g"][0]
        kth = k.sb("kth", [96, TK], BF16)
        qth = k.sb("qth", [96, T], BF16)
        rk = k.sb("rk", [128, TK // 128])
        sqt = [k.sb(f"hsq{i}", [96, 512]) for i in range(2)]
        qf = [k.sb(f"hqf{i}", [96, 512]) for i in range(2)]
        rq = [k.sb(f"hrq{i}", [96, 512]) for i in range(2)]
        t1 = k.sb("ht1", [96, 512])
        t2 = k.sb("ht2", [96, 512])
        pts = [k.sb(f"hpt{i}", [128, 512], BF16) for i in range(4)]
        oa = [k.sb(f"hoa{i}", [65, 512]) for i in range(2)]
        gts = [k.sb(f"hgt{i}", [64, 512]) for i in range(2)]
        obs = [k.sb(f"hob{i}", [64, 512], BF16) for i in range(2)]
        npt = 0
        npo = 0
        for h in range(8):
            prk = self.pf[4]
            for ks in range(0, TK, 512):
                w = min(512, TK - ks)
                ps = self.next_pf()
                for c in range(2):
                    k.mm(ps[0:64, :w], kvk[:, c, h * 64:(h + 1) * 64], ckv_all[:, c, ks:ks + w], c == 0, c == 1,
                         [kvk, ckv_all], [ps])
                s_ = sqt[(ks // 512) % 2]
                if "k_noact" not in cfg.debug:
                    k.act(s_[0:64, :w], ps[0:64, :w], AF.Square, [ps], [s_])
                for j in range(w // 128):
                    kt = ks // 128 + j
                    if "mla_k1" in cfg.debug:
                        continue
                    k.mm(prk[:, kt:kt + 1], s_[0:64, j * 128:(j + 1) * 128], self.ones_f[0:64, 0:1], True, True,
                         [s_, self.ones_f], [prk])
                if "k_nots" not in cfg.debug:
                    k.ts(kth[0:64, ks:ks + w], ps[0:64, :w], self.vecs[0:64, kcol:kcol + 1], None, ALU.mult, None,
                         [ps] + V, [kth])
            k.cp(kth[64:96, :], krot[64:96, :], [krot], [kth], en=("dve" if "mla_k2" in cfg.debug else "pool"))
            if "k_nork" in cfg.debug:
                return
            k.tt(rk[:], prk[:, 0:TK // 128], ssr[:], ALU.add, [prk, ssr], [rk])
            k.ts(rk[:], rk[:], 1.0, 96e-6, ALU.mult, ALU.add, [rk], [rk])
            k.act(rk[:], rk[:], AF.Sqrt, [rk], [rk])
            k.op("dve", lambda e: e.reciprocal(out=rk[:], in_=rk[:]), [rk], [rk])
            if "mla_s5" in cfg.debug:
                return
            for tt in range(T // 512):
                tsl = slice(tt * 512, (tt + 1) * 512)
                ps = self.next_pf()
                for c in range(3):
                    k.mm(ps[0:96, :], qup[:, c, h * 96:(h + 1) * 96], qn[:, c, tsl], c == 0, c == 2, [qup, qn], [ps])
                s_, f_, r_ = sqt[tt % 2], qf[tt % 2], rq[tt % 2]
                k.act(s_[:, :], ps[0:96, :], AF.Square, [ps], [s_])
                p2 = self.next_pf()
                k.mm(p2[0:96, :], self.ones_f[0:96, 0:96], s_[:, :], True, True, [self.ones_f, s_], [p2])
                self.rstd_from_sum(r_[:, :], p2[0:96, :], 96, (r_, p2))
                k.stt(f_[:, :], ps[0:96, :], self.vecs[0:96, gcol:gcol + 1], r_[:, :], ALU.mult, ALU.mult,
                      [ps, r_] + V, [f_])
                k.cp(qth[0:64, tsl], f_[0:64, :], [f_], [qth], en="pool")
                if tt * 512 >= TP:
                    po = tt * 512 - TP
                    pr = self.next_pf()
                    k.mm(pr[0:96, :], pm96[64:96, :], f_[64:96, :], True, True, [pm96, f_], [pr])
                    k.tt(t1[64:96, :], pr[64:96, :], cs[64:96, 1, po:po + 512], ALU.mult, [pr, cs], [t1])
                    k.tt(t2[64:96, :], f_[64:96, :], cs[64:96, 0, po:po + 512], ALU.mult, [f_, cs], [t2])
                    k.tt(qth[64:96, tsl], t1[64:96, :], t2[64:96, :], ALU.add, [t1, t2], [qth], en="pool")
                else:
                    k.cp(qth[64:96, tsl], f_[64:96, :], [f_], [qth], en="pool")
            if "mla_s6" in cfg.debug:
                return
            if h == 0:
                self.dump("mla_kth", kth, kth[:, :], [96, TK], BF16)
                self.dump("mla_qth", qth, qth[:, :], [96, T], BF16)
                self.dump("mla_rk", rk, rk[:, :], [128, TK // 128])
            for (s0, ln, is_s) in cfg.seqs:
                k0 = TP if is_s else s0
                nk = ln + 256 if is_s else ln
                qw = min(512, ln)
                for qg in range(ln // qw):
                    q0 = s0 + qg * qw
                    po = self.pf[4 + (npo % 2)]
                    npo += 1
                    nkt = nk // 128
                    pend = []

                    def _pv(item):
                        pt_, kt_, j_ = item
                        k.mm(po[0:65, :qw], vall[:, kt_, h, :], pt_[:, :qw], j_ == 0, j_ == nkt - 1, [vall, pt_], [po])
                    for j in range(nkt):
                        kt = k0 // 128 + j
                        pS = self.next_pf()
                        k.mm(pS[:, :qw], kth[0:96, kt * 128:(kt + 1) * 128], qth[0:96, q0:q0 + qw], True, True,
                             [kth, qth], [pS])
                        pt = pts[npt % 4]
                        npt += 1
                        k.act(pt[:, :qw], pS[:, :qw], AF.Exp, [pS, rk], [pt], scale=rk[:, kt:kt + 1])
                        pend.append((pt, kt, j))
                        if len(pend) > 2:
                            _pv(pend.pop(0))
                    while pend:
                        _pv(pend.pop(0))
                    o_ = oa[qg % 2]
                    k.cp(o_[:, :qw], po[0:65, :qw], [po], [o_])
                    k.op("dve", lambda e: e.reciprocal(out=o_[64:65, :qw], in_=o_[64:65, :qw]), [o_], [o_])
                    pbc = self.next_pf()
                    k.mm(pbc[0:64, :qw], self.ones_f[64:65, 0:64], o_[64:65, :qw], True, True, [self.ones_f, o_], [pbc])
                    g_ = gts[qg % 2]
                    k.dma("sp", g_[:, :qw], self.zT[zo_g + h * 64:zo_g + (h + 1) * 64, q0:q0 + qw], [self.zT], [g_])
                    k.act(g_[:, :qw], g_[:, :qw], AF.Silu, [g_], [g_])
                    k.tt(o_[0:64, :qw], o_[0:64, :qw], pbc[0:64, :qw], ALU.mult, [o_, pbc], [o_])
                    ob = obs[qg % 2]
                    k.tt(ob[:, :qw], o_[0:64, :qw], g_[:, :qw], ALU.mult, [o_, g_], [ob])
                    k.dma("pool", self.o_scr[1, h * 64:(h + 1) * 64, q0:q0 + qw], ob[:, :qw], [ob], [self.o_scr])


    def phase_ssd(self, l):
        cfg, k = self.cfg, self.k
        T, TP = cfg.T, cfg.TP
        NCH = T // 128
        V = [self.vecs]
        zo_x, _ = ZROWS["ssd_xbc"]
        zo_m, _ = ZROWS["misc"]
        x_tm = k.sb("sx_tm", [128, NCH, 512], BF16)
        b_tm = k.sb("sb_tm", [128, NCH, 128], BF16)
        bT = k.sb("s_bT", [128, T], BF16)
        cT = k.sb("s_cT", [128, T], BF16)
        selm = k.sb("s_selm", [64, 16, 128])
        seld = k.sb("s_seld", [64, 2, 8])
        maskneg = k.sb("s_mask", [128, 2, 128])
        bcv = k.sb("s_bcv", [128, 8 + 512])
        k.dma("sp", selm[:], self.selm_d[:, :, :], (), [selm])
        k.dma("sp", seld[:], self.seld_d[:, :, :], (), [seld])
        k.dma("sp", maskneg[:], self.maskneg_d[:, :, :], (), [maskneg])
        k.dma("sp", bcv[:], self.ssd_bc[l], (), [bcv])
        with k.phase():
            xl = [k.sb(f"s_xl{i}", [128, T]) for i in range(2)]
            xc = [k.sb(f"s_xc{i}", [128, T]) for i in range(2)]
            for ft in range(6):
                a, c_ = xl[ft % 2], xc[ft % 2]
                k.dma("sp", a[:], self.zT[zo_x + ft * 128:zo_x + (ft + 1) * 128, :], [self.zT], [a])
                self.dwconv(c_, a, lambda kk: self.vcol("ssd_cw", kk * 6 + ft), self.vcol("ssd_cb", ft), V)
                k.act(c_[:], c_[:], AF.Silu, [c_], [c_])
                if ft == 4:
                    k.cp(bT[:], c_[:], [c_], [bT], en="pool")
                if ft == 5:
                    k.cp(cT[:], c_[:], [c_], [cT], en="pool")
                if ft <= 4:
                    for ch in range(NCH):
                        ps = self.next_pf()
                        k.tr(ps[:, 0:128], c_[:, ch * 128:(ch + 1) * 128], self.ident_f[:], [c_, self.ident_f], [ps])
                        dst = x_tm[:, ch, ft * 128:(ft + 1) * 128] if ft < 4 else b_tm[:, ch, :]
                        k.cp(dst, ps[:, 0:128], [ps], [x_tm if ft < 4 else b_tm], en=("act" if ch % 2 else "dve"))
        cs = k.sb("s_cs", [64, T])
        sc3t = k.sb("s_sc3", [64, 3, T])
        nega = k.sb("s_nega", [64, 1])
        _cm_stack = contextlib.ExitStack()
        k.barrier()
        k.stacks.append(_cm_stack)
        cmask = k.sb("s_cmask", [64, T + 1])

        class _V:
            def __init__(self, tl, j):
                self.tl, self.j, self.res, self.psum = tl, j, tl.res, False

            def __getitem__(self, key):
                if not isinstance(key, tuple):
                    key = (key,)
                return self.tl[(key[0], self.j) + tuple(key[1:])]
        sc3 = sc3t
        dtt = sc3t
        D0 = lambda *a: sc3t[(a[0], 0) + tuple(a[1:])] if a else sc3t[:, 0, :]
        k.memset(sc3t[:, 0, :], 0.0, [sc3t])
        k.dma("sp", sc3t[0:8, 0, :], self.zT[zo_m + 32:zo_m + 40, :], [self.zT], [sc3t])
        k.dma("sp", sc3t[32:40, 0, :], self.zT[zo_m + 40:zo_m + 48, :], [self.zT], [sc3t])
        k.dma("sp", cmask[:], self.cmask_d[:, :], (), [cmask])
        k.act(sc3t[:, 0, :], sc3t[:, 0, :], AF.Exp, [sc3t] + V, [sc3t], bias=self.vecs[0:64, self.vl.cols["ssd_dtb"][0]:self.vl.cols["ssd_dtb"][0] + 1])
        k.act(sc3t[:, 0, :], sc3t[:, 0, :], AF.Ln, [sc3t], [sc3t], bias=1.0)
        k.act(nega[:], self.vecs[0:64, self.vl.cols["ssd_alog"][0]:self.vl.cols["ssd_alog"][0] + 1], AF.Exp, V, [nega])
        k.ts(nega[:], nega[:], -1.0, None, ALU.mult, None, [nega], [nega])
        k.ts(sc3t[:, 2, :], sc3t[:, 0, :], nega[:, 0:1], None, ALU.mult, None, [sc3t, nega], [sc3t])
        k.scan(cs[0:32, :], cmask[0:32, 0:T], sc3t[0:32, 2, :], 0.0, [cmask, sc3t], [cs])
        k.scan(cs[32:64, :][:, ::-1], cmask[32:64, 1:T + 1][:, ::-1], sc3t[32:64, 2, :][:, ::-1], 0.0, [cmask, sc3t], [cs])
        k.barrier()
        k.stacks.pop()
        _cm_stack.close()
        c3 = lambda ap: ap.rearrange("p (c t) -> p c t", t=128)
        k.tt(c3(sc3[0:32, 1, :]), c3(cs[0:32, :])[:, :, 127:128].to_broadcast([32, NCH, 128]), c3(cs[0:32, :]),
             ALU.subtract, [cs], [sc3])
        k.tt(c3(sc3[32:64, 1, :]), c3(cs[32:64, :])[:, :, 0:1].to_broadcast([32, NCH, 128]), c3(cs[32:64, :]),
             ALU.subtract, [cs], [sc3])
        k.act(sc3[:, 1, :], sc3[:, 1, :], AF.Exp, [sc3], [sc3])
        k.tt(sc3[:, 1, :], sc3[:, 1, :], sc3[:, 0, :], ALU.mult, [sc3], [sc3])
        k.act(sc3[:, 2, :], cs[:], AF.Exp, [cs], [sc3])
        self.dump("ssd_cs", cs, cs[:, :], [64, T])
        self.dump("ssd_sc3", sc3t, sc3t[:, :, :], [64, 3, T])
        self.dump("ssd_xtm", x_tm, x_tm[:, :, :], [128, NCH, 512], BF16)
        self.dump("ssd_bT", bT, bT[:, :], [128, T], BF16)
        Hf = k.sb("s_H", [128, 4, 64])
        Hb = k.sb("s_Hb", [128, 4, 64], BF16)
        selg = k.sb("s_selg", [64, 128])
        k.dma("sp", selg[:], self.selg_d[:, :], (), [selg])
        sctm = [k.sb(f"s_sctm{i}", [128, 3, 64]) for i in range(2)]
        cstm = [k.sb(f"s_cstm{i}", [128, 64]) for i in range(2)]
        xd = [k.sb(f"s_xd{i}", [128, 8, 64], BF16) for i in range(2)]
        xdd = [k.sb(f"s_xdd{i}", [128, 8, 64], BF16) for i in range(2)]
        lt = [k.sb(f"s_lt{i}", [128, 4, 128]) for i in range(2)]
        mT = [k.sb(f"s_mT{i}", [128, 4, 128], BF16) for i in range(2)]
        yo = [k.sb(f"s_yo{i}", [128, 512]) for i in range(2)]
        yf = [k.sb("s_yf0", [128, 512])] * 2
        zg = [k.sb("s_zg0", [128, 512])] * 2
        et = k.sb("s_et", [128, 4])
        etr = k.sb("s_etr", [64, 4])
        h0t = k.sb("s_h0t", [64, 8, 64])
        fint = k.sb("s_fint", [64, 8, 64])
        junk = yf[0]
        ssq = k.sb("s_ssq", [128, 1])
        ob = [k.sb(f"s_ob{i}", [128, 4, 128], BF16) for i in range(2)]
        it = 0
        for (s0, ln, is_s) in cfg.seqs:
            si = s0 // ln
            chs = list(range(s0 // 128, (s0 + ln) // 128))
            for d in range(2):
                if is_s:
                    k.dma("sp", h0t[:], self.ssd_h0[l, d], (), [h0t])
                    for h in range(8):
                        ps = self.next_pf()
                        if h < 4:
                            k.tr(ps[0:64, 0:64], h0t[:, h, :], self.ident_f[0:64, 0:64], [h0t, self.ident_f], [ps])
                            k.cp(Hf[0:64, h, :], ps[0:64, 0:64], [ps], [Hf])
                        else:
                            k.tr(ps[:, 0:64], h0t[:, h - 1:h + 1, :].rearrange("p a n -> p (a n)"),
                                 self.ident_f[0:64, 0:64], [h0t, self.ident_f], [ps])
                            k.cp(Hf[64:128, h - 4, :], ps[64:128, 0:64], [ps], [Hf])
                else:
                    k.memset(Hf[:], 0.0, [Hf], en="dve")
                k.cp(Hb[:], Hf[:], [Hf], [Hb])
                for ch in (chs if d == 0 else chs[::-1]):
                    it += 1
                    tsl = slice(ch * 128, (ch + 1) * 128)
                    st_, ct_, xd_, xdd_ = sctm[it % 2], cstm[it % 2], xd[it % 2], xdd[it % 2]
                    pt = self.next_pf()
                    for j in range(3):
                        k.tr(pt[:, j * 64:(j + 1) * 64], sc3[:, j, tsl], self.ident_f[0:64, 0:64], [sc3, self.ident_f], [pt])
                    k.tr(pt[:, 192:256], cs[:, tsl], self.ident_f[0:64, 0:64], [cs, self.ident_f], [pt])
                    k.cp(st_[:], pt[:, 0:192].rearrange("p (j r) -> p j r", j=3), [pt], [st_])
                    k.cp(ct_[:], pt[:, 192:256], [pt], [ct_])
                    r0 = d * 32
                    xv = x_tm[:, ch, :].rearrange("p (h q) -> p h q", h=8)
                    k.tt(xd_[:], xv, st_[:, 0, r0:r0 + 8].unsqueeze(2).to_broadcast([128, 8, 64]), ALU.mult,
                         [x_tm, st_], [xd_])
                    k.tt(xdd_[:], xv, st_[:, 1, r0:r0 + 8].unsqueeze(2).to_broadcast([128, 8, 64]), ALU.mult,
                         [x_tm, st_], [xdd_], en="pool")
                    py = self.pf[4]
                    pyo = self.pf[5]
                    for g in range(2):
                        lt_, m_ = lt[g], mT[g]
                        pb_ = self.next_pf()
                        for hh in range(4):
                            h = g * 4 + hh
                            k.mm(pb_[:, hh * 128:(hh + 1) * 128], selm[:, d * 8 + h, :], cs[:, tsl], True, True,
                                 [selm, cs], [pb_])
                        k.tt(lt_[:], pb_[:, :].rearrange("p (h t) -> p h t", h=4),
                             ct_[:, r0 + g * 4:r0 + g * 4 + 4].unsqueeze(2).to_broadcast([128, 4, 128]), ALU.subtract,
                             [pb_, ct_], [lt_])
                        k.tt(lt_[:], lt_[:], maskneg[:, d:d + 1, :].to_broadcast([128, 4, 128]), ALU.add,
                             [lt_, maskneg], [lt_])
                        k.act(lt_[:], lt_[:], AF.Exp, [lt_], [lt_])
                        pg = self.next_pf()
                        k.mm(pg[:, 0:128], bT[g * 64:(g + 1) * 64, tsl], cT[g * 64:(g + 1) * 64, tsl], True, True,
                             [bT, cT], [pg])
                        k.tt(m_[:], lt_[:], pg[:, 0:128].unsqueeze(1).to_broadcast([128, 4, 128]), ALU.mult,
                             [lt_, pg], [m_])
                        for hh in range(4):
                            h = g * 4 + hh
                            k.mm(py[:, h * 64:(h + 1) * 64], m_[:, hh, :], xd_[:, h, :], True, True, [m_, xd_], [py])
                        k.mm(pyo[:, g * 256:(g + 1) * 256], cT[g * 64:(g + 1) * 64, tsl],
                             Hb[g * 64:(g + 1) * 64, :, :], True, True, [cT, Hb], [pyo])
                    y_ = yo[it % 2]
                    k.tt(y_[:].rearrange("p (h q) -> p h q", h=8), pyo[:, :].rearrange("p (h q) -> p h q", h=8),
                         st_[:, 2, r0:r0 + 8].unsqueeze(2).to_broadcast([128, 8, 64]), ALU.mult, [pyo, st_], [y_])
                    k.tt(y_[:], y_[:], py[:, :], ALU.add, [y_, py], [y_])
                    ps_ = self.next_pf()
                    for g in range(2):
                        k.mm(ps_[g * 64:(g + 1) * 64, 0:256], b_tm[:, ch, g * 64:(g + 1) * 64],
                             xdd_[:, g * 4:(g + 1) * 4, :], True, True, [b_tm, xdd_], [ps_])
                    e_tok = ch * 128 + (127 if d == 0 else 0)
                    k.ts(etr[:], seld[:, d, 0:4], cs[:, e_tok:e_tok + 1], None, ALU.mult, None, [seld, cs], [etr])
                    pe_ = self.next_pf()
                    k.mm(pe_[:, 0:4], selg[:, :], etr[:], True, True, [selg, etr], [pe_])
                    k.act(et[:], pe_[:, 0:4], AF.Exp, [pe_], [et])
                    k.tt(Hf[:], Hf[:], et[:].unsqueeze(2).to_broadcast([128, 4, 64]), ALU.mult, [Hf, et], [Hf])
                    k.tt(Hf[:], Hf[:], ps_[:, 0:256].rearrange("p (h q) -> p h q", h=4), ALU.add, [Hf, ps_], [Hf])
                    k.cp(Hb[:], Hf[:], [Hf], [Hb], en="pool")
                    if it == 1:
                        self.dump("ssd_lt", lt[1], lt[1][:, :, :], [128, 4, 128])
                        self.dump("ssd_mT", mT[1], mT[1][:, :, :], [128, 4, 128], BF16)
                        self.dump("ssd_y0", y_, y_[:, :], [128, 512])
                        self.dump("ssd_H1", Hf, Hf[:, :, :], [128, 4, 64])
                        self.dump("ssd_sctm", st_, st_[:, :, :], [128, 3, 64])
                    if d == 0:
                        k.dma("pool", self.ssd_y[tsl, :], y_[:], [y_], [self.ssd_y])
                    else:
                        f_, z_ = yf[it % 2], zg[it % 2]
                        k.dma("sp", f_[:], self.ssd_y[tsl, :], [self.ssd_y], [f_])
                        k.dma("sp", z_[:], self.zg_tm[tsl, :], [self.zg_tm], [z_])
                        k.tt(y_[:], y_[:], f_[:], ALU.add, [y_, f_], [y_])
                        k.tt(f_[:].rearrange("p (h q) -> p h q", h=8), xv,
                             bcv[:, 0:8].unsqueeze(2).to_broadcast([128, 8, 64]), ALU.mult, [x_tm, bcv], [f_])
                        k.tt(y_[:], y_[:], f_[:], ALU.add, [y_, f_], [y_])
                        k.act(z_[:], z_[:], AF.Silu, [z_], [z_])
                        k.tt(y_[:], y_[:], z_[:], ALU.mult, [y_, z_], [y_])
                        k.act(junk[:], y_[:], AF.Square, [y_], [junk, ssq], accum_out=ssq[:])
                        self.rstd_from_sum(ssq[:], ssq[:], 512, (ssq, ssq))
                        k.stt(y_[:], y_[:], ssq[:, 0:1], bcv[:, 8:8 + 512], ALU.mult, ALU.mult, [y_, ssq, bcv], [y_])
                        o_ = ob[it % 2]
                        for ft in range(4):
                            pt2 = self.next_pf()
                            k.tr(pt2[:, 0:128], y_[:, ft * 128:(ft + 1) * 128], self.ident_f[:], [y_, self.ident_f], [pt2])
                            k.cp(o_[:, ft, :], pt2[:, 0:128], [pt2], [o_], en=("act" if ft % 2 else "dve"))
                        k.dma("pool", self.o_scr[2, :, tsl].rearrange("(c p) t -> p c t", p=128), o_[:], [o_], [self.o_scr])
                if not is_s:
                    for h in range(8):
                        ps = self.next_pf()
                        g_ = h // 4
                        k.tr(ps[0:64, 0:64], Hf[g_ * 64:(g_ + 1) * 64, h % 4, :],
                             self.ident_f[g_ * 64:(g_ + 1) * 64, g_ * 64:(g_ + 1) * 64], [Hf, self.ident_f], [ps])
                        k.cp(fint[:, h, :], ps[0:64, 0:64], [ps], [fint])
                    k.dma("pool", self.o_ssd_fin[l, si, d], fint[:], [fint], [self.o_ssd_fin])


    def phase_rwkv(self, l):
        cfg, k = self.cfg, self.k
        T, TP = cfg.T, cfg.TP
        TS = 512
        V = [self.vecs]
        NC64 = T // 64
        zr = {n: ZROWS[n][0] for n in ("rw_r", "rw_k", "rw_v", "rw_lora", "rw_gate")}
        pc_all = k.sb("rw_pc", [128, 4, 2, NC64])
        blk = k.sb("rw_blk", [128, 128])
        k.dma("sp", blk[:], self.blk64_d[:, :], (), [blk])
        vc = lambda name, j: self.vcol(name, j)
        with k.phase():
            lw = k.sb("rw_lw", [128, 2, 512])
            cm = k.sb("rw_cm", [128, TS + 1])
            omk = k.sb("rw_omk", [128, 4])
            k.dma("sp", lw[:], self.rw_lw[l], (), [lw])
            k.dma("sp", cm[:], self.cmask64_d[:, :], (), [cm])
            k.ts(omk[:], self.vcol("rw_ka", 0, 4), -1.0, 1.0, ALU.mult, ALU.add, V, [omk])
            xe = k.sb("rw_xe", [128, TS + 2])
            sh = k.sb("rw_sh", [128, TS])
            lora = k.sb("rw_lora", [128, TS])
            names = ("r", "k", "v", "kk", "a", "kd", "be", "lw_", "cl", "e1", "e2", "t1", "t2", "bon")
            tl = {n: k.sb("rw_" + n, [128, TS]) for n in names}

            def mix(dst, zrow, mucol, a, is_s, s0, ln):
                b_ = a + TS
                k.dma("sp", xe[:, 1:TS + 1], self.zT[zrow:zrow + 128, a:b_], [self.zT], [xe])
                if is_s:
                    if a > s0:
                        k.dma("sp", xe[:, 0:1], self.zT[zrow:zrow + 128, a - 1:a], [self.zT], [xe], allow_slow_non_contiguous=True)
                    else:
                        k.memset(xe[:, 0:1], 0.0, [xe])
                    if b_ < s0 + ln:
                        k.dma("sp", xe[:, TS + 1:TS + 2], self.zT[zrow:zrow + 128, b_:b_ + 1], [self.zT], [xe], allow_slow_non_contiguous=True)
                    else:
                        k.memset(xe[:, TS + 1:TS + 2], 0.0, [xe])
                    k.tt(sh[:], xe[:, 0:TS], xe[:, 2:TS + 2], ALU.add, [xe], [sh])
                else:
                    ns = TS // ln
                    k.memset(sh[:], 0.0, [sh], en="dve")
                    v3 = lambda ap: ap.rearrange("p (s t) -> p s t", s=ns)
                    xv = v3(xe[:, 1:TS + 1])
                    k.tt(v3(sh[:])[:, :, 1:ln], v3(sh[:])[:, :, 1:ln], xv[:, :, 0:ln - 1], ALU.add, [sh, xe], [sh])
                    k.tt(v3(sh[:])[:, :, 0:ln - 1], v3(sh[:])[:, :, 0:ln - 1], xv[:, :, 1:ln], ALU.add, [sh, xe], [sh])
                k.stt(sh[:], sh[:], 0.5, xe[:, 1:TS + 1], ALU.mult, ALU.subtract, [sh, xe], [sh])
                k.stt(dst, sh[:], mucol, xe[:, 1:TS + 1], ALU.mult, ALU.add, [sh, xe] + V, [dst_tl[0]])

            dst_tl = [None]
            for a in range(0, T, TS):
                is_s = a >= TP
                s0, ln = (TP, cfg.ls) if is_s else (0, cfg.lp)
                c64 = a // 64
                dst_tl[0] = lora
                mix(lora[:], zr["rw_lora"], vc("rw_mu", 12), a, is_s, s0, ln)
                k.act(lora[0:64, :], lora[0:64, :], AF.Tanh, [lora], [lora])
                for ft in range(4):
                    R, K_, Vv, KK = tl["r"], tl["k"], tl["v"], tl["kk"]
                    for (dst, nm, mi) in ((R, "rw_r", 0), (K_, "rw_k", 4), (Vv, "rw_v", 8)):
                        dst_tl[0] = dst
                        mix(dst[:], zr[nm] + ft * 128, vc("rw_mu", mi + ft), a, is_s, s0, ln)
                    k.dma("pool", self.rw_v[ft * 128:(ft + 1) * 128, a:a + TS], Vv[:], [Vv], [self.rw_v])
                    t1, t2 = tl["t1"], tl["t2"]
                    k.ts(KK[:], K_[:], vc("rw_kk", ft), None, ALU.mult, None, [K_] + V, [KK])
                    k.tt(t1[:], KK[:], KK[:], ALU.mult, [KK], [t1])
                    ps = self.next_pf()
                    k.mm(ps[:, :], blk[:, :], t1[:], True, True, [blk, t1], [ps])
                    k.ts(t2[:], ps[:, :], 1e-12, None, ALU.add, None, [ps], [t2])
                    k.act(t2[:], t2[:], AF.Sqrt, [t2], [t2])
                    k.op("dve", lambda e: e.reciprocal(out=t2[:], in_=t2[:]), [t2], [t2])
                    k.tt(KK[:], KK[:], t2[:], ALU.mult, [KK, t2], [KK])
                    for d in range(2):
                        A_, KD, BE, LW, CL, E1, E2, BON = (tl[n] for n in ("a", "kd", "be", "lw_", "cl", "e1", "e2", "bon"))
                        ps = self.next_pf()
                        k.mm(ps[:, :], lw[0:64, d, ft * 128:(ft + 1) * 128], lora[0:64, :], True, True, [lw, lora], [ps])
                        k.act(LW[:], ps[:, :], AF.Sigmoid, [ps] + V, [LW], bias=vc("rw_w0", d * 4 + ft))
                        k.ts(LW[:], LW[:], -0.6065306597126334, None, ALU.mult, None, [LW], [LW])
                        ps2 = self.next_pf()
                        k.mm(ps2[:, :], lw[64:128, d, ft * 128:(ft + 1) * 128], lora[64:128, :], True, True, [lw, lora], [ps2])
                        k.act(A_[:], ps2[:, :], AF.Sigmoid, [ps2] + V, [A_], bias=vc("rw_a0", d * 4 + ft))
                        k.ts(KD[:], A_[:], vc("rw_ka", ft), omk[:, ft:ft + 1], ALU.mult, ALU.add, [A_, omk] + V, [KD])
                        k.tt(KD[:], KD[:], K_[:], ALU.mult, [KD, K_], [KD])
                        k.tt(BE[:], KK[:], A_[:], ALU.mult, [KK, A_], [BE])
                        k.stt(t1[:], R[:], vc("rw_rk", ft), KD[:], ALU.mult, ALU.mult, [R, KD] + V, [t1])
                        ps3 = self.next_pf()
                        k.mm(ps3[:, :], blk[:, :], t1[:], True, True, [blk, t1], [ps3])
                        if d == 0:
                            k.tt(BON[:], ps3[:, :], Vv[:], ALU.mult, [ps3, Vv], [BON])
                        else:
                            k.tt(t1[:], ps3[:, :], Vv[:], ALU.mult, [ps3, Vv], [t1])
                            k.tt(BON[:], BON[:], t1[:], ALU.add, [BON, t1], [BON])
                            k.dma("pool", self.rw_bonus[ft * 128:(ft + 1) * 128, a:a + TS], BON[:], [BON], [self.rw_bonus])
                        if d == 0:
                            k.scan(CL[:], cm[:, 0:TS], LW[:], 0.0, [cm, LW], [CL])
                        else:
                            k.scan(CL[:, ::-1], cm[:, 1:TS + 1][:, ::-1], LW[:, ::-1], 0.0, [cm, LW], [CL])
                        c3 = lambda ap: ap.rearrange("p (c t) -> p c t", t=64)
                        e_ = 63 if d == 0 else 0
                        ce = c3(CL[:])[:, :, e_:e_ + 1]
                        k.act(pc_all[:, ft, d, c64:c64 + TS // 64], ce.rearrange("p c o -> p (c o)"), AF.Exp, [CL], [pc_all])
                        O = lambda j: self.rw_ops[d, j, ft * 128:(ft + 1) * 128, a:a + TS]
                        k.act(E1[:], CL[:], AF.Exp, [CL], [E1])
                        k.tt(t1[:], R[:], E1[:], ALU.mult, [R, E1], [t1])
                        k.dma("pool", O(0), t1[:], [t1], [self.rw_ops])
                        k.act(E2[:], CL[:], AF.Exp, [CL], [E2], scale=-1.0)
                        k.tt(t2[:], KD[:], E2[:], ALU.mult, [KD, E2], [t2])
                        k.dma("pool", O(1), t2[:], [t2], [self.rw_ops])
                        k.tt(t1[:], BE[:], E2[:], ALU.mult, [BE, E2], [t1])
                        k.dma("pool", O(2), t1[:], [t1], [self.rw_ops])
                        k.tt(E1[:], CL[:], LW[:], ALU.subtract, [CL, LW], [E1])
                        k.act(E1[:], E1[:], AF.Exp, [E1], [E1])
                        k.tt(t2[:], KK[:], E1[:], ALU.mult, [KK, E1], [t2])
                        k.dma("pool", O(3), t2[:], [t2], [self.rw_ops])
                        k.tt(c3(E2[:]), ce.to_broadcast([128, TS // 64, 64]), c3(CL[:]), ALU.subtract, [CL], [E2])
                        k.act(E2[:], E2[:], AF.Exp, [E2], [E2])
                        k.tt(t1[:], KD[:], E2[:], ALU.mult, [KD, E2], [t1])
                        k.dma("pool", O(4), t1[:], [t1], [self.rw_ops])
                        k.tt(t2[:], BE[:], E2[:], ALU.mult, [BE, E2], [t2])
                        k.dma("pool", O(5), t2[:], [t2], [self.rw_ops])
        if "rw_prep_only" in cfg.debug:
            return
        with k.phase():
            msk = k.sb("rw_msk", [128, 2, 3, 128])
            k.dma("sp", msk[:], self.rw_mask_d[:, :, :, :], (), [msk])
            ops = [[k.sb(f"rw_op{j}_{i}", [128, 4, 128]) for j in range(7)] for i in range(2)]
            pmask = k.sb("rw_pmask", [128, 2])
            k.dma("sp", pmask[:], self.pmask_d[:, :], (), [pmask])
            mk = {nm: [k.sb(f"rw_mk{nm}{p_}", [128, 4, 128]) for p_ in range(2)] for nm in ("A", "B", "R")}
            NDT = BF16 if RW_BF16 else F32
            if RW_BF16:
                mkb = {nm: [k.sb(f"rw_mkb{nm}{p_}", [128, 4, 128], NDT) for p_ in range(2)] for nm in ("A", "B", "R")}
                Lb = {nm: k.sb(f"rw_Lb{nm}", [128, 4, 128], NDT) for nm in ("A", "B", "K")}
                TTb = k.sb("rw_TTb", [128, 8, 128], NDT)
            tms = [k.sb(f"rw_tm{j}", [128, 512]) for j in range(3)]
            NT = [k.sb(f"rw_NT{i}", [128, 8, 128], BF16 if RW_BF16 else F32) for i in range(2)]
            NN = [k.sb(f"rw_NN{i}", [128, 8, 128], BF16 if RW_BF16 else F32) for i in range(2)]
            TT = k.sb("rw_TT", [128, 8, 128])
            AakT = k.sb("rw_AakT", [128, 8, 128])
            ArkT = k.sb("rw_ArkT", [128, 8, 128])
            ArbT = k.sb("rw_ArbT", [128, 8, 128])
            W1 = k.sb("rw_W1", [128, 512])
            Wt = k.sb("rw_Wt", [128, 512])
            Ut = k.sb("rw_Ut", [128, 512])
            ST = k.sb("rw_ST", [128, 4, 64])
            h0t = k.sb("rw_h0t", [64, 8, 64])
            fint = k.sb("rw_fint", [64, 8, 64])
            osb = [k.sb(f"rw_osb{i}", [128, 4, 128]) for i in range(2)]
            it = 0
            for (s0, ln, is_s) in cfg.seqs:
                si = s0 // ln
                wins = list(range(s0 // 128, (s0 + ln) // 128))
                for d in range(2):
                    if is_s:
                        k.dma("sp", h0t[:], self.rw_h0[l, d], (), [h0t])
                        for h in range(8):
                            ps = self.next_pf()
                            if h % 2 == 0:
                                k.tr(ps[0:64, 0:64], h0t[:, h, :], self.ident_f[0:64, 0:64], [h0t, self.ident_f], [ps])
                                k.cp(ST[0:64, h // 2, :], ps[0:64, 0:64], [ps], [ST])
                            else:
                                k.tr(ps[:, 0:64], h0t[:, h - 1:h + 1, :].rearrange("p a n -> p (a n)"),
                                     self.ident_f[0:64, 0:64], [h0t, self.ident_f], [ps])
                                k.cp(ST[64:128, h // 2, :], ps[64:128, 0:64], [ps], [ST])
                    else:
                        k.memset(ST[:], 0.0, [ST], en="dve")
                    for w in (wins if d == 0 else wins[::-1]):
                        it += 1
                        w0 = w * 128
                        op = ops[it % 2]
                        for j in range(6):
                            k.dma("sp", op[j][:], self.rw_ops[d, j, :, w0:w0 + 128].rearrange("(f p) t -> p f t", p=128),
                                  [self.rw_ops], [op[j]])
                        k.dma("sp", op[6][:], self.rw_v[:, w0:w0 + 128].rearrange("(f p) t -> p f t", p=128),
                              [self.rw_v], [op[6]])
                        Rt, Kt, Bt, At, Kh, Bh, Vf = op
                        for (src, dst, neg) in ((Vf, tms[0], False), (Kh, tms[1], False), (Bh, tms[2], True)):
                            ps = self.next_pf()
                            for ft in range(4):
                                k.tr(ps[:, ft * 128:(ft + 1) * 128], src[:, ft, :], self.ident_f[:], [src, self.ident_f], [ps])
                            if neg:
                                k.ts(dst[:], ps[:, :], -1.0, None, ALU.mult, None, [ps], [dst])
                            else:
                                k.cp(dst[:], ps[:, :], [ps], [dst], en="act")
                        Vtm, Khtm, Bhtm = tms
                        hb = lambda h: (h % 2) * 64
                        for nm, src in (("A", At), ("B", Bt), ("R", Rt)):
                            for p_ in range(2):
                                k.ts(mk[nm][p_][:], src[:], pmask[:, p_:p_ + 1], None, ALU.mult, None, [src, pmask], [mk[nm][p_]],
                                     en=("pool" if p_ else "dve"))
                        if RW_BF16:
                            for nm, src in (("A", At), ("B", Bt), ("R", Rt)):
                                for p_ in range(2):
                                    k.cp(mkb[nm][p_][:], mk[nm][p_][:], [mk[nm][p_]], [mkb[nm][p_]], en=("pool" if p_ else "act"))
                            for nm, src in (("A", At), ("B", Bt), ("K", Kt)):
                                k.cp(Lb[nm][:], src[:], [src], [Lb[nm]], en=("pool" if nm == "B" else "act"))
                        else:
                            mkb = mk
                            Lb = dict(A=At, B=Bt, K=Kt)
                        specs = ((Lb["B"], "A", NT[0], 0, -1.0), (Lb["A"], "B", NN[0], 1, -1.0), (Lb["K"], "A", AakT, 0, 1.0),
                                 (Lb["K"], "R", ArkT, 2, 1.0), (Lb["B"], "R", ArbT, 2, -1.0))
                        for (L_, rn, dst, mi, sg) in specs:
                            for half in range(2):
                                ps = self.next_pf()
                                for hh in range(4):
                                    h = half * 4 + hh
                                    R_ = mkb[rn][h % 2]
                                    k.mm(ps[:, hh * 128:(hh + 1) * 128], L_[:, h // 2, :], R_[:, h // 2, :],
                                         True, True, [L_, R_], [ps])
                                k.stt(dst[:, half * 4:(half + 1) * 4, :], ps[:, :].rearrange("p (h t) -> p h t", h=4), sg,
                                      msk[:, d, mi:mi + 1, :].to_broadcast([128, 4, 128]), ALU.mult, ALU.mult, [ps, msk], [dst])
                        k.tt(TT[:], NT[0][:], self.ident_f[:, :].unsqueeze(1).to_broadcast([128, 8, 128]), ALU.add,
                             [NT[0], self.ident_f], [TT])
                        if RW_BF16:
                            k.cp(TTb[:], TT[:], [TT], [TTb], en="pool")
                        else:
                            TTb = TT
                        cur = 0
                        for lev in range(1, 6):
                            nxt = 1 - cur
                            for half in range(2):
                                hs = slice(half * 4, (half + 1) * 4)
                                pn = self.next_pf()
                                for hh in range(4):
                                    h = half * 4 + hh
                                    k.mm(pn[:, hh * 128:(hh + 1) * 128], NT[cur][:, h, :], NN[cur][:, h, :], True, True,
                                         [NT[cur], NN[cur]], [pn])
                                k.cp(NN[nxt][:, hs, :], pn[:, :].rearrange("p (h t) -> p h t", h=4), [pn], [NN[nxt]], en="act")
                                if lev < 5:
                                    pt_ = self.next_pf()
                                    for hh in range(4):
                                        h = half * 4 + hh
                                        k.mm(pt_[:, hh * 128:(hh + 1) * 128], NN[cur][:, h, :], NT[cur][:, h, :], True, True,
                                             [NT[cur], NN[cur]], [pt_])
                                    k.cp(NT[nxt][:, hs, :], pt_[:, :].rearrange("p (h t) -> p h t", h=4), [pt_], [NT[nxt]], en="act")
                                pp = self.next_pf()
                                for hh in range(4):
                                    h = half * 4 + hh
                                    k.mm(pp[:, hh * 128:(hh + 1) * 128], NN[nxt][:, h, :], TTb[:, h, :], True, True,
                                         [NN[nxt], TTb], [pp])
                                k.tt(TT[:, hs, :], TT[:, hs, :], pp[:, :].rearrange("p (h t) -> p h t", h=4), ALU.add,
                                     [TT, pp], [TT])
                                if lev < 5 and RW_BF16:
                                    k.cp(TTb[:, hs, :], TT[:, hs, :], [TT], [TTb], en="pool")
                            cur = nxt
                        pw = self.next_pf()
                        for h in range(8):
                            k.mm(pw[:, h * 64:(h + 1) * 64], AakT[:, h, :], Vtm[:, h * 64:(h + 1) * 64], True, True,
                                 [AakT, Vtm], [pw])
                        k.cp(W1[:], pw[:, :], [pw], [W1])
                        po = self.pf[4 + it % 2]
                        o_ = osb[it % 2]
                        for cb in ((0, 64) if d == 0 else (64, 0)):
                            cs_ = slice(cb, cb + 64)
                            c64i = (w0 + cb) // 64
                            px = self.next_pf()
                            for h in range(8):
                                b0 = hb(h)
                                k.mm(px[cs_, h * 64:(h + 1) * 64], mk["A"][h % 2][:, h // 2, cs_], ST[:, h // 2, :],
                                     True, True, [mk["A"][h % 2], ST], [px])
                            k.tt(Wt[cs_, :], W1[cs_, :], px[cs_, :], ALU.add, [W1, px], [Wt])
                            pu = self.next_pf()
                            for h in range(8):
                                k.mm(pu[cs_, h * 64:(h + 1) * 64], TT[cs_, h, cs_], Wt[cs_, h * 64:(h + 1) * 64], True, True,
                                     [TT, Wt], [pu])
                            k.cp(Ut[cs_, :], pu[cs_, :], [pu], [Ut])
                            for h in range(8):
                                b0 = hb(h)
                                ft = h // 2
                                oreg = po[b0:b0 + 64, ft * 128 + cb:ft * 128 + cb + 64]
                                k.mm(oreg, ST[:, ft, :], mk["R"][h % 2][:, ft, cs_], True, False, [ST, mk["R"][h % 2]], [po])
                                k.mm(oreg, Vtm[cs_, h * 64:(h + 1) * 64], ArkT[cs_, h, cs_], False, False, [Vtm, ArkT], [po])
                                k.mm(oreg, Ut[cs_, h * 64:(h + 1) * 64], ArbT[cs_, h, cs_], False, True, [Ut, ArbT], [po])
                            pS = self.next_pf()
                            for h in range(8):
                                b0 = hb(h)
                                ft = h // 2
                                sreg = pS[b0:b0 + 64, ft * 64:(ft + 1) * 64]
                                k.mm(sreg, Khtm[cs_, h * 64:(h + 1) * 64], Vtm[cs_, h * 64:(h + 1) * 64], True, False,
                                     [Khtm, Vtm], [pS])
                                k.mm(sreg, Bhtm[cs_, h * 64:(h + 1) * 64], Ut[cs_, h * 64:(h + 1) * 64], False, True,
                                     [Bhtm, Ut], [pS])
                            k.tt(ST[:], ST[:], pc_all[:, :, d, c64i:c64i + 1].to_broadcast([128, 4, 64]), ALU.mult,
                                 [ST, pc_all], [ST])
                            k.tt(ST[:], ST[:], pS[:, 0:256].rearrange("p (f i) -> p f i", f=4), ALU.add, [ST, pS], [ST])
                        k.cp(o_[:], po[:, :].rearrange("p (f t) -> p f t", f=4), [po], [o_], en="act")
                        k.dma("pool", self.rw_o[d, :, w0:w0 + 128].rearrange("(f p) t -> p f t", p=128), o_[:], [o_], [self.rw_o])
                    if not is_s:
                        for h in range(8):
                            b0 = (h % 2) * 64
                            ps = self.next_pf()
                            k.tr(ps[0:64, 0:64], ST[b0:b0 + 64, h // 2, :], self.ident_f[b0:b0 + 64, b0:b0 + 64],
                                 [ST, self.ident_f], [ps])
                            k.cp(fint[:, h, :], ps[0:64, 0:64], [ps], [fint])
                        k.dma("pool", self.o_rw_fin[si, l, d].rearrange("h i j -> i h j"), fint[:], [fint], [self.o_rw_fin])
        with k.phase():
            of = [k.sb(f"rw_of{i}", [128, TS]) for i in range(2)]
            ob_ = [k.sb(f"rw_obb{i}", [128, TS]) for i in range(2)]
            bo = [k.sb(f"rw_bo{i}", [128, TS]) for i in range(2)]
            gt = [k.sb(f"rw_gt{i}", [128, TS]) for i in range(2)]
            xc = [k.sb(f"rw_xc{i}", [128, TS]) for i in range(2)]
            sq = [k.sb(f"rw_sq{i}", [128, TS]) for i in range(2)]
            oo = [k.sb(f"rw_oo{i}", [128, TS], BF16) for i in range(2)]
            n = 0
            for a in range(0, T, TS):
                for ft in range(4):
                    n += 1
                    i = n % 2
                    rs_ = slice(ft * 128, (ft + 1) * 128)
                    k.dma("sp", of[i][:], self.rw_o[0, rs_, a:a + TS], [self.rw_o], [of[i]])
                    k.dma("sp", ob_[i][:], self.rw_o[1, rs_, a:a + TS], [self.rw_o], [ob_[i]])
                    k.dma("sp", bo[i][:], self.rw_bonus[rs_, a:a + TS], [self.rw_bonus], [bo[i]])
                    k.dma("sp", gt[i][:], self.zT[zr["rw_gate"] + ft * 128:zr["rw_gate"] + (ft + 1) * 128, a:a + TS],
                          [self.zT], [gt[i]])
                    k.tt(of[i][:], of[i][:], ob_[i][:], ALU.add, [of[i], ob_[i]], [of[i]])
                    pm = self.next_pf()
                    k.mm(pm[:, :], blk[:, :], of[i][:], True, True, [blk, of[i]], [pm])
                    k.stt(xc[i][:], pm[:, :], -1.0 / 64, of[i][:], ALU.mult, ALU.add, [pm, of[i]], [xc[i]])
                    k.tt(sq[i][:], xc[i][:], xc[i][:], ALU.mult, [xc[i]], [sq[i]])
                    pv = self.next_pf()
                    k.mm(pv[:, :], blk[:, :], sq[i][:], True, True, [blk, sq[i]], [pv])
                    k.ts(sq[i][:], pv[:, :], 1.0 / 64, 64e-5, ALU.mult, ALU.add, [pv], [sq[i]])
                    k.act(sq[i][:], sq[i][:], AF.Sqrt, [sq[i]], [sq[i]])
                    k.op("dve", lambda e: e.reciprocal(out=sq[i][:], in_=sq[i][:]), [sq[i]], [sq[i]])
                    k.tt(xc[i][:], xc[i][:], sq[i][:], ALU.mult, [xc[i], sq[i]], [xc[i]])
                    k.ts(xc[i][:], xc[i][:], vc("rw_ln_g", ft), vc("rw_ln_b", ft), ALU.mult, ALU.add, [xc[i]] + V, [xc[i]])
                    k.tt(xc[i][:], xc[i][:], bo[i][:], ALU.add, [xc[i], bo[i]], [xc[i]])
                    k.act(gt[i][:], gt[i][:], AF.Silu, [gt[i]], [gt[i]])
                    k.tt(oo[i][:], xc[i][:], gt[i][:], ALU.mult, [xc[i], gt[i]], [oo[i]])
                    k.dma("pool", self.o_scr[0, rs_, a:a + TS], oo[i][:], [oo[i]], [self.o_scr])


def host_weights(inp):
    w_in = np.asarray(inp["w_in"], np.float32)
    cols = []
    o = _IW
    cols.append(w_in[:, :, o["rw_pre"]:o["rw_pre"] + 1664])
    cols.append(w_in[:, :, o["rw_gate"]:o["rw_gate"] + 512])
    cols.append(w_in[:, :, o["mla_q"]:o["mla_q"] + 384])
    cols.append(w_in[:, :, o["mla_ckv"]:o["mla_ckv"] + 256])
    misc = np.zeros((NL, D, 128), np.float32)
    misc[:, :, 0:32] = w_in[:, :, o["mla_krope"]:o["mla_krope"] + 32]
    misc[:, :, 32:48] = w_in[:, :, o["ssd_dt"]:o["ssd_dt"] + 16]
    cols.append(misc)
    cols.append(w_in[:, :, o["mla_gate"]:o["mla_gate"] + 512])
    cols.append(w_in[:, :, o["ssd_xbc"]:o["ssd_xbc"] + 768])
    cols.append(w_in[:, :, o["lru_x"]:o["lru_x"] + 512])
    cols.append(w_in[:, :, o["lru_gate"]:o["lru_gate"] + 512])
    cols.append(w_in[:, :, o["ssd_gate"]:o["ssd_gate"] + 512])
    w_in_a = np.ascontiguousarray(np.concatenate(cols, -1))
    assert w_in_a.shape[-1] == NZ_ALL
    sel2 = np.zeros((2, 2, 128), np.float32)
    sel2[0, 0] = 1.0
    sel2[1, 1] = 1.0
    lw = np.zeros((NL, 2, 2, 4, 128, 128), np.float32)
    for d in range(2):
        for gi, nm in enumerate(("lru_wa", "lru_wx")):
            w = np.asarray(inp[nm], np.float32)
            for ft in range(4):
                for kb in range(2):
                    lw[:, d, gi, ft, kb * 64:(kb + 1) * 64, kb * 64:(kb + 1) * 64] = w[:, d, ft * 2 + kb]
    lru_w = np.ascontiguousarray(lw.reshape(NL, 16, 128, 128).transpose(0, 2, 1, 3))
    kvu = np.asarray(inp["mla_kv_up"], np.float32).reshape(NL, 256, 8, 128)
    inv = 10000.0 ** (-np.arange(8, dtype=np.float32) / 8)
    W_pm = np.zeros((32, 96), np.float32)
    for m_ in range(32):
        sw = m_ + 8 if (m_ % 16) < 8 else m_ - 8
        W_pm[sw, 64 + m_] = 1.0
    selm = np.zeros((64, 16, 128), np.float32)
    seld = np.zeros((64, 2, 8), np.float32)
    selg = np.zeros((64, 128), np.float32)
    for d_ in range(2):
        for h_ in range(8):
            selm[d_ * 32 + h_, d_ * 8 + h_, :] = 1.0
            seld[d_ * 32 + h_, d_, h_ % 4] = 1.0
            selg[d_ * 32 + h_, (h_ // 4) * 64:(h_ // 4 + 1) * 64] = 1.0
    ii = np.arange(128)
    maskneg = np.zeros((128, 2, 128), np.float32)
    maskneg[:, 0, :] = np.where(ii[:, None] <= ii[None, :], 0.0, -30000.0)
    maskneg[:, 1, :] = np.where(ii[:, None] >= ii[None, :], 0.0, -30000.0)
    ssd_bc = np.concatenate([np.broadcast_to(np.asarray(inp["ssd_d"], np.float32)[:, None, :], (NL, 128, 8)),
                             np.broadcast_to(np.asarray(inp["ssd_norm_g"], np.float32)[:, None, :], (NL, 128, 512))], -1)
    rw_lw = np.concatenate([np.asarray(inp["rw_w2"], np.float32), np.asarray(inp["rw_a2"], np.float32)], 2)
    rw_lw = np.ascontiguousarray(rw_lw.transpose(0, 2, 1, 3))
    same = (ii[:, None] // 64) == (ii[None, :] // 64)
    rw_mask = np.zeros((128, 2, 3, 128), np.float32)
    rw_mask[:, 0, 0, :] = same & (ii[:, None] < ii[None, :])
    rw_mask[:, 0, 1, :] = same & (ii[:, None] > ii[None, :])
    rw_mask[:, 0, 2, :] = same & (ii[:, None] <= ii[None, :])
    rw_mask[:, 1, 0, :] = same & (ii[:, None] > ii[None, :])
    rw_mask[:, 1, 1, :] = same & (ii[:, None] < ii[None, :])
    rw_mask[:, 1, 2, :] = same & (ii[:, None] >= ii[None, :])
    cm64 = np.ones((128, 513), np.float32)
    cm64[:, ::64] = 0.0
    W = dict(
        pmask=np.ascontiguousarray(np.stack([(ii < 64), (ii >= 64)], 1).astype(np.float32)),
        rw_lw=rw_lw, rw_mask=rw_mask, blk64=same.astype(np.float32), cmask64=cm64,
        selm=selm, seld=seld, selg=selg, maskneg=maskneg, ssd_bc=np.ascontiguousarray(ssd_bc),
        q_up=np.ascontiguousarray(inp["mla_q_up"], np.float32),
        kv_up_k=np.ascontiguousarray(kvu[:, :, :, :64].reshape(NL, 256, 512)),
        kv_up_v=np.ascontiguousarray(kvu[:, :, :, 64:].reshape(NL, 256, 512)),
        pm96=W_pm,
        _inv=inv,
        w_in_m=np.ascontiguousarray(w_in[:, :, o["merge"]:]),
        w_branch=np.ascontiguousarray(inp["w_branch"], np.float32),
        w_out=np.ascontiguousarray(inp["w_out"], np.float32),
        lru_w=lru_w,
        ada_w=np.ascontiguousarray(inp["ada_w"], np.float32),
        ada_bg=np.ascontiguousarray(np.asarray(inp["ada_b"], np.float32)[:, None, 2048:]),
        w_in_a=w_in_a,
        vecs=host_vecs(inp).pack(),
        ident=np.eye(128, dtype=np.float32),
        sel2=sel2,
    )
    return W


def core_inputs(inp, W, core, cfg):
    b = core // 2
    xp = np.asarray(inp["x_prompt"], np.float32)[core * cfg.n_prompt:(core + 1) * cfg.n_prompt, :cfg.lp]
    xs = np.asarray(inp["x_sample"], np.float32)[b, :cfg.ls]
    x_all = np.ascontiguousarray(np.concatenate([xp.reshape(-1, D), xs], 0))
    cond = np.stack([np.asarray(inp["c_ctx"], np.float32), np.asarray(inp["c"], np.float32)[b]], 0)
    condT = np.ascontiguousarray(cond.reshape(2, 8, 128).transpose(2, 1, 0))
    m = dict(W)
    inv = m.pop("_inv")
    t = np.arange(cfg.ls)
    row = (t // 64).astype(np.float32)
    col = (t % 64).astype(np.float32)
    ar, ac = row[None, :] * inv[:, None], col[None, :] * inv[:, None]
    cosT = np.concatenate([np.cos(ar), np.cos(ar), np.cos(ac), np.cos(ac)], 0)
    sinT = np.concatenate([-np.sin(ar), np.sin(ar), -np.sin(ac), np.sin(ac)], 0)
    cm = np.ones((64, cfg.T + 1), np.float32)
    cm[:, ::128] = 0.0
    m["cmask"] = cm
    m["ssd_h0"] = np.ascontiguousarray(np.asarray(inp["state_ssd"], np.float32)[b].transpose(0, 1, 3, 2, 4))
    m["rw_h0"] = np.ascontiguousarray(np.asarray(inp["state_rwkv"], np.float32)[b].transpose(0, 1, 3, 2, 4))
    m["rope_cs"] = np.ascontiguousarray(np.stack([cosT, sinT], 0).astype(np.float32))
    m["cache_ckv"] = np.ascontiguousarray(np.asarray(inp["cache_mla_ckv"], np.float32)[b])
    m["cache_kr"] = np.ascontiguousarray(np.asarray(inp["cache_mla_krope"], np.float32)[b])
    sl = np.asarray(inp["state_lru"], np.float32)[b]
    lru_h0 = np.ascontiguousarray(sl.reshape(NL, 2, 4, 128).transpose(0, 3, 1, 2))
    m.update(x_all=x_all, condT=condT, lru_h0=lru_h0)
    return m


_PROG = {}


def kernel(**inp):
    cfg = Cfg(n_prompt=4, lp=256, ls=4096, debug=DEBUG_FLAGS)
    if "p" not in _PROG:
        prog = Prog(cfg)
        prog.build()
        _PROG["p"] = prog
    prog = _PROG["p"]
    W = host_weights(inp)
    in_maps = []
    for core in range(8):
        m = core_inputs(inp, W, core, cfg)
        in_maps.append({k_: np.ascontiguousarray(v) for k_, v in m.items() if k_ in prog.inputs})
    res = run_bass_kernel_spmd(prog.nc, in_maps, core_ids=list(range(8)))
    R = res.results
    npq = cfg.n_prompt
    y_prompt = np.concatenate([R[c]["y_all"][:cfg.TP].reshape(npq, cfg.lp, D) for c in range(8)], 0)
    y_sample = np.stack([R[2 * b]["y_all"][cfg.TP:] for b in range(4)], 0)
    ckv = np.concatenate([R[c]["o_ckv"].reshape(NL, 256, npq, cfg.lp).transpose(2, 0, 3, 1) for c in range(8)], 0)
    kr = np.concatenate([R[c]["o_kr"].reshape(NL, 32, npq, cfg.lp).transpose(2, 0, 3, 1) for c in range(8)], 0)
    if "o_rw_fin" in R[0]:
        rw = np.concatenate([R[c]["o_rw_fin"] for c in range(8)], 0)
    else:
        rw = np.zeros((32, NL, 2, 8, 64, 64), np.float32)
    if "o_ssd_fin" in R[0]:
        ssd = np.concatenate([R[c]["o_ssd_fin"].transpose(1, 0, 2, 4, 3, 5) for c in range(8)], 0)
    else:
        ssd = np.zeros((32, NL, 2, 8, 64, 64), np.float32)
    lru = np.concatenate([R[c]["o_lru_fin"].transpose(2, 0, 3, 4, 1).reshape(npq, NL, 2, 512) for c in range(8)], 0)
    f = lambda a: np.ascontiguousarray(a, dtype=np.float32)
    return (f(y_prompt), f(y_sample), f(ckv), f(kr), f(rw), f(ssd), f(lru))
```

```python
import contextlib
import numpy as np
import ml_dtypes
import concourse.bass as bass
import concourse.mybir as mybir
from concourse.bass_utils import run_bass_kernel_spmd

F32 = mybir.dt.float32
BF16 = mybir.dt.bfloat16
AF = mybir.ActivationFunctionType
ALU = mybir.AluOpType
AX = mybir.AxisListType

D = 1024
NL = 2
NDMA_SLOTS = 6
RW_BF16 = False
RW_OBF16 = True
DEBUG_FLAGS = ()
SAME_ENGINE_SYNC = ("act", "dve", "pool")


class Res:
    __slots__ = ("w", "r", "name")

    def __init__(self, name=""):
        self.w = []
        self.r = {}
        self.name = name


class Tl:
    def __init__(self, h, name, psum=False):
        self.h = h
        self.res = Res(name)
        self.name = name
        self.psum = psum

    def __getitem__(self, k):
        return self.h[k]


class Eng:
    def __init__(self, name, h, sem):
        self.name = name
        self.h = h
        self.sem = sem
        self.count = 0
        self.waited = {}
        self.dma_sems = []
        self.dma_vals = []
        self.dma_n = 0


def _res(x):
    return x.res if isinstance(x, Tl) else x


class KB:
    def __init__(self, nc):
        self.nc = nc
        self.es = contextlib.ExitStack()
        self.stacks = [self.es]
        self.eng = {}
        self.uid = 0
        for name, h in (("pe", nc.tensor), ("dve", nc.vector), ("act", nc.scalar),
                        ("pool", nc.gpsimd), ("sp", nc.sync)):
            sem = self.es.enter_context(nc.semaphore("s_" + name))
            self.eng[name] = Eng(name, h, sem)
        for qn in ("sp", "pool", "act"):
            E = self.eng[qn]
            for i in range(NDMA_SLOTS):
                E.dma_sems.append(self.es.enter_context(nc.semaphore(f"d_{qn}{i}")))
                E.dma_vals.append(0)

    def sb(self, name, shape, dtype=F32):
        self.uid += 1
        nm = f"{name}_{self.uid}"
        h = self.stacks[-1].enter_context(self.nc.sbuf_tensor(nm, list(shape), dtype))
        return Tl(h, nm)

    def ps(self, name, shape, dtype=F32):
        self.uid += 1
        nm = f"{name}_{self.uid}"
        h = self.stacks[-1].enter_context(self.nc.psum_tensor(nm, list(shape), dtype))
        return Tl(h, nm, psum=True)

    def dram(self, name, shape, dtype=F32, kind="Internal"):
        h = self.nc.dram_tensor(name, list(shape), dtype, kind=kind)
        return Tl(h.ap(), name)

    @contextlib.contextmanager
    def phase(self):
        self.barrier()
        st = contextlib.ExitStack()
        self.stacks.append(st)
        try:
            with st:
                yield
                self.barrier()
        finally:
            self.stacks.pop()

    def _wait(self, E, ev):
        sem, val, src = ev
        if src is E and E.name not in SAME_ENGINE_SYNC:
            return
        k = id(sem)
        if E.waited.get(k, 0) >= val:
            return
        E.h.wait_ge(sem, val)
        E.waited[k] = val

    def _deps(self, E, reads, writes):
        for r in reads:
            for ev in _res(r).w:
                self._wait(E, ev)
        for w in writes:
            rs = _res(w)
            for ev in rs.w:
                self._wait(E, ev)
            for ev in rs.r.values():
                self._wait(E, ev)

    def _commit(self, ev, key, reads, writes):
        for r in reads:
            _res(r).r[key] = ev
        for w in writes:
            rs = _res(w)
            rs.w = [ev]
            rs.r = {}

    def op(self, en, fn, reads=(), writes=()):
        E = self.eng[en]
        pr = [r for r in reads if isinstance(r, Tl) and r.psum]
        if pr:
            reads = [r for r in reads if not (isinstance(r, Tl) and r.psum)]
            writes = list(writes) + [r for r in pr if r not in writes]
        self._deps(E, reads, writes)
        ins = fn(E.h)
        E.count += 1
        ins.then_inc(E.sem, 1)
        ev = (E.sem, E.count, E)
        self._commit(ev, en, reads, writes)
        return ins

    def dma(self, qn, out, in_, reads=(), writes=(), **kw):
        E = self.eng[qn]
        self._deps(E, reads, writes)
        slot = E.dma_n % NDMA_SLOTS
        E.dma_n += 1
        sem = E.dma_sems[slot]
        pv = E.dma_vals[slot]
        if pv > 0:
            self._wait(E, (sem, pv, None))
        E.h.dma_start(out=out, in_=in_, **kw).then_inc(sem, 16)
        E.dma_vals[slot] = pv + 16
        ev = (sem, pv + 16, None)
        self._commit(ev, ("dma", qn, slot), reads, writes)

    def barrier(self):
        evs = []
        for E in self.eng.values():
            if E.count:
                evs.append((E.sem, E.count, E))
            for s, v in zip(E.dma_sems, E.dma_vals):
                if v:
                    evs.append((s, v, None))
        for E in self.eng.values():
            for ev in evs:
                self._wait(E, ev)

    def mm(self, out, lhsT, rhs, start, stop, reads, writes):
        return self.op("pe", lambda e: e.matmul(out, lhsT=lhsT, rhs=rhs, start=start, stop=stop),
                       reads, writes)

    def tr(self, out, in_, ident, reads, writes):
        return self.op("pe", lambda e: e.transpose(out, in_, ident), reads, writes)

    def act(self, out, in_, func, reads, writes, bias=None, scale=None, accum_out=None, en="act"):
        kw = {}
        if bias is not None:
            kw["bias"] = bias
        if scale is not None:
            kw["scale"] = scale
        if accum_out is not None:
            kw["accum_out"] = accum_out
        return self.op(en, lambda e: e.activation(out=out, in_=in_, func=func, **kw), reads, writes)

    def tt(self, out, in0, in1, op, reads, writes, en="dve"):
        return self.op(en, lambda e: e.tensor_tensor(out=out, in0=in0, in1=in1, op=op), reads, writes)

    def ts(self, out, in0, s1, s2, op0, op1, reads, writes, en="dve", accum_out=None):
        kw = {}
        if accum_out is not None:
            kw["accum_out"] = accum_out
        if op1 is None:
            return self.op(en, lambda e: e.tensor_scalar(out=out, in0=in0, scalar1=s1, scalar2=None,
                                                         op0=op0, **kw), reads, writes)
        return self.op(en, lambda e: e.tensor_scalar(out=out, in0=in0, scalar1=s1, scalar2=s2,
                                                     op0=op0, op1=op1, **kw), reads, writes)

    def stt(self, out, in0, scalar, in1, op0, op1, reads, writes):
        return self.op("dve", lambda e: e.scalar_tensor_tensor(out=out, in0=in0, scalar=scalar, in1=in1,
                                                               op0=op0, op1=op1), reads, writes)

    def cp(self, out, in_, reads, writes, en="dve"):
        if en == "act":
            return self.op("act", lambda e: e.copy(out=out, in_=in_), reads, writes)
        return self.op(en, lambda e: e.tensor_copy(out=out, in_=in_), reads, writes)

    def scan(self, out, d0, d1, init, reads, writes, op0=ALU.mult, op1=ALU.add):
        return self.op("dve", lambda e: e.tensor_tensor_scan(out=out, data0=d0, data1=d1, initial=init,
                                                             op0=op0, op1=op1), reads, writes)

    def memset(self, ap, val, writes, en="pool"):
        return self.op(en, lambda e: e.memset(ap, val), (), writes)

    def finish(self):
        self.barrier()


ZROWS = {}
_o = 0
for _n, _w in (("rw_r", 512), ("rw_k", 512), ("rw_v", 512), ("rw_lora", 128), ("rw_gate", 512),
               ("mla_q", 384), ("mla_ckv", 256), ("misc", 128), ("mla_gate", 512),
               ("ssd_xbc", 768), ("lru_x", 512), ("lru_gate", 512)):
    ZROWS[_n] = (_o, _w)
    _o += _w
NZ_FM = _o
NZ_ALL = NZ_FM + 512
_IW = dict(rw_pre=0, rw_gate=1664, mla_q=2176, mla_ckv=2560, mla_krope=2816, mla_gate=2848,
           ssd_gate=3360, ssd_xbc=3872, ssd_dt=4640, lru_x=4656, lru_gate=5168, merge=5680)


class VecPack:
    def __init__(self):
        self.cols = {}
        self.n = 0
        self.data = []

    def add(self, name, arr2d):
        k = arr2d.shape[-1]
        self.cols[name] = (self.n, k)
        self.n += k
        self.data.append(np.asarray(arr2d, np.float32))

    def fm(self, name, v):
        v = np.asarray(v, np.float32)
        n = v.shape[-1]
        if n % 128:
            pad = 128 - n % 128
            v = np.concatenate([v, np.zeros(v.shape[:-1] + (pad,), np.float32)], -1)
        k = v.shape[-1] // 128
        self.add(name, v.reshape(v.shape[0], k, 128).transpose(0, 2, 1))

    def pack(self):
        return np.ascontiguousarray(np.concatenate(self.data, -1))


def vec_layout():
    vp = VecPack()
    z = lambda *s: np.zeros(s, np.float32)
    vp.fm("ada_b_ss", z(NL, 2048))
    vp.fm("norm_g", z(NL, 1024))
    vp.fm("mla_qa_g", z(NL, 384))
    vp.fm("mla_kva_g", z(NL, 256))
    vp.fm("mla_qn_g", z(NL, 96))
    vp.fm("mla_kn_g", z(NL, 96))
    vp.fm("rw_mu", z(NL, 1664))
    vp.fm("rw_w0", z(NL, 1024))
    vp.fm("rw_a0", z(NL, 1024))
    vp.fm("rw_kk", z(NL, 512))
    vp.fm("rw_ka", z(NL, 512))
    vp.fm("rw_rk", z(NL, 512))
    vp.fm("rw_ln_g", z(NL, 512))
    vp.fm("rw_ln_b", z(NL, 512))
    vp.fm("ssd_cw", z(NL, 4 * 768))
    vp.fm("ssd_cb", z(NL, 768))
    vp.fm("ssd_dtb", z(NL, 64))
    vp.fm("ssd_alog", z(NL, 64))
    vp.fm("lru_cw", z(NL, 4 * 512))
    vp.fm("lru_cb", z(NL, 512))
    vp.fm("lru_ba", z(NL, 2 * 512))
    vp.fm("lru_bx", z(NL, 2 * 512))
    vp.fm("lru_lam", z(NL, 2 * 512))
    return vp


def host_vecs(inp):
    vp = VecPack()
    vp.fm("ada_b_ss", inp["ada_b"][:, :2048])
    vp.fm("norm_g", inp["norm_g"])
    r2 = lambda a: np.asarray(a, np.float32).reshape(NL, -1)
    vp.fm("mla_qa_g", inp["mla_qa_g"])
    vp.fm("mla_kva_g", inp["mla_kva_g"])
    vp.fm("mla_qn_g", inp["mla_qn_g"])
    vp.fm("mla_kn_g", inp["mla_kn_g"])
    vp.fm("rw_mu", inp["rw_mu"])
    vp.fm("rw_w0", r2(inp["rw_w0"]))
    vp.fm("rw_a0", r2(inp["rw_a0"]))
    vp.fm("rw_kk", inp["rw_kk"])
    vp.fm("rw_ka", inp["rw_ka"])
    vp.fm("rw_rk", r2(inp["rw_rk"]))
    vp.fm("rw_ln_g", inp["rw_ln_g"])
    vp.fm("rw_ln_b", inp["rw_ln_b"])
    vp.fm("ssd_cw", r2(inp["ssd_conv_w"]))
    vp.fm("ssd_cb", inp["ssd_conv_b"])
    def d64(a):
        a = np.asarray(a, np.float32)
        o_ = np.zeros((NL, 64), np.float32)
        o_[:, 0:8] = a[:, 0]
        o_[:, 32:40] = a[:, 1]
        return o_
    vp.fm("ssd_dtb", d64(inp["ssd_dt_bias"]))
    vp.fm("ssd_alog", d64(inp["ssd_a_log"]))
    vp.fm("lru_cw", r2(inp["lru_conv_w"]))
    vp.fm("lru_cb", inp["lru_conv_b"])
    vp.fm("lru_ba", r2(inp["lru_ba"]))
    vp.fm("lru_bx", r2(inp["lru_bx"]))
    vp.fm("lru_lam", r2(inp["lru_lambda"]))
    return vp


class Cfg:
    def __init__(self, n_prompt=4, lp=256, ls=4096, debug=()):
        self.n_prompt = n_prompt
        self.lp = lp
        self.ls = ls
        self.TP = n_prompt * lp
        self.T = self.TP + ls
        self.seqs = [(i * lp, lp, False) for i in range(n_prompt)] + [(self.TP, ls, True)]
        self.debug = set(debug)
        assert self.T % 512 == 0 and lp % 128 == 0 and ls % 512 == 0 and self.TP % 512 == 0


class Prog:
    def __init__(self, cfg):
        self.cfg = cfg
        self.nc = bass.Bass("TRN2", target_bir_lowering=False)
        self.k = KB(self.nc)
        self.inputs = {}
        self.outputs = {}
        self.vl = vec_layout()

    def din(self, name, shape, dtype=F32):
        t = self.k.dram(name, shape, dtype, kind="ExternalInput")
        self.inputs[name] = t
        return t

    def dout(self, name, shape, dtype=F32):
        t = self.k.dram(name, shape, dtype, kind="ExternalOutput")
        self.outputs[name] = t
        return t

    def vcol(self, name, j=0, n=1):
        o, k = self.vl.cols[name]
        assert j + n <= k
        return self.vecs[:, o + j:o + j + n]

    def build(self):
        cfg, k = self.cfg, self.k
        T = cfg.T
        with k.es:
            self.x_in = self.din("x_all", [T, D])
            self.condT = self.din("condT", [128, 8, 2])
            self.ada_w = self.din("ada_w", [NL, D, 3 * D])
            self.ada_bg = self.din("ada_bg", [NL, 1, D])
            self.w_in_a = self.din("w_in_a", [NL, D, NZ_ALL])
            self.vecs_d = self.din("vecs", [NL, 128, self.vl.n])
            self.ident_d = self.din("ident", [128, 128])
            self.sel2_d = self.din("sel2", [2, 2, 128])
            self.lru_w = self.din("lru_w", [NL, 128, 16, 128])
            self.lru_h0 = self.din("lru_h0", [NL, 128, 2, 4])
            self.o_lru_fin = self.dout("o_lru_fin", [NL, 128, cfg.n_prompt, 2, 4])
            self.q_up = self.din("q_up", [NL, 384, 768])
            self.kv_up_k = self.din("kv_up_k", [NL, 256, 512])
            self.kv_up_v = self.din("kv_up_v", [NL, 256, 512])
            self.cache_ckv = self.din("cache_ckv", [NL, 256, 256])
            self.cache_kr = self.din("cache_kr", [NL, 256, 32])
            self.rope_cs = self.din("rope_cs", [2, 32, cfg.ls])
            self.pm96_d = self.din("pm96", [32, 96])
            self.o_ckv = self.dout("o_ckv", [NL, 256, cfg.TP])
            self.o_kr = self.dout("o_kr", [NL, 32, cfg.TP])
            self.ssd_bc = self.din("ssd_bc", [NL, 128, 8 + 512])
            self.ssd_h0 = self.din("ssd_h0", [NL, 2, 64, 8, 64])
            self.selm_d = self.din("selm", [64, 16, 128])
            self.seld_d = self.din("seld", [64, 2, 8])
            self.selg_d = self.din("selg", [64, 128])
            self.maskneg_d = self.din("maskneg", [128, 2, 128])
            self.cmask_d = self.din("cmask", [64, T + 1])
            self.o_ssd_fin = self.dout("o_ssd_fin", [NL, cfg.n_prompt, 2, 64, 8, 64])
            self.ssd_y = k.dram("ssd_y_scr", [T, 512])
            self.rw_lw = self.din("rw_lw", [NL, 128, 2, 512])
            self.rw_h0 = self.din("rw_h0", [NL, 2, 64, 8, 64])
            self.rw_mask_d = self.din("rw_mask", [128, 2, 3, 128])
            self.blk64_d = self.din("blk64", [128, 128])
            self.pmask_d = self.din("pmask", [128, 2])
            self.cmask64_d = self.din("cmask64", [128, 513])
            self.o_rw_fin = self.dout("o_rw_fin", [cfg.n_prompt, NL, 2, 8, 64, 64])
            self.rw_ops = k.dram("rw_ops_scr", [2, 6, 512, T])
            self.rw_v = k.dram("rw_v_scr", [512, T])
            self.rw_bonus = k.dram("rw_bonus_scr", [512, T])
            self.rw_o = k.dram("rw_o_scr", [2, 512, T])
            self.w_in_m = self.din("w_in_m", [NL, D, 4 * D])
            self.w_branch = self.din("w_branch", [NL, 4, 512, D])
            self.w_out = self.din("w_out", [NL, D, D])
            self.y_all = self.dout("y_all", [T, D])
            self.y1 = k.dram("y1_scr", [T, D])
            self.mT_scr = k.dram("mT_scr", [D, T], BF16)
            self.o_scr = k.dram("o_scr", [4, 512, T], BF16)
            if "o" in cfg.debug:
                self.dbg_o = self.dout("dbg_o", [4, 512, T], BF16)
            self.zT = k.dram("zT_scr", [NZ_FM, T])
            self.zg_tm = k.dram("zg_tm_scr", [T, 512])
            self.hT_scr = k.dram("hT_scr", [D, T], BF16)
            if "z" in cfg.debug:
                self.dbg_zT = self.dout("dbg_zT", [NZ_FM, T])
                self.dbg_zg = self.dout("dbg_zg", [T, 512])
            self.vecs = k.sb("vecs", [128, self.vl.n])
            self.ident_f = k.sb("ident_f", [128, 128])
            self.ident_b = k.sb("ident_b", [128, 128], BF16)
            self.sel2 = k.sb("sel2", [2, 2, 128])
            self.sc = k.sb("sc", [128, 8, 2])
            self.gmod = k.sb("gmod", [128, 8, 2])
            self.shiftc = k.sb("shiftc", [128, 8, 2])
            self.gate_bc = [k.sb(f"gate_bc{g}", [128, D]) for g in range(2)]
            self.ones_f = k.sb("ones_f", [128, 128])
            k.memset(self.ones_f[:], 1.0, [self.ones_f])
            self.pf = [k.ps(f"pf{i}", [128, 512]) for i in range(6)]
            self.pb = [k.ps(f"pb{i}", [128, 1024], BF16) for i in range(2)]
            self.pfi = 0
            k.dma("sp", self.ident_f[:], self.ident_d[:, :], (), [self.ident_f])
            k.dma("sp", self.sel2[:], self.sel2_d[:, :, :], (), [self.sel2])
            k.cp(self.ident_b[:], self.ident_f[:], [self.ident_f], [self.ident_b])
            k.dma("sp", self.sc[:], self.condT[:, :, :], (), [self.sc])
            k.act(self.sc[:], self.sc[:], AF.Silu, [self.sc], [self.sc])

            for l in range(NL):
                self.layer(l)
                if l == 0 and "stop0" in cfg.debug:
                    break
            k.finish()
        return self.nc

    def dump(self, name, tl, ap, shape, dtype=F32):
        if name not in self.cfg.debug:
            return
        t = self.dout("dump_" + name, shape, dtype)
        self.k.dma("sp", t[tuple(slice(None) for _ in shape)], ap, [tl], [t])

    def next_pf(self):
        p = self.pf[self.pfi % 4]
        self.pfi += 1
        return p

    def layer(self, l):
        cfg, k = self.cfg, self.k
        with k.phase():
            k.dma("sp", self.vecs[:], self.vecs_d[l], (), [self.vecs])
            self.phase_mod(l)
        if "zero_o" in cfg.debug:
            with k.phase():
                zt = k.sb("zt", [128, cfg.T], BF16)
                k.memset(zt[:], 0.0, [zt])
                for m in range(4):
                    for ft in range(4):
                        k.dma("sp", self.o_scr[m, ft * 128:(ft + 1) * 128, :], zt[:], [zt], [self.o_scr])
        with k.phase():
            self.phase_front(l)
        with k.phase():
            self.phase_lru(l)
        if "norw" not in cfg.debug:
            with k.phase():
                self.phase_rwkv(l)
        if "nossd" not in cfg.debug:
            with k.phase():
                self.phase_ssd(l)
        if "nomla" not in cfg.debug:
            with k.phase():
                self.phase_mla(l)
        with k.phase():
            self.phase_merge(l)
        with k.phase():
            self.phase_out(l)
        if "y1" in cfg.debug and l == 0:
            k.barrier()
            d_ = self.dout("dbg_y1", [cfg.T, D])
            k.dma("sp", d_[:, :], self.y1[:, :], [self.y1], [d_])
            k.barrier()
        if "o" in cfg.debug and l == 0:
            k.barrier()
            k.dma("sp", self.dbg_o[:, :, :], self.o_scr[:, :, :], [self.o_scr], [self.dbg_o])
            k.barrier()

    def phase_mod(self, l):
        k = self.k
        wst = [k.sb(f"adaw{i}", [128, 8, 512]) for i in range(2)]
        modc = k.sb("modc", [128, 16, 2])
        grow = k.sb("grow", [2, D])
        gb = k.sb("gb", [2, D])
        for g in range(2):
            k.dma("pool", gb[g:g + 1, :], self.ada_bg[l], (), [gb])
        for blk in range(6):
            w = wst[blk % 2]
            k.dma("sp", w[:], self.ada_w[l][:, blk * 512:(blk + 1) * 512].rearrange("(c p) n -> p c n", p=128),
                  (), [w])
            if blk < 4:
                for jt in range(4):
                    ps = self.next_pf()
                    for c in range(8):
                        k.mm(ps[:, 0:2], w[:, c, jt * 128:(jt + 1) * 128], self.sc[:, c, :], c == 0, c == 7,
                             [w, self.sc], [ps])
                    k.cp(modc[:, blk * 4 + jt, :], ps[:, 0:2], [ps], [modc])
            else:
                ps = self.next_pf()
                for c in range(8):
                    k.mm(ps[0:2, :], self.sc[:, c, :], w[:, c, :], c == 0, c == 7, [w, self.sc], [ps])
                hs = slice((blk - 4) * 512, (blk - 3) * 512)
                k.tt(grow[:, hs], ps[0:2, :], gb[:, hs], ALU.add, [ps, gb], [grow])
        ab = self.vcol("ada_b_ss", 0, 16)
        for g in range(2):
            k.tt(modc[:, :, g], modc[:, :, g], ab, ALU.add, [modc, self.vecs], [modc])
            k.cp(self.shiftc[:, :, g], modc[:, 0:8, g], [modc], [self.shiftc])
            k.stt(self.gmod[:, :, g], modc[:, 8:16, g], 1.0, self.vcol("norm_g", 0, 8), ALU.add, ALU.mult,
                  [modc, self.vecs], [self.gmod])
            for half in range(2):
                ps = self.next_pf()
                k.mm(ps[:, :], self.sel2[:, g, :], grow[:, half * 512:(half + 1) * 512], True, True,
                     [self.sel2, grow], [ps])
                k.cp(self.gate_bc[g][:, half * 512:(half + 1) * 512], ps[:, :], [ps], [self.gate_bc[g]])

    def phase_front(self, l):
        cfg, k = self.cfg, self.k
        T = cfg.T
        x_src = self.x_in if l == 0 else self.y1
        hT = k.sb("hT", [128, 8, T], BF16)
        with k.phase():
            xt = [k.sb(f"xt{i}", [128, D]) for i in range(3)]
            xn = [k.sb(f"xn{i}", [128, D], BF16) for i in range(2)]
            junk = k.sb("junk", [128, D], BF16)
            ss = [k.sb(f"ss{i}", [128, 1]) for i in range(2)]
            for st in range(T // 128):
                g = 0 if st * 128 < cfg.TP else 1
                x = xt[st % 3]
                xb = xn[st % 2]
                s = ss[st % 2]
                pb = self.pb[st % 2]
                k.dma("sp", x[:], x_src[st * 128:(st + 1) * 128, :], [x_src], [x])
                k.act(junk[:], x[:], AF.Square, [x], [junk, s], accum_out=s[:])
                k.ts(s[:], s[:], 1.0 / D, 1e-6, ALU.mult, ALU.add, [s], [s])
                k.act(s[:], s[:], AF.Sqrt, [s], [s])
                k.op("dve", lambda e: e.reciprocal(out=s[:], in_=s[:]), [s], [s])
                k.act(xb[:], x[:], AF.Copy, [x, s], [xb], scale=s[:])
                for c in range(8):
                    k.tr(pb[:, c * 128:(c + 1) * 128], xb[:, c * 128:(c + 1) * 128], self.ident_b[:],
                         [xb, self.ident_b], [pb])
                ho = hT[:, :, st * 128:(st + 1) * 128]
                pv = pb[:].rearrange("p (c t) -> p c t", c=8)
                k.tt(ho, pv, self.gmod[:, :, g:g + 1].to_broadcast([128, 8, 128]), ALU.mult,
                     [pb, self.gmod], [hT])
                k.tt(ho, ho, self.shiftc[:, :, g:g + 1].to_broadcast([128, 8, 128]), ALU.add,
                     [hT, self.shiftc], [hT])
        for c in range(8):
            k.dma("pool", self.hT_scr[c * 128:(c + 1) * 128, :], hT[:, c, :], [hT], [self.hT_scr])
        wst = [k.sb(f"wst{i}", [128, 8, 512]) for i in range(2)]
        wbf = [k.sb(f"wbf{i}", [128, 8, 512], BF16) for i in range(2)]
        zst = [k.sb(f"zst{i}", [128, 512]) for i in range(4)]
        blocks = [(c0, min(512, NZ_FM - c0), False) for c0 in range(0, NZ_FM, 512)] + [(NZ_FM, 512, True)]
        zi = 0
        for blk, (c0, ncol, tm_block) in enumerate(blocks):
            ws, wb = wst[blk % 2], wbf[blk % 2]
            k.dma("sp", ws[:, :, :ncol], self.w_in_a[l][:, c0:c0 + ncol].rearrange("(c p) n -> p c n", p=128),
                  (), [ws])
            k.cp(wb[:, :, :ncol], ws[:, :, :ncol], [ws], [wb], en="pool")
            for tt in range(T // 512):
                ts_ = slice(tt * 512, (tt + 1) * 512)
                if not tm_block:
                    for jt in range(ncol // 128):
                        ps = self.next_pf()
                        for c in range(8):
                            k.mm(ps[:, :], wb[:, c, jt * 128:(jt + 1) * 128], hT[:, c, ts_], c == 0, c == 7,
                                 [wb, hT], [ps])
                        z = zst[zi % 4]
                        if zi % 2 == 0:
                            k.cp(z[:], ps[:, :], [ps], [z])
                        else:
                            k.cp(z[:], ps[:, :], [ps], [z], en="act")
                        zi += 1
                        r0 = c0 + jt * 128
                        k.dma("pool", self.zT[r0:r0 + 128, ts_], z[:], [z], [self.zT])
                else:
                    for sub in range(4):
                        ps = self.next_pf()
                        t0 = tt * 512 + sub * 128
                        for c in range(8):
                            k.mm(ps[:, :], hT[:, c, t0:t0 + 128], wb[:, c, :], c == 0, c == 7, [wb, hT], [ps])
                        z = zst[zi % 4]
                        if zi % 2 == 0:
                            k.cp(z[:], ps[:, :], [ps], [z])
                        else:
                            k.cp(z[:], ps[:, :], [ps], [z], en="act")
                        zi += 1
                        k.dma("pool", self.zg_tm[t0:t0 + 128, :], z[:], [z], [self.zg_tm])
        if "z" in cfg.debug and l == 0:
            k.barrier()
            k.dma("sp", self.dbg_zT[:, :], self.zT[:, :], [self.zT], [self.dbg_zT])
            k.dma("sp", self.dbg_zg[:, :], self.zg_tm[:, :], [self.zg_tm], [self.dbg_zg])


    def groups(self):
        cfg = self.cfg
        return [(0, cfg.n_prompt, cfg.lp), (cfg.TP, 1, cfg.ls)]

    def gview(self, ap2d, grp, lo, hi):
        s0, ns, ln = grp
        return ap2d[:, s0:s0 + ns * ln].rearrange("p (s t) -> p s t", s=ns)[:, :, lo:hi]

    def dwconv(self, xc, xl, wcol, bcol, reads):
        k = self.k
        T = self.cfg.T
        k.ts(xc[:, :T], xl[:, :T], wcol(2), bcol, ALU.mult, ALU.add, [xl] + reads, [xc])
        for grp in self.groups():
            ln = grp[2]
            for kk, off in ((0, -2), (1, -1), (3, 1)):
                if off < 0:
                    src = self.gview(xl[:, :T], grp, 0, ln + off)
                    dst = self.gview(xc[:, :T], grp, -off, ln)
                else:
                    src = self.gview(xl[:, :T], grp, off, ln)
                    dst = self.gview(xc[:, :T], grp, 0, ln - off)
                k.stt(dst, src, wcol(kk), dst, ALU.mult, ALU.add, [xl, xc] + reads, [xc])

    def phase_lru(self, l):
        cfg, k = self.cfg, self.k
        T = cfg.T
        zo, _ = ZROWS["lru_x"]
        go, _ = ZROWS["lru_gate"]
        wts = k.sb("lru_wts", [128, 16, 128])
        h0 = k.sb("lru_h0", [128, 2, 4])
        clam = k.sb("lru_clam", [128, 8])
        fin = k.sb("lru_fin", [128, cfg.n_prompt, 2, 4])
        k.dma("sp", wts[:], self.lru_w[l], (), [wts])
        k.dma("sp", h0[:], self.lru_h0[l], (), [h0])
        k.act(clam[:], self.vcol("lru_lam", 0, 8), AF.Exp, [self.vecs], [clam], scale=-1.0)
        k.act(clam[:], clam[:], AF.Ln, [clam], [clam], bias=1.0)
        k.ts(clam[:], clam[:], -8.0, None, ALU.mult, None, [clam], [clam])
        xl = k.sb("lru_xl", [128, T])
        gt = k.sb("lru_gt", [128, T])
        xc = k.sb("lru_xc", [128, T])
        ta = k.sb("lru_ta", [128, T])
        tb = k.sb("lru_tb", [128, T])
        hh = [k.sb(f"lru_h{d}", [128, T]) for d in range(2)]
        ob = k.sb("lru_ob", [128, T], BF16)
        V = [self.vecs]
        for ft in range(4):
            k.dma("sp", xl[:], self.zT[zo + ft * 128:zo + (ft + 1) * 128, :], [self.zT], [xl])
            k.dma("sp", gt[:], self.zT[go + ft * 128:go + (ft + 1) * 128, :], [self.zT], [gt])
            self.dwconv(xc, xl, lambda kk: self.vcol("lru_cw", kk * 4 + ft), self.vcol("lru_cb", ft), V)
            k.act(gt[:], gt[:], AF.Silu, [gt], [gt])
            for d in range(2):
                for tt in range(T // 512):
                    sl = slice(tt * 512, (tt + 1) * 512)
                    pa = self.next_pf()
                    k.mm(pa[:, :], wts[:, (d * 2 + 0) * 4 + ft, :], xc[:, sl], True, True, [wts, xc], [pa])
                    px = self.next_pf()
                    k.mm(px[:, :], wts[:, (d * 2 + 1) * 4 + ft, :], xc[:, sl], True, True, [wts, xc], [px])
                    k.act(ta[:, sl], pa[:, :], AF.Sigmoid, [pa] + V, [ta], bias=self.vcol("lru_ba", d * 4 + ft))
                    k.act(tb[:, sl], px[:, :], AF.Sigmoid, [px] + V, [tb], bias=self.vcol("lru_bx", d * 4 + ft))
                k.act(ta[:], ta[:], AF.Exp, [ta, clam], [ta], scale=clam[:, d * 4 + ft:d * 4 + ft + 1])
                if ft == 0 and d == 0:
                    self.dump("lru_clam", clam, clam[:], [128, 8])
                    self.dump("lru_a", ta, ta[:], [128, T])
                    self.dump("lru_gi", tb, tb[:], [128, T])
                    self.dump("lru_xc", xc, xc[:], [128, T])
                k.tt(tb[:], tb[:], xc[:], ALU.mult, [tb, xc], [tb])
                h = hh[d]
                k.tt(h[:], ta[:], ta[:], ALU.mult, [ta], [h])
                k.act(h[:], h[:], AF.Sqrt, [h], [h], scale=-1.0, bias=1.0)
                k.tt(tb[:], tb[:], h[:], ALU.mult, [tb, h], [tb])
                if ft == 0 and d == 0:
                    self.dump("lru_sq", h, h[:], [128, T])
                    self.dump("lru_u", tb, tb[:], [128, T])
                for (s0, ln, is_s) in cfg.seqs:
                    sl = slice(s0, s0 + ln)
                    init = h0[:, d, ft:ft + 1] if is_s else 0.0
                    rd = [ta, tb] + ([h0] if is_s else [])
                    if d == 0:
                        k.scan(h[:, sl], ta[:, sl], tb[:, sl], init, rd, [h])
                    else:
                        rv = lambda t: t[:, s0:s0 + ln][:, ::-1]
                        k.scan(rv(h), rv(ta), rv(tb), init, rd, [h])
                    if not is_s:
                        si = s0 // ln
                        e = s0 + ln - 1 if d == 0 else s0
                        k.cp(fin[:, si, d, ft:ft + 1], h[:, e:e + 1], [h], [fin], en="pool")
            k.tt(hh[0][:], hh[0][:], hh[1][:], ALU.add, [hh[0], hh[1]], [hh[0]])
            k.tt(ob[:], hh[0][:], gt[:], ALU.mult, [hh[0], gt], [ob])
            k.dma("pool", self.o_scr[3, ft * 128:(ft + 1) * 128, :], ob[:], [ob], [self.o_scr])
        k.dma("pool", self.o_lru_fin[l], fin[:], [fin], [self.o_lru_fin])


    def phase_merge(self, l):
        cfg, k = self.cfg, self.k
        T = cfg.T
        wm = k.sb("wm", [128, 8, 4 * D], BF16)
        wbr = k.sb("wbr", [128, 16, D], BF16)
        with k.phase():
            stg = [k.sb(f"mstg{i}", [128, 4096]) for i in range(2)]
            si = 0
            for blk in range(8):
                st = stg[si % 2]
                si += 1
                k.dma("sp", st[:].rearrange("p (c n) -> p c n", c=8),
                      self.w_in_m[l][:, blk * 512:(blk + 1) * 512].rearrange("(c p) n -> p c n", p=128), (), [st])
                k.cp(wm[:, :, blk * 512:(blk + 1) * 512], st[:].rearrange("p (c n) -> p c n", c=8), [st], [wm], en="pool")
            for m in range(4):
                st = stg[si % 2]
                si += 1
                k.dma("sp", st[:].rearrange("p (c n) -> p c n", c=4),
                      self.w_branch[l, m].rearrange("(c p) n -> p c n", p=128), (), [st])
                k.cp(wbr[:, m * 4:(m + 1) * 4, :], st[:].rearrange("p (c n) -> p c n", c=4), [st], [wbr], en="pool")
        hts = [k.sb(f"mh{i}", [128, 8, 512], BF16) for i in range(2)]
        ots = [k.sb(f"mo{i}", [128, 16, 512], BF16) for i in range(2)]
        sgs = [k.sb(f"msg{i}", [128, 512]) for i in range(2)]
        tmp = [k.sb(f"mtmp{i}", [128, 512]) for i in range(2)]
        acc = [k.sb(f"macc{i}", [128, 512]) for i in range(2)]
        mts = [k.sb(f"mt{i}", [128, 8, 512], BF16) for i in range(2)]
        n = 0
        for tt in range(T // 512):
            tsl = slice(tt * 512, (tt + 1) * 512)
            ht, ot, mt = hts[tt % 2], ots[tt % 2], mts[tt % 2]
            k.dma("sp", ht[:], self.hT_scr[:, tsl].rearrange("(c p) t -> p c t", p=128), [self.hT_scr], [ht])
            for m in range(4):
                k.dma("sp", ot[:, m * 4:(m + 1) * 4, :], self.o_scr[m, :, tsl].rearrange("(c p) t -> p c t", p=128),
                      [self.o_scr], [ot])
            for dt in range(8):
                a = acc[dt % 2]
                for m in range(4):
                    pl = self.next_pf()
                    for c in range(8):
                        k.mm(pl[:, :], wm[:, c, m * D + dt * 128:m * D + (dt + 1) * 128], ht[:, c, :], c == 0, c == 7,
                             [wm, ht], [pl])
                    pp = self.next_pf()
                    for cc in range(4):
                        k.mm(pp[:, :], wbr[:, m * 4 + cc, dt * 128:(dt + 1) * 128], ot[:, m * 4 + cc, :], cc == 0, cc == 3,
                             [wbr, ot], [pp])
                    sg = sgs[n % 2]
                    n += 1
                    k.act(sg[:], pl[:, :], AF.Sigmoid, [pl], [sg])
                    if m == 0:
                        k.tt(a[:], sg[:], pp[:, :], ALU.mult, [sg, pp], [a])
                    else:
                        t_ = tmp[n % 2]
                        k.tt(t_[:], sg[:], pp[:, :], ALU.mult, [sg, pp], [t_])
                        if m < 3:
                            k.tt(a[:], a[:], t_[:], ALU.add, [a, t_], [a], en="pool")
                        else:
                            k.tt(mt[:, dt, :], a[:], t_[:], ALU.add, [a, t_], [mt], en="pool")
            k.dma("pool", self.mT_scr[:, tsl].rearrange("(c p) t -> p c t", p=128), mt[:], [mt], [self.mT_scr])

    def phase_out(self, l):
        cfg, k = self.cfg, self.k
        T = cfg.T
        x_src = self.x_in if l == 0 else self.y1
        y_dst = self.y1 if l < NL - 1 else self.y_all
        wo = k.sb("wo", [128, 8, D], BF16)
        stg = [k.sb(f"ostg{i}", [128, 4096]) for i in range(2)]
        for hb in range(2):
            st = stg[hb]
            k.dma("sp", st[:].rearrange("p (c n) -> p c n", c=8),
                  self.w_out[l][:, hb * 512:(hb + 1) * 512].rearrange("(c p) n -> p c n", p=128), (), [st])
            k.cp(wo[:, :, hb * 512:(hb + 1) * 512], st[:].rearrange("p (c n) -> p c n", c=8), [st], [wo], en="pool")
        mts = [k.sb(f"omt{i}", [128, 8, 128], BF16) for i in range(3)]
        xts = [k.sb(f"oxt{i}", [128, D]) for i in range(3)]
        yts = [k.sb(f"oyt{i}", [128, D]) for i in range(3)]
        for st_ in range(T // 128):
            g = 0 if st_ * 128 < cfg.TP else 1
            tsl = slice(st_ * 128, (st_ + 1) * 128)
            mt, xt, yt = mts[st_ % 3], xts[st_ % 3], yts[st_ % 3]
            k.dma("sp", mt[:], self.mT_scr[:, tsl].rearrange("(c p) t -> p c t", p=128), [self.mT_scr], [mt])
            k.dma("sp", xt[:], x_src[tsl, :], [x_src], [xt])
            for hb in range(2):
                hs = slice(hb * 512, (hb + 1) * 512)
                ps = self.next_pf()
                for c in range(8):
                    k.mm(ps[:, :], mt[:, c, :], wo[:, c, hs], c == 0, c == 7, [mt, wo], [ps])
                k.tt(yt[:, hs], ps[:, :], self.gate_bc[g][:, hs], ALU.mult, [ps, self.gate_bc[g]], [yt])
                k.tt(yt[:, hs], yt[:, hs], xt[:, hs], ALU.add, [yt, xt], [yt], en="pool")
            k.dma("pool", y_dst[tsl, :], yt[:], [yt], [y_dst])


    def rstd_from_sum(self, out, ps, n, reads):
        k = self.k
        tl, pt = reads
        k.ts(out, ps, 1.0 / n, 1e-6, ALU.mult, ALU.add, [pt], [tl])
        k.act(out, out, AF.Sqrt, [tl], [tl])
        k.op("dve", lambda e: e.reciprocal(out=out, in_=out), [tl], [tl])

    def phase_mla(self, l):
        cfg, k = self.cfg, self.k
        T, TP, ls = cfg.T, cfg.TP, cfg.ls
        TK = T + 256
        kidx = lambda t: t if t < TP else t + 256
        V = [self.vecs]
        qup = k.sb("qup", [128, 3, 768], BF16)
        kvk = k.sb("kvk", [128, 2, 512], BF16)
        kvv = k.sb("kvv", [128, 2, 512], BF16)
        pm96 = k.sb("pm96", [96, 96])
        cs = k.sb("ropecs", [96, 2, ls], BF16)
        ckv_all = k.sb("ckv_all", [128, 2, TK], BF16)
        krot = k.sb("krot", [96, TK], BF16)
        ssr = k.sb("ssr", [128, TK // 128])
        qn = k.sb("qn", [128, 3, T], BF16)
        vall = k.sb("vall", [128, TK // 128, 8, 65], BF16)
        with k.phase():
            kr_all = k.sb("kr_all", [96, TK])
            with k.phase():
                stg = k.sb("mlastg", [128, 4096])
                k.dma("sp", stg[:, :3 * 768].rearrange("p (c n) -> p c n", c=3),
                      self.q_up[l].rearrange("(c p) n -> p c n", p=128), (), [stg])
                k.cp(qup[:], stg[:, :3 * 768].rearrange("p (c n) -> p c n", c=3), [stg], [qup])
                k.dma("sp", stg[:, :1024].rearrange("p (c n) -> p c n", c=2),
                      self.kv_up_k[l].rearrange("(c p) n -> p c n", p=128), (), [stg])
                k.cp(kvk[:], stg[:, :1024].rearrange("p (c n) -> p c n", c=2), [stg], [kvk])
                k.dma("sp", stg[:, :1024].rearrange("p (c n) -> p c n", c=2),
                      self.kv_up_v[l].rearrange("(c p) n -> p c n", p=128), (), [stg])
                k.cp(kvv[:], stg[:, :1024].rearrange("p (c n) -> p c n", c=2), [stg], [kvv])
                k.dma("sp", pm96[64:96, :], self.pm96_d[:, :], (), [pm96])
                for j in range(2):
                    k.dma("sp", stg[64:96, :ls], self.rope_cs[j], (), [stg])
                    k.cp(cs[64:96, j, :], stg[64:96, :ls], [stg], [cs])
            k.memset(vall[:, :, :, 64:65], 1.0, [vall])
            if "mla_s0" in cfg.debug:
                return
            ctm = k.sb("ctm", [128, 2, 256])
            krtm = k.sb("krtm", [128, 2, 32])
            k.dma("sp", ctm[:], self.cache_ckv[l].rearrange("(a p) f -> p a f", p=128), (), [ctm])
            k.dma("sp", krtm[:], self.cache_kr[l].rearrange("(a p) f -> p a f", p=128), (), [krtm])
            for a in range(2):
                for c in range(2):
                    ps = self.next_pf()
                    k.tr(ps[:, 0:128], ctm[:, a, c * 128:(c + 1) * 128], self.ident_f[:], [ctm, self.ident_f], [ps])
                    k.cp(ckv_all[:, c, TP + a * 128:TP + (a + 1) * 128], ps[:, 0:128], [ps], [ckv_all])
                ps = self.next_pf()
                kpad = k.sb(f"kpad{a}", [128, 96])
                k.memset(kpad[:], 0.0, [kpad])
                k.cp(kpad[:, 64:96], krtm[:, a, :], [krtm, kpad], [kpad])
                k.tr(ps[0:96, 0:128], kpad[:, :], self.ident_f[:], [kpad, self.ident_f], [ps])
                k.cp(kr_all[64:96, TP + a * 128:TP + (a + 1) * 128], ps[64:96, 0:128], [ps], [kr_all])
            if "mla_s1" in cfg.debug:
                return
            zo_c, _ = ZROWS["mla_ckv"]
            zo_q, _ = ZROWS["mla_q"]
            zo_m, _ = ZROWS["misc"]
            xs = [k.sb(f"mx{i}", [128, 3, 512]) for i in range(2)]
            sq = [k.sb(f"msq{i}", [128, 3, 512]) for i in range(2)]
            rs = [k.sb(f"mrs{i}", [128, 512]) for i in range(2)]
            cn = [k.sb(f"mcn{i}", [128, 2, 512]) for i in range(2)]
            for tt in range(T // 512):
                tsl = slice(tt * 512, (tt + 1) * 512)
                ksl = slice(kidx(tt * 512), kidx(tt * 512) + 512)
                x, q2, r_, c_ = xs[tt % 2], sq[tt % 2], rs[tt % 2], cn[tt % 2]
                for (zo, nch, gname, is_q) in ((zo_c, 2, "mla_kva_g", False), (zo_q, 3, "mla_qa_g", True)):
                    k.dma("sp", x[:, :nch, :], self.zT[zo:zo + nch * 128, tsl].rearrange("(c p) t -> p c t", p=128),
                          [self.zT], [x])
                    k.act(q2[:, :nch, :], x[:, :nch, :], AF.Square, [x], [q2])
                    ps = self.next_pf()
                    for c in range(nch):
                        k.mm(ps[:, :], self.ones_f[:, :], q2[:, c, :], c == 0, c == nch - 1, [self.ones_f, q2], [ps])
                    self.rstd_from_sum(r_[:], ps[:, :], nch * 128, (r_, ps))
                    for c in range(nch):
                        if is_q:
                            k.stt(qn[:, c, tsl], x[:, c, :], self.vcol(gname, c), r_[:], ALU.mult, ALU.mult,
                                  [x, r_] + V, [qn])
                        else:
                            k.stt(c_[:, c, :], x[:, c, :], self.vcol(gname, c), r_[:], ALU.mult, ALU.mult,
                                  [x, r_] + V, [c_])
                    if not is_q:
                        k.cp(ckv_all[:, :, ksl], c_[:, :, :], [c_], [ckv_all], en="pool")
                        if tt * 512 < TP:
                            k.dma("pool", self.o_ckv[l][:, tsl].rearrange("(c p) t -> p c t", p=128), c_[:, :, :],
                                  [c_], [self.o_ckv])
            if "mla_s2" in cfg.debug:
                return
            k.dma("sp", kr_all[64:96, 0:TP], self.zT[zo_m:zo_m + 32, 0:TP], [self.zT], [kr_all])
            k.dma("sp", kr_all[64:96, TP + 256:TK], self.zT[zo_m:zo_m + 32, TP:T], [self.zT], [kr_all])
            k.dma("pool", self.o_kr[l], kr_all[64:96, 0:TP], [kr_all], [self.o_kr])
            krs = k.sb("krs", [96, 512])
            krg = k.sb("krg", [96, 512])
            t1 = k.sb("krt1", [96, 512])
            pss = self.pf[5]
            lat0 = TP + 256
            segs = [(ks, min(512, lat0 - ks)) for ks in range(0, lat0, 512)] + \
                   [(ks, min(512, TK - ks)) for ks in range(lat0, TK, 512)]
            for ks, w in segs:
                k.act(krs[64:96, :w], kr_all[64:96, ks:ks + w], AF.Square, [kr_all], [krs])
                for j in range(w // 128):
                    kt = ks // 128 + j
                    k.mm(pss[:, kt:kt + 1], krs[64:96, j * 128:(j + 1) * 128], self.ones_f[64:96, 0:1], True, True,
                         [krs, self.ones_f], [pss])
                k.ts(krg[64:96, :w], kr_all[64:96, ks:ks + w], self.vecs[64:96, self.vl.cols["mla_kn_g"][0]:self.vl.cols["mla_kn_g"][0] + 1],
                     None, ALU.mult, None, [kr_all] + V, [krg])
                if ks >= lat0:
                    pr = self.next_pf()
                    k.mm(pr[0:96, :w], pm96[64:96, :], krg[64:96, :w], True, True, [pm96, krg], [pr])
                    po = ks - lat0
                    k.tt(t1[64:96, :w], pr[64:96, :w], cs[64:96, 1, po:po + w], ALU.mult, [pr, cs], [t1])
                    k.tt(krg[64:96, :w], krg[64:96, :w], cs[64:96, 0, po:po + w], ALU.mult, [krg, cs], [krg])
                    k.tt(krot[64:96, ks:ks + w], krg[64:96, :w], t1[64:96, :w], ALU.add, [krg, t1], [krot])
                else:
                    k.cp(krot[64:96, ks:ks + w], krg[64:96, :w], [krg], [krot])
            k.cp(ssr[:], pss[:, 0:TK // 128], [pss], [ssr])
            if "mla_s3" in cfg.debug:
                return
            for kt in range(TK // 128):
                ps = self.next_pf()
                for c in range(2):
                    k.mm(ps[:, :], ckv_all[:, c, kt * 128:(kt + 1) * 128], kvv[:, c, :], c == 0, c == 1,
                         [ckv_all, kvv], [ps])
                k.cp(vall[:, kt, :, 0:64], ps[:, :].rearrange("p (h d) -> p h d", h=8), [ps], [vall],
                     en=("act" if kt % 2 else "dve"))
        if "mla_s4" in cfg.debug:
            return
        zo_g, _ = ZROWS["mla_gate"]
        gcol = self.vl.cols["mla_qn_g"][0]
        kcol = self.vl.cols["mla_kn_g"][0]
        kth = k.sb("kth", [96, TK], BF16)
        qth = k.sb("qth", [96, T], BF16)
        rk = k.sb("rk", [128, TK // 128])
        sqt = [k.sb(f"hsq{i}", [96, 512]) for i in range(2)]
        qf = [k.sb(f"hqf{i}", [96, 512]) for i in range(2)]
        rq = [k.sb(f"hrq{i}", [96, 512]) for i in range(2)]
        t1 = k.sb("ht1", [96, 512])
        t2 = k.sb("ht2", [96, 512])
        pts = [k.sb(f"hpt{i}", [128, 512], BF16) for i in range(4)]
        oa = [k.sb(f"hoa{i}", [65, 512]) for i in range(2)]
        gts = [k.sb(f"hgt{i}", [64, 512]) for i in range(2)]
        obs = [k.sb(f"hob{i}", [64, 512], BF16) for i in range(2)]
        npt = 0
        npo = 0
        for h in range(8):
            prk = self.pf[4]
            for ks in range(0, TK, 512):
                w = min(512, TK - ks)
                ps = self.next_pf()
                for c in range(2):
                    k.mm(ps[0:64, :w], kvk[:, c, h * 64:(h + 1) * 64], ckv_all[:, c, ks:ks + w], c == 0, c == 1,
                         [kvk, ckv_all], [ps])
                s_ = sqt[(ks // 512) % 2]
                if "k_noact" not in cfg.debug:
                    k.act(s_[0:64, :w], ps[0:64, :w], AF.Square, [ps], [s_])
                for j in range(w // 128):
                    kt = ks // 128 + j
                    if "mla_k1" in cfg.debug:
                        continue
                    k.mm(prk[:, kt:kt + 1], s_[0:64, j * 128:(j + 1) * 128], self.ones_f[0:64, 0:1], True, True,
                         [s_, self.ones_f], [prk])
                if "k_nots" not in cfg.debug:
                    k.ts(kth[0:64, ks:ks + w], ps[0:64, :w], self.vecs[0:64, kcol:kcol + 1], None, ALU.mult, None,
                         [ps] + V, [kth])
            k.cp(kth[64:96, :], krot[64:96, :], [krot], [kth], en=("dve" if "mla_k2" in cfg.debug else "pool"))
            if "k_nork" in cfg.debug:
                return
            k.tt(rk[:], prk[:, 0:TK // 128], ssr[:], ALU.add, [prk, ssr], [rk])
            k.ts(rk[:], rk[:], 1.0, 96e-6, ALU.mult, ALU.add, [rk], [rk])
            k.act(rk[:], rk[:], AF.Sqrt, [rk], [rk])
            k.op("dve", lambda e: e.reciprocal(out=rk[:], in_=rk[:]), [rk], [rk])
            if "mla_s5" in cfg.debug:
                return
            for tt in range(T // 512):
                tsl = slice(tt * 512, (tt + 1) * 512)
                ps = self.next_pf()
                for c in range(3):
                    k.mm(ps[0:96, :], qup[:, c, h * 96:(h + 1) * 96], qn[:, c, tsl], c == 0, c == 2, [qup, qn], [ps])
                s_, f_, r_ = sqt[tt % 2], qf[tt % 2], rq[tt % 2]
                k.act(s_[:, :], ps[0:96, :], AF.Square, [ps], [s_])
                p2 = self.next_pf()
                k.mm(p2[0:96, :], self.ones_f[0:96, 0:96], s_[:, :], True, True, [self.ones_f, s_], [p2])
                self.rstd_from_sum(r_[:, :], p2[0:96, :], 96, (r_, p2))
                k.stt(f_[:, :], ps[0:96, :], self.vecs[0:96, gcol:gcol + 1], r_[:, :], ALU.mult, ALU.mult,
                      [ps, r_] + V, [f_])
                k.cp(qth[0:64, tsl], f_[0:64, :], [f_], [qth], en="pool")
                if tt * 512 >= TP:
                    po = tt * 512 - TP
                    pr = self.next_pf()
                    k.mm(pr[0:96, :], pm96[64:96, :], f_[64:96, :], True, True, [pm96, f_], [pr])
                    k.tt(t1[64:96, :], pr[64:96, :], cs[64:96, 1, po:po + 512], ALU.mult, [pr, cs], [t1])
                    k.tt(t2[64:96, :], f_[64:96, :], cs[64:96, 0, po:po + 512], ALU.mult, [f_, cs], [t2])
                    k.tt(qth[64:96, tsl], t1[64:96, :], t2[64:96, :], ALU.add, [t1, t2], [qth], en="pool")
                else:
                    k.cp(qth[64:96, tsl], f_[64:96, :], [f_], [qth], en="pool")
            if "mla_s6" in cfg.debug:
                return
            if h == 0:
                self.dump("mla_kth", kth, kth[:, :], [96, TK], BF16)
                self.dump("mla_qth", qth, qth[:, :], [96, T], BF16)
                self.dump("mla_rk", rk, rk[:, :], [128, TK // 128])
            for (s0, ln, is_s) in cfg.seqs:
                k0 = TP if is_s else s0
                nk = ln + 256 if is_s else ln
                qw = min(512, ln)
                for qg in range(ln // qw):
                    q0 = s0 + qg * qw
                    po = self.pf[4 + (npo % 2)]
                    npo += 1
                    nkt = nk // 128
                    pend = []

                    def _pv(item):
                        pt_, kt_, j_ = item
                        k.mm(po[0:65, :qw], vall[:, kt_, h, :], pt_[:, :qw], j_ == 0, j_ == nkt - 1, [vall, pt_], [po])
                    for j in range(nkt):
                        kt = k0 // 128 + j
                        pS = self.next_pf()
                        k.mm(pS[:, :qw], kth[0:96, kt * 128:(kt + 1) * 128], qth[0:96, q0:q0 + qw], True, True,
                             [kth, qth], [pS])
                        pt = pts[npt % 4]
                        npt += 1
                        k.act(pt[:, :qw], pS[:, :qw], AF.Exp, [pS, rk], [pt], scale=rk[:, kt:kt + 1])
                        pend.append((pt, kt, j))
                        if len(pend) > 2:
                            _pv(pend.pop(0))
                    while pend:
                        _pv(pend.pop(0))
                    o_ = oa[qg % 2]
                    k.cp(o_[:, :qw], po[0:65, :qw], [po], [o_])
                    k.op("dve", lambda e: e.reciprocal(out=o_[64:65, :qw], in_=o_[64:65, :qw]), [o_], [o_])
                    pbc = self.next_pf()
                    k.mm(pbc[0:64, :qw], self.ones_f[64:65, 0:64], o_[64:65, :qw], True, True, [self.ones_f, o_], [pbc])
                    g_ = gts[qg % 2]
                    k.dma("sp", g_[:, :qw], self.zT[zo_g + h * 64:zo_g + (h + 1) * 64, q0:q0 + qw], [self.zT], [g_])
                    k.act(g_[:, :qw], g_[:, :qw], AF.Silu, [g_], [g_])
                    k.tt(o_[0:64, :qw], o_[0:64, :qw], pbc[0:64, :qw], ALU.mult, [o_, pbc], [o_])
                    ob = obs[qg % 2]
                    k.tt(ob[:, :qw], o_[0:64, :qw], g_[:, :qw], ALU.mult, [o_, g_], [ob])
                    k.dma("pool", self.o_scr[1, h * 64:(h + 1) * 64, q0:q0 + qw], ob[:, :qw], [ob], [self.o_scr])


    def phase_ssd(self, l):
        cfg, k = self.cfg, self.k
        T, TP = cfg.T, cfg.TP
        NCH = T // 128
        V = [self.vecs]
        zo_x, _ = ZROWS["ssd_xbc"]
        zo_m, _ = ZROWS["misc"]
        x_tm = k.sb("sx_tm", [128, NCH, 512], BF16)
        b_tm = k.sb("sb_tm", [128, NCH, 128], BF16)
        bT = k.sb("s_bT", [128, T], BF16)
        cT = k.sb("s_cT", [128, T], BF16)
        selm = k.sb("s_selm", [64, 16, 128])
        seld = k.sb("s_seld", [64, 2, 8])
        maskneg = k.sb("s_mask", [128, 2, 128])
        bcv = k.sb("s_bcv", [128, 8 + 512])
        k.dma("sp", selm[:], self.selm_d[:, :, :], (), [selm])
        k.dma("sp", seld[:], self.seld_d[:, :, :], (), [seld])
        k.dma("sp", maskneg[:], self.maskneg_d[:, :, :], (), [maskneg])
        k.dma("sp", bcv[:], self.ssd_bc[l], (), [bcv])
        with k.phase():
            xl = [k.sb(f"s_xl{i}", [128, T]) for i in range(2)]
            xc = [k.sb(f"s_xc{i}", [128, T]) for i in range(2)]
            for ft in range(6):
                a, c_ = xl[ft % 2], xc[ft % 2]
                k.dma("sp", a[:], self.zT[zo_x + ft * 128:zo_x + (ft + 1) * 128, :], [self.zT], [a])
                self.dwconv(c_, a, lambda kk: self.vcol("ssd_cw", kk * 6 + ft), self.vcol("ssd_cb", ft), V)
                k.act(c_[:], c_[:], AF.Silu, [c_], [c_])
                if ft == 4:
                    k.cp(bT[:], c_[:], [c_], [bT], en="pool")
                if ft == 5:
                    k.cp(cT[:], c_[:], [c_], [cT], en="pool")
                if ft <= 4:
                    for ch in range(NCH):
                        ps = self.next_pf()
                        k.tr(ps[:, 0:128], c_[:, ch * 128:(ch + 1) * 128], self.ident_f[:], [c_, self.ident_f], [ps])
                        dst = x_tm[:, ch, ft * 128:(ft + 1) * 128] if ft < 4 else b_tm[:, ch, :]
                        k.cp(dst, ps[:, 0:128], [ps], [x_tm if ft < 4 else b_tm], en=("act" if ch % 2 else "dve"))
        cs = k.sb("s_cs", [64, T])
        sc3t = k.sb("s_sc3", [64, 3, T])
        nega = k.sb("s_nega", [64, 1])
        _cm_stack = contextlib.ExitStack()
        k.barrier()
        k.stacks.append(_cm_stack)
        cmask = k.sb("s_cmask", [64, T + 1])

        class _V:
            def __init__(self, tl, j):
                self.tl, self.j, self.res, self.psum = tl, j, tl.res, False

            def __getitem__(self, key):
                if not isinstance(key, tuple):
                    key = (key,)
                return self.tl[(key[0], self.j) + tuple(key[1:])]
        sc3 = sc3t
        dtt = sc3t
        D0 = lambda *a: sc3t[(a[0], 0) + tuple(a[1:])] if a else sc3t[:, 0, :]
        k.memset(sc3t[:, 0, :], 0.0, [sc3t])
        k.dma("sp", sc3t[0:8, 0, :], self.zT[zo_m + 32:zo_m + 40, :], [self.zT], [sc3t])
        k.dma("sp", sc3t[32:40, 0, :], self.zT[zo_m + 40:zo_m + 48, :], [self.zT], [sc3t])
        k.dma("sp", cmask[:], self.cmask_d[:, :], (), [cmask])
        k.act(sc3t[:, 0, :], sc3t[:, 0, :], AF.Exp, [sc3t] + V, [sc3t], bias=self.vecs[0:64, self.vl.cols["ssd_dtb"][0]:self.vl.cols["ssd_dtb"][0] + 1])
        k.act(sc3t[:, 0, :], sc3t[:, 0, :], AF.Ln, [sc3t], [sc3t], bias=1.0)
        k.act(nega[:], self.vecs[0:64, self.vl.cols["ssd_alog"][0]:self.vl.cols["ssd_alog"][0] + 1], AF.Exp, V, [nega])
        k.ts(nega[:], nega[:], -1.0, None, ALU.mult, None, [nega], [nega])
        k.ts(sc3t[:, 2, :], sc3t[:, 0, :], nega[:, 0:1], None, ALU.mult, None, [sc3t, nega], [sc3t])
        k.scan(cs[0:32, :], cmask[0:32, 0:T], sc3t[0:32, 2, :], 0.0, [cmask, sc3t], [cs])
        k.scan(cs[32:64, :][:, ::-1], cmask[32:64, 1:T + 1][:, ::-1], sc3t[32:64, 2, :][:, ::-1], 0.0, [cmask, sc3t], [cs])
        k.barrier()
        k.stacks.pop()
        _cm_stack.close()
        c3 = lambda ap: ap.rearrange("p (c t) -> p c t", t=128)
        k.tt(c3(sc3[0:32, 1, :]), c3(cs[0:32, :])[:, :, 127:128].to_broadcast([32, NCH, 128]), c3(cs[0:32, :]),
             ALU.subtract, [cs], [sc3])
        k.tt(c3(sc3[32:64, 1, :]), c3(cs[32:64, :])[:, :, 0:1].to_broadcast([32, NCH, 128]), c3(cs[32:64, :]),
             ALU.subtract, [cs], [sc3])
        k.act(sc3[:, 1, :], sc3[:, 1, :], AF.Exp, [sc3], [sc3])
        k.tt(sc3[:, 1, :], sc3[:, 1, :], sc3[:, 0, :], ALU.mult, [sc3], [sc3])
        k.act(sc3[:, 2, :], cs[:], AF.Exp, [cs], [sc3])
        self.dump("ssd_cs", cs, cs[:, :], [64, T])
        self.dump("ssd_sc3", sc3t, sc3t[:, :, :], [64, 3, T])
        self.dump("ssd_xtm", x_tm, x_tm[:, :, :], [128, NCH, 512], BF16)
        self.dump("ssd_bT", bT, bT[:, :], [128, T], BF16)
        Hf = k.sb("s_H", [128, 4, 64])
        Hb = k.sb("s_Hb", [128, 4, 64], BF16)
        selg = k.sb("s_selg", [64, 128])
        k.dma("sp", selg[:], self.selg_d[:, :], (), [selg])
        sctm = [k.sb(f"s_sctm{i}", [128, 3, 64]) for i in range(2)]
        cstm = [k.sb(f"s_cstm{i}", [128, 64]) for i in range(2)]
        xd = [k.sb(f"s_xd{i}", [128, 8, 64], BF16) for i in range(2)]
        xdd = [k.sb(f"s_xdd{i}", [128, 8, 64], BF16) for i in range(2)]
        lt = [k.sb(f"s_lt{i}", [128, 4, 128]) for i in range(2)]
        mT = [k.sb(f"s_mT{i}", [128, 4, 128], BF16) for i in range(2)]
        yo = [k.sb(f"s_yo{i}", [128, 512]) for i in range(2)]
        yf = [k.sb("s_yf0", [128, 512])] * 2
        zg = [k.sb("s_zg0", [128, 512])] * 2
        et = k.sb("s_et", [128, 4])
        etr = k.sb("s_etr", [64, 4])
        h0t = k.sb("s_h0t", [64, 8, 64])
        fint = k.sb("s_fint", [64, 8, 64])
        junk = yf[0]
        ssq = k.sb("s_ssq", [128, 1])
        ob = [k.sb(f"s_ob{i}", [128, 4, 128], BF16) for i in range(2)]
        it = 0
        for (s0, ln, is_s) in cfg.seqs:
            si = s0 // ln
            chs = list(range(s0 // 128, (s0 + ln) // 128))
            for d in range(2):
                if is_s:
                    k.dma("sp", h0t[:], self.ssd_h0[l, d], (), [h0t])
                    for h in range(8):
                        ps = self.next_pf()
                        if h < 4:
                            k.tr(ps[0:64, 0:64], h0t[:, h, :], self.ident_f[0:64, 0:64], [h0t, self.ident_f], [ps])
                            k.cp(Hf[0:64, h, :], ps[0:64, 0:64], [ps], [Hf])
                        else:
                            k.tr(ps[:, 0:64], h0t[:, h - 1:h + 1, :].rearrange("p a n -> p (a n)"),
                                 self.ident_f[0:64, 0:64], [h0t, self.ident_f], [ps])
                            k.cp(Hf[64:128, h - 4, :], ps[64:128, 0:64], [ps], [Hf])
                else:
                    k.memset(Hf[:], 0.0, [Hf], en="dve")
                k.cp(Hb[:], Hf[:], [Hf], [Hb])
                for ch in (chs if d == 0 else chs[::-1]):
                    it += 1
                    tsl = slice(ch * 128, (ch + 1) * 128)
                    st_, ct_, xd_, xdd_ = sctm[it % 2], cstm[it % 2], xd[it % 2], xdd[it % 2]
                    pt = self.next_pf()
                    for j in range(3):
                        k.tr(pt[:, j * 64:(j + 1) * 64], sc3[:, j, tsl], self.ident_f[0:64, 0:64], [sc3, self.ident_f], [pt])
                    k.tr(pt[:, 192:256], cs[:, tsl], self.ident_f[0:64, 0:64], [cs, self.ident_f], [pt])
                    k.cp(st_[:], pt[:, 0:192].rearrange("p (j r) -> p j r", j=3), [pt], [st_])
                    k.cp(ct_[:], pt[:, 192:256], [pt], [ct_])
                    r0 = d * 32
                    xv = x_tm[:, ch, :].rearrange("p (h q) -> p h q", h=8)
                    k.tt(xd_[:], xv, st_[:, 0, r0:r0 + 8].unsqueeze(2).to_broadcast([128, 8, 64]), ALU.mult,
                         [x_tm, st_], [xd_])
                    k.tt(xdd_[:], xv, st_[:, 1, r0:r0 + 8].unsqueeze(2).to_broadcast([128, 8, 64]), ALU.mult,
                         [x_tm, st_], [xdd_], en="pool")
                    py = self.pf[4]
                    pyo = self.pf[5]
                    for g in range(2):
                        lt_, m_ = lt[g], mT[g]
                        pb_ = self.next_pf()
                        for hh in range(4):
                            h = g * 4 + hh
                            k.mm(pb_[:, hh * 128:(hh + 1) * 128], selm[:, d * 8 + h, :], cs[:, tsl], True, True,
                                 [selm, cs], [pb_])
                        k.tt(lt_[:], pb_[:, :].rearrange("p (h t) -> p h t", h=4),
                             ct_[:, r0 + g * 4:r0 + g * 4 + 4].unsqueeze(2).to_broadcast([128, 4, 128]), ALU.subtract,
                             [pb_, ct_], [lt_])
                        k.tt(lt_[:], lt_[:], maskneg[:, d:d + 1, :].to_broadcast([128, 4, 128]), ALU.add,
                             [lt_, maskneg], [lt_])
                        k.act(lt_[:], lt_[:], AF.Exp, [lt_], [lt_])
                        pg = self.next_pf()
                        k.mm(pg[:, 0:128], bT[g * 64:(g + 1) * 64, tsl], cT[g * 64:(g + 1) * 64, tsl], True, True,
                             [bT, cT], [pg])
                        k.tt(m_[:], lt_[:], pg[:, 0:128].unsqueeze(1).to_broadcast([128, 4, 128]), ALU.mult,
                             [lt_, pg], [m_])
                        for hh in range(4):
                            h = g * 4 + hh
                            k.mm(py[:, h * 64:(h + 1) * 64], m_[:, hh, :], xd_[:, h, :], True, True, [m_, xd_], [py])
                        k.mm(pyo[:, g * 256:(g + 1) * 256], cT[g * 64:(g + 1) * 64, tsl],
                             Hb[g * 64:(g + 1) * 64, :, :], True, True, [cT, Hb], [pyo])
                    y_ = yo[it % 2]
                    k.tt(y_[:].rearrange("p (h q) -> p h q", h=8), pyo[:, :].rearrange("p (h q) -> p h q", h=8),
                         st_[:, 2, r0:r0 + 8].unsqueeze(2).to_broadcast([128, 8, 64]), ALU.mult, [pyo, st_], [y_])
                    k.tt(y_[:], y_[:], py[:, :], ALU.add, [y_, py], [y_])
                    ps_ = self.next_pf()
                    for g in range(2):
                        k.mm(ps_[g * 64:(g + 1) * 64, 0:256], b_tm[:, ch, g * 64:(g + 1) * 64],
                             xdd_[:, g * 4:(g + 1) * 4, :], True, True, [b_tm, xdd_], [ps_])
                    e_tok = ch * 128 + (127 if d == 0 else 0)
                    k.ts(etr[:], seld[:, d, 0:4], cs[:, e_tok:e_tok + 1], None, ALU.mult, None, [seld, cs], [etr])
                    pe_ = self.next_pf()
                    k.mm(pe_[:, 0:4], selg[:, :], etr[:], True, True, [selg, etr], [pe_])
                    k.act(et[:], pe_[:, 0:4], AF.Exp, [pe_], [et])
                    k.tt(Hf[:], Hf[:], et[:].unsqueeze(2).to_broadcast([128, 4, 64]), ALU.mult, [Hf, et], [Hf])
                    k.tt(Hf[:], Hf[:], ps_[:, 0:256].rearrange("p (h q) -> p h q", h=4), ALU.add, [Hf, ps_], [Hf])
                    k.cp(Hb[:], Hf[:], [Hf], [Hb], en="pool")
                    if it == 1:
                        self.dump("ssd_lt", lt[1], lt[1][:, :, :], [128, 4, 128])
                        self.dump("ssd_mT", mT[1], mT[1][:, :, :], [128, 4, 128], BF16)
                        self.dump("ssd_y0", y_, y_[:, :], [128, 512])
                        self.dump("ssd_H1", Hf, Hf[:, :, :], [128, 4, 64])
                        self.dump("ssd_sctm", st_, st_[:, :, :], [128, 3, 64])
                    if d == 0:
                        k.dma("pool", self.ssd_y[tsl, :], y_[:], [y_], [self.ssd_y])
                    else:
                        f_, z_ = yf[it % 2], zg[it % 2]
                        k.dma("sp", f_[:], self.ssd_y[tsl, :], [self.ssd_y], [f_])
                        k.dma("sp", z_[:], self.zg_tm[tsl, :], [self.zg_tm], [z_])
                        k.tt(y_[:], y_[:], f_[:], ALU.add, [y_, f_], [y_])
                        k.tt(f_[:].rearrange("p (h q) -> p h q", h=8), xv,
                             bcv[:, 0:8].unsqueeze(2).to_broadcast([128, 8, 64]), ALU.mult, [x_tm, bcv], [f_])
                        k.tt(y_[:], y_[:], f_[:], ALU.add, [y_, f_], [y_])
                        k.act(z_[:], z_[:], AF.Silu, [z_], [z_])
                        k.tt(y_[:], y_[:], z_[:], ALU.mult, [y_, z_], [y_])
                        k.act(junk[:], y_[:], AF.Square, [y_], [junk, ssq], accum_out=ssq[:])
                        self.rstd_from_sum(ssq[:], ssq[:], 512, (ssq, ssq))
                        k.stt(y_[:], y_[:], ssq[:, 0:1], bcv[:, 8:8 + 512], ALU.mult, ALU.mult, [y_, ssq, bcv], [y_])
                        o_ = ob[it % 2]
                        for ft in range(4):
                            pt2 = self.next_pf()
                            k.tr(pt2[:, 0:128], y_[:, ft * 128:(ft + 1) * 128], self.ident_f[:], [y_, self.ident_f], [pt2])
                            k.cp(o_[:, ft, :], pt2[:, 0:128], [pt2], [o_], en=("act" if ft % 2 else "dve"))
                        k.dma("pool", self.o_scr[2, :, tsl].rearrange("(c p) t -> p c t", p=128), o_[:], [o_], [self.o_scr])
                if not is_s:
                    for h in range(8):
                        ps = self.next_pf()
                        g_ = h // 4
                        k.tr(ps[0:64, 0:64], Hf[g_ * 64:(g_ + 1) * 64, h % 4, :],
                             self.ident_f[g_ * 64:(g_ + 1) * 64, g_ * 64:(g_ + 1) * 64], [Hf, self.ident_f], [ps])
                        k.cp(fint[:, h, :], ps[0:64, 0:64], [ps], [fint])
                    k.dma("pool", self.o_ssd_fin[l, si, d], fint[:], [fint], [self.o_ssd_fin])


    def phase_rwkv(self, l):
        cfg, k = self.cfg, self.k
        T, TP = cfg.T, cfg.TP
        TS = 512
        V = [self.vecs]
        NC64 = T // 64
        zr = {n: ZROWS[n][0] for n in ("rw_r", "rw_k", "rw_v", "rw_lora", "rw_gate")}
        pc_all = k.sb("rw_pc", [128, 4, 2, NC64])
        blk = k.sb("rw_blk", [128, 128])
        k.dma("sp", blk[:], self.blk64_d[:, :], (), [blk])
        vc = lambda name, j: self.vcol(name, j)
        with k.phase():
            lw = k.sb("rw_lw", [128, 2, 512])
            cm = k.sb("rw_cm", [128, TS + 1])
            omk = k.sb("rw_omk", [128, 4])
            k.dma("sp", lw[:], self.rw_lw[l], (), [lw])
            k.dma("sp", cm[:], self.cmask64_d[:, :], (), [cm])
            k.ts(omk[:], self.vcol("rw_ka", 0, 4), -1.0, 1.0, ALU.mult, ALU.add, V, [omk])
            xe = k.sb("rw_xe", [128, TS + 2])
            sh = k.sb("rw_sh", [128, TS])
            lora = k.sb("rw_lora", [128, TS])
            names = ("r", "k", "v", "kk", "a", "kd", "be", "lw_", "cl", "e1", "e2", "t1", "t2", "bon")
            tl = {n: k.sb("rw_" + n, [128, TS]) for n in names}

            def mix(dst, zrow, mucol, a, is_s, s0, ln):
                b_ = a + TS
                k.dma("sp", xe[:, 1:TS + 1], self.zT[zrow:zrow + 128, a:b_], [self.zT], [xe])
                if is_s:
                    if a > s0:
                        k.dma("sp", xe[:, 0:1], self.zT[zrow:zrow + 128, a - 1:a], [self.zT], [xe], allow_slow_non_contiguous=True)
                    else:
                        k.memset(xe[:, 0:1], 0.0, [xe])
                    if b_ < s0 + ln:
                        k.dma("sp", xe[:, TS + 1:TS + 2], self.zT[zrow:zrow + 128, b_:b_ + 1], [self.zT], [xe], allow_slow_non_contiguous=True)
                    else:
                        k.memset(xe[:, TS + 1:TS + 2], 0.0, [xe])
                    k.tt(sh[:], xe[:, 0:TS], xe[:, 2:TS + 2], ALU.add, [xe], [sh])
                else:
                    ns = TS // ln
                    k.memset(sh[:], 0.0, [sh], en="dve")
                    v3 = lambda ap: ap.rearrange("p (s t) -> p s t", s=ns)
                    xv = v3(xe[:, 1:TS + 1])
                    k.tt(v3(sh[:])[:, :, 1:ln], v3(sh[:])[:, :, 1:ln], xv[:, :, 0:ln - 1], ALU.add, [sh, xe], [sh])
                    k.tt(v3(sh[:])[:, :, 0:ln - 1], v3(sh[:])[:, :, 0:ln - 1], xv[:, :, 1:ln], ALU.add, [sh, xe], [sh])
                k.stt(sh[:], sh[:], 0.5, xe[:, 1:TS + 1], ALU.mult, ALU.subtract, [sh, xe], [sh])
                k.stt(dst, sh[:], mucol, xe[:, 1:TS + 1], ALU.mult, ALU.add, [sh, xe] + V, [dst_tl[0]])

            dst_tl = [None]
            for a in range(0, T, TS):
                is_s = a >= TP
                s0, ln = (TP, cfg.ls) if is_s else (0, cfg.lp)
                c64 = a // 64
                dst_tl[0] = lora
                mix(lora[:], zr["rw_lora"], vc("rw_mu", 12), a, is_s, s0, ln)
                k.act(lora[0:64, :], lora[0:64, :], AF.Tanh, [lora], [lora])
                for ft in range(4):
                    R, K_, Vv, KK = tl["r"], tl["k"], tl["v"], tl["kk"]
                    for (dst, nm, mi) in ((R, "rw_r", 0), (K_, "rw_k", 4), (Vv, "rw_v", 8)):
                        dst_tl[0] = dst
                        mix(dst[:], zr[nm] + ft * 128, vc("rw_mu", mi + ft), a, is_s, s0, ln)
                    k.dma("pool", self.rw_v[ft * 128:(ft + 1) * 128, a:a + TS], Vv[:], [Vv], [self.rw_v])
                    t1, t2 = tl["t1"], tl["t2"]
                    k.ts(KK[:], K_[:], vc("rw_kk", ft), None, ALU.mult, None, [K_] + V, [KK])
                    k.tt(t1[:], KK[:], KK[:], ALU.mult, [KK], [t1])
                    ps = self.next_pf()
                    k.mm(ps[:, :], blk[:, :], t1[:], True, True, [blk, t1], [ps])
                    k.ts(t2[:], ps[:, :], 1e-12, None, ALU.add, None, [ps], [t2])
                    k.act(t2[:], t2[:], AF.Sqrt, [t2], [t2])
                    k.op("dve", lambda e: e.reciprocal(out=t2[:], in_=t2[:]), [t2], [t2])
                    k.tt(KK[:], KK[:], t2[:], ALU.mult, [KK, t2], [KK])
                    for d in range(2):
                        A_, KD, BE, LW, CL, E1, E2, BON = (tl[n] for n in ("a", "kd", "be", "lw_", "cl", "e1", "e2", "bon"))
                        ps = self.next_pf()
                        k.mm(ps[:, :], lw[0:64, d, ft * 128:(ft + 1) * 128], lora[0:64, :], True, True, [lw, lora], [ps])
                        k.act(LW[:], ps[:, :], AF.Sigmoid, [ps] + V, [LW], bias=vc("rw_w0", d * 4 + ft))
                        k.ts(LW[:], LW[:], -0.6065306597126334, None, ALU.mult, None, [LW], [LW])
                        ps2 = self.next_pf()
                        k.mm(ps2[:, :], lw[64:128, d, ft * 128:(ft + 1) * 128], lora[64:128, :], True, True, [lw, lora], [ps2])
                        k.act(A_[:], ps2[:, :], AF.Sigmoid, [ps2] + V, [A_], bias=vc("rw_a0", d * 4 + ft))
                        k.ts(KD[:], A_[:], vc("rw_ka", ft), omk[:, ft:ft + 1], ALU.mult, ALU.add, [A_, omk] + V, [KD])
                        k.tt(KD[:], KD[:], K_[:], ALU.mult, [KD, K_], [KD])
                        k.tt(BE[:], KK[:], A_[:], ALU.mult, [KK, A_], [BE])
                        k.stt(t1[:], R[:], vc("rw_rk", ft), KD[:], ALU.mult, ALU.mult, [R, KD] + V, [t1])
                        ps3 = self.next_pf()
                        k.mm(ps3[:, :], blk[:, :], t1[:], True, True, [blk, t1], [ps3])
                        if d == 0:
                            k.tt(BON[:], ps3[:, :], Vv[:], ALU.mult, [ps3, Vv], [BON])
                        else:
                            k.tt(t1[:], ps3[:, :], Vv[:], ALU.mult, [ps3, Vv], [t1])
                            k.tt(BON[:], BON[:], t1[:], ALU.add, [BON, t1], [BON])
                            k.dma("pool", self.rw_bonus[ft * 128:(ft + 1) * 128, a:a + TS], BON[:], [BON], [self.rw_bonus])
                        if d == 0:
                            k.scan(CL[:], cm[:, 0:TS], LW[:], 0.0, [cm, LW], [CL])
                        else:
                            k.scan(CL[:, ::-1], cm[:, 1:TS + 1][:, ::-1], LW[:, ::-1], 0.0, [cm, LW], [CL])
                        c3 = lambda ap: ap.rearrange("p (c t) -> p c t", t=64)
                        e_ = 63 if d == 0 else 0
                        ce = c3(CL[:])[:, :, e_:e_ + 1]
                        k.act(pc_all[:, ft, d, c64:c64 + TS // 64], ce.rearrange("p c o -> p (c o)"), AF.Exp, [CL], [pc_all])
                        O = lambda j: self.rw_ops[d, j, ft * 128:(ft + 1) * 128, a:a + TS]
                        k.act(E1[:], CL[:], AF.Exp, [CL], [E1])
                        k.tt(t1[:], R[:], E1[:], ALU.mult, [R, E1], [t1])
                        k.dma("pool", O(0), t1[:], [t1], [self.rw_ops])
                        k.act(E2[:], CL[:], AF.Exp, [CL], [E2], scale=-1.0)
                        k.tt(t2[:], KD[:], E2[:], ALU.mult, [KD, E2], [t2])
                        k.dma("pool", O(1), t2[:], [t2], [self.rw_ops])
                        k.tt(t1[:], BE[:], E2[:], ALU.mult, [BE, E2], [t1])
                        k.dma("pool", O(2), t1[:], [t1], [self.rw_ops])
                        k.tt(E1[:], CL[:], LW[:], ALU.subtract, [CL, LW], [E1])
                        k.act(E1[:], E1[:], AF.Exp, [E1], [E1])
                        k.tt(t2[:], KK[:], E1[:], ALU.mult, [KK, E1], [t2])
                        k.dma("pool", O(3), t2[:], [t2], [self.rw_ops])
                        k.tt(c3(E2[:]), ce.to_broadcast([128, TS // 64, 64]), c3(CL[:]), ALU.subtract, [CL], [E2])
                        k.act(E2[:], E2[:], AF.Exp, [E2], [E2])
                        k.tt(t1[:], KD[:], E2[:], ALU.mult, [KD, E2], [t1])
                        k.dma("pool", O(4), t1[:], [t1], [self.rw_ops])
                        k.tt(t2[:], BE[:], E2[:], ALU.mult, [BE, E2], [t2])
                        k.dma("pool", O(5), t2[:], [t2], [self.rw_ops])
        if "rw_prep_only" in cfg.debug:
            return
        with k.phase():
            msk = k.sb("rw_msk", [128, 2, 3, 128])
            k.dma("sp", msk[:], self.rw_mask_d[:, :, :, :], (), [msk])
            ops = [[k.sb(f"rw_op{j}_{i}", [128, 4, 128]) for j in range(7)] for i in range(2)]
            pmask = k.sb("rw_pmask", [128, 2])
            k.dma("sp", pmask[:], self.pmask_d[:, :], (), [pmask])
            mk = {nm: [k.sb(f"rw_mk{nm}{p_}", [128, 4, 128]) for p_ in range(2)] for nm in ("A", "B", "R")}
            NDT = BF16 if RW_BF16 else F32
            if RW_BF16:
                mkb = {nm: [k.sb(f"rw_mkb{nm}{p_}", [128, 4, 128], NDT) for p_ in range(2)] for nm in ("A", "B", "R")}
                Lb = {nm: k.sb(f"rw_Lb{nm}", [128, 4, 128], NDT) for nm in ("A", "B", "K")}
                TTb = k.sb("rw_TTb", [128, 8, 128], NDT)
            tms = [k.sb(f"rw_tm{j}", [128, 512]) for j in range(3)]
            NT = [k.sb(f"rw_NT{i}", [128, 8, 128], BF16 if RW_BF16 else F32) for i in range(2)]
            NN = [k.sb(f"rw_NN{i}", [128, 8, 128], BF16 if RW_BF16 else F32) for i in range(2)]
            TT = k.sb("rw_TT", [128, 8, 128])
            AakT = k.sb("rw_AakT", [128, 8, 128])
            ODT = BF16 if RW_OBF16 else F32
            ArkT = k.sb("rw_ArkT", [128, 8, 128], ODT)
            ArbT = k.sb("rw_ArbT", [128, 8, 128], ODT)
            mkRb = [k.sb(f"rw_mkRb{p_}", [128, 4, 128], ODT) for p_ in range(2)]
            Ktb = k.sb("rw_Ktb", [128, 4, 128], ODT)
            Btb = k.sb("rw_Btb", [128, 4, 128], ODT)
            Vtb = k.sb("rw_Vtb", [128, 512], ODT)
            Utb = k.sb("rw_Utb", [128, 512], ODT)
            STb = k.sb("rw_STb", [128, 4, 64], ODT)
            W1 = k.sb("rw_W1", [128, 512])
            Wt = k.sb("rw_Wt", [128, 512])
            Ut = k.sb("rw_Ut", [128, 512])
            ST = k.sb("rw_ST", [128, 4, 64])
            h0t = k.sb("rw_h0t", [64, 8, 64])
            fint = k.sb("rw_fint", [64, 8, 64])
            osb = [k.sb(f"rw_osb{i}", [128, 4, 128]) for i in range(2)]
            it = 0
            for (s0, ln, is_s) in cfg.seqs:
                si = s0 // ln
                wins = list(range(s0 // 128, (s0 + ln) // 128))
                for d in range(2):
                    if is_s:
                        k.dma("sp", h0t[:], self.rw_h0[l, d], (), [h0t])
                        for h in range(8):
                            ps = self.next_pf()
                            if h % 2 == 0:
                                k.tr(ps[0:64, 0:64], h0t[:, h, :], self.ident_f[0:64, 0:64], [h0t, self.ident_f], [ps])
                                k.cp(ST[0:64, h // 2, :], ps[0:64, 0:64], [ps], [ST])
                            else:
                                k.tr(ps[:, 0:64], h0t[:, h - 1:h + 1, :].rearrange("p a n -> p (a n)"),
                                     self.ident_f[0:64, 0:64], [h0t, self.ident_f], [ps])
                                k.cp(ST[64:128, h // 2, :], ps[64:128, 0:64], [ps], [ST])
                    else:
                        k.memset(ST[:], 0.0, [ST], en="dve")
                    for w in (wins if d == 0 else wins[::-1]):
                        it += 1
                        w0 = w * 128
                        op = ops[it % 2]
                        for j in range(6):
                            k.dma("sp", op[j][:], self.rw_ops[d, j, :, w0:w0 + 128].rearrange("(f p) t -> p f t", p=128),
                                  [self.rw_ops], [op[j]])
                        k.dma("sp", op[6][:], self.rw_v[:, w0:w0 + 128].rearrange("(f p) t -> p f t", p=128),
                              [self.rw_v], [op[6]])
                        Rt, Kt, Bt, At, Kh, Bh, Vf = op
                        for (src, dst, neg) in ((Vf, tms[0], False), (Kh, tms[1], False), (Bh, tms[2], True)):
                            ps = self.next_pf()
                            for ft in range(4):
                                k.tr(ps[:, ft * 128:(ft + 1) * 128], src[:, ft, :], self.ident_f[:], [src, self.ident_f], [ps])
                            if neg:
                                k.ts(dst[:], ps[:, :], -1.0, None, ALU.mult, None, [ps], [dst])
                            else:
                                k.cp(dst[:], ps[:, :], [ps], [dst], en="act")
                        Vtm, Khtm, Bhtm = tms
                        hb = lambda h: (h % 2) * 64
                        for nm, src in (("A", At), ("B", Bt)):
                            for p_ in range(2):
                                k.ts(mk[nm][p_][:], src[:], pmask[:, p_:p_ + 1], None, ALU.mult, None, [src, pmask], [mk[nm][p_]],
                                     en=("pool" if p_ else "dve"))
                        for p_ in range(2):
                            k.ts(mkRb[p_][:], Rt[:], pmask[:, p_:p_ + 1], None, ALU.mult, None, [Rt, pmask], [mkRb[p_]],
                                 en=("pool" if p_ else "dve"))
                        k.cp(Ktb[:], Kt[:], [Kt], [Ktb], en="act")
                        k.cp(Btb[:], Bt[:], [Bt], [Btb], en="pool")
                        k.cp(Vtb[:], tms[0][:], [tms[0]], [Vtb], en="act")
                        if RW_BF16:
                            for nm, src in (("A", At), ("B", Bt), ("R", Rt)):
                                for p_ in range(2):
                                    k.cp(mkb[nm][p_][:], mk[nm][p_][:], [mk[nm][p_]], [mkb[nm][p_]], en=("pool" if p_ else "act"))
                            for nm, src in (("A", At), ("B", Bt), ("K", Kt)):
                                k.cp(Lb[nm][:], src[:], [src], [Lb[nm]], en=("pool" if nm == "B" else "act"))
                        else:
                            mkb = dict(A=mk["A"], B=mk["B"], R=mkRb)
                            Lb = dict(A=At, B=Bt, K=Kt, Kr=Ktb, Br=Btb)
                        specs = ((Lb["B"], "A", NT[0], 0, -1.0), (Lb["A"], "B", NN[0], 1, -1.0), (Lb["K"], "A", AakT, 0, 1.0),
                                 (Lb["Kr"], "R", ArkT, 2, 1.0), (Lb["Br"], "R", ArbT, 2, -1.0))
                        for (L_, rn, dst, mi, sg) in specs:
                            for half in range(2):
                                ps = self.next_pf()
                                for hh in range(4):
                                    h = half * 4 + hh
                                    R_ = mkb[rn][h % 2]
                                    k.mm(ps[:, hh * 128:(hh + 1) * 128], L_[:, h // 2, :], R_[:, h // 2, :],
                                         True, True, [L_, R_], [ps])
                                k.stt(dst[:, half * 4:(half + 1) * 4, :], ps[:, :].rearrange("p (h t) -> p h t", h=4), sg,
                                      msk[:, d, mi:mi + 1, :].to_broadcast([128, 4, 128]), ALU.mult, ALU.mult, [ps, msk], [dst])
                        k.tt(TT[:], NT[0][:], self.ident_f[:, :].unsqueeze(1).to_broadcast([128, 8, 128]), ALU.add,
                             [NT[0], self.ident_f], [TT])
                        if RW_BF16:
                            k.cp(TTb[:], TT[:], [TT], [TTb], en="pool")
                        else:
                            TTb = TT
                        cur = 0
                        for lev in range(1, 6):
                            nxt = 1 - cur
                            for half in range(2):
                                hs = slice(half * 4, (half + 1) * 4)
                                pn = self.next_pf()
                                for hh in range(4):
                                    h = half * 4 + hh
                                    k.mm(pn[:, hh * 128:(hh + 1) * 128], NT[cur][:, h, :], NN[cur][:, h, :], True, True,
                                         [NT[cur], NN[cur]], [pn])
                                k.cp(NN[nxt][:, hs, :], pn[:, :].rearrange("p (h t) -> p h t", h=4), [pn], [NN[nxt]], en="act")
                                if lev < 5:
                                    pt_ = self.next_pf()
                                    for hh in range(4):
                                        h = half * 4 + hh
                                        k.mm(pt_[:, hh * 128:(hh + 1) * 128], NN[cur][:, h, :], NT[cur][:, h, :], True, True,
                                             [NT[cur], NN[cur]], [pt_])
                                    k.cp(NT[nxt][:, hs, :], pt_[:, :].rearrange("p (h t) -> p h t", h=4), [pt_], [NT[nxt]], en="act")
                                pp = self.next_pf()
                                for hh in range(4):
                                    h = half * 4 + hh
                                    k.mm(pp[:, hh * 128:(hh + 1) * 128], NN[nxt][:, h, :], TTb[:, h, :], True, True,
                                         [NN[nxt], TTb], [pp])
                                k.tt(TT[:, hs, :], TT[:, hs, :], pp[:, :].rearrange("p (h t) -> p h t", h=4), ALU.add,
                                     [TT, pp], [TT])
                                if lev < 5 and RW_BF16:
                                    k.cp(TTb[:, hs, :], TT[:, hs, :], [TT], [TTb], en="pool")
                            cur = nxt
                        pw = self.next_pf()
                        for h in range(8):
                            k.mm(pw[:, h * 64:(h + 1) * 64], AakT[:, h, :], Vtm[:, h * 64:(h + 1) * 64], True, True,
                                 [AakT, Vtm], [pw])
                        k.cp(W1[:], pw[:, :], [pw], [W1])
                        po = self.pf[4 + it % 2]
                        o_ = osb[it % 2]
                        for cb in ((0, 64) if d == 0 else (64, 0)):
                            cs_ = slice(cb, cb + 64)
                            c64i = (w0 + cb) // 64
                            px = self.next_pf()
                            for h in range(8):
                                b0 = hb(h)
                                k.mm(px[cs_, h * 64:(h + 1) * 64], mk["A"][h % 2][:, h // 2, cs_], ST[:, h // 2, :],
                                     True, True, [mk["A"][h % 2], ST], [px])
                            k.tt(Wt[cs_, :], W1[cs_, :], px[cs_, :], ALU.add, [W1, px], [Wt])
                            pu = self.next_pf()
                            for h in range(8):
                                k.mm(pu[cs_, h * 64:(h + 1) * 64], TT[cs_, h, cs_], Wt[cs_, h * 64:(h + 1) * 64], True, True,
                                     [TT, Wt], [pu])
                            k.cp(Ut[cs_, :], pu[cs_, :], [pu], [Ut])
                            k.cp(STb[:], ST[:], [ST], [STb], en="pool")
                            k.cp(Utb[cs_, :], Ut[cs_, :], [Ut], [Utb], en="act")
                            pS = self.next_pf()
                            for h in range(8):
                                b0 = hb(h)
                                ft = h // 2
                                sreg = pS[b0:b0 + 64, ft * 64:(ft + 1) * 64]
                                k.mm(sreg, Khtm[cs_, h * 64:(h + 1) * 64], Vtm[cs_, h * 64:(h + 1) * 64], True, False,
                                     [Khtm, Vtm], [pS])
                                k.mm(sreg, Bhtm[cs_, h * 64:(h + 1) * 64], Ut[cs_, h * 64:(h + 1) * 64], False, True,
                                     [Bhtm, Ut], [pS])
                            for h in range(8):
                                b0 = hb(h)
                                ft = h // 2
                                oreg = po[b0:b0 + 64, ft * 128 + cb:ft * 128 + cb + 64]
                                k.mm(oreg, STb[:, ft, :], mkRb[h % 2][:, ft, cs_], True, False, [STb, mkRb[h % 2]], [po])
                                k.mm(oreg, Vtb[cs_, h * 64:(h + 1) * 64], ArkT[cs_, h, cs_], False, False, [Vtb, ArkT], [po])
                                k.mm(oreg, Utb[cs_, h * 64:(h + 1) * 64], ArbT[cs_, h, cs_], False, True, [Utb, ArbT], [po])
                            k.tt(ST[:], ST[:], pc_all[:, :, d, c64i:c64i + 1].to_broadcast([128, 4, 64]), ALU.mult,
                                 [ST, pc_all], [ST])
                            k.tt(ST[:], ST[:], pS[:, 0:256].rearrange("p (f i) -> p f i", f=4), ALU.add, [ST, pS], [ST])
                        k.cp(o_[:], po[:, :].rearrange("p (f t) -> p f t", f=4), [po], [o_], en="act")
                        k.dma("pool", self.rw_o[d, :, w0:w0 + 128].rearrange("(f p) t -> p f t", p=128), o_[:], [o_], [self.rw_o])
                    if not is_s:
                        for h in range(8):
                            b0 = (h % 2) * 64
                            ps = self.next_pf()
                            k.tr(ps[0:64, 0:64], ST[b0:b0 + 64, h // 2, :], self.ident_f[b0:b0 + 64, b0:b0 + 64],
                                 [ST, self.ident_f], [ps])
                            k.cp(fint[:, h, :], ps[0:64, 0:64], [ps], [fint])
                        k.dma("pool", self.o_rw_fin[si, l, d].rearrange("h i j -> i h j"), fint[:], [fint], [self.o_rw_fin])
        with k.phase():
            of = [k.sb(f"rw_of{i}", [128, TS]) for i in range(2)]
            ob_ = [k.sb(f"rw_obb{i}", [128, TS]) for i in range(2)]
            bo = [k.sb(f"rw_bo{i}", [128, TS]) for i in range(2)]
            gt = [k.sb(f"rw_gt{i}", [128, TS]) for i in range(2)]
            xc = [k.sb(f"rw_xc{i}", [128, TS]) for i in range(2)]
            sq = [k.sb(f"rw_sq{i}", [128, TS]) for i in range(2)]
            oo = [k.sb(f"rw_oo{i}", [128, TS], BF16) for i in range(2)]
            n = 0
            for a in range(0, T, TS):
                for ft in range(4):
                    n += 1
                    i = n % 2
                    rs_ = slice(ft * 128, (ft + 1) * 128)
                    k.dma("sp", of[i][:], self.rw_o[0, rs_, a:a + TS], [self.rw_o], [of[i]])
                    k.dma("sp", ob_[i][:], self.rw_o[1, rs_, a:a + TS], [self.rw_o], [ob_[i]])
                    k.dma("sp", bo[i][:], self.rw_bonus[rs_, a:a + TS], [self.rw_bonus], [bo[i]])
                    k.dma("sp", gt[i][:], self.zT[zr["rw_gate"] + ft * 128:zr["rw_gate"] + (ft + 1) * 128, a:a + TS],
                          [self.zT], [gt[i]])
                    k.tt(of[i][:], of[i][:], ob_[i][:], ALU.add, [of[i], ob_[i]], [of[i]])
                    pm = self.next_pf()
                    k.mm(pm[:, :], blk[:, :], of[i][:], True, True, [blk, of[i]], [pm])
                    k.stt(xc[i][:], pm[:, :], -1.0 / 64, of[i][:], ALU.mult, ALU.add, [pm, of[i]], [xc[i]])
                    k.tt(sq[i][:], xc[i][:], xc[i][:], ALU.mult, [xc[i]], [sq[i]])
                    pv = self.next_pf()
                    k.mm(pv[:, :], blk[:, :], sq[i][:], True, True, [blk, sq[i]], [pv])
                    k.ts(sq[i][:], pv[:, :], 1.0 / 64, 64e-5, ALU.mult, ALU.add, [pv], [sq[i]])
                    k.act(sq[i][:], sq[i][:], AF.Sqrt, [sq[i]], [sq[i]])
                    k.op("dve", lambda e: e.reciprocal(out=sq[i][:], in_=sq[i][:]), [sq[i]], [sq[i]])
                    k.tt(xc[i][:], xc[i][:], sq[i][:], ALU.mult, [xc[i], sq[i]], [xc[i]])
                    k.ts(xc[i][:], xc[i][:], vc("rw_ln_g", ft), vc("rw_ln_b", ft), ALU.mult, ALU.add, [xc[i]] + V, [xc[i]])
                    k.tt(xc[i][:], xc[i][:], bo[i][:], ALU.add, [xc[i], bo[i]], [xc[i]])
                    k.act(gt[i][:], gt[i][:], AF.Silu, [gt[i]], [gt[i]])
                    k.tt(oo[i][:], xc[i][:], gt[i][:], ALU.mult, [xc[i], gt[i]], [oo[i]])
                    k.dma("pool", self.o_scr[0, rs_, a:a + TS], oo[i][:], [oo[i]], [self.o_scr])


def host_weights(inp):
    w_in = np.asarray(inp["w_in"], np.float32)
    cols = []
    o = _IW
    cols.append(w_in[:, :, o["rw_pre"]:o["rw_pre"] + 1664])
    cols.append(w_in[:, :, o["rw_gate"]:o["rw_gate"] + 512])
    cols.append(w_in[:, :, o["mla_q"]:o["mla_q"] + 384])
    cols.append(w_in[:, :, o["mla_ckv"]:o["mla_ckv"] + 256])
    misc = np.zeros((NL, D, 128), np.float32)
    misc[:, :, 0:32] = w_in[:, :, o["mla_krope"]:o["mla_krope"] + 32]
    misc[:, :, 32:48] = w_in[:, :, o["ssd_dt"]:o["ssd_dt"] + 16]
    cols.append(misc)
    cols.append(w_in[:, :, o["mla_gate"]:o["mla_gate"] + 512])
    cols.append(w_in[:, :, o["ssd_xbc"]:o["ssd_xbc"] + 768])
    cols.append(w_in[:, :, o["lru_x"]:o["lru_x"] + 512])
    cols.append(w_in[:, :, o["lru_gate"]:o["lru_gate"] + 512])
    cols.append(w_in[:, :, o["ssd_gate"]:o["ssd_gate"] + 512])
    w_in_a = np.ascontiguousarray(np.concatenate(cols, -1))
    assert w_in_a.shape[-1] == NZ_ALL
    sel2 = np.zeros((2, 2, 128), np.float32)
    sel2[0, 0] = 1.0
    sel2[1, 1] = 1.0
    lw = np.zeros((NL, 2, 2, 4, 128, 128), np.float32)
    for d in range(2):
        for gi, nm in enumerate(("lru_wa", "lru_wx")):
            w = np.asarray(inp[nm], np.float32)
            for ft in range(4):
                for kb in range(2):
                    lw[:, d, gi, ft, kb * 64:(kb + 1) * 64, kb * 64:(kb + 1) * 64] = w[:, d, ft * 2 + kb]
    lru_w = np.ascontiguousarray(lw.reshape(NL, 16, 128, 128).transpose(0, 2, 1, 3))
    kvu = np.asarray(inp["mla_kv_up"], np.float32).reshape(NL, 256, 8, 128)
    inv = 10000.0 ** (-np.arange(8, dtype=np.float32) / 8)
    W_pm = np.zeros((32, 96), np.float32)
    for m_ in range(32):
        sw = m_ + 8 if (m_ % 16) < 8 else m_ - 8
        W_pm[sw, 64 + m_] = 1.0
    selm = np.zeros((64, 16, 128), np.float32)
    seld = np.zeros((64, 2, 8), np.float32)
    selg = np.zeros((64, 128), np.float32)
    for d_ in range(2):
        for h_ in range(8):
            selm[d_ * 32 + h_, d_ * 8 + h_, :] = 1.0
            seld[d_ * 32 + h_, d_, h_ % 4] = 1.0
            selg[d_ * 32 + h_, (h_ // 4) * 64:(h_ // 4 + 1) * 64] = 1.0
    ii = np.arange(128)
    maskneg = np.zeros((128, 2, 128), np.float32)
    maskneg[:, 0, :] = np.where(ii[:, None] <= ii[None, :], 0.0, -30000.0)
    maskneg[:, 1, :] = np.where(ii[:, None] >= ii[None, :], 0.0, -30000.0)
    ssd_bc = np.concatenate([np.broadcast_to(np.asarray(inp["ssd_d"], np.float32)[:, None, :], (NL, 128, 8)),
                             np.broadcast_to(np.asarray(inp["ssd_norm_g"], np.float32)[:, None, :], (NL, 128, 512))], -1)
    rw_lw = np.concatenate([np.asarray(inp["rw_w2"], np.float32), np.asarray(inp["rw_a2"], np.float32)], 2)
    rw_lw = np.ascontiguousarray(rw_lw.transpose(0, 2, 1, 3))
    same = (ii[:, None] // 64) == (ii[None, :] // 64)
    rw_mask = np.zeros((128, 2, 3, 128), np.float32)
    rw_mask[:, 0, 0, :] = same & (ii[:, None] < ii[None, :])
    rw_mask[:, 0, 1, :] = same & (ii[:, None] > ii[None, :])
    rw_mask[:, 0, 2, :] = same & (ii[:, None] <= ii[None, :])
    rw_mask[:, 1, 0, :] = same & (ii[:, None] > ii[None, :])
    rw_mask[:, 1, 1, :] = same & (ii[:, None] < ii[None, :])
    rw_mask[:, 1, 2, :] = same & (ii[:, None] >= ii[None, :])
    cm64 = np.ones((128, 513), np.float32)
    cm64[:, ::64] = 0.0
    W = dict(
        pmask=np.ascontiguousarray(np.stack([(ii < 64), (ii >= 64)], 1).astype(np.float32)),
        rw_lw=rw_lw, rw_mask=rw_mask, blk64=same.astype(np.float32), cmask64=cm64,
        selm=selm, seld=seld, selg=selg, maskneg=maskneg, ssd_bc=np.ascontiguousarray(ssd_bc),
        q_up=np.ascontiguousarray(inp["mla_q_up"], np.float32),
        kv_up_k=np.ascontiguousarray(kvu[:, :, :, :64].reshape(NL, 256, 512)),
        kv_up_v=np.ascontiguousarray(kvu[:, :, :, 64:].reshape(NL, 256, 512)),
        pm96=W_pm,
        _inv=inv,
        w_in_m=np.ascontiguousarray(w_in[:, :, o["merge"]:]),
        w_branch=np.ascontiguousarray(inp["w_branch"], np.float32),
        w_out=np.ascontiguousarray(inp["w_out"], np.float32),
        lru_w=lru_w,
        ada_w=np.ascontiguousarray(inp["ada_w"], np.float32),
        ada_bg=np.ascontiguousarray(np.asarray(inp["ada_b"], np.float32)[:, None, 2048:]),
        w_in_a=w_in_a,
        vecs=host_vecs(inp).pack(),
        ident=np.eye(128, dtype=np.float32),
        sel2=sel2,
    )
    return W


def core_inputs(inp, W, core, cfg):
    b = core // 2
    xp = np.asarray(inp["x_prompt"], np.float32)[core * cfg.n_prompt:(core + 1) * cfg.n_prompt, :cfg.lp]
    xs = np.asarray(inp["x_sample"], np.float32)[b, :cfg.ls]
    x_all = np.ascontiguousarray(np.concatenate([xp.reshape(-1, D), xs], 0))
    cond = np.stack([np.asarray(inp["c_ctx"], np.float32), np.asarray(inp["c"], np.float32)[b]], 0)
    condT = np.ascontiguousarray(cond.reshape(2, 8, 128).transpose(2, 1, 0))
    m = dict(W)
    inv = m.pop("_inv")
    t = np.arange(cfg.ls)
    row = (t // 64).astype(np.float32)
    col = (t % 64).astype(np.float32)
    ar, ac = row[None, :] * inv[:, None], col[None, :] * inv[:, None]
    cosT = np.concatenate([np.cos(ar), np.cos(ar), np.cos(ac), np.cos(ac)], 0)
    sinT = np.concatenate([-np.sin(ar), np.sin(ar), -np.sin(ac), np.sin(ac)], 0)
    cm = np.ones((64, cfg.T + 1), np.float32)
    cm[:, ::128] = 0.0
    m["cmask"] = cm
    m["ssd_h0"] = np.ascontiguousarray(np.asarray(inp["state_ssd"], np.float32)[b].transpose(0, 1, 3, 2, 4))
    m["rw_h0"] = np.ascontiguousarray(np.asarray(inp["state_rwkv"], np.float32)[b].transpose(0, 1, 3, 2, 4))
    m["rope_cs"] = np.ascontiguousarray(np.stack([cosT, sinT], 0).astype(np.float32))
    m["cache_ckv"] = np.ascontiguousarray(np.asarray(inp["cache_mla_ckv"], np.float32)[b])
    m["cache_kr"] = np.ascontiguousarray(np.asarray(inp["cache_mla_krope"], np.float32)[b])
    sl = np.asarray(inp["state_lru"], np.float32)[b]
    lru_h0 = np.ascontiguousarray(sl.reshape(NL, 2, 4, 128).transpose(0, 3, 1, 2))
    m.update(x_all=x_all, condT=condT, lru_h0=lru_h0)
    return m


_PROG = {}


def kernel(**inp):
    cfg = Cfg(n_prompt=4, lp=256, ls=4096, debug=DEBUG_FLAGS)
    if "p" not in _PROG:
        prog = Prog(cfg)
        prog.build()
        _PROG["p"] = prog
    prog = _PROG["p"]
    W = host_weights(inp)
    in_maps = []
    for core in range(8):
        m = core_inputs(inp, W, core, cfg)
        in_maps.append({k_: np.ascontiguousarray(v) for k_, v in m.items() if k_ in prog.inputs})
    res = run_bass_kernel_spmd(prog.nc, in_maps, core_ids=list(range(8)))
    R = res.results
    npq = cfg.n_prompt
    y_prompt = np.concatenate([R[c]["y_all"][:cfg.TP].reshape(npq, cfg.lp, D) for c in range(8)], 0)
    y_sample = np.stack([R[2 * b]["y_all"][cfg.TP:] for b in range(4)], 0)
    ckv = np.concatenate([R[c]["o_ckv"].reshape(NL, 256, npq, cfg.lp).transpose(2, 0, 3, 1) for c in range(8)], 0)
    kr = np.concatenate([R[c]["o_kr"].reshape(NL, 32, npq, cfg.lp).transpose(2, 0, 3, 1) for c in range(8)], 0)
    if "o_rw_fin" in R[0]:
        rw = np.concatenate([R[c]["o_rw_fin"] for c in range(8)], 0)
    else:
        rw = np.zeros((32, NL, 2, 8, 64, 64), np.float32)
    if "o_ssd_fin" in R[0]:
        ssd = np.concatenate([R[c]["o_ssd_fin"].transpose(1, 0, 2, 4, 3, 5) for c in range(8)], 0)
    else:
        ssd = np.zeros((32, NL, 2, 8, 64, 64), np.float32)
    lru = np.concatenate([R[c]["o_lru_fin"].transpose(2, 0, 3, 4, 1).reshape(npq, NL, 2, 512) for c in range(8)], 0)
    f = lambda a: np.ascontiguousarray(a, dtype=np.float32)
    return (f(y_prompt), f(y_sample), f(ckv), f(kr), f(rw), f(ssd), f(lru))
```

```python
import contextlib
import numpy as np
import ml_dtypes
import concourse.bass as bass
import concourse.mybir as mybir
from concourse.bass_utils import run_bass_kernel_spmd

F32 = mybir.dt.float32
BF16 = mybir.dt.bfloat16
AF = mybir.ActivationFunctionType
ALU = mybir.AluOpType
AX = mybir.AxisListType

D = 1024
NL = 2
NDMA_SLOTS = 6
RW_BF16 = False
RW_OBF16 = True
DEBUG_FLAGS = ()
SAME_ENGINE_SYNC = ("act", "dve", "pool")


class Res:
    __slots__ = ("w", "r", "name")

    def __init__(self, name=""):
        self.w = []
        self.r = {}
        self.name = name


class Tl:
    def __init__(self, h, name, psum=False):
        self.h = h
        self.res = Res(name)
        self.name = name
        self.psum = psum

    def __getitem__(self, k):
        return self.h[k]


class Eng:
    def __init__(self, name, h, sem):
        self.name = name
        self.h = h
        self.sem = sem
        self.count = 0
        self.waited = {}
        self.dma_sems = []
        self.dma_vals = []
        self.dma_n = 0


def _res(x):
    return x.res if isinstance(x, Tl) else x


class KB:
    def __init__(self, nc):
        self.nc = nc
        self.es = contextlib.ExitStack()
        self.stacks = [self.es]
        self.eng = {}
        self.uid = 0
        for name, h in (("pe", nc.tensor), ("dve", nc.vector), ("act", nc.scalar),
                        ("pool", nc.gpsimd), ("sp", nc.sync)):
            sem = self.es.enter_context(nc.semaphore("s_" + name))
            self.eng[name] = Eng(name, h, sem)
        for qn in ("sp", "pool", "act"):
            E = self.eng[qn]
            for i in range(NDMA_SLOTS):
                E.dma_sems.append(self.es.enter_context(nc.semaphore(f"d_{qn}{i}")))
                E.dma_vals.append(0)

    def sb(self, name, shape, dtype=F32):
        self.uid += 1
        nm = f"{name}_{self.uid}"
        h = self.stacks[-1].enter_context(self.nc.sbuf_tensor(nm, list(shape), dtype))
        return Tl(h, nm)

    def ps(self, name, shape, dtype=F32):
        self.uid += 1
        nm = f"{name}_{self.uid}"
        h = self.stacks[-1].enter_context(self.nc.psum_tensor(nm, list(shape), dtype))
        return Tl(h, nm, psum=True)

    def dram(self, name, shape, dtype=F32, kind="Internal"):
        h = self.nc.dram_tensor(name, list(shape), dtype, kind=kind)
        return Tl(h.ap(), name)

    @contextlib.contextmanager
    def phase(self):
        self.barrier()
        st = contextlib.ExitStack()
        self.stacks.append(st)
        try:
            with st:
                yield
                self.barrier()
        finally:
            self.stacks.pop()

    def _wait(self, E, ev):
        sem, val, src = ev
        if src is E and E.name not in SAME_ENGINE_SYNC:
            return
        k = id(sem)
        if E.waited.get(k, 0) >= val:
            return
        E.h.wait_ge(sem, val)
        E.waited[k] = val

    def _deps(self, E, reads, writes):
        for r in reads:
            for ev in _res(r).w:
                self._wait(E, ev)
        for w in writes:
            rs = _res(w)
            for ev in rs.w:
                self._wait(E, ev)
            for ev in rs.r.values():
                self._wait(E, ev)

    def _commit(self, ev, key, reads, writes):
        for r in reads:
            _res(r).r[key] = ev
        for w in writes:
            rs = _res(w)
            rs.w = [ev]
            rs.r = {}

    def op(self, en, fn, reads=(), writes=()):
        E = self.eng[en]
        pr = [r for r in reads if isinstance(r, Tl) and r.psum]
        if pr:
            reads = [r for r in reads if not (isinstance(r, Tl) and r.psum)]
            writes = list(writes) + [r for r in pr if r not in writes]
        self._deps(E, reads, writes)
        ins = fn(E.h)
        E.count += 1
        ins.then_inc(E.sem, 1)
        ev = (E.sem, E.count, E)
        self._commit(ev, en, reads, writes)
        return ins

    def dma(self, qn, out, in_, reads=(), writes=(), **kw):
        E = self.eng[qn]
        self._deps(E, reads, writes)
        slot = E.dma_n % NDMA_SLOTS
        E.dma_n += 1
        sem = E.dma_sems[slot]
        pv = E.dma_vals[slot]
        if pv > 0:
            self._wait(E, (sem, pv, None))
        E.h.dma_start(out=out, in_=in_, **kw).then_inc(sem, 16)
        E.dma_vals[slot] = pv + 16
        ev = (sem, pv + 16, None)
        self._commit(ev, ("dma", qn, slot), reads, writes)

    def barrier(self):
        evs = []
        for E in self.eng.values():
            if E.count:
                evs.append((E.sem, E.count, E))
            for s, v in zip(E.dma_sems, E.dma_vals):
                if v:
                    evs.append((s, v, None))
        for E in self.eng.values():
            for ev in evs:
                self._wait(E, ev)

    def mm(self, out, lhsT, rhs, start, stop, reads, writes):
        return self.op("pe", lambda e: e.matmul(out, lhsT=lhsT, rhs=rhs, start=start, stop=stop),
                       reads, writes)

    def tr(self, out, in_, ident, reads, writes):
        return self.op("pe", lambda e: e.transpose(out, in_, ident), reads, writes)

    def act(self, out, in_, func, reads, writes, bias=None, scale=None, accum_out=None, en="act"):
        kw = {}
        if bias is not None:
            kw["bias"] = bias
        if scale is not None:
            kw["scale"] = scale
        if accum_out is not None:
            kw["accum_out"] = accum_out
        return self.op(en, lambda e: e.activation(out=out, in_=in_, func=func, **kw), reads, writes)

    def tt(self, out, in0, in1, op, reads, writes, en="dve"):
        return self.op(en, lambda e: e.tensor_tensor(out=out, in0=in0, in1=in1, op=op), reads, writes)

    def ts(self, out, in0, s1, s2, op0, op1, reads, writes, en="dve", accum_out=None):
        kw = {}
        if accum_out is not None:
            kw["accum_out"] = accum_out
        if op1 is None:
            return self.op(en, lambda e: e.tensor_scalar(out=out, in0=in0, scalar1=s1, scalar2=None,
                                                         op0=op0, **kw), reads, writes)
        return self.op(en, lambda e: e.tensor_scalar(out=out, in0=in0, scalar1=s1, scalar2=s2,
                                                     op0=op0, op1=op1, **kw), reads, writes)

    def stt(self, out, in0, scalar, in1, op0, op1, reads, writes):
        return self.op("dve", lambda e: e.scalar_tensor_tensor(out=out, in0=in0, scalar=scalar, in1=in1,
                                                               op0=op0, op1=op1), reads, writes)

    def cp(self, out, in_, reads, writes, en="dve"):
        if en == "act":
            return self.op("act", lambda e: e.copy(out=out, in_=in_), reads, writes)
        return self.op(en, lambda e: e.tensor_copy(out=out, in_=in_), reads, writes)

    def scan(self, out, d0, d1, init, reads, writes, op0=ALU.mult, op1=ALU.add):
        return self.op("dve", lambda e: e.tensor_tensor_scan(out=out, data0=d0, data1=d1, initial=init,
                                                             op0=op0, op1=op1), reads, writes)

    def memset(self, ap, val, writes, en="pool"):
        return self.op(en, lambda e: e.memset(ap, val), (), writes)

    def finish(self):
        self.barrier()


ZROWS = {}
_o = 0
for _n, _w in (("rw_r", 512), ("rw_k", 512), ("rw_v", 512), ("rw_lora", 128), ("rw_gate", 512),
               ("mla_q", 384), ("mla_ckv", 256), ("misc", 128), ("mla_gate", 512),
               ("ssd_xbc", 768), ("lru_x", 512), ("lru_gate", 512)):
    ZROWS[_n] = (_o, _w)
    _o += _w
NZ_FM = _o
NZ_ALL = NZ_FM + 512
_IW = dict(rw_pre=0, rw_gate=1664, mla_q=2176, mla_ckv=2560, mla_krope=2816, mla_gate=2848,
           ssd_gate=3360, ssd_xbc=3872, ssd_dt=4640, lru_x=4656, lru_gate=5168, merge=5680)


class VecPack:
    def __init__(self):
        self.cols = {}
        self.n = 0
        self.data = []

    def add(self, name, arr2d):
        k = arr2d.shape[-1]
        self.cols[name] = (self.n, k)
        self.n += k
        self.data.append(np.asarray(arr2d, np.float32))

    def fm(self, name, v):
        v = np.asarray(v, np.float32)
        n = v.shape[-1]
        if n % 128:
            pad = 128 - n % 128
            v = np.concatenate([v, np.zeros(v.shape[:-1] + (pad,), np.float32)], -1)
        k = v.shape[-1] // 128
        self.add(name, v.reshape(v.shape[0], k, 128).transpose(0, 2, 1))

    def pack(self):
        return np.ascontiguousarray(np.concatenate(self.data, -1))


def vec_layout():
    vp = VecPack()
    z = lambda *s: np.zeros(s, np.float32)
    vp.fm("ada_b_ss", z(NL, 2048))
    vp.fm("norm_g", z(NL, 1024))
    vp.fm("mla_qa_g", z(NL, 384))
    vp.fm("mla_kva_g", z(NL, 256))
    vp.fm("mla_qn_g", z(NL, 96))
    vp.fm("mla_kn_g", z(NL, 96))
    vp.fm("rw_mu", z(NL, 1664))
    vp.fm("rw_w0", z(NL, 1024))
    vp.fm("rw_a0", z(NL, 1024))
    vp.fm("rw_kk", z(NL, 512))
    vp.fm("rw_ka", z(NL, 512))
    vp.fm("rw_rk", z(NL, 512))
    vp.fm("rw_ln_g", z(NL, 512))
    vp.fm("rw_ln_b", z(NL, 512))
    vp.fm("ssd_cw", z(NL, 4 * 768))
    vp.fm("ssd_cb", z(NL, 768))
    vp.fm("ssd_dtb", z(NL, 64))
    vp.fm("ssd_alog", z(NL, 64))
    vp.fm("lru_cw", z(NL, 4 * 512))
    vp.fm("lru_cb", z(NL, 512))
    vp.fm("lru_ba", z(NL, 2 * 512))
    vp.fm("lru_bx", z(NL, 2 * 512))
    vp.fm("lru_lam", z(NL, 2 * 512))
    return vp


def host_vecs(inp):
    vp = VecPack()
    vp.fm("ada_b_ss", inp["ada_b"][:, :2048])
    vp.fm("norm_g", inp["norm_g"])
    r2 = lambda a: np.asarray(a, np.float32).reshape(NL, -1)
    vp.fm("mla_qa_g", inp["mla_qa_g"])
    vp.fm("mla_kva_g", inp["mla_kva_g"])
    vp.fm("mla_qn_g", inp["mla_qn_g"])
    vp.fm("mla_kn_g", inp["mla_kn_g"])
    vp.fm("rw_mu", inp["rw_mu"])
    vp.fm("rw_w0", r2(inp["rw_w0"]))
    vp.fm("rw_a0", r2(inp["rw_a0"]))
    vp.fm("rw_kk", inp["rw_kk"])
    vp.fm("rw_ka", inp["rw_ka"])
    vp.fm("rw_rk", r2(inp["rw_rk"]))
    vp.fm("rw_ln_g", inp["rw_ln_g"])
    vp.fm("rw_ln_b", inp["rw_ln_b"])
    vp.fm("ssd_cw", r2(inp["ssd_conv_w"]))
    vp.fm("ssd_cb", inp["ssd_conv_b"])
    def d64(a):
        a = np.asarray(a, np.float32)
        o_ = np.zeros((NL, 64), np.float32)
        o_[:, 0:8] = a[:, 0]
        o_[:, 32:40] = a[:, 1]
        return o_
    vp.fm("ssd_dtb", d64(inp["ssd_dt_bias"]))
    vp.fm("ssd_alog", d64(inp["ssd_a_log"]))
    vp.fm("lru_cw", r2(inp["lru_conv_w"]))
    vp.fm("lru_cb", inp["lru_conv_b"])
    vp.fm("lru_ba", r2(inp["lru_ba"]))
    vp.fm("lru_bx", r2(inp["lru_bx"]))
    vp.fm("lru_lam", r2(inp["lru_lambda"]))
    return vp


class Cfg:
    def __init__(self, n_prompt=4, lp=256, ls=4096, debug=()):
        self.n_prompt = n_prompt
        self.lp = lp
        self.ls = ls
        self.TP = n_prompt * lp
        self.T = self.TP + ls
        self.seqs = [(i * lp, lp, False) for i in range(n_prompt)] + [(self.TP, ls, True)]
        self.debug = set(debug)
        assert self.T % 512 == 0 and lp % 128 == 0 and ls % 512 == 0 and self.TP % 512 == 0


class Prog:
    def __init__(self, cfg):
        self.cfg = cfg
        self.nc = bass.Bass("TRN2", target_bir_lowering=False)
        self.k = KB(self.nc)
        self.inputs = {}
        self.outputs = {}
        self.vl = vec_layout()

    def din(self, name, shape, dtype=F32):
        t = self.k.dram(name, shape, dtype, kind="ExternalInput")
        self.inputs[name] = t
        return t

    def dout(self, name, shape, dtype=F32):
        t = self.k.dram(name, shape, dtype, kind="ExternalOutput")
        self.outputs[name] = t
        return t

    def vcol(self, name, j=0, n=1):
        o, k = self.vl.cols[name]
        assert j + n <= k
        return self.vecs[:, o + j:o + j + n]

    def build(self):
        cfg, k = self.cfg, self.k
        T = cfg.T
        with k.es:
            self.x_in = self.din("x_all", [T, D])
            self.condT = self.din("condT", [128, 8, 2])
            self.ada_w = self.din("ada_w", [NL, D, 3 * D])
            self.ada_bg = self.din("ada_bg", [NL, 1, D])
            self.w_in_a = self.din("w_in_a", [NL, D, NZ_ALL])
            self.vecs_d = self.din("vecs", [NL, 128, self.vl.n])
            self.ident_d = self.din("ident", [128, 128])
            self.sel2_d = self.din("sel2", [2, 2, 128])
            self.lru_w = self.din("lru_w", [NL, 128, 16, 128])
            self.lru_h0 = self.din("lru_h0", [NL, 128, 2, 4])
            self.o_lru_fin = self.dout("o_lru_fin", [NL, 128, cfg.n_prompt, 2, 4])
            self.q_up = self.din("q_up", [NL, 384, 768])
            self.kv_up_k = self.din("kv_up_k", [NL, 256, 512])
            self.kv_up_v = self.din("kv_up_v", [NL, 256, 512])
            self.cache_ckv = self.din("cache_ckv", [NL, 256, 256])
            self.cache_kr = self.din("cache_kr", [NL, 256, 32])
            self.rope_cs = self.din("rope_cs", [2, 32, cfg.ls])
            self.pm96_d = self.din("pm96", [32, 96])
            self.o_ckv = self.dout("o_ckv", [NL, 256, cfg.TP])
            self.o_kr = self.dout("o_kr", [NL, 32, cfg.TP])
            self.ssd_bc = self.din("ssd_bc", [NL, 128, 8 + 512])
            self.ssd_h0 = self.din("ssd_h0", [NL, 2, 64, 8, 64])
            self.selm_d = self.din("selm", [64, 16, 128])
            self.seld_d = self.din("seld", [64, 2, 8])
            self.selg_d = self.din("selg", [64, 128])
            self.maskneg_d = self.din("maskneg", [128, 2, 128])
            self.cmask_d = self.din("cmask", [64, T + 1])
            self.o_ssd_fin = self.dout("o_ssd_fin", [NL, cfg.n_prompt, 2, 64, 8, 64])
            self.ssd_y = k.dram("ssd_y_scr", [T, 512])
            self.rw_lw = self.din("rw_lw", [NL, 128, 2, 512])
            self.rw_h0 = self.din("rw_h0", [NL, 2, 64, 8, 64])
            self.rw_mask_d = self.din("rw_mask", [128, 2, 3, 128])
            self.blk64_d = self.din("blk64", [128, 128])
            self.pmask_d = self.din("pmask", [128, 2])
            self.cmask64_d = self.din("cmask64", [128, 513])
            self.o_rw_fin = self.dout("o_rw_fin", [cfg.n_prompt, NL, 2, 8, 64, 64])
            self.rw_ops = k.dram("rw_ops_scr", [2, 6, 512, T])
            self.rw_v = k.dram("rw_v_scr", [512, T])
            self.rw_bonus = k.dram("rw_bonus_scr", [512, T])
            self.rw_o = k.dram("rw_o_scr", [2, 512, T])
            self.w_in_m = self.din("w_in_m", [NL, D, 4 * D])
            self.w_branch = self.din("w_branch", [NL, 4, 512, D])
            self.w_out = self.din("w_out", [NL, D, D])
            self.y_all = self.dout("y_all", [T, D])
            self.y1 = k.dram("y1_scr", [T, D])
            self.mT_scr = k.dram("mT_scr", [D, T], BF16)
            self.o_scr = k.dram("o_scr", [4, 512, T], BF16)
            if "o" in cfg.debug:
                self.dbg_o = self.dout("dbg_o", [4, 512, T], BF16)
            self.zT = k.dram("zT_scr", [NZ_FM, T])
            self.zg_tm = k.dram("zg_tm_scr", [T, 512])
            self.hT_scr = k.dram("hT_scr", [D, T], BF16)
            if "z" in cfg.debug:
                self.dbg_zT = self.dout("dbg_zT", [NZ_FM, T])
                self.dbg_zg = self.dout("dbg_zg", [T, 512])
            self.vecs = k.sb("vecs", [128, self.vl.n])
            self.ident_f = k.sb("ident_f", [128, 128])
            self.ident_b = k.sb("ident_b", [128, 128], BF16)
            self.sel2 = k.sb("sel2", [2, 2, 128])
            self.sc = k.sb("sc", [128, 8, 2])
            self.gmod = k.sb("gmod", [128, 8, 2])
            self.shiftc = k.sb("shiftc", [128, 8, 2])
            self.gate_bc = [k.sb(f"gate_bc{g}", [128, D]) for g in range(2)]
            self.ones_f = k.sb("ones_f", [128, 128])
            k.memset(self.ones_f[:], 1.0, [self.ones_f])
            self.pf = [k.ps(f"pf{i}", [128, 512]) for i in range(6)]
            self.pb = [k.ps(f"pb{i}", [128, 1024], BF16) for i in range(2)]
            self.pfi = 0
            k.dma("sp", self.ident_f[:], self.ident_d[:, :], (), [self.ident_f])
            k.dma("sp", self.sel2[:], self.sel2_d[:, :, :], (), [self.sel2])
            k.cp(self.ident_b[:], self.ident_f[:], [self.ident_f], [self.ident_b])
            k.dma("sp", self.sc[:], self.condT[:, :, :], (), [self.sc])
            k.act(self.sc[:], self.sc[:], AF.Silu, [self.sc], [self.sc])

            for l in range(NL):
                self.layer(l)
                if l == 0 and "stop0" in cfg.debug:
                    break
            k.finish()
        return self.nc

    def dump(self, name, tl, ap, shape, dtype=F32):
        if name not in self.cfg.debug:
            return
        t = self.dout("dump_" + name, shape, dtype)
        self.k.dma("sp", t[tuple(slice(None) for _ in shape)], ap, [tl], [t])

    def next_pf(self):
        p = self.pf[self.pfi % 4]
        self.pfi += 1
        return p

    def layer(self, l):
        cfg, k = self.cfg, self.k
        with k.phase():
            k.dma("sp", self.vecs[:], self.vecs_d[l], (), [self.vecs])
            self.phase_mod(l)
        if "zero_o" in cfg.debug:
            with k.phase():
                zt = k.sb("zt", [128, cfg.T], BF16)
                k.memset(zt[:], 0.0, [zt])
                for m in range(4):
                    for ft in range(4):
                        k.dma("sp", self.o_scr[m, ft * 128:(ft + 1) * 128, :], zt[:], [zt], [self.o_scr])
        with k.phase():
            self.phase_front(l)
        with k.phase():
            self.phase_lru(l)
        if "norw" not in cfg.debug:
            with k.phase():
                self.phase_rwkv(l)
        if "nossd" not in cfg.debug:
            with k.phase():
                self.phase_ssd(l)
        if "nomla" not in cfg.debug:
            with k.phase():
                self.phase_mla(l)
        with k.phase():
            self.phase_merge(l)
        with k.phase():
            self.phase_out(l)
        if "y1" in cfg.debug and l == 0:
            k.barrier()
            d_ = self.dout("dbg_y1", [cfg.T, D])
            k.dma("sp", d_[:, :], self.y1[:, :], [self.y1], [d_])
            k.barrier()
        if "o" in cfg.debug and l == 0:
            k.barrier()
            k.dma("sp", self.dbg_o[:, :, :], self.o_scr[:, :, :], [self.o_scr], [self.dbg_o])
            k.barrier()

    def phase_mod(self, l):
        k = self.k
        wst = [k.sb(f"adaw{i}", [128, 8, 512]) for i in range(2)]
        modc = k.sb("modc", [128, 16, 2])
        grow = k.sb("grow", [2, D])
        gb = k.sb("gb", [2, D])
        for g in range(2):
            k.dma("pool", gb[g:g + 1, :], self.ada_bg[l], (), [gb])
        for blk in range(6):
            w = wst[blk % 2]
            k.dma("sp", w[:], self.ada_w[l][:, blk * 512:(blk + 1) * 512].rearrange("(c p) n -> p c n", p=128),
                  (), [w])
            if blk < 4:
                for jt in range(4):
                    ps = self.next_pf()
                    for c in range(8):
                        k.mm(ps[:, 0:2], w[:, c, jt * 128:(jt + 1) * 128], self.sc[:, c, :], c == 0, c == 7,
                             [w, self.sc], [ps])
                    k.cp(modc[:, blk * 4 + jt, :], ps[:, 0:2], [ps], [modc])
            else:
                ps = self.next_pf()
                for c in range(8):
                    k.mm(ps[0:2, :], self.sc[:, c, :], w[:, c, :], c == 0, c == 7, [w, self.sc], [ps])
                hs = slice((blk - 4) * 512, (blk - 3) * 512)
                k.tt(grow[:, hs], ps[0:2, :], gb[:, hs], ALU.add, [ps, gb], [grow])
        ab = self.vcol("ada_b_ss", 0, 16)
        for g in range(2):
            k.tt(modc[:, :, g], modc[:, :, g], ab, ALU.add, [modc, self.vecs], [modc])
            k.cp(self.shiftc[:, :, g], modc[:, 0:8, g], [modc], [self.shiftc])
            k.stt(self.gmod[:, :, g], modc[:, 8:16, g], 1.0, self.vcol("norm_g", 0, 8), ALU.add, ALU.mult,
                  [modc, self.vecs], [self.gmod])
            for half in range(2):
                ps = self.next_pf()
                k.mm(ps[:, :], self.sel2[:, g, :], grow[:, half * 512:(half + 1) * 512], True, True,
                     [self.sel2, grow], [ps])
                k.cp(self.gate_bc[g][:, half * 512:(half + 1) * 512], ps[:, :], [ps], [self.gate_bc[g]])

    def phase_front(self, l):
        cfg, k = self.cfg, self.k
        T = cfg.T
        x_src = self.x_in if l == 0 else self.y1
        hT = k.sb("hT", [128, 8, T], BF16)
        with k.phase():
            xt = [k.sb(f"xt{i}", [128, D]) for i in range(3)]
            xn = [k.sb(f"xn{i}", [128, D], BF16) for i in range(2)]
            junk = k.sb("junk", [128, D], BF16)
            ss = [k.sb(f"ss{i}", [128, 1]) for i in range(2)]
            for st in range(T // 128):
                g = 0 if st * 128 < cfg.TP else 1
                x = xt[st % 3]
                xb = xn[st % 2]
                s = ss[st % 2]
                pb = self.pb[st % 2]
                k.dma("sp", x[:], x_src[st * 128:(st + 1) * 128, :], [x_src], [x])
                k.act(junk[:], x[:], AF.Square, [x], [junk, s], accum_out=s[:])
                k.ts(s[:], s[:], 1.0 / D, 1e-6, ALU.mult, ALU.add, [s], [s])
                k.act(s[:], s[:], AF.Sqrt, [s], [s])
                k.op("dve", lambda e: e.reciprocal(out=s[:], in_=s[:]), [s], [s])
                k.act(xb[:], x[:], AF.Copy, [x, s], [xb], scale=s[:])
                for c in range(8):
                    k.tr(pb[:, c * 128:(c + 1) * 128], xb[:, c * 128:(c + 1) * 128], self.ident_b[:],
                         [xb, self.ident_b], [pb])
                ho = hT[:, :, st * 128:(st + 1) * 128]
                pv = pb[:].rearrange("p (c t) -> p c t", c=8)
                k.tt(ho, pv, self.gmod[:, :, g:g + 1].to_broadcast([128, 8, 128]), ALU.mult,
                     [pb, self.gmod], [hT])
                k.tt(ho, ho, self.shiftc[:, :, g:g + 1].to_broadcast([128, 8, 128]), ALU.add,
                     [hT, self.shiftc], [hT])
        for c in range(8):
            k.dma("pool", self.hT_scr[c * 128:(c + 1) * 128, :], hT[:, c, :], [hT], [self.hT_scr])
        wst = [k.sb(f"wst{i}", [128, 8, 512]) for i in range(2)]
        wbf = [k.sb(f"wbf{i}", [128, 8, 512], BF16) for i in range(2)]
        zst = [k.sb(f"zst{i}", [128, 512]) for i in range(4)]
        blocks = [(c0, min(512, NZ_FM - c0), False) for c0 in range(0, NZ_FM, 512)] + [(NZ_FM, 512, True)]
        zi = 0
        for blk, (c0, ncol, tm_block) in enumerate(blocks):
            ws, wb = wst[blk % 2], wbf[blk % 2]
            k.dma("sp", ws[:, :, :ncol], self.w_in_a[l][:, c0:c0 + ncol].rearrange("(c p) n -> p c n", p=128),
                  (), [ws])
            k.cp(wb[:, :, :ncol], ws[:, :, :ncol], [ws], [wb], en="pool")
            for tt in range(T // 512):
                ts_ = slice(tt * 512, (tt + 1) * 512)
                if not tm_block:
                    for jt in range(ncol // 128):
                        ps = self.next_pf()
                        for c in range(8):
                            k.mm(ps[:, :], wb[:, c, jt * 128:(jt + 1) * 128], hT[:, c, ts_], c == 0, c == 7,
                                 [wb, hT], [ps])
                        z = zst[zi % 4]
                        if zi % 2 == 0:
                            k.cp(z[:], ps[:, :], [ps], [z])
                        else:
                            k.cp(z[:], ps[:, :], [ps], [z], en="act")
                        zi += 1
                        r0 = c0 + jt * 128
                        k.dma("pool", self.zT[r0:r0 + 128, ts_], z[:], [z], [self.zT])
                else:
                    for sub in range(4):
                        ps = self.next_pf()
                        t0 = tt * 512 + sub * 128
                        for c in range(8):
                            k.mm(ps[:, :], hT[:, c, t0:t0 + 128], wb[:, c, :], c == 0, c == 7, [wb, hT], [ps])
                        z = zst[zi % 4]
                        if zi % 2 == 0:
                            k.cp(z[:], ps[:, :], [ps], [z])
                        else:
                            k.cp(z[:], ps[:, :], [ps], [z], en="act")
                        zi += 1
                        k.dma("pool", self.zg_tm[t0:t0 + 128, :], z[:], [z], [self.zg_tm])
        if "z" in cfg.debug and l == 0:
            k.barrier()
            k.dma("sp", self.dbg_zT[:, :], self.zT[:, :], [self.zT], [self.dbg_zT])
            k.dma("sp", self.dbg_zg[:, :], self.zg_tm[:, :], [self.zg_tm], [self.dbg_zg])


    def groups(self):
        cfg = self.cfg
        return [(0, cfg.n_prompt, cfg.lp), (cfg.TP, 1, cfg.ls)]

    def gview(self, ap2d, grp, lo, hi):
        s0, ns, ln = grp
        return ap2d[:, s0:s0 + ns * ln].rearrange("p (s t) -> p s t", s=ns)[:, :, lo:hi]

    def dwconv(self, xc, xl, wcol, bcol, reads):
        k = self.k
        T = self.cfg.T
        k.ts(xc[:, :T], xl[:, :T], wcol(2), bcol, ALU.mult, ALU.add, [xl] + reads, [xc])
        for grp in self.groups():
            ln = grp[2]
            for kk, off in ((0, -2), (1, -1), (3, 1)):
                if off < 0:
                    src = self.gview(xl[:, :T], grp, 0, ln + off)
                    dst = self.gview(xc[:, :T], grp, -off, ln)
                else:
                    src = self.gview(xl[:, :T], grp, off, ln)
                    dst = self.gview(xc[:, :T], grp, 0, ln - off)
                k.stt(dst, src, wcol(kk), dst, ALU.mult, ALU.add, [xl, xc] + reads, [xc])

    def phase_lru(self, l):
        cfg, k = self.cfg, self.k
        T = cfg.T
        zo, _ = ZROWS["lru_x"]
        go, _ = ZROWS["lru_gate"]
        wts = k.sb("lru_wts", [128, 16, 128])
        h0 = k.sb("lru_h0", [128, 2, 4])
        clam = k.sb("lru_clam", [128, 8])
        fin = k.sb("lru_fin", [128, cfg.n_prompt, 2, 4])
        k.dma("sp", wts[:], self.lru_w[l], (), [wts])
        k.dma("sp", h0[:], self.lru_h0[l], (), [h0])
        k.act(clam[:], self.vcol("lru_lam", 0, 8), AF.Exp, [self.vecs], [clam], scale=-1.0)
        k.act(clam[:], clam[:], AF.Ln, [clam], [clam], bias=1.0)
        k.ts(clam[:], clam[:], -8.0, None, ALU.mult, None, [clam], [clam])
        xl = k.sb("lru_xl", [128, T])
        gt = k.sb("lru_gt", [128, T])
        xc = k.sb("lru_xc", [128, T])
        ta = k.sb("lru_ta", [128, T])
        tb = k.sb("lru_tb", [128, T])
        hh = [k.sb(f"lru_h{d}", [128, T]) for d in range(2)]
        ob = k.sb("lru_ob", [128, T], BF16)
        V = [self.vecs]
        for ft in range(4):
            k.dma("sp", xl[:], self.zT[zo + ft * 128:zo + (ft + 1) * 128, :], [self.zT], [xl])
            k.dma("sp", gt[:], self.zT[go + ft * 128:go + (ft + 1) * 128, :], [self.zT], [gt])
            self.dwconv(xc, xl, lambda kk: self.vcol("lru_cw", kk * 4 + ft), self.vcol("lru_cb", ft), V)
            k.act(gt[:], gt[:], AF.Silu, [gt], [gt])
            for d in range(2):
                for tt in range(T // 512):
                    sl = slice(tt * 512, (tt + 1) * 512)
                    pa = self.next_pf()
                    k.mm(pa[:, :], wts[:, (d * 2 + 0) * 4 + ft, :], xc[:, sl], True, True, [wts, xc], [pa])
                    px = self.next_pf()
                    k.mm(px[:, :], wts[:, (d * 2 + 1) * 4 + ft, :], xc[:, sl], True, True, [wts, xc], [px])
                    k.act(ta[:, sl], pa[:, :], AF.Sigmoid, [pa] + V, [ta], bias=self.vcol("lru_ba", d * 4 + ft))
                    k.act(tb[:, sl], px[:, :], AF.Sigmoid, [px] + V, [tb], bias=self.vcol("lru_bx", d * 4 + ft))
                k.act(ta[:], ta[:], AF.Exp, [ta, clam], [ta], scale=clam[:, d * 4 + ft:d * 4 + ft + 1])
                if ft == 0 and d == 0:
                    self.dump("lru_clam", clam, clam[:], [128, 8])
                    self.dump("lru_a", ta, ta[:], [128, T])
                    self.dump("lru_gi", tb, tb[:], [128, T])
                    self.dump("lru_xc", xc, xc[:], [128, T])
                k.tt(tb[:], tb[:], xc[:], ALU.mult, [tb, xc], [tb])
                h = hh[d]
                k.tt(h[:], ta[:], ta[:], ALU.mult, [ta], [h])
                k.act(h[:], h[:], AF.Sqrt, [h], [h], scale=-1.0, bias=1.0)
                k.tt(tb[:], tb[:], h[:], ALU.mult, [tb, h], [tb])
                if ft == 0 and d == 0:
                    self.dump("lru_sq", h, h[:], [128, T])
                    self.dump("lru_u", tb, tb[:], [128, T])
                for (s0, ln, is_s) in cfg.seqs:
                    sl = slice(s0, s0 + ln)
                    init = h0[:, d, ft:ft + 1] if is_s else 0.0
                    rd = [ta, tb] + ([h0] if is_s else [])
                    if d == 0:
                        k.scan(h[:, sl], ta[:, sl], tb[:, sl], init, rd, [h])
                    else:
                        rv = lambda t: t[:, s0:s0 + ln][:, ::-1]
                        k.scan(rv(h), rv(ta), rv(tb), init, rd, [h])
                    if not is_s:
                        si = s0 // ln
                        e = s0 + ln - 1 if d == 0 else s0
                        k.cp(fin[:, si, d, ft:ft + 1], h[:, e:e + 1], [h], [fin], en="pool")
            k.tt(hh[0][:], hh[0][:], hh[1][:], ALU.add, [hh[0], hh[1]], [hh[0]])
            k.tt(ob[:], hh[0][:], gt[:], ALU.mult, [hh[0], gt], [ob])
            k.dma("pool", self.o_scr[3, ft * 128:(ft + 1) * 128, :], ob[:], [ob], [self.o_scr])
        k.dma("pool", self.o_lru_fin[l], fin[:], [fin], [self.o_lru_fin])


    def phase_merge(self, l):
        cfg, k = self.cfg, self.k
        T = cfg.T
        wm = k.sb("wm", [128, 8, 4 * D], BF16)
        wbr = k.sb("wbr", [128, 16, D], BF16)
        with k.phase():
            stg = [k.sb(f"mstg{i}", [128, 4096]) for i in range(2)]
            si = 0
            for blk in range(8):
                st = stg[si % 2]
                si += 1
                k.dma("sp", st[:].rearrange("p (c n) -> p c n", c=8),
                      self.w_in_m[l][:, blk * 512:(blk + 1) * 512].rearrange("(c p) n -> p c n", p=128), (), [st])
                k.cp(wm[:, :, blk * 512:(blk + 1) * 512], st[:].rearrange("p (c n) -> p c n", c=8), [st], [wm], en="pool")
            for m in range(4):
                st = stg[si % 2]
                si += 1
                k.dma("sp", st[:].rearrange("p (c n) -> p c n", c=4),
                      self.w_branch[l, m].rearrange("(c p) n -> p c n", p=128), (), [st])
                k.cp(wbr[:, m * 4:(m + 1) * 4, :], st[:].rearrange("p (c n) -> p c n", c=4), [st], [wbr], en="pool")
        hts = [k.sb(f"mh{i}", [128, 8, 512], BF16) for i in range(2)]
        ots = [k.sb(f"mo{i}", [128, 16, 512], BF16) for i in range(2)]
        sgs = [k.sb(f"msg{i}", [128, 512]) for i in range(2)]
        tmp = [k.sb(f"mtmp{i}", [128, 512]) for i in range(2)]
        acc = [k.sb(f"macc{i}", [128, 512]) for i in range(2)]
        mts = [k.sb(f"mt{i}", [128, 8, 512], BF16) for i in range(2)]
        n = 0
        for tt in range(T // 512):
            tsl = slice(tt * 512, (tt + 1) * 512)
            ht, ot, mt = hts[tt % 2], ots[tt % 2], mts[tt % 2]
            k.dma("sp", ht[:], self.hT_scr[:, tsl].rearrange("(c p) t -> p c t", p=128), [self.hT_scr], [ht])
            for m in range(4):
                k.dma("sp", ot[:, m * 4:(m + 1) * 4, :], self.o_scr[m, :, tsl].rearrange("(c p) t -> p c t", p=128),
                      [self.o_scr], [ot])
            for dt in range(8):
                a = acc[dt % 2]
                for m in range(4):
                    pl = self.next_pf()
                    for c in range(8):
                        k.mm(pl[:, :], wm[:, c, m * D + dt * 128:m * D + (dt + 1) * 128], ht[:, c, :], c == 0, c == 7,
                             [wm, ht], [pl])
                    pp = self.next_pf()
                    for cc in range(4):
                        k.mm(pp[:, :], wbr[:, m * 4 + cc, dt * 128:(dt + 1) * 128], ot[:, m * 4 + cc, :], cc == 0, cc == 3,
                             [wbr, ot], [pp])
                    sg = sgs[n % 2]
                    n += 1
                    k.act(sg[:], pl[:, :], AF.Sigmoid, [pl], [sg])
                    if m == 0:
                        k.tt(a[:], sg[:], pp[:, :], ALU.mult, [sg, pp], [a])
                    else:
                        t_ = tmp[n % 2]
                        k.tt(t_[:], sg[:], pp[:, :], ALU.mult, [sg, pp], [t_])
                        if m < 3:
                            k.tt(a[:], a[:], t_[:], ALU.add, [a, t_], [a], en="pool")
                        else:
                            k.tt(mt[:, dt, :], a[:], t_[:], ALU.add, [a, t_], [mt], en="pool")
            k.dma("pool", self.mT_scr[:, tsl].rearrange("(c p) t -> p c t", p=128), mt[:], [mt], [self.mT_scr])

    def phase_out(self, l):
        cfg, k = self.cfg, self.k
        T = cfg.T
        x_src = self.x_in if l == 0 else self.y1
        y_dst = self.y1 if l < NL - 1 else self.y_all
        wo = k.sb("wo", [128, 8, D], BF16)
        stg = [k.sb(f"ostg{i}", [128, 4096]) for i in range(2)]
        for hb in range(2):
            st = stg[hb]
            k.dma("sp", st[:].rearrange("p (c n) -> p c n", c=8),
                  self.w_out[l][:, hb * 512:(hb + 1) * 512].rearrange("(c p) n -> p c n", p=128), (), [st])
            k.cp(wo[:, :, hb * 512:(hb + 1) * 512], st[:].rearrange("p (c n) -> p c n", c=8), [st], [wo], en="pool")
        mts = [k.sb(f"omt{i}", [128, 8, 128], BF16) for i in range(3)]
        xts = [k.sb(f"oxt{i}", [128, D]) for i in range(3)]
        yts = [k.sb(f"oyt{i}", [128, D]) for i in range(3)]
        for st_ in range(T // 128):
            g = 0 if st_ * 128 < cfg.TP else 1
            tsl = slice(st_ * 128, (st_ + 1) * 128)
            mt, xt, yt = mts[st_ % 3], xts[st_ % 3], yts[st_ % 3]
            k.dma("sp", mt[:], self.mT_scr[:, tsl].rearrange("(c p) t -> p c t", p=128), [self.mT_scr], [mt])
            k.dma("sp", xt[:], x_src[tsl, :], [x_src], [xt])
            for hb in range(2):
                hs = slice(hb * 512, (hb + 1) * 512)
                ps = self.next_pf()
                for c in range(8):
                    k.mm(ps[:, :], mt[:, c, :], wo[:, c, hs], c == 0, c == 7, [mt, wo], [ps])
                k.tt(yt[:, hs], ps[:, :], self.gate_bc[g][:, hs], ALU.mult, [ps, self.gate_bc[g]], [yt])
                k.tt(yt[:, hs], yt[:, hs], xt[:, hs], ALU.add, [yt, xt], [yt], en="pool")
            k.dma("pool", y_dst[tsl, :], yt[:], [yt], [y_dst])


    def rstd_from_sum(self, out, ps, n, reads):
        k = self.k
        tl, pt = reads
        k.ts(out, ps, 1.0 / n, 1e-6, ALU.mult, ALU.add, [pt], [tl])
        k.act(out, out, AF.Sqrt, [tl], [tl])
        k.op("dve", lambda e: e.reciprocal(out=out, in_=out), [tl], [tl])

    def phase_mla(self, l):
        cfg, k = self.cfg, self.k
        T, TP, ls = cfg.T, cfg.TP, cfg.ls
        TK = T + 256
        kidx = lambda t: t if t < TP else t + 256
        V = [self.vecs]
        qup = k.sb("qup", [128, 3, 768], BF16)
        kvk = k.sb("kvk", [128, 2, 512], BF16)
        kvv = k.sb("kvv", [128, 2, 512], BF16)
        pm96 = k.sb("pm96", [96, 96])
        cs = k.sb("ropecs", [96, 2, ls], BF16)
        ckv_all = k.sb("ckv_all", [128, 2, TK], BF16)
        krot = k.sb("krot", [96, TK], BF16)
        ssr = k.sb("ssr", [128, TK // 128])
        qn = k.sb("qn", [128, 3, T], BF16)
        vall = k.sb("vall", [128, TK // 128, 8, 65], BF16)
        with k.phase():
            kr_all = k.sb("kr_all", [96, TK])
            with k.phase():
                stg = k.sb("mlastg", [128, 4096])
                k.dma("sp", stg[:, :3 * 768].rearrange("p (c n) -> p c n", c=3),
                      self.q_up[l].rearrange("(c p) n -> p c n", p=128), (), [stg])
                k.cp(qup[:], stg[:, :3 * 768].rearrange("p (c n) -> p c n", c=3), [stg], [qup])
                k.dma("sp", stg[:, :1024].rearrange("p (c n) -> p c n", c=2),
                      self.kv_up_k[l].rearrange("(c p) n -> p c n", p=128), (), [stg])
                k.cp(kvk[:], stg[:, :1024].rearrange("p (c n) -> p c n", c=2), [stg], [kvk])
                k.dma("sp", stg[:, :1024].rearrange("p (c n) -> p c n", c=2),
                      self.kv_up_v[l].rearrange("(c p) n -> p c n", p=128), (), [stg])
                k.cp(kvv[:], stg[:, :1024].rearrange("p (c n) -> p c n", c=2), [stg], [kvv])
                k.dma("sp", pm96[64:96, :], self.pm96_d[:, :], (), [pm96])
                for j in range(2):
                    k.dma("sp", stg[64:96, :ls], self.rope_cs[j], (), [stg])
                    k.cp(cs[64:96, j, :], stg[64:96, :ls], [stg], [cs])
            k.memset(vall[:, :, :, 64:65], 1.0, [vall])
            if "mla_s0" in cfg.debug:
                return
            ctm = k.sb("ctm", [128, 2, 256])
            krtm = k.sb("krtm", [128, 2, 32])
            k.dma("sp", ctm[:], self.cache_ckv[l].rearrange("(a p) f -> p a f", p=128), (), [ctm])
            k.dma("sp", krtm[:], self.cache_kr[l].rearrange("(a p) f -> p a f", p=128), (), [krtm])
            for a in range(2):
                for c in range(2):
                    ps = self.next_pf()
                    k.tr(ps[:, 0:128], ctm[:, a, c * 128:(c + 1) * 128], self.ident_f[:], [ctm, self.ident_f], [ps])
                    k.cp(ckv_all[:, c, TP + a * 128:TP + (a + 1) * 128], ps[:, 0:128], [ps], [ckv_all])
                ps = self.next_pf()
                kpad = k.sb(f"kpad{a}", [128, 96])
                k.memset(kpad[:], 0.0, [kpad])
                k.cp(kpad[:, 64:96], krtm[:, a, :], [krtm, kpad], [kpad])
                k.tr(ps[0:96, 0:128], kpad[:, :], self.ident_f[:], [kpad, self.ident_f], [ps])
                k.cp(kr_all[64:96, TP + a * 128:TP + (a + 1) * 128], ps[64:96, 0:128], [ps], [kr_all])
            if "mla_s1" in cfg.debug:
                return
            zo_c, _ = ZROWS["mla_ckv"]
            zo_q, _ = ZROWS["mla_q"]
            zo_m, _ = ZROWS["misc"]
            xs = [k.sb(f"mx{i}", [128, 3, 512]) for i in range(2)]
            sq = [k.sb(f"msq{i}", [128, 3, 512]) for i in range(2)]
            rs = [k.sb(f"mrs{i}", [128, 512]) for i in range(2)]
            cn = [k.sb(f"mcn{i}", [128, 2, 512]) for i in range(2)]
            for tt in range(T // 512):
                tsl = slice(tt * 512, (tt + 1) * 512)
                ksl = slice(kidx(tt * 512), kidx(tt * 512) + 512)
                x, q2, r_, c_ = xs[tt % 2], sq[tt % 2], rs[tt % 2], cn[tt % 2]
                for (zo, nch, gname, is_q) in ((zo_c, 2, "mla_kva_g", False), (zo_q, 3, "mla_qa_g", True)):
                    k.dma("sp", x[:, :nch, :], self.zT[zo:zo + nch * 128, tsl].rearrange("(c p) t -> p c t", p=128),
                          [self.zT], [x])
                    k.act(q2[:, :nch, :], x[:, :nch, :], AF.Square, [x], [q2])
                    ps = self.next_pf()
                    for c in range(nch):
                        k.mm(ps[:, :], self.ones_f[:, :], q2[:, c, :], c == 0, c == nch - 1, [self.ones_f, q2], [ps])
                    self.rstd_from_sum(r_[:], ps[:, :], nch * 128, (r_, ps))
                    for c in range(nch):
                        if is_q:
                            k.stt(qn[:, c, tsl], x[:, c, :], self.vcol(gname, c), r_[:], ALU.mult, ALU.mult,
                                  [x, r_] + V, [qn])
                        else:
                            k.stt(c_[:, c, :], x[:, c, :], self.vcol(gname, c), r_[:], ALU.mult, ALU.mult,
                                  [x, r_] + V, [c_])
                    if not is_q:
                        k.cp(ckv_all[:, :, ksl], c_[:, :, :], [c_], [ckv_all], en="pool")
                        if tt * 512 < TP:
                            k.dma("pool", self.o_ckv[l][:, tsl].rearrange("(c p) t -> p c t", p=128), c_[:, :, :],
                                  [c_], [self.o_ckv])
            if "mla_s2" in cfg.debug:
                return
            k.dma("sp", kr_all[64:96, 0:TP], self.zT[zo_m:zo_m + 32, 0:TP], [self.zT], [kr_all])
            k.dma("sp", kr_all[64:96, TP + 256:TK], self.zT[zo_m:zo_m + 32, TP:T], [self.zT], [kr_all])
            k.dma("pool", self.o_kr[l], kr_all[64:96, 0:TP], [kr_all], [self.o_kr])
            krs = k.sb("krs", [96, 512])
            krg = k.sb("krg", [96, 512])
            t1 = k.sb("krt1", [96, 512])
            pss = self.pf[5]
            lat0 = TP + 256
            segs = [(ks, min(512, lat0 - ks)) for ks in range(0, lat0, 512)] + \
                   [(ks, min(512, TK - ks)) for ks in range(lat0, TK, 512)]
            for ks, w in segs:
                k.act(krs[64:96, :w], kr_all[64:96, ks:ks + w], AF.Square, [kr_all], [krs])
                for j in range(w // 128):
                    kt = ks // 128 + j
                    k.mm(pss[:, kt:kt + 1], krs[64:96, j * 128:(j + 1) * 128], self.ones_f[64:96, 0:1], True, True,
                         [krs, self.ones_f], [pss])
                k.ts(krg[64:96, :w], kr_all[64:96, ks:ks + w], self.vecs[64:96, self.vl.cols["mla_kn_g"][0]:self.vl.cols["mla_kn_g"][0] + 1],
                     None, ALU.mult, None, [kr_all] + V, [krg])
                if ks >= lat0:
                    pr = self.next_pf()
                    k.mm(pr[0:96, :w], pm96[64:96, :], krg[64:96, :w], True, True, [pm96, krg], [pr])
                    po = ks - lat0
                    k.tt(t1[64:96, :w], pr[64:96, :w], cs[64:96, 1, po:po + w], ALU.mult, [pr, cs], [t1])
                    k.tt(krg[64:96, :w], krg[64:96, :w], cs[64:96, 0, po:po + w], ALU.mult, [krg, cs], [krg])
                    k.tt(krot[64:96, ks:ks + w], krg[64:96, :w], t1[64:96, :w], ALU.add, [krg, t1], [krot])
                else:
                    k.cp(krot[64:96, ks:ks + w], krg[64:96, :w], [krg], [krot])
            k.cp(ssr[:], pss[:, 0:TK // 128], [pss], [ssr])
            if "mla_s3" in cfg.debug:
                return
            for kt in range(TK // 128):
                ps = self.next_pf()
                for c in range(2):
                    k.mm(ps[:, :], ckv_all[:, c, kt * 128:(kt + 1) * 128], kvv[:, c, :], c == 0, c == 1,
                         [ckv_all, kvv], [ps])
                k.cp(vall[:, kt, :, 0:64], ps[:, :].rearrange("p (h d) -> p h d", h=8), [ps], [vall],
                     en=("act" if kt % 2 else "dve"))
        if "mla_s4" in cfg.debug:
            return
        zo_g, _ = ZROWS["mla_gate"]
        gcol = self.vl.cols["mla_qn_g"][0]
        kcol = self.vl.cols["mla_kn_g"][0]
        kth = k.sb("kth", [96, TK], BF16)
        qth = k.sb("qth", [96, T], BF16)
        rk = k.sb("rk", [128, TK // 128])
        sqt = [k.sb(f"hsq{i}", [96, 512]) for i in range(2)]
        qf = [k.sb(f"hqf{i}", [96, 512]) for i in range(2)]
        rq = [k.sb(f"hrq{i}", [96, 512]) for i in range(2)]
        t1 = k.sb("ht1", [96, 512])
        t2 = k.sb("ht2", [96, 512])
        pts = [k.sb(f"hpt{i}", [128, 512], BF16) for i in range(4)]
        oa = [k.sb(f"hoa{i}", [65, 512]) for i in range(2)]
        gts = [k.sb(f"hgt{i}", [64, 512]) for i in range(2)]
        obs = [k.sb(f"hob{i}", [64, 512], BF16) for i in range(2)]
        npt = 0
        npo = 0
        for h in range(8):
            prk = self.pf[4]
            for ks in range(0, TK, 512):
                w = min(512, TK - ks)
                ps = self.next_pf()
                for c in range(2):
                    k.mm(ps[0:64, :w], kvk[:, c, h * 64:(h + 1) * 64], ckv_all[:, c, ks:ks + w], c == 0, c == 1,
                         [kvk, ckv_all], [ps])
                s_ = sqt[(ks // 512) % 2]
                if "k_noact" not in cfg.debug:
                    k.act(s_[0:64, :w], ps[0:64, :w], AF.Square, [ps], [s_])
                for j in range(w // 128):
                    kt = ks // 128 + j
                    if "mla_k1" in cfg.debug:
                        continue
                    k.mm(prk[:, kt:kt + 1], s_[0:64, j * 128:(j + 1) * 128], self.ones_f[0:64, 0:1], True, True,
                         [s_, self.ones_f], [prk])
                if "k_nots" not in cfg.debug:
                    k.ts(kth[0:64, ks:ks + w], ps[0:64, :w], self.vecs[0:64, kcol:kcol + 1], None, ALU.mult, None,
                         [ps] + V, [kth])
            k.cp(kth[64:96, :], krot[64:96, :], [krot], [kth], en=("dve" if "mla_k2" in cfg.debug else "pool"))
            if "k_nork" in cfg.debug:
                return
            k.tt(rk[:], prk[:, 0:TK // 128], ssr[:], ALU.add, [prk, ssr], [rk])
            k.ts(rk[:], rk[:], 1.0, 96e-6, ALU.mult, ALU.add, [rk], [rk])
            k.act(rk[:], rk[:], AF.Sqrt, [rk], [rk])
            k.op("dve", lambda e: e.reciprocal(out=rk[:], in_=rk[:]), [rk], [rk])
            if "mla_s5" in cfg.debug:
                return
            for tt in range(T // 512):
                tsl = slice(tt * 512, (tt + 1) * 512)
                ps = self.next_pf()
                for c in range(3):
                    k.mm(ps[0:96, :], qup[:, c, h * 96:(h + 1) * 96], qn[:, c, tsl], c == 0, c == 2, [qup, qn], [ps])
                s_, f_, r_ = sqt[tt % 2], qf[tt % 2], rq[tt % 2]
                k.act(s_[:, :], ps[0:96, :], AF.Square, [ps], [s_])
                p2 = self.next_pf()
                k.mm(p2[0:96, :], self.ones_f[0:96, 0:96], s_[:, :], True, True, [self.ones_f, s_], [p2])
                self.rstd_from_sum(r_[:, :], p2[0:96, :], 96, (r_, p2))
                k.stt(f_[:, :], ps[0:96, :], self.vecs[0:96, gcol:gcol + 1], r_[:, :], ALU.mult, ALU.mult,
                      [ps, r_] + V, [f_])
                k.cp(qth[0:64, tsl], f_[0:64, :], [f_], [qth], en="pool")
                if tt * 512 >= TP:
                    po = tt * 512 - TP
                    pr = self.next_pf()
                    k.mm(pr[0:96, :], pm96[64:96, :], f_[64:96, :], True, True, [pm96, f_], [pr])
                    k.tt(t1[64:96, :], pr[64:96, :], cs[64:96, 1, po:po + 512], ALU.mult, [pr, cs], [t1])
                    k.tt(t2[64:96, :], f_[64:96, :], cs[64:96, 0, po:po + 512], ALU.mult, [f_, cs], [t2])
                    k.tt(qth[64:96, tsl], t1[64:96, :], t2[64:96, :], ALU.add, [t1, t2], [qth], en="pool")
                else:
                    k.cp(qth[64:96, tsl], f_[64:96, :], [f_], [qth], en="pool")
            if "mla_s6" in cfg.debug:
                return
            if h == 0:
                self.dump("mla_kth", kth, kth[:, :], [96, TK], BF16)
                self.dump("mla_qth", qth, qth[:, :], [96, T], BF16)
                self.dump("mla_rk", rk, rk[:, :], [128, TK // 128])
            for (s0, ln, is_s) in cfg.seqs:
                k0 = TP if is_s else s0
                nk = ln + 256 if is_s else ln
                qw = min(512, ln)
                for qg in range(ln // qw):
                    q0 = s0 + qg * qw
                    po = self.pf[4 + (npo % 2)]
                    npo += 1
                    nkt = nk // 128
                    pend = []

                    def _pv(item):
                        pt_, kt_, j_ = item
                        k.mm(po[0:65, :qw], vall[:, kt_, h, :], pt_[:, :qw], j_ == 0, j_ == nkt - 1, [vall, pt_], [po])
                    for j in range(nkt):
                        kt = k0 // 128 + j
                        pS = self.next_pf()
                        k.mm(pS[:, :qw], kth[0:96, kt * 128:(kt + 1) * 128], qth[0:96, q0:q0 + qw], True, True,
                             [kth, qth], [pS])
                        pt = pts[npt % 4]
                        npt += 1
                        k.act(pt[:, :qw], pS[:, :qw], AF.Exp, [pS, rk], [pt], scale=rk[:, kt:kt + 1])
                        pend.append((pt, kt, j))
                        if len(pend) > 2:
                            _pv(pend.pop(0))
                    while pend:
                        _pv(pend.pop(0))
                    o_ = oa[qg % 2]
                    k.cp(o_[:, :qw], po[0:65, :qw], [po], [o_])
                    k.op("dve", lambda e: e.reciprocal(out=o_[64:65, :qw], in_=o_[64:65, :qw]), [o_], [o_])
                    pbc = self.next_pf()
                    k.mm(pbc[0:64, :qw], self.ones_f[64:65, 0:64], o_[64:65, :qw], True, True, [self.ones_f, o_], [pbc])
                    g_ = gts[qg % 2]
                    k.dma("sp", g_[:, :qw], self.zT[zo_g + h * 64:zo_g + (h + 1) * 64, q0:q0 + qw], [self.zT], [g_])
                    k.act(g_[:, :qw], g_[:, :qw], AF.Silu, [g_], [g_])
                    k.tt(o_[0:64, :qw], o_[0:64, :qw], pbc[0:64, :qw], ALU.mult, [o_, pbc], [o_])
                    ob = obs[qg % 2]
                    k.tt(ob[:, :qw], o_[0:64, :qw], g_[:, :qw], ALU.mult, [o_, g_], [ob])
                    k.dma("pool", self.o_scr[1, h * 64:(h + 1) * 64, q0:q0 + qw], ob[:, :qw], [ob], [self.o_scr])


    def phase_ssd(self, l):
        cfg, k = self.cfg, self.k
        T, TP = cfg.T, cfg.TP
        NCH = T // 128
        V = [self.vecs]
        zo_x, _ = ZROWS["ssd_xbc"]
        zo_m, _ = ZROWS["misc"]
        x_tm = k.sb("sx_tm", [128, NCH, 512], BF16)
        b_tm = k.sb("sb_tm", [128, NCH, 128], BF16)
        bT = k.sb("s_bT", [128, T], BF16)
        cT = k.sb("s_cT", [128, T], BF16)
        selm = k.sb("s_selm", [64, 16, 128])
        seld = k.sb("s_seld", [64, 2, 8])
        maskneg = k.sb("s_mask", [128, 2, 128])
        bcv = k.sb("s_bcv", [128, 8 + 512])
        k.dma("sp", selm[:], self.selm_d[:, :, :], (), [selm])
        k.dma("sp", seld[:], self.seld_d[:, :, :], (), [seld])
        k.dma("sp", maskneg[:], self.maskneg_d[:, :, :], (), [maskneg])
        k.dma("sp", bcv[:], self.ssd_bc[l], (), [bcv])
        with k.phase():
            xl = [k.sb(f"s_xl{i}", [128, T]) for i in range(2)]
            xc = [k.sb(f"s_xc{i}", [128, T]) for i in range(2)]
            for ft in range(6):
                a, c_ = xl[ft % 2], xc[ft % 2]
                k.dma("sp", a[:], self.zT[zo_x + ft * 128:zo_x + (ft + 1) * 128, :], [self.zT], [a])
                self.dwconv(c_, a, lambda kk: self.vcol("ssd_cw", kk * 6 + ft), self.vcol("ssd_cb", ft), V)
                k.act(c_[:], c_[:], AF.Silu, [c_], [c_])
                if ft == 4:
                    k.cp(bT[:], c_[:], [c_], [bT], en="pool")
                if ft == 5:
                    k.cp(cT[:], c_[:], [c_], [cT], en="pool")
                if ft <= 4:
                    for ch in range(NCH):
                        ps = self.next_pf()
                        k.tr(ps[:, 0:128], c_[:, ch * 128:(ch + 1) * 128], self.ident_f[:], [c_, self.ident_f], [ps])
                        dst = x_tm[:, ch, ft * 128:(ft + 1) * 128] if ft < 4 else b_tm[:, ch, :]
                        k.cp(dst, ps[:, 0:128], [ps], [x_tm if ft < 4 else b_tm], en=("act" if ch % 2 else "dve"))
        cs = k.sb("s_cs", [64, T])
        sc3t = k.sb("s_sc3", [64, 3, T])
        nega = k.sb("s_nega", [64, 1])
        _cm_stack = contextlib.ExitStack()
        k.barrier()
        k.stacks.append(_cm_stack)
        cmask = k.sb("s_cmask", [64, T + 1])

        class _V:
            def __init__(self, tl, j):
                self.tl, self.j, self.res, self.psum = tl, j, tl.res, False

            def __getitem__(self, key):
                if not isinstance(key, tuple):
                    key = (key,)
                return self.tl[(key[0], self.j) + tuple(key[1:])]
        sc3 = sc3t
        dtt = sc3t
        D0 = lambda *a: sc3t[(a[0], 0) + tuple(a[1:])] if a else sc3t[:, 0, :]
        k.memset(sc3t[:, 0, :], 0.0, [sc3t])
        k.dma("sp", sc3t[0:8, 0, :], self.zT[zo_m + 32:zo_m + 40, :], [self.zT], [sc3t])
        k.dma("sp", sc3t[32:40, 0, :], self.zT[zo_m + 40:zo_m + 48, :], [self.zT], [sc3t])
        k.dma("sp", cmask[:], self.cmask_d[:, :], (), [cmask])
        k.act(sc3t[:, 0, :], sc3t[:, 0, :], AF.Exp, [sc3t] + V, [sc3t], bias=self.vecs[0:64, self.vl.cols["ssd_dtb"][0]:self.vl.cols["ssd_dtb"][0] + 1])
        k.act(sc3t[:, 0, :], sc3t[:, 0, :], AF.Ln, [sc3t], [sc3t], bias=1.0)
        k.act(nega[:], self.vecs[0:64, self.vl.cols["ssd_alog"][0]:self.vl.cols["ssd_alog"][0] + 1], AF.Exp, V, [nega])
        k.ts(nega[:], nega[:], -1.0, None, ALU.mult, None, [nega], [nega])
        k.ts(sc3t[:, 2, :], sc3t[:, 0, :], nega[:, 0:1], None, ALU.mult, None, [sc3t, nega], [sc3t])
        k.scan(cs[0:32, :], cmask[0:32, 0:T], sc3t[0:32, 2, :], 0.0, [cmask, sc3t], [cs])
        k.scan(cs[32:64, :][:, ::-1], cmask[32:64, 1:T + 1][:, ::-1], sc3t[32:64, 2, :][:, ::-1], 0.0, [cmask, sc3t], [cs])
        k.barrier()
        k.stacks.pop()
        _cm_stack.close()
        c3 = lambda ap: ap.rearrange("p (c t) -> p c t", t=128)
        k.tt(c3(sc3[0:32, 1, :]), c3(cs[0:32, :])[:, :, 127:128].to_broadcast([32, NCH, 128]), c3(cs[0:32, :]),
             ALU.subtract, [cs], [sc3])
        k.tt(c3(sc3[32:64, 1, :]), c3(cs[32:64, :])[:, :, 0:1].to_broadcast([32, NCH, 128]), c3(cs[32:64, :]),
             ALU.subtract, [cs], [sc3])
        k.act(sc3[:, 1, :], sc3[:, 1, :], AF.Exp, [sc3], [sc3])
        k.tt(sc3[:, 1, :], sc3[:, 1, :], sc3[:, 0, :], ALU.mult, [sc3], [sc3])
        k.act(sc3[:, 2, :], cs[:], AF.Exp, [cs], [sc3])
        self.dump("ssd_cs", cs, cs[:, :], [64, T])
        self.dump("ssd_sc3", sc3t, sc3t[:, :, :], [64, 3, T])
        self.dump("ssd_xtm", x_tm, x_tm[:, :, :], [128, NCH, 512], BF16)
        self.dump("ssd_bT", bT, bT[:, :], [128, T], BF16)
        Hf = k.sb("s_H", [128, 4, 64])
        Hb = k.sb("s_Hb", [128, 4, 64], BF16)
        selg = k.sb("s_selg", [64, 128])
        k.dma("sp", selg[:], self.selg_d[:, :], (), [selg])
        sctm = [k.sb(f"s_sctm{i}", [128, 3, 64]) for i in range(2)]
        cstm = [k.sb(f"s_cstm{i}", [128, 64]) for i in range(2)]
        xd = [k.sb(f"s_xd{i}", [128, 8, 64], BF16) for i in range(2)]
        xdd = [k.sb(f"s_xdd{i}", [128, 8, 64], BF16) for i in range(2)]
        lt = [k.sb(f"s_lt{i}", [128, 4, 128]) for i in range(2)]
        mT = [k.sb(f"s_mT{i}", [128, 4, 128], BF16) for i in range(2)]
        yo = [k.sb(f"s_yo{i}", [128, 512]) for i in range(2)]
        yf = [k.sb("s_yf0", [128, 512])] * 2
        zg = [k.sb("s_zg0", [128, 512])] * 2
        et = k.sb("s_et", [128, 4])
        etr = k.sb("s_etr", [64, 4])
        h0t = k.sb("s_h0t", [64, 8, 64])
        fint = k.sb("s_fint", [64, 8, 64])
        junk = yf[0]
        ssq = k.sb("s_ssq", [128, 1])
        ob = [k.sb(f"s_ob{i}", [128, 4, 128], BF16) for i in range(2)]
        it = 0
        for (s0, ln, is_s) in cfg.seqs:
            si = s0 // ln
            chs = list(range(s0 // 128, (s0 + ln) // 128))
            for d in range(2):
                if is_s:
                    k.dma("sp", h0t[:], self.ssd_h0[l, d], (), [h0t])
                    for h in range(8):
                        ps = self.next_pf()
                        if h < 4:
                            k.tr(ps[0:64, 0:64], h0t[:, h, :], self.ident_f[0:64, 0:64], [h0t, self.ident_f], [ps])
                            k.cp(Hf[0:64, h, :], ps[0:64, 0:64], [ps], [Hf])
                        else:
                            k.tr(ps[:, 0:64], h0t[:, h - 1:h + 1, :].rearrange("p a n -> p (a n)"),
                                 self.ident_f[0:64, 0:64], [h0t, self.ident_f], [ps])
                            k.cp(Hf[64:128, h - 4, :], ps[64:128, 0:64], [ps], [Hf])
                else:
                    k.memset(Hf[:], 0.0, [Hf], en="dve")
                k.cp(Hb[:], Hf[:], [Hf], [Hb])
                for ch in (chs if d == 0 else chs[::-1]):
                    it += 1
                    tsl = slice(ch * 128, (ch + 1) * 128)
                    st_, ct_, xd_, xdd_ = sctm[it % 2], cstm[it % 2], xd[it % 2], xdd[it % 2]
                    pt = self.next_pf()
                    for j in range(3):
                        k.tr(pt[:, j * 64:(j + 1) * 64], sc3[:, j, tsl], self.ident_f[0:64, 0:64], [sc3, self.ident_f], [pt])
                    k.tr(pt[:, 192:256], cs[:, tsl], self.ident_f[0:64, 0:64], [cs, self.ident_f], [pt])
                    k.cp(st_[:], pt[:, 0:192].rearrange("p (j r) -> p j r", j=3), [pt], [st_])
                    k.cp(ct_[:], pt[:, 192:256], [pt], [ct_])
                    r0 = d * 32
                    xv = x_tm[:, ch, :].rearrange("p (h q) -> p h q", h=8)
                    k.tt(xd_[:], xv, st_[:, 0, r0:r0 + 8].unsqueeze(2).to_broadcast([128, 8, 64]), ALU.mult,
                         [x_tm, st_], [xd_])
                    k.tt(xdd_[:], xv, st_[:, 1, r0:r0 + 8].unsqueeze(2).to_broadcast([128, 8, 64]), ALU.mult,
                         [x_tm, st_], [xdd_], en="pool")
                    py = self.pf[4]
                    pyo = self.pf[5]
                    for g in range(2):
                        lt_, m_ = lt[g], mT[g]
                        pb_ = self.next_pf()
                        for hh in range(4):
                            h = g * 4 + hh
                            k.mm(pb_[:, hh * 128:(hh + 1) * 128], selm[:, d * 8 + h, :], cs[:, tsl], True, True,
                                 [selm, cs], [pb_])
                        k.tt(lt_[:], pb_[:, :].rearrange("p (h t) -> p h t", h=4),
                             ct_[:, r0 + g * 4:r0 + g * 4 + 4].unsqueeze(2).to_broadcast([128, 4, 128]), ALU.subtract,
                             [pb_, ct_], [lt_])
                        k.tt(lt_[:], lt_[:], maskneg[:, d:d + 1, :].to_broadcast([128, 4, 128]), ALU.add,
                             [lt_, maskneg], [lt_])
                        k.act(lt_[:], lt_[:], AF.Exp, [lt_], [lt_])
                        pg = self.next_pf()
                        k.mm(pg[:, 0:128], bT[g * 64:(g + 1) * 64, tsl], cT[g * 64:(g + 1) * 64, tsl], True, True,
                             [bT, cT], [pg])
                        k.tt(m_[:], lt_[:], pg[:, 0:128].unsqueeze(1).to_broadcast([128, 4, 128]), ALU.mult,
                             [lt_, pg], [m_])
                        for hh in range(4):
                            h = g * 4 + hh
                            k.mm(py[:, h * 64:(h + 1) * 64], m_[:, hh, :], xd_[:, h, :], True, True, [m_, xd_], [py])
                        k.mm(pyo[:, g * 256:(g + 1) * 256], cT[g * 64:(g + 1) * 64, tsl],
                             Hb[g * 64:(g + 1) * 64, :, :], True, True, [cT, Hb], [pyo])
                    y_ = yo[it % 2]
                    k.tt(y_[:].rearrange("p (h q) -> p h q", h=8), pyo[:, :].rearrange("p (h q) -> p h q", h=8),
                         st_[:, 2, r0:r0 + 8].unsqueeze(2).to_broadcast([128, 8, 64]), ALU.mult, [pyo, st_], [y_])
                    k.tt(y_[:], y_[:], py[:, :], ALU.add, [y_, py], [y_])
                    ps_ = self.next_pf()
                    for g in range(2):
                        k.mm(ps_[g * 64:(g + 1) * 64, 0:256], b_tm[:, ch, g * 64:(g + 1) * 64],
                             xdd_[:, g * 4:(g + 1) * 4, :], True, True, [b_tm, xdd_], [ps_])
                    e_tok = ch * 128 + (127 if d == 0 else 0)
                    k.ts(etr[:], seld[:, d, 0:4], cs[:, e_tok:e_tok + 1], None, ALU.mult, None, [seld, cs], [etr])
                    pe_ = self.next_pf()
                    k.mm(pe_[:, 0:4], selg[:, :], etr[:], True, True, [selg, etr], [pe_])
                    k.act(et[:], pe_[:, 0:4], AF.Exp, [pe_], [et])
                    k.tt(Hf[:], Hf[:], et[:].unsqueeze(2).to_broadcast([128, 4, 64]), ALU.mult, [Hf, et], [Hf])
                    k.tt(Hf[:], Hf[:], ps_[:, 0:256].rearrange("p (h q) -> p h q", h=4), ALU.add, [Hf, ps_], [Hf])
                    k.cp(Hb[:], Hf[:], [Hf], [Hb], en="pool")
                    if it == 1:
                        self.dump("ssd_lt", lt[1], lt[1][:, :, :], [128, 4, 128])
                        self.dump("ssd_mT", mT[1], mT[1][:, :, :], [128, 4, 128], BF16)
                        self.dump("ssd_y0", y_, y_[:, :], [128, 512])
                        self.dump("ssd_H1", Hf, Hf[:, :, :], [128, 4, 64])
                        self.dump("ssd_sctm", st_, st_[:, :, :], [128, 3, 64])
                    if d == 0:
                        k.dma("pool", self.ssd_y[tsl, :], y_[:], [y_], [self.ssd_y])
                    else:
                        f_, z_ = yf[it % 2], zg[it % 2]
                        k.dma("sp", f_[:], self.ssd_y[tsl, :], [self.ssd_y], [f_])
                        k.dma("sp", z_[:], self.zg_tm[tsl, :], [self.zg_tm], [z_])
                        k.tt(y_[:], y_[:], f_[:], ALU.add, [y_, f_], [y_])
                        k.tt(f_[:].rearrange("p (h q) -> p h q", h=8), xv,
                             bcv[:, 0:8].unsqueeze(2).to_broadcast([128, 8, 64]), ALU.mult, [x_tm, bcv], [f_])
                        k.tt(y_[:], y_[:], f_[:], ALU.add, [y_, f_], [y_])
                        k.act(z_[:], z_[:], AF.Silu, [z_], [z_])
                        k.tt(y_[:], y_[:], z_[:], ALU.mult, [y_, z_], [y_])
                        k.act(junk[:], y_[:], AF.Square, [y_], [junk, ssq], accum_out=ssq[:])
                        self.rstd_from_sum(ssq[:], ssq[:], 512, (ssq, ssq))
                        k.stt(y_[:], y_[:], ssq[:, 0:1], bcv[:, 8:8 + 512], ALU.mult, ALU.mult, [y_, ssq, bcv], [y_])
                        o_ = ob[it % 2]
                        for ft in range(4):
                            pt2 = self.next_pf()
                            k.tr(pt2[:, 0:128], y_[:, ft * 128:(ft + 1) * 128], self.ident_f[:], [y_, self.ident_f], [pt2])
                            k.cp(o_[:, ft, :], pt2[:, 0:128], [pt2], [o_], en=("act" if ft % 2 else "dve"))
                        k.dma("pool", self.o_scr[2, :, tsl].rearrange("(c p) t -> p c t", p=128), o_[:], [o_], [self.o_scr])
                if not is_s:
                    for h in range(8):
                        ps = self.next_pf()
                        g_ = h // 4
                        k.tr(ps[0:64, 0:64], Hf[g_ * 64:(g_ + 1) * 64, h % 4, :],
                             self.ident_f[g_ * 64:(g_ + 1) * 64, g_ * 64:(g_ + 1) * 64], [Hf, self.ident_f], [ps])
                        k.cp(fint[:, h, :], ps[0:64, 0:64], [ps], [fint])
                    k.dma("pool", self.o_ssd_fin[l, si, d], fint[:], [fint], [self.o_ssd_fin])


    def phase_rwkv(self, l):
        cfg, k = self.cfg, self.k
        T, TP = cfg.T, cfg.TP
        TS = 512
        V = [self.vecs]
        NC64 = T // 64
        zr = {n: ZROWS[n][0] for n in ("rw_r", "rw_k", "rw_v", "rw_lora", "rw_gate")}
        pc_all = k.sb("rw_pc", [128, 4, 2, NC64])
        blk = k.sb("rw_blk", [128, 128])
        k.dma("sp", blk[:], self.blk64_d[:, :], (), [blk])
        vc = lambda name, j: self.vcol(name, j)
        with k.phase():
            lw = k.sb("rw_lw", [128, 2, 512])
            cm = k.sb("rw_cm", [128, TS + 1])
            omk = k.sb("rw_omk", [128, 4])
            k.dma("sp", lw[:], self.rw_lw[l], (), [lw])
            k.dma("sp", cm[:], self.cmask64_d[:, :], (), [cm])
            k.ts(omk[:], self.vcol("rw_ka", 0, 4), -1.0, 1.0, ALU.mult, ALU.add, V, [omk])
            xes = [k.sb(f"rw_xe{i}", [128, TS + 2]) for i in range(3)]
            shs = [k.sb(f"rw_sh{i}", [128, TS]) for i in range(3)]
            mixn = [0]
            lora = k.sb("rw_lora", [128, TS])
            ostg = [[k.sb(f"rw_ostg{d_}_{j_}", [128, TS]) for j_ in range(6)] for d_ in range(2)]
            vstg = [k.sb(f"rw_vstg{i}", [128, TS]) for i in range(2)]
            bstg = [k.sb(f"rw_bstg{i}", [128, TS]) for i in range(2)]
            names = ("r", "k", "v", "kk", "a", "kd", "be", "lw_", "cl", "e1", "e2", "t1", "t2", "bon")
            tl = {n: k.sb("rw_" + n, [128, TS]) for n in names}

            def mix(dst, zrow, mucol, a, is_s, s0, ln):
                b_ = a + TS
                mixn[0] += 1
                xe, sh = xes[mixn[0] % 3], shs[mixn[0] % 3]
                k.dma("sp", xe[:, 1:TS + 1], self.zT[zrow:zrow + 128, a:b_], [self.zT], [xe])
                if is_s:
                    if a > s0:
                        k.dma("sp", xe[:, 0:1], self.zT[zrow:zrow + 128, a - 1:a], [self.zT], [xe], allow_slow_non_contiguous=True)
                    else:
                        k.memset(xe[:, 0:1], 0.0, [xe])
                    if b_ < s0 + ln:
                        k.dma("sp", xe[:, TS + 1:TS + 2], self.zT[zrow:zrow + 128, b_:b_ + 1], [self.zT], [xe], allow_slow_non_contiguous=True)
                    else:
                        k.memset(xe[:, TS + 1:TS + 2], 0.0, [xe])
                    k.tt(sh[:], xe[:, 0:TS], xe[:, 2:TS + 2], ALU.add, [xe], [sh])
                else:
                    ns = TS // ln
                    k.memset(sh[:], 0.0, [sh], en="dve")
                    v3 = lambda ap: ap.rearrange("p (s t) -> p s t", s=ns)
                    xv = v3(xe[:, 1:TS + 1])
                    k.tt(v3(sh[:])[:, :, 1:ln], v3(sh[:])[:, :, 1:ln], xv[:, :, 0:ln - 1], ALU.add, [sh, xe], [sh])
                    k.tt(v3(sh[:])[:, :, 0:ln - 1], v3(sh[:])[:, :, 0:ln - 1], xv[:, :, 1:ln], ALU.add, [sh, xe], [sh])
                k.stt(sh[:], sh[:], 0.5, xe[:, 1:TS + 1], ALU.mult, ALU.subtract, [sh, xe], [sh])
                k.stt(dst, sh[:], mucol, xe[:, 1:TS + 1], ALU.mult, ALU.add, [sh, xe] + V, [dst_tl[0]])

            dst_tl = [None]
            for a in range(0, T, TS):
                is_s = a >= TP
                s0, ln = (TP, cfg.ls) if is_s else (0, cfg.lp)
                c64 = a // 64
                dst_tl[0] = lora
                mix(lora[:], zr["rw_lora"], vc("rw_mu", 12), a, is_s, s0, ln)
                k.act(lora[0:64, :], lora[0:64, :], AF.Tanh, [lora], [lora])
                for ft in range(4):
                    R, K_, KK = tl["r"], tl["k"], tl["kk"]
                    Vv = vstg[ft % 2]
                    tl["bon"] = bstg[ft % 2]
                    for (dst, nm, mi) in ((R, "rw_r", 0), (K_, "rw_k", 4), (Vv, "rw_v", 8)):
                        dst_tl[0] = dst
                        mix(dst[:], zr[nm] + ft * 128, vc("rw_mu", mi + ft), a, is_s, s0, ln)
                    k.dma("pool", self.rw_v[ft * 128:(ft + 1) * 128, a:a + TS], Vv[:], [Vv], [self.rw_v])
                    t1, t2 = tl["t1"], tl["t2"]
                    k.ts(KK[:], K_[:], vc("rw_kk", ft), None, ALU.mult, None, [K_] + V, [KK])
                    k.tt(t1[:], KK[:], KK[:], ALU.mult, [KK], [t1])
                    ps = self.next_pf()
                    k.mm(ps[:, :], blk[:, :], t1[:], True, True, [blk, t1], [ps])
                    k.ts(t2[:], ps[:, :], 1e-12, None, ALU.add, None, [ps], [t2])
                    k.act(t2[:], t2[:], AF.Sqrt, [t2], [t2])
                    k.op("dve", lambda e: e.reciprocal(out=t2[:], in_=t2[:]), [t2], [t2])
                    k.tt(KK[:], KK[:], t2[:], ALU.mult, [KK, t2], [KK])
                    for d in range(2):
                        A_, KD, BE, LW, CL, E1, E2, BON = (tl[n] for n in ("a", "kd", "be", "lw_", "cl", "e1", "e2", "bon"))
                        ps = self.next_pf()
                        k.mm(ps[:, :], lw[0:64, d, ft * 128:(ft + 1) * 128], lora[0:64, :], True, True, [lw, lora], [ps])
                        k.act(LW[:], ps[:, :], AF.Sigmoid, [ps] + V, [LW], bias=vc("rw_w0", d * 4 + ft))
                        k.ts(LW[:], LW[:], -0.6065306597126334, None, ALU.mult, None, [LW], [LW])
                        ps2 = self.next_pf()
                        k.mm(ps2[:, :], lw[64:128, d, ft * 128:(ft + 1) * 128], lora[64:128, :], True, True, [lw, lora], [ps2])
                        k.act(A_[:], ps2[:, :], AF.Sigmoid, [ps2] + V, [A_], bias=vc("rw_a0", d * 4 + ft))
                        k.ts(KD[:], A_[:], vc("rw_ka", ft), omk[:, ft:ft + 1], ALU.mult, ALU.add, [A_, omk] + V, [KD])
                        k.tt(KD[:], KD[:], K_[:], ALU.mult, [KD, K_], [KD])
                        k.tt(BE[:], KK[:], A_[:], ALU.mult, [KK, A_], [BE])
                        k.stt(t1[:], R[:], vc("rw_rk", ft), KD[:], ALU.mult, ALU.mult, [R, KD] + V, [t1])
                        ps3 = self.next_pf()
                        k.mm(ps3[:, :], blk[:, :], t1[:], True, True, [blk, t1], [ps3])
                        if d == 0:
                            k.tt(BON[:], ps3[:, :], Vv[:], ALU.mult, [ps3, Vv], [BON])
                        else:
                            k.tt(t1[:], ps3[:, :], Vv[:], ALU.mult, [ps3, Vv], [t1])
                            k.tt(BON[:], BON[:], t1[:], ALU.add, [BON, t1], [BON])
                            k.dma("pool", self.rw_bonus[ft * 128:(ft + 1) * 128, a:a + TS], BON[:], [BON], [self.rw_bonus])
                        if d == 0:
                            k.scan(CL[:], cm[:, 0:TS], LW[:], 0.0, [cm, LW], [CL])
                        else:
                            k.scan(CL[:, ::-1], cm[:, 1:TS + 1][:, ::-1], LW[:, ::-1], 0.0, [cm, LW], [CL])
                        c3 = lambda ap: ap.rearrange("p (c t) -> p c t", t=64)
                        e_ = 63 if d == 0 else 0
                        ce = c3(CL[:])[:, :, e_:e_ + 1]
                        k.act(pc_all[:, ft, d, c64:c64 + TS // 64], ce.rearrange("p c o -> p (c o)"), AF.Exp, [CL], [pc_all])
                        O = lambda j: self.rw_ops[d, j, ft * 128:(ft + 1) * 128, a:a + TS]
                        k.act(E1[:], CL[:], AF.Exp, [CL], [E1])
                        k.tt(ostg[d][0][:], R[:], E1[:], ALU.mult, [R, E1], [ostg[d][0]])
                        k.dma("pool", O(0), ostg[d][0][:], [ostg[d][0]], [self.rw_ops])
                        k.act(E2[:], CL[:], AF.Exp, [CL], [E2], scale=-1.0)
                        k.tt(ostg[d][1][:], KD[:], E2[:], ALU.mult, [KD, E2], [ostg[d][1]])
                        k.dma("pool", O(1), ostg[d][1][:], [ostg[d][1]], [self.rw_ops])
                        k.tt(ostg[d][2][:], BE[:], E2[:], ALU.mult, [BE, E2], [ostg[d][2]])
                        k.dma("pool", O(2), ostg[d][2][:], [ostg[d][2]], [self.rw_ops])
                        k.tt(E1[:], CL[:], LW[:], ALU.subtract, [CL, LW], [E1])
                        k.act(E1[:], E1[:], AF.Exp, [E1], [E1])
                        k.tt(ostg[d][3][:], KK[:], E1[:], ALU.mult, [KK, E1], [ostg[d][3]])
                        k.dma("pool", O(3), ostg[d][3][:], [ostg[d][3]], [self.rw_ops])
                        k.tt(c3(E2[:]), ce.to_broadcast([128, TS // 64, 64]), c3(CL[:]), ALU.subtract, [CL], [E2])
                        k.act(E2[:], E2[:], AF.Exp, [E2], [E2])
                        k.tt(ostg[d][4][:], KD[:], E2[:], ALU.mult, [KD, E2], [ostg[d][4]])
                        k.dma("pool", O(4), ostg[d][4][:], [ostg[d][4]], [self.rw_ops])
                        k.tt(ostg[d][5][:], BE[:], E2[:], ALU.mult, [BE, E2], [ostg[d][5]])
                        k.dma("pool", O(5), ostg[d][5][:], [ostg[d][5]], [self.rw_ops])
        if "rw_prep_only" in cfg.debug:
            return
        with k.phase():
            msk = k.sb("rw_msk", [128, 2, 3, 128])
            k.dma("sp", msk[:], self.rw_mask_d[:, :, :, :], (), [msk])
            ops = [[k.sb(f"rw_op{j}_{i}", [128, 4, 128]) for j in range(7)] for i in range(2)]
            pmask = k.sb("rw_pmask", [128, 2])
            k.dma("sp", pmask[:], self.pmask_d[:, :], (), [pmask])
            mk = {nm: [k.sb(f"rw_mk{nm}{p_}", [128, 4, 128]) for p_ in range(2)] for nm in ("A", "B", "R")}
            NDT = BF16 if RW_BF16 else F32
            if RW_BF16:
                mkb = {nm: [k.sb(f"rw_mkb{nm}{p_}", [128, 4, 128], NDT) for p_ in range(2)] for nm in ("A", "B", "R")}
                Lb = {nm: k.sb(f"rw_Lb{nm}", [128, 4, 128], NDT) for nm in ("A", "B", "K")}
                TTb = k.sb("rw_TTb", [128, 8, 128], NDT)
            tms = [k.sb(f"rw_tm{j}", [128, 512]) for j in range(3)]
            NT = [k.sb(f"rw_NT{i}", [128, 8, 128], BF16 if RW_BF16 else F32) for i in range(2)]
            NN = [k.sb(f"rw_NN{i}", [128, 8, 128], BF16 if RW_BF16 else F32) for i in range(2)]
            TT = k.sb("rw_TT", [128, 8, 128])
            AakT = k.sb("rw_AakT", [128, 8, 128])
            ODT = BF16 if RW_OBF16 else F32
            ArkT = k.sb("rw_ArkT", [128, 8, 128], ODT)
            ArbT = k.sb("rw_ArbT", [128, 8, 128], ODT)
            mkRb = [k.sb(f"rw_mkRb{p_}", [128, 4, 128], ODT) for p_ in range(2)]
            Ktb = k.sb("rw_Ktb", [128, 4, 128], ODT)
            Btb = k.sb("rw_Btb", [128, 4, 128], ODT)
            Vtb = k.sb("rw_Vtb", [128, 512], ODT)
            Utb = k.sb("rw_Utb", [128, 512], ODT)
            STb = k.sb("rw_STb", [128, 4, 64], ODT)
            W1 = k.sb("rw_W1", [128, 512])
            Wt = k.sb("rw_Wt", [128, 512])
            Ut = k.sb("rw_Ut", [128, 512])
            ST = k.sb("rw_ST", [128, 4, 64])
            h0t = k.sb("rw_h0t", [64, 8, 64])
            fint = k.sb("rw_fint", [64, 8, 64])
            osb = [k.sb(f"rw_osb{i}", [128, 4, 128]) for i in range(2)]
            it = 0
            for (s0, ln, is_s) in cfg.seqs:
                si = s0 // ln
                wins = list(range(s0 // 128, (s0 + ln) // 128))
                for d in range(2):
                    if is_s:
                        k.dma("sp", h0t[:], self.rw_h0[l, d], (), [h0t])
                        for h in range(8):
                            ps = self.next_pf()
                            if h % 2 == 0:
                                k.tr(ps[0:64, 0:64], h0t[:, h, :], self.ident_f[0:64, 0:64], [h0t, self.ident_f], [ps])
                                k.cp(ST[0:64, h // 2, :], ps[0:64, 0:64], [ps], [ST])
                            else:
                                k.tr(ps[:, 0:64], h0t[:, h - 1:h + 1, :].rearrange("p a n -> p (a n)"),
                                     self.ident_f[0:64, 0:64], [h0t, self.ident_f], [ps])
                                k.cp(ST[64:128, h // 2, :], ps[64:128, 0:64], [ps], [ST])
                    else:
                        k.memset(ST[:], 0.0, [ST], en="dve")
                    wlist = (wins if d == 0 else wins[::-1])

                    def _load(op_, w_):
                        w0_ = w_ * 128
                        for j in range(6):
                            k.dma("sp", op_[j][:], self.rw_ops[d, j, :, w0_:w0_ + 128].rearrange("(f p) t -> p f t", p=128),
                                  [self.rw_ops], [op_[j]])
                        k.dma("sp", op_[6][:], self.rw_v[:, w0_:w0_ + 128].rearrange("(f p) t -> p f t", p=128),
                              [self.rw_v], [op_[6]])
                    _load(ops[(it + 1) % 2], wlist[0])
                    for wi, w in enumerate(wlist):
                        it += 1
                        w0 = w * 128
                        op = ops[it % 2]
                        if wi + 1 < len(wlist):
                            _load(ops[(it + 1) % 2], wlist[wi + 1])
                        Rt, Kt, Bt, At, Kh, Bh, Vf = op
                        for (src, dst, neg) in ((Vf, tms[0], False), (Kh, tms[1], False), (Bh, tms[2], True)):
                            ps = self.next_pf()
                            for ft in range(4):
                                k.tr(ps[:, ft * 128:(ft + 1) * 128], src[:, ft, :], self.ident_f[:], [src, self.ident_f], [ps])
                            if neg:
                                k.ts(dst[:], ps[:, :], -1.0, None, ALU.mult, None, [ps], [dst])
                            else:
                                k.cp(dst[:], ps[:, :], [ps], [dst], en="act")
                        Vtm, Khtm, Bhtm = tms
                        hb = lambda h: (h % 2) * 64
                        for nm, src in (("A", At), ("B", Bt)):
                            for p_ in range(2):
                                k.ts(mk[nm][p_][:], src[:], pmask[:, p_:p_ + 1], None, ALU.mult, None, [src, pmask], [mk[nm][p_]],
                                     en=("pool" if p_ else "dve"))
                        for p_ in range(2):
                            k.ts(mkRb[p_][:], Rt[:], pmask[:, p_:p_ + 1], None, ALU.mult, None, [Rt, pmask], [mkRb[p_]],
                                 en=("pool" if p_ else "dve"))
                        k.cp(Ktb[:], Kt[:], [Kt], [Ktb], en="act")
                        k.cp(Btb[:], Bt[:], [Bt], [Btb], en="pool")
                        k.cp(Vtb[:], tms[0][:], [tms[0]], [Vtb], en="act")
                        if RW_BF16:
                            for nm, src in (("A", At), ("B", Bt), ("R", Rt)):
                                for p_ in range(2):
                                    k.cp(mkb[nm][p_][:], mk[nm][p_][:], [mk[nm][p_]], [mkb[nm][p_]], en=("pool" if p_ else "act"))
                            for nm, src in (("A", At), ("B", Bt), ("K", Kt)):
                                k.cp(Lb[nm][:], src[:], [src], [Lb[nm]], en=("pool" if nm == "B" else "act"))
                        else:
                            mkb = dict(A=mk["A"], B=mk["B"], R=mkRb)
                            Lb = dict(A=At, B=Bt, K=Kt, Kr=Ktb, Br=Btb)
                        specs = ((Lb["B"], "A", NT[0], 0, -1.0), (Lb["A"], "B", NN[0], 1, -1.0), (Lb["K"], "A", AakT, 0, 1.0),
                                 (Lb["Kr"], "R", ArkT, 2, 1.0), (Lb["Br"], "R", ArbT, 2, -1.0))
                        for (L_, rn, dst, mi, sg) in specs:
                            for half in range(2):
                                ps = self.next_pf()
                                for hh in range(4):
                                    h = half * 4 + hh
                                    R_ = mkb[rn][h % 2]
                                    k.mm(ps[:, hh * 128:(hh + 1) * 128], L_[:, h // 2, :], R_[:, h // 2, :],
                                         True, True, [L_, R_], [ps])
                                k.stt(dst[:, half * 4:(half + 1) * 4, :], ps[:, :].rearrange("p (h t) -> p h t", h=4), sg,
                                      msk[:, d, mi:mi + 1, :].to_broadcast([128, 4, 128]), ALU.mult, ALU.mult, [ps, msk], [dst])
                        k.tt(TT[:], NT[0][:], self.ident_f[:, :].unsqueeze(1).to_broadcast([128, 8, 128]), ALU.add,
                             [NT[0], self.ident_f], [TT])
                        if RW_BF16:
                            k.cp(TTb[:], TT[:], [TT], [TTb], en="pool")
                        else:
                            TTb = TT
                        cur = 0
                        for lev in range(1, 6):
                            nxt = 1 - cur
                            for half in range(2):
                                hs = slice(half * 4, (half + 1) * 4)
                                pn = self.next_pf()
                                for hh in range(4):
                                    h = half * 4 + hh
                                    k.mm(pn[:, hh * 128:(hh + 1) * 128], NT[cur][:, h, :], NN[cur][:, h, :], True, True,
                                         [NT[cur], NN[cur]], [pn])
                                k.cp(NN[nxt][:, hs, :], pn[:, :].rearrange("p (h t) -> p h t", h=4), [pn], [NN[nxt]], en="act")
                                if lev < 5:
                                    pt_ = self.next_pf()
                                    for hh in range(4):
                                        h = half * 4 + hh
                                        k.mm(pt_[:, hh * 128:(hh + 1) * 128], NN[cur][:, h, :], NT[cur][:, h, :], True, True,
                                             [NT[cur], NN[cur]], [pt_])
                                    k.cp(NT[nxt][:, hs, :], pt_[:, :].rearrange("p (h t) -> p h t", h=4), [pt_], [NT[nxt]], en="act")
                                pp = self.next_pf()
                                for hh in range(4):
                                    h = half * 4 + hh
                                    k.mm(pp[:, hh * 128:(hh + 1) * 128], NN[nxt][:, h, :], TTb[:, h, :], True, True,
                                         [NN[nxt], TTb], [pp])
                                k.tt(TT[:, hs, :], TT[:, hs, :], pp[:, :].rearrange("p (h t) -> p h t", h=4), ALU.add,
                                     [TT, pp], [TT])
                                if lev < 5 and RW_BF16:
                                    k.cp(TTb[:, hs, :], TT[:, hs, :], [TT], [TTb], en="pool")
                            cur = nxt
                        pw = self.next_pf()
                        for h in range(8):
                            k.mm(pw[:, h * 64:(h + 1) * 64], AakT[:, h, :], Vtm[:, h * 64:(h + 1) * 64], True, True,
                                 [AakT, Vtm], [pw])
                        k.cp(W1[:], pw[:, :], [pw], [W1])
                        po = self.pf[4 + it % 2]
                        o_ = osb[it % 2]
                        for cb in ((0, 64) if d == 0 else (64, 0)):
                            cs_ = slice(cb, cb + 64)
                            c64i = (w0 + cb) // 64
                            px = self.next_pf()
                            for h in range(8):
                                b0 = hb(h)
                                k.mm(px[cs_, h * 64:(h + 1) * 64], mk["A"][h % 2][:, h // 2, cs_], ST[:, h // 2, :],
                                     True, True, [mk["A"][h % 2], ST], [px])
                            k.tt(Wt[cs_, :], W1[cs_, :], px[cs_, :], ALU.add, [W1, px], [Wt])
                            pu = self.next_pf()
                            for h in range(8):
                                k.mm(pu[cs_, h * 64:(h + 1) * 64], TT[cs_, h, cs_], Wt[cs_, h * 64:(h + 1) * 64], True, True,
                                     [TT, Wt], [pu])
                            k.cp(Ut[cs_, :], pu[cs_, :], [pu], [Ut])
                            k.cp(STb[:], ST[:], [ST], [STb], en="pool")
                            k.cp(Utb[cs_, :], Ut[cs_, :], [Ut], [Utb], en="act")
                            pS = self.next_pf()
                            for h in range(8):
                                b0 = hb(h)
                                ft = h // 2
                                sreg = pS[b0:b0 + 64, ft * 64:(ft + 1) * 64]
                                k.mm(sreg, Khtm[cs_, h * 64:(h + 1) * 64], Vtm[cs_, h * 64:(h + 1) * 64], True, False,
                                     [Khtm, Vtm], [pS])
                                k.mm(sreg, Bhtm[cs_, h * 64:(h + 1) * 64], Ut[cs_, h * 64:(h + 1) * 64], False, True,
                                     [Bhtm, Ut], [pS])
                            for h in range(8):
                                b0 = hb(h)
                                ft = h // 2
                                oreg = po[b0:b0 + 64, ft * 128 + cb:ft * 128 + cb + 64]
                                k.mm(oreg, STb[:, ft, :], mkRb[h % 2][:, ft, cs_], True, False, [STb, mkRb[h % 2]], [po])
                                k.mm(oreg, Vtb[cs_, h * 64:(h + 1) * 64], ArkT[cs_, h, cs_], False, False, [Vtb, ArkT], [po])
                                k.mm(oreg, Utb[cs_, h * 64:(h + 1) * 64], ArbT[cs_, h, cs_], False, True, [Utb, ArbT], [po])
                            k.tt(ST[:], ST[:], pc_all[:, :, d, c64i:c64i + 1].to_broadcast([128, 4, 64]), ALU.mult,
                                 [ST, pc_all], [ST])
                            k.tt(ST[:], ST[:], pS[:, 0:256].rearrange("p (f i) -> p f i", f=4), ALU.add, [ST, pS], [ST])
                        k.cp(o_[:], po[:, :].rearrange("p (f t) -> p f t", f=4), [po], [o_], en="act")
                        k.dma("pool", self.rw_o[d, :, w0:w0 + 128].rearrange("(f p) t -> p f t", p=128), o_[:], [o_], [self.rw_o])
                    if not is_s:
                        for h in range(8):
                            b0 = (h % 2) * 64
                            ps = self.next_pf()
                            k.tr(ps[0:64, 0:64], ST[b0:b0 + 64, h // 2, :], self.ident_f[b0:b0 + 64, b0:b0 + 64],
                                 [ST, self.ident_f], [ps])
                            k.cp(fint[:, h, :], ps[0:64, 0:64], [ps], [fint])
                        k.dma("pool", self.o_rw_fin[si, l, d].rearrange("h i j -> i h j"), fint[:], [fint], [self.o_rw_fin])
        with k.phase():
            of = [k.sb(f"rw_of{i}", [128, TS]) for i in range(2)]
            ob_ = [k.sb(f"rw_obb{i}", [128, TS]) for i in range(2)]
            bo = [k.sb(f"rw_bo{i}", [128, TS]) for i in range(2)]
            gt = [k.sb(f"rw_gt{i}", [128, TS]) for i in range(2)]
            xc = [k.sb(f"rw_xc{i}", [128, TS]) for i in range(2)]
            sq = [k.sb(f"rw_sq{i}", [128, TS]) for i in range(2)]
            oo = [k.sb(f"rw_oo{i}", [128, TS], BF16) for i in range(2)]
            n = 0
            for a in range(0, T, TS):
                for ft in range(4):
                    n += 1
                    i = n % 2
                    rs_ = slice(ft * 128, (ft + 1) * 128)
                    k.dma("sp", of[i][:], self.rw_o[0, rs_, a:a + TS], [self.rw_o], [of[i]])
                    k.dma("sp", ob_[i][:], self.rw_o[1, rs_, a:a + TS], [self.rw_o], [ob_[i]])
                    k.dma("sp", bo[i][:], self.rw_bonus[rs_, a:a + TS], [self.rw_bonus], [bo[i]])
                    k.dma("sp", gt[i][:], self.zT[zr["rw_gate"] + ft * 128:zr["rw_gate"] + (ft + 1) * 128, a:a + TS],
                          [self.zT], [gt[i]])
                    k.tt(of[i][:], of[i][:], ob_[i][:], ALU.add, [of[i], ob_[i]], [of[i]])
                    pm = self.next_pf()
                    k.mm(pm[:, :], blk[:, :], of[i][:], True, True, [blk, of[i]], [pm])
                    k.stt(xc[i][:], pm[:, :], -1.0 / 64, of[i][:], ALU.mult, ALU.add, [pm, of[i]], [xc[i]])
                    k.tt(sq[i][:], xc[i][:], xc[i][:], ALU.mult, [xc[i]], [sq[i]])
                    pv = self.next_pf()
                    k.mm(pv[:, :], blk[:, :], sq[i][:], True, True, [blk, sq[i]], [pv])
                    k.ts(sq[i][:], pv[:, :], 1.0 / 64, 64e-5, ALU.mult, ALU.add, [pv], [sq[i]])
                    k.act(sq[i][:], sq[i][:], AF.Sqrt, [sq[i]], [sq[i]])
                    k.op("dve", lambda e: e.reciprocal(out=sq[i][:], in_=sq[i][:]), [sq[i]], [sq[i]])
                    k.tt(xc[i][:], xc[i][:], sq[i][:], ALU.mult, [xc[i], sq[i]], [xc[i]])
                    k.ts(xc[i][:], xc[i][:], vc("rw_ln_g", ft), vc("rw_ln_b", ft), ALU.mult, ALU.add, [xc[i]] + V, [xc[i]])
                    k.tt(xc[i][:], xc[i][:], bo[i][:], ALU.add, [xc[i], bo[i]], [xc[i]])
                    k.act(gt[i][:], gt[i][:], AF.Silu, [gt[i]], [gt[i]])
                    k.tt(oo[i][:], xc[i][:], gt[i][:], ALU.mult, [xc[i], gt[i]], [oo[i]])
                    k.dma("pool", self.o_scr[0, rs_, a:a + TS], oo[i][:], [oo[i]], [self.o_scr])


def host_weights(inp):
    w_in = np.asarray(inp["w_in"], np.float32)
    cols = []
    o = _IW
    cols.append(w_in[:, :, o["rw_pre"]:o["rw_pre"] + 1664])
    cols.append(w_in[:, :, o["rw_gate"]:o["rw_gate"] + 512])
    cols.append(w_in[:, :, o["mla_q"]:o["mla_q"] + 384])
    cols.append(w_in[:, :, o["mla_ckv"]:o["mla_ckv"] + 256])
    misc = np.zeros((NL, D, 128), np.float32)
    misc[:, :, 0:32] = w_in[:, :, o["mla_krope"]:o["mla_krope"] + 32]
    misc[:, :, 32:48] = w_in[:, :, o["ssd_dt"]:o["ssd_dt"] + 16]
    cols.append(misc)
    cols.append(w_in[:, :, o["mla_gate"]:o["mla_gate"] + 512])
    cols.append(w_in[:, :, o["ssd_xbc"]:o["ssd_xbc"] + 768])
    cols.append(w_in[:, :, o["lru_x"]:o["lru_x"] + 512])
    cols.append(w_in[:, :, o["lru_gate"]:o["lru_gate"] + 512])
    cols.append(w_in[:, :, o["ssd_gate"]:o["ssd_gate"] + 512])
    w_in_a = np.ascontiguousarray(np.concatenate(cols, -1))
    assert w_in_a.shape[-1] == NZ_ALL
    sel2 = np.zeros((2, 2, 128), np.float32)
    sel2[0, 0] = 1.0
    sel2[1, 1] = 1.0
    lw = np.zeros((NL, 2, 2, 4, 128, 128), np.float32)
    for d in range(2):
        for gi, nm in enumerate(("lru_wa", "lru_wx")):
            w = np.asarray(inp[nm], np.float32)
            for ft in range(4):
                for kb in range(2):
                    lw[:, d, gi, ft, kb * 64:(kb + 1) * 64, kb * 64:(kb + 1) * 64] = w[:, d, ft * 2 + kb]
    lru_w = np.ascontiguousarray(lw.reshape(NL, 16, 128, 128).transpose(0, 2, 1, 3))
    kvu = np.asarray(inp["mla_kv_up"], np.float32).reshape(NL, 256, 8, 128)
    inv = 10000.0 ** (-np.arange(8, dtype=np.float32) / 8)
    W_pm = np.zeros((32, 96), np.float32)
    for m_ in range(32):
        sw = m_ + 8 if (m_ % 16) < 8 else m_ - 8
        W_pm[sw, 64 + m_] = 1.0
    selm = np.zeros((64, 16, 128), np.float32)
    seld = np.zeros((64, 2, 8), np.float32)
    selg = np.zeros((64, 128), np.float32)
    for d_ in range(2):
        for h_ in range(8):
            selm[d_ * 32 + h_, d_ * 8 + h_, :] = 1.0
            seld[d_ * 32 + h_, d_, h_ % 4] = 1.0
            selg[d_ * 32 + h_, (h_ // 4) * 64:(h_ // 4 + 1) * 64] = 1.0
    ii = np.arange(128)
    maskneg = np.zeros((128, 2, 128), np.float32)
    maskneg[:, 0, :] = np.where(ii[:, None] <= ii[None, :], 0.0, -30000.0)
    maskneg[:, 1, :] = np.where(ii[:, None] >= ii[None, :], 0.0, -30000.0)
    ssd_bc = np.concatenate([np.broadcast_to(np.asarray(inp["ssd_d"], np.float32)[:, None, :], (NL, 128, 8)),
                             np.broadcast_to(np.asarray(inp["ssd_norm_g"], np.float32)[:, None, :], (NL, 128, 512))], -1)
    rw_lw = np.concatenate([np.asarray(inp["rw_w2"], np.float32), np.asarray(inp["rw_a2"], np.float32)], 2)
    rw_lw = np.ascontiguousarray(rw_lw.transpose(0, 2, 1, 3))
    same = (ii[:, None] // 64) == (ii[None, :] // 64)
    rw_mask = np.zeros((128, 2, 3, 128), np.float32)
    rw_mask[:, 0, 0, :] = same & (ii[:, None] < ii[None, :])
    rw_mask[:, 0, 1, :] = same & (ii[:, None] > ii[None, :])
    rw_mask[:, 0, 2, :] = same & (ii[:, None] <= ii[None, :])
    rw_mask[:, 1, 0, :] = same & (ii[:, None] > ii[None, :])
    rw_mask[:, 1, 1, :] = same & (ii[:, None] < ii[None, :])
    rw_mask[:, 1, 2, :] = same & (ii[:, None] >= ii[None, :])
    cm64 = np.ones((128, 513), np.float32)
    cm64[:, ::64] = 0.0
    W = dict(
        pmask=np.ascontiguousarray(np.stack([(ii < 64), (ii >= 64)], 1).astype(np.float32)),
        rw_lw=rw_lw, rw_mask=rw_mask, blk64=same.astype(np.float32), cmask64=cm64,
        selm=selm, seld=seld, selg=selg, maskneg=maskneg, ssd_bc=np.ascontiguousarray(ssd_bc),
        q_up=np.ascontiguousarray(inp["mla_q_up"], np.float32),
        kv_up_k=np.ascontiguousarray(kvu[:, :, :, :64].reshape(NL, 256, 512)),
        kv_up_v=np.ascontiguousarray(kvu[:, :, :, 64:].reshape(NL, 256, 512)),
        pm96=W_pm,
        _inv=inv,
        w_in_m=np.ascontiguousarray(w_in[:, :, o["merge"]:]),
        w_branch=np.ascontiguousarray(inp["w_branch"], np.float32),
        w_out=np.ascontiguousarray(inp["w_out"], np.float32),
        lru_w=lru_w,
        ada_w=np.ascontiguousarray(inp["ada_w"], np.float32),
        ada_bg=np.ascontiguousarray(np.asarray(inp["ada_b"], np.float32)[:, None, 2048:]),
        w_in_a=w_in_a,
        vecs=host_vecs(inp).pack(),
        ident=np.eye(128, dtype=np.float32),
        sel2=sel2,
    )
    return W


def core_inputs(inp, W, core, cfg):
    b = core // 2
    xp = np.asarray(inp["x_prompt"], np.float32)[core * cfg.n_prompt:(core + 1) * cfg.n_prompt, :cfg.lp]
    xs = np.asarray(inp["x_sample"], np.float32)[b, :cfg.ls]
    x_all = np.ascontiguousarray(np.concatenate([xp.reshape(-1, D), xs], 0))
    cond = np.stack([np.asarray(inp["c_ctx"], np.float32), np.asarray(inp["c"], np.float32)[b]], 0)
    condT = np.ascontiguousarray(cond.reshape(2, 8, 128).transpose(2, 1, 0))
    m = dict(W)
    inv = m.pop("_inv")
    t = np.arange(cfg.ls)
    row = (t // 64).astype(np.float32)
    col = (t % 64).astype(np.float32)
    ar, ac = row[None, :] * inv[:, None], col[None, :] * inv[:, None]
    cosT = np.concatenate([np.cos(ar), np.cos(ar), np.cos(ac), np.cos(ac)], 0)
    sinT = np.concatenate([-np.sin(ar), np.sin(ar), -np.sin(ac), np.sin(ac)], 0)
    cm = np.ones((64, cfg.T + 1), np.float32)
    cm[:, ::128] = 0.0
    m["cmask"] = cm
    m["ssd_h0"] = np.ascontiguousarray(np.asarray(inp["state_ssd"], np.float32)[b].transpose(0, 1, 3, 2, 4))
    m["rw_h0"] = np.ascontiguousarray(np.asarray(inp["state_rwkv"], np.float32)[b].transpose(0, 1, 3, 2, 4))
    m["rope_cs"] = np.ascontiguousarray(np.stack([cosT, sinT], 0).astype(np.float32))
    m["cache_ckv"] = np.ascontiguousarray(np.asarray(inp["cache_mla_ckv"], np.float32)[b])
    m["cache_kr"] = np.ascontiguousarray(np.asarray(inp["cache_mla_krope"], np.float32)[b])
    sl = np.asarray(inp["state_lru"], np.float32)[b]
    lru_h0 = np.ascontiguousarray(sl.reshape(NL, 2, 4, 128).transpose(0, 3, 1, 2))
    m.update(x_all=x_all, condT=condT, lru_h0=lru_h0)
    return m


_PROG = {}


def kernel(**inp):
    cfg = Cfg(n_prompt=4, lp=256, ls=4096, debug=DEBUG_FLAGS)
    if "p" not in _PROG:
        prog = Prog(cfg)
        prog.build()
        _PROG["p"] = prog
    prog = _PROG["p"]
    W = host_weights(inp)
    in_maps = []
    for core in range(8):
        m = core_inputs(inp, W, core, cfg)
        in_maps.append({k_: np.ascontiguousarray(v) for k_, v in m.items() if k_ in prog.inputs})
    res = run_bass_kernel_spmd(prog.nc, in_maps, core_ids=list(range(8)))
    R = res.results
    npq = cfg.n_prompt
    y_prompt = np.concatenate([R[c]["y_all"][:cfg.TP].reshape(npq, cfg.lp, D) for c in range(8)], 0)
    y_sample = np.stack([R[2 * b]["y_all"][cfg.TP:] for b in range(4)], 0)
    ckv = np.concatenate([R[c]["o_ckv"].reshape(NL, 256, npq, cfg.lp).transpose(2, 0, 3, 1) for c in range(8)], 0)
    kr = np.concatenate([R[c]["o_kr"].reshape(NL, 32, npq, cfg.lp).transpose(2, 0, 3, 1) for c in range(8)], 0)
    if "o_rw_fin" in R[0]:
        rw = np.concatenate([R[c]["o_rw_fin"] for c in range(8)], 0)
    else:
        rw = np.zeros((32, NL, 2, 8, 64, 64), np.float32)
    if "o_ssd_fin" in R[0]:
        ssd = np.concatenate([R[c]["o_ssd_fin"].transpose(1, 0, 2, 4, 3, 5) for c in range(8)], 0)
    else:
        ssd = np.zeros((32, NL, 2, 8, 64, 64), np.float32)
    lru = np.concatenate([R[c]["o_lru_fin"].transpose(2, 0, 3, 4, 1).reshape(npq, NL, 2, 512) for c in range(8)], 0)
    f = lambda a: np.ascontiguousarray(a, dtype=np.float32)
    return (f(y_prompt), f(y_sample), f(ckv), f(kr), f(rw), f(ssd), f(lru))
```

```python
import contextlib
import numpy as np
import ml_dtypes
import concourse.bass as bass
import concourse.mybir as mybir
from concourse.bass_utils import run_bass_kernel_spmd

F32 = mybir.dt.float32
BF16 = mybir.dt.bfloat16
AF = mybir.ActivationFunctionType
ALU = mybir.AluOpType
AX = mybir.AxisListType

D = 1024
NL = 2
NDMA_SLOTS = 10
RW_BF16 = False
RW_OBF16 = True
DEBUG_FLAGS = ()
SAME_ENGINE_SYNC = ("act", "dve", "pool")


class Res:
    __slots__ = ("w", "r", "name")

    def __init__(self, name=""):
        self.w = []
        self.r = {}
        self.name = name


class Tl:
    def __init__(self, h, name, psum=False, multi=False):
        self.h = h
        self.res = Res(name)
        self.name = name
        self.psum = psum
        self.res_multi = multi

    def __getitem__(self, k):
        return self.h[k]


class Eng:
    def __init__(self, name, h, sem):
        self.name = name
        self.h = h
        self.sem = sem
        self.count = 0
        self.waited = {}
        self.dma_sems = []
        self.dma_vals = []
        self.dma_n = 0


def _res(x):
    return x.res if isinstance(x, Tl) else x


class KB:
    def __init__(self, nc):
        self.nc = nc
        self.es = contextlib.ExitStack()
        self.stacks = [self.es]
        self.eng = {}
        self.uid = 0
        for name, h in (("pe", nc.tensor), ("dve", nc.vector), ("act", nc.scalar),
                        ("pool", nc.gpsimd), ("sp", nc.sync)):
            sem = self.es.enter_context(nc.semaphore("s_" + name))
            self.eng[name] = Eng(name, h, sem)
        for qn in ("sp", "pool", "act"):
            E = self.eng[qn]
            for i in range(NDMA_SLOTS):
                E.dma_sems.append(self.es.enter_context(nc.semaphore(f"d_{qn}{i}")))
                E.dma_vals.append(0)

    def sb(self, name, shape, dtype=F32):
        self.uid += 1
        nm = f"{name}_{self.uid}"
        h = self.stacks[-1].enter_context(self.nc.sbuf_tensor(nm, list(shape), dtype))
        return Tl(h, nm)

    def ps(self, name, shape, dtype=F32):
        self.uid += 1
        nm = f"{name}_{self.uid}"
        h = self.stacks[-1].enter_context(self.nc.psum_tensor(nm, list(shape), dtype))
        return Tl(h, nm, psum=True)

    def dram(self, name, shape, dtype=F32, kind="Internal"):
        h = self.nc.dram_tensor(name, list(shape), dtype, kind=kind)
        return Tl(h.ap(), name, multi=True)

    @contextlib.contextmanager
    def phase(self):
        self.barrier()
        st = contextlib.ExitStack()
        self.stacks.append(st)
        try:
            with st:
                yield
                self.barrier()
        finally:
            self.stacks.pop()

    def _wait(self, E, ev):
        sem, val, src = ev
        if src is E and E.name not in SAME_ENGINE_SYNC:
            return
        k = id(sem)
        if E.waited.get(k, 0) >= val:
            return
        E.h.wait_ge(sem, val)
        E.waited[k] = val

    def _deps(self, E, reads, writes):
        for r in reads:
            for ev in _res(r).w:
                self._wait(E, ev)
        for w in writes:
            rs = _res(w)
            if not (isinstance(w, Tl) and w.res_multi and not rs.r):
                for ev in rs.w:
                    self._wait(E, ev)
            for ev in rs.r.values():
                self._wait(E, ev)

    def _commit(self, ev, key, reads, writes):
        for r in reads:
            _res(r).r[key] = ev
        for w in writes:
            rs = _res(w)
            if isinstance(w, Tl) and w.res_multi and not rs.r:
                d_ = {id(e[0]): e for e in rs.w}
                d_[id(ev[0])] = ev
                rs.w = list(d_.values())
            else:
                rs.w = [ev]
                rs.r = {}

    def op(self, en, fn, reads=(), writes=()):
        E = self.eng[en]
        pr = [r for r in reads if isinstance(r, Tl) and r.psum]
        if pr:
            reads = [r for r in reads if not (isinstance(r, Tl) and r.psum)]
            writes = list(writes) + [r for r in pr if r not in writes]
        self._deps(E, reads, writes)
        ins = fn(E.h)
        E.count += 1
        ins.then_inc(E.sem, 1)
        ev = (E.sem, E.count, E)
        self._commit(ev, en, reads, writes)
        return ins

    def dma(self, qn, out, in_, reads=(), writes=(), **kw):
        E = self.eng[qn]
        self._deps(E, reads, writes)
        slot = E.dma_n % NDMA_SLOTS
        E.dma_n += 1
        sem = E.dma_sems[slot]
        pv = E.dma_vals[slot]
        if pv > 0:
            self._wait(E, (sem, pv, None))
        E.h.dma_start(out=out, in_=in_, **kw).then_inc(sem, 16)
        E.dma_vals[slot] = pv + 16
        ev = (sem, pv + 16, None)
        self._commit(ev, ("dma", qn, slot), reads, writes)

    def barrier(self):
        evs = []
        for E in self.eng.values():
            if E.count:
                evs.append((E.sem, E.count, E))
            for s, v in zip(E.dma_sems, E.dma_vals):
                if v:
                    evs.append((s, v, None))
        for E in self.eng.values():
            for ev in evs:
                self._wait(E, ev)

    def mm(self, out, lhsT, rhs, start, stop, reads, writes):
        return self.op("pe", lambda e: e.matmul(out, lhsT=lhsT, rhs=rhs, start=start, stop=stop),
                       reads, writes)

    def tr(self, out, in_, ident, reads, writes):
        return self.op("pe", lambda e: e.transpose(out, in_, ident), reads, writes)

    def act(self, out, in_, func, reads, writes, bias=None, scale=None, accum_out=None, en="act"):
        kw = {}
        if bias is not None:
            kw["bias"] = bias
        if scale is not None:
            kw["scale"] = scale
        if accum_out is not None:
            kw["accum_out"] = accum_out
        return self.op(en, lambda e: e.activation(out=out, in_=in_, func=func, **kw), reads, writes)

    def tt(self, out, in0, in1, op, reads, writes, en="dve"):
        return self.op(en, lambda e: e.tensor_tensor(out=out, in0=in0, in1=in1, op=op), reads, writes)

    def ts(self, out, in0, s1, s2, op0, op1, reads, writes, en="dve", accum_out=None):
        kw = {}
        if accum_out is not None:
            kw["accum_out"] = accum_out
        if op1 is None:
            return self.op(en, lambda e: e.tensor_scalar(out=out, in0=in0, scalar1=s1, scalar2=None,
                                                         op0=op0, **kw), reads, writes)
        return self.op(en, lambda e: e.tensor_scalar(out=out, in0=in0, scalar1=s1, scalar2=s2,
                                                     op0=op0, op1=op1, **kw), reads, writes)

    def stt(self, out, in0, scalar, in1, op0, op1, reads, writes):
        return self.op("dve", lambda e: e.scalar_tensor_tensor(out=out, in0=in0, scalar=scalar, in1=in1,
                                                               op0=op0, op1=op1), reads, writes)

    def cp(self, out, in_, reads, writes, en="dve"):
        if en == "act":
            return self.op("act", lambda e: e.copy(out=out, in_=in_), reads, writes)
        return self.op(en, lambda e: e.tensor_copy(out=out, in_=in_), reads, writes)

    def scan(self, out, d0, d1, init, reads, writes, op0=ALU.mult, op1=ALU.add):
        return self.op("dve", lambda e: e.tensor_tensor_scan(out=out, data0=d0, data1=d1, initial=init,
                                                             op0=op0, op1=op1), reads, writes)

    def memset(self, ap, val, writes, en="pool"):
        return self.op(en, lambda e: e.memset(ap, val), (), writes)

    def finish(self):
        self.barrier()


ZROWS = {}
_o = 0
for _n, _w in (("rw_r", 512), ("rw_k", 512), ("rw_v", 512), ("rw_lora", 128), ("rw_gate", 512),
               ("mla_q", 384), ("mla_ckv", 256), ("misc", 128), ("mla_gate", 512),
               ("ssd_xbc", 768), ("lru_x", 512), ("lru_gate", 512)):
    ZROWS[_n] = (_o, _w)
    _o += _w
NZ_FM = _o
NZ_ALL = NZ_FM + 512
_IW = dict(rw_pre=0, rw_gate=1664, mla_q=2176, mla_ckv=2560, mla_krope=2816, mla_gate=2848,
           ssd_gate=3360, ssd_xbc=3872, ssd_dt=4640, lru_x=4656, lru_gate=5168, merge=5680)


class VecPack:
    def __init__(self):
        self.cols = {}
        self.n = 0
        self.data = []

    def add(self, name, arr2d):
        k = arr2d.shape[-1]
        self.cols[name] = (self.n, k)
        self.n += k
        self.data.append(np.asarray(arr2d, np.float32))

    def fm(self, name, v):
        v = np.asarray(v, np.float32)
        n = v.shape[-1]
        if n % 128:
            pad = 128 - n % 128
            v = np.concatenate([v, np.zeros(v.shape[:-1] + (pad,), np.float32)], -1)
        k = v.shape[-1] // 128
        self.add(name, v.reshape(v.shape[0], k, 128).transpose(0, 2, 1))

    def pack(self):
        return np.ascontiguousarray(np.concatenate(self.data, -1))


def vec_layout():
    vp = VecPack()
    z = lambda *s: np.zeros(s, np.float32)
    vp.fm("ada_b_ss", z(NL, 2048))
    vp.fm("norm_g", z(NL, 1024))
    vp.fm("mla_qa_g", z(NL, 384))
    vp.fm("mla_kva_g", z(NL, 256))
    vp.fm("mla_qn_g", z(NL, 96))
    vp.fm("mla_kn_g", z(NL, 96))
    vp.fm("rw_mu", z(NL, 1664))
    vp.fm("rw_w0", z(NL, 1024))
    vp.fm("rw_a0", z(NL, 1024))
    vp.fm("rw_kk", z(NL, 512))
    vp.fm("rw_ka", z(NL, 512))
    vp.fm("rw_rk", z(NL, 512))
    vp.fm("rw_ln_g", z(NL, 512))
    vp.fm("rw_ln_b", z(NL, 512))
    vp.fm("ssd_cw", z(NL, 4 * 768))
    vp.fm("ssd_cb", z(NL, 768))
    vp.fm("ssd_dtb", z(NL, 64))
    vp.fm("ssd_alog", z(NL, 64))
    vp.fm("lru_cw", z(NL, 4 * 512))
    vp.fm("lru_cb", z(NL, 512))
    vp.fm("lru_ba", z(NL, 2 * 512))
    vp.fm("lru_bx", z(NL, 2 * 512))
    vp.fm("lru_lam", z(NL, 2 * 512))
    return vp


def host_vecs(inp):
    vp = VecPack()
    vp.fm("ada_b_ss", inp["ada_b"][:, :2048])
    vp.fm("norm_g", inp["norm_g"])
    r2 = lambda a: np.asarray(a, np.float32).reshape(NL, -1)
    vp.fm("mla_qa_g", inp["mla_qa_g"])
    vp.fm("mla_kva_g", inp["mla_kva_g"])
    vp.fm("mla_qn_g", inp["mla_qn_g"])
    vp.fm("mla_kn_g", inp["mla_kn_g"])
    vp.fm("rw_mu", inp["rw_mu"])
    vp.fm("rw_w0", r2(inp["rw_w0"]))
    vp.fm("rw_a0", r2(inp["rw_a0"]))
    vp.fm("rw_kk", inp["rw_kk"])
    vp.fm("rw_ka", inp["rw_ka"])
    vp.fm("rw_rk", r2(inp["rw_rk"]))
    vp.fm("rw_ln_g", inp["rw_ln_g"])
    vp.fm("rw_ln_b", inp["rw_ln_b"])
    vp.fm("ssd_cw", r2(inp["ssd_conv_w"]))
    vp.fm("ssd_cb", inp["ssd_conv_b"])
    def d64(a):
        a = np.asarray(a, np.float32)
        o_ = np.zeros((NL, 64), np.float32)
        o_[:, 0:8] = a[:, 0]
        o_[:, 32:40] = a[:, 1]
        return o_
    vp.fm("ssd_dtb", d64(inp["ssd_dt_bias"]))
    vp.fm("ssd_alog", d64(inp["ssd_a_log"]))
    vp.fm("lru_cw", r2(inp["lru_conv_w"]))
    vp.fm("lru_cb", inp["lru_conv_b"])
    vp.fm("lru_ba", r2(inp["lru_ba"]))
    vp.fm("lru_bx", r2(inp["lru_bx"]))
    vp.fm("lru_lam", r2(inp["lru_lambda"]))
    return vp


class Cfg:
    def __init__(self, n_prompt=4, lp=256, ls=4096, debug=()):
        self.n_prompt = n_prompt
        self.lp = lp
        self.ls = ls
        self.TP = n_prompt * lp
        self.T = self.TP + ls
        self.seqs = [(i * lp, lp, False) for i in range(n_prompt)] + [(self.TP, ls, True)]
        self.debug = set(debug)
        assert self.T % 512 == 0 and lp % 128 == 0 and ls % 512 == 0 and self.TP % 512 == 0


class Prog:
    def __init__(self, cfg):
        self.cfg = cfg
        self.nc = bass.Bass("TRN2", target_bir_lowering=False)
        self.k = KB(self.nc)
        self.inputs = {}
        self.outputs = {}
        self.vl = vec_layout()

    def din(self, name, shape, dtype=F32):
        t = self.k.dram(name, shape, dtype, kind="ExternalInput")
        self.inputs[name] = t
        return t

    def dout(self, name, shape, dtype=F32):
        t = self.k.dram(name, shape, dtype, kind="ExternalOutput")
        self.outputs[name] = t
        return t

    def vcol(self, name, j=0, n=1):
        o, k = self.vl.cols[name]
        assert j + n <= k
        return self.vecs[:, o + j:o + j + n]

    def build(self):
        cfg, k = self.cfg, self.k
        T = cfg.T
        with k.es:
            self.x_in = self.din("x_all", [T, D])
            self.condT = self.din("condT", [128, 8, 2])
            self.ada_w = self.din("ada_w", [NL, D, 3 * D])
            self.ada_bg = self.din("ada_bg", [NL, 1, D])
            self.w_in_a = self.din("w_in_a", [NL, D, NZ_ALL])
            self.vecs_d = self.din("vecs", [NL, 128, self.vl.n])
            self.ident_d = self.din("ident", [128, 128])
            self.sel2_d = self.din("sel2", [2, 2, 128])
            self.lru_w = self.din("lru_w", [NL, 128, 16, 128])
            self.lru_h0 = self.din("lru_h0", [NL, 128, 2, 4])
            self.o_lru_fin = self.dout("o_lru_fin", [NL, 128, cfg.n_prompt, 2, 4])
            self.q_up = self.din("q_up", [NL, 384, 768])
            self.kv_up_k = self.din("kv_up_k", [NL, 256, 512])
            self.kv_up_v = self.din("kv_up_v", [NL, 256, 512])
            self.cache_ckv = self.din("cache_ckv", [NL, 256, 256])
            self.cache_kr = self.din("cache_kr", [NL, 256, 32])
            self.rope_cs = self.din("rope_cs", [2, 32, cfg.ls])
            self.pm96_d = self.din("pm96", [32, 96])
            self.o_ckv = self.dout("o_ckv", [NL, 256, cfg.TP])
            self.o_kr = self.dout("o_kr", [NL, 32, cfg.TP])
            self.ssd_bc = self.din("ssd_bc", [NL, 128, 8 + 512])
            self.ssd_h0 = self.din("ssd_h0", [NL, 2, 64, 8, 64])
            self.selm_d = self.din("selm", [64, 16, 128])
            self.seld_d = self.din("seld", [64, 2, 8])
            self.selg_d = self.din("selg", [64, 128])
            self.maskneg_d = self.din("maskneg", [128, 2, 128])
            self.cmask_d = self.din("cmask", [64, T + 1])
            self.o_ssd_fin = self.dout("o_ssd_fin", [NL, cfg.n_prompt, 2, 64, 8, 64])
            self.ssd_y = k.dram("ssd_y_scr", [T, 512])
            self.rw_lw = self.din("rw_lw", [NL, 128, 2, 512])
            self.rw_h0 = self.din("rw_h0", [NL, 2, 64, 8, 64])
            self.rw_mask_d = self.din("rw_mask", [128, 2, 3, 128])
            self.blk64_d = self.din("blk64", [128, 128])
            self.pmask_d = self.din("pmask", [128, 2])
            self.cmask64_d = self.din("cmask64", [128, 513])
            self.o_rw_fin = self.dout("o_rw_fin", [cfg.n_prompt, NL, 2, 8, 64, 64])
            self.rw_ops = k.dram("rw_ops_scr", [2, 6, 512, T])
            self.rw_v = k.dram("rw_v_scr", [512, T])
            self.rw_bonus = k.dram("rw_bonus_scr", [512, T])
            self.rw_o = k.dram("rw_o_scr", [2, 512, T])
            self.w_in_m = self.din("w_in_m", [NL, D, 4 * D])
            self.w_branch = self.din("w_branch", [NL, 4, 512, D])
            self.w_out = self.din("w_out", [NL, D, D])
            self.y_all = self.dout("y_all", [T, D])
            self.y1 = k.dram("y1_scr", [T, D])
            self.mT_scr = k.dram("mT_scr", [D, T], BF16)
            self.o_scr = k.dram("o_scr", [4, 512, T], BF16)
            if "o" in cfg.debug:
                self.dbg_o = self.dout("dbg_o", [4, 512, T], BF16)
            self.zT = k.dram("zT_scr", [NZ_FM, T])
            self.zg_tm = k.dram("zg_tm_scr", [T, 512])
            self.hT_scr = k.dram("hT_scr", [D, T], BF16)
            if "z" in cfg.debug:
                self.dbg_zT = self.dout("dbg_zT", [NZ_FM, T])
                self.dbg_zg = self.dout("dbg_zg", [T, 512])
            self.vecs = k.sb("vecs", [128, self.vl.n])
            self.ident_f = k.sb("ident_f", [128, 128])
            self.ident_b = k.sb("ident_b", [128, 128], BF16)
            self.sel2 = k.sb("sel2", [2, 2, 128])
            self.sc = k.sb("sc", [128, 8, 2])
            self.gmod = k.sb("gmod", [128, 8, 2])
            self.shiftc = k.sb("shiftc", [128, 8, 2])
            self.gate_bc = [k.sb(f"gate_bc{g}", [128, D]) for g in range(2)]
            self.ones_f = k.sb("ones_f", [128, 128])
            k.memset(self.ones_f[:], 1.0, [self.ones_f])
            self.pf = [k.ps(f"pf{i}", [128, 512]) for i in range(6)]
            self.pb = [k.ps(f"pb{i}", [128, 1024], BF16) for i in range(2)]
            self.pfi = 0
            k.dma("sp", self.ident_f[:], self.ident_d[:, :], (), [self.ident_f])
            k.dma("sp", self.sel2[:], self.sel2_d[:, :, :], (), [self.sel2])
            k.cp(self.ident_b[:], self.ident_f[:], [self.ident_f], [self.ident_b])
            k.dma("sp", self.sc[:], self.condT[:, :, :], (), [self.sc])
            k.act(self.sc[:], self.sc[:], AF.Silu, [self.sc], [self.sc])

            for l in range(NL):
                self.layer(l)
                if l == 0 and "stop0" in cfg.debug:
                    break
            k.finish()
        return self.nc

    def dump(self, name, tl, ap, shape, dtype=F32):
        if name not in self.cfg.debug:
            return
        t = self.dout("dump_" + name, shape, dtype)
        self.k.dma("sp", t[tuple(slice(None) for _ in shape)], ap, [tl], [t])

    def next_pf(self):
        p = self.pf[self.pfi % 4]
        self.pfi += 1
        return p

    def layer(self, l):
        cfg, k = self.cfg, self.k
        with k.phase():
            k.dma("sp", self.vecs[:], self.vecs_d[l], (), [self.vecs])
            self.phase_mod(l)
        if "zero_o" in cfg.debug:
            with k.phase():
                zt = k.sb("zt", [128, cfg.T], BF16)
                k.memset(zt[:], 0.0, [zt])
                for m in range(4):
                    for ft in range(4):
                        k.dma("sp", self.o_scr[m, ft * 128:(ft + 1) * 128, :], zt[:], [zt], [self.o_scr])
        with k.phase():
            self.phase_front(l)
        with k.phase():
            self.phase_lru(l)
        if "norw" not in cfg.debug:
            with k.phase():
                self.phase_rwkv(l)
        if "nossd" not in cfg.debug:
            with k.phase():
                self.phase_ssd(l)
        if "nomla" not in cfg.debug:
            with k.phase():
                self.phase_mla(l)
        with k.phase():
            self.phase_merge(l)
        with k.phase():
            self.phase_out(l)
        if "y1" in cfg.debug and l == 0:
            k.barrier()
            d_ = self.dout("dbg_y1", [cfg.T, D])
            k.dma("sp", d_[:, :], self.y1[:, :], [self.y1], [d_])
            k.barrier()
        if "o" in cfg.debug and l == 0:
            k.barrier()
            k.dma("sp", self.dbg_o[:, :, :], self.o_scr[:, :, :], [self.o_scr], [self.dbg_o])
            k.barrier()

    def phase_mod(self, l):
        k = self.k
        wst = [k.sb(f"adaw{i}", [128, 8, 512]) for i in range(2)]
        modc = k.sb("modc", [128, 16, 2])
        grow = k.sb("grow", [2, D])
        gb = k.sb("gb", [2, D])
        for g in range(2):
            k.dma("pool", gb[g:g + 1, :], self.ada_bg[l], (), [gb])
        for blk in range(6):
            w = wst[blk % 2]
            k.dma("sp", w[:], self.ada_w[l][:, blk * 512:(blk + 1) * 512].rearrange("(c p) n -> p c n", p=128),
                  (), [w])
            if blk < 4:
                for jt in range(4):
                    ps = self.next_pf()
                    for c in range(8):
                        k.mm(ps[:, 0:2], w[:, c, jt * 128:(jt + 1) * 128], self.sc[:, c, :], c == 0, c == 7,
                             [w, self.sc], [ps])
                    k.cp(modc[:, blk * 4 + jt, :], ps[:, 0:2], [ps], [modc])
            else:
                ps = self.next_pf()
                for c in range(8):
                    k.mm(ps[0:2, :], self.sc[:, c, :], w[:, c, :], c == 0, c == 7, [w, self.sc], [ps])
                hs = slice((blk - 4) * 512, (blk - 3) * 512)
                k.tt(grow[:, hs], ps[0:2, :], gb[:, hs], ALU.add, [ps, gb], [grow])
        ab = self.vcol("ada_b_ss", 0, 16)
        for g in range(2):
            k.tt(modc[:, :, g], modc[:, :, g], ab, ALU.add, [modc, self.vecs], [modc])
            k.cp(self.shiftc[:, :, g], modc[:, 0:8, g], [modc], [self.shiftc])
            k.stt(self.gmod[:, :, g], modc[:, 8:16, g], 1.0, self.vcol("norm_g", 0, 8), ALU.add, ALU.mult,
                  [modc, self.vecs], [self.gmod])
            for half in range(2):
                ps = self.next_pf()
                k.mm(ps[:, :], self.sel2[:, g, :], grow[:, half * 512:(half + 1) * 512], True, True,
                     [self.sel2, grow], [ps])
                k.cp(self.gate_bc[g][:, half * 512:(half + 1) * 512], ps[:, :], [ps], [self.gate_bc[g]])

    def phase_front(self, l):
        cfg, k = self.cfg, self.k
        T = cfg.T
        x_src = self.x_in if l == 0 else self.y1
        hT = k.sb("hT", [128, 8, T], BF16)
        with k.phase():
            xt = [k.sb(f"xt{i}", [128, D]) for i in range(3)]
            xn = [k.sb(f"xn{i}", [128, D], BF16) for i in range(2)]
            junk = k.sb("junk", [128, D], BF16)
            ss = [k.sb(f"ss{i}", [128, 1]) for i in range(2)]
            for st in range(T // 128):
                g = 0 if st * 128 < cfg.TP else 1
                x = xt[st % 3]
                xb = xn[st % 2]
                s = ss[st % 2]
                pb = self.pb[st % 2]
                k.dma("sp", x[:], x_src[st * 128:(st + 1) * 128, :], [x_src], [x])
                k.act(junk[:], x[:], AF.Square, [x], [junk, s], accum_out=s[:])
                k.ts(s[:], s[:], 1.0 / D, 1e-6, ALU.mult, ALU.add, [s], [s])
                k.act(s[:], s[:], AF.Sqrt, [s], [s])
                k.op("dve", lambda e: e.reciprocal(out=s[:], in_=s[:]), [s], [s])
                k.act(xb[:], x[:], AF.Copy, [x, s], [xb], scale=s[:])
                for c in range(8):
                    k.tr(pb[:, c * 128:(c + 1) * 128], xb[:, c * 128:(c + 1) * 128], self.ident_b[:],
                         [xb, self.ident_b], [pb])
                ho = hT[:, :, st * 128:(st + 1) * 128]
                pv = pb[:].rearrange("p (c t) -> p c t", c=8)
                k.tt(ho, pv, self.gmod[:, :, g:g + 1].to_broadcast([128, 8, 128]), ALU.mult,
                     [pb, self.gmod], [hT])
                k.tt(ho, ho, self.shiftc[:, :, g:g + 1].to_broadcast([128, 8, 128]), ALU.add,
                     [hT, self.shiftc], [hT])
        for c in range(8):
            k.dma("pool", self.hT_scr[c * 128:(c + 1) * 128, :], hT[:, c, :], [hT], [self.hT_scr])
        wst = [k.sb(f"wst{i}", [128, 8, 512]) for i in range(2)]
        wbf = [k.sb(f"wbf{i}", [128, 8, 512], BF16) for i in range(2)]
        zst = [k.sb(f"zst{i}", [128, 512]) for i in range(4)]
        blocks = [(c0, min(512, NZ_FM - c0), False) for c0 in range(0, NZ_FM, 512)] + [(NZ_FM, 512, True)]
        zi = 0
        for blk, (c0, ncol, tm_block) in enumerate(blocks):
            ws, wb = wst[blk % 2], wbf[blk % 2]
            k.dma("sp", ws[:, :, :ncol], self.w_in_a[l][:, c0:c0 + ncol].rearrange("(c p) n -> p c n", p=128),
                  (), [ws])
            k.cp(wb[:, :, :ncol], ws[:, :, :ncol], [ws], [wb], en="pool")
            for tt in range(T // 512):
                ts_ = slice(tt * 512, (tt + 1) * 512)
                if not tm_block:
                    for jt in range(ncol // 128):
                        ps = self.next_pf()
                        for c in range(8):
                            k.mm(ps[:, :], wb[:, c, jt * 128:(jt + 1) * 128], hT[:, c, ts_], c == 0, c == 7,
                                 [wb, hT], [ps])
                        z = zst[zi % 4]
                        if zi % 2 == 0:
                            k.cp(z[:], ps[:, :], [ps], [z])
                        else:
                            k.cp(z[:], ps[:, :], [ps], [z], en="act")
                        zi += 1
                        r0 = c0 + jt * 128
                        k.dma("pool", self.zT[r0:r0 + 128, ts_], z[:], [z], [self.zT])
                else:
                    for sub in range(4):
                        ps = self.next_pf()
                        t0 = tt * 512 + sub * 128
                        for c in range(8):
                            k.mm(ps[:, :], hT[:, c, t0:t0 + 128], wb[:, c, :], c == 0, c == 7, [wb, hT], [ps])
                        z = zst[zi % 4]
                        if zi % 2 == 0:
                            k.cp(z[:], ps[:, :], [ps], [z])
                        else:
                            k.cp(z[:], ps[:, :], [ps], [z], en="act")
                        zi += 1
                        k.dma("pool", self.zg_tm[t0:t0 + 128, :], z[:], [z], [self.zg_tm])
        if "z" in cfg.debug and l == 0:
            k.barrier()
            k.dma("sp", self.dbg_zT[:, :], self.zT[:, :], [self.zT], [self.dbg_zT])
            k.dma("sp", self.dbg_zg[:, :], self.zg_tm[:, :], [self.zg_tm], [self.dbg_zg])


    def groups(self):
        cfg = self.cfg
        return [(0, cfg.n_prompt, cfg.lp), (cfg.TP, 1, cfg.ls)]

    def gview(self, ap2d, grp, lo, hi):
        s0, ns, ln = grp
        return ap2d[:, s0:s0 + ns * ln].rearrange("p (s t) -> p s t", s=ns)[:, :, lo:hi]

    def dwconv(self, xc, xl, wcol, bcol, reads):
        k = self.k
        T = self.cfg.T
        k.ts(xc[:, :T], xl[:, :T], wcol(2), bcol, ALU.mult, ALU.add, [xl] + reads, [xc])
        for grp in self.groups():
            ln = grp[2]
            for kk, off in ((0, -2), (1, -1), (3, 1)):
                if off < 0:
                    src = self.gview(xl[:, :T], grp, 0, ln + off)
                    dst = self.gview(xc[:, :T], grp, -off, ln)
                else:
                    src = self.gview(xl[:, :T], grp, off, ln)
                    dst = self.gview(xc[:, :T], grp, 0, ln - off)
                k.stt(dst, src, wcol(kk), dst, ALU.mult, ALU.add, [xl, xc] + reads, [xc])

    def phase_lru(self, l):
        cfg, k = self.cfg, self.k
        T = cfg.T
        zo, _ = ZROWS["lru_x"]
        go, _ = ZROWS["lru_gate"]
        wts = k.sb("lru_wts", [128, 16, 128])
        h0 = k.sb("lru_h0", [128, 2, 4])
        clam = k.sb("lru_clam", [128, 8])
        fin = k.sb("lru_fin", [128, cfg.n_prompt, 2, 4])
        k.dma("sp", wts[:], self.lru_w[l], (), [wts])
        k.dma("sp", h0[:], self.lru_h0[l], (), [h0])
        k.act(clam[:], self.vcol("lru_lam", 0, 8), AF.Exp, [self.vecs], [clam], scale=-1.0)
        k.act(clam[:], clam[:], AF.Ln, [clam], [clam], bias=1.0)
        k.ts(clam[:], clam[:], -8.0, None, ALU.mult, None, [clam], [clam])
        xl = k.sb("lru_xl", [128, T])
        gt = k.sb("lru_gt", [128, T])
        xc = k.sb("lru_xc", [128, T])
        ta = k.sb("lru_ta", [128, T])
        tb = k.sb("lru_tb", [128, T])
        hh = [k.sb(f"lru_h{d}", [128, T]) for d in range(2)]
        ob = k.sb("lru_ob", [128, T], BF16)
        V = [self.vecs]
        for ft in range(4):
            k.dma("sp", xl[:], self.zT[zo + ft * 128:zo + (ft + 1) * 128, :], [self.zT], [xl])
            k.dma("sp", gt[:], self.zT[go + ft * 128:go + (ft + 1) * 128, :], [self.zT], [gt])
            self.dwconv(xc, xl, lambda kk: self.vcol("lru_cw", kk * 4 + ft), self.vcol("lru_cb", ft), V)
            k.act(gt[:], gt[:], AF.Silu, [gt], [gt])
            for d in range(2):
                for tt in range(T // 512):
                    sl = slice(tt * 512, (tt + 1) * 512)
                    pa = self.next_pf()
                    k.mm(pa[:, :], wts[:, (d * 2 + 0) * 4 + ft, :], xc[:, sl], True, True, [wts, xc], [pa])
                    px = self.next_pf()
                    k.mm(px[:, :], wts[:, (d * 2 + 1) * 4 + ft, :], xc[:, sl], True, True, [wts, xc], [px])
                    k.act(ta[:, sl], pa[:, :], AF.Sigmoid, [pa] + V, [ta], bias=self.vcol("lru_ba", d * 4 + ft))
                    k.act(tb[:, sl], px[:, :], AF.Sigmoid, [px] + V, [tb], bias=self.vcol("lru_bx", d * 4 + ft))
                k.act(ta[:], ta[:], AF.Exp, [ta, clam], [ta], scale=clam[:, d * 4 + ft:d * 4 + ft + 1])
                if ft == 0 and d == 0:
                    self.dump("lru_clam", clam, clam[:], [128, 8])
                    self.dump("lru_a", ta, ta[:], [128, T])
                    self.dump("lru_gi", tb, tb[:], [128, T])
                    self.dump("lru_xc", xc, xc[:], [128, T])
                k.tt(tb[:], tb[:], xc[:], ALU.mult, [tb, xc], [tb])
                h = hh[d]
                k.tt(h[:], ta[:], ta[:], ALU.mult, [ta], [h])
                k.act(h[:], h[:], AF.Sqrt, [h], [h], scale=-1.0, bias=1.0)
                k.tt(tb[:], tb[:], h[:], ALU.mult, [tb, h], [tb])
                if ft == 0 and d == 0:
                    self.dump("lru_sq", h, h[:], [128, T])
                    self.dump("lru_u", tb, tb[:], [128, T])
                for (s0, ln, is_s) in cfg.seqs:
                    sl = slice(s0, s0 + ln)
                    init = h0[:, d, ft:ft + 1] if is_s else 0.0
                    rd = [ta, tb] + ([h0] if is_s else [])
                    if d == 0:
                        k.scan(h[:, sl], ta[:, sl], tb[:, sl], init, rd, [h])
                    else:
                        rv = lambda t: t[:, s0:s0 + ln][:, ::-1]
                        k.scan(rv(h), rv(ta), rv(tb), init, rd, [h])
                    if not is_s:
                        si = s0 // ln
                        e = s0 + ln - 1 if d == 0 else s0
                        k.cp(fin[:, si, d, ft:ft + 1], h[:, e:e + 1], [h], [fin], en="pool")
            k.tt(hh[0][:], hh[0][:], hh[1][:], ALU.add, [hh[0], hh[1]], [hh[0]])
            k.tt(ob[:], hh[0][:], gt[:], ALU.mult, [hh[0], gt], [ob])
            k.dma("pool", self.o_scr[3, ft * 128:(ft + 1) * 128, :], ob[:], [ob], [self.o_scr])
        k.dma("pool", self.o_lru_fin[l], fin[:], [fin], [self.o_lru_fin])


    def phase_merge(self, l):
        cfg, k = self.cfg, self.k
        T = cfg.T
        wm = k.sb("wm", [128, 8, 4 * D], BF16)
        wbr = k.sb("wbr", [128, 16, D], BF16)
        with k.phase():
            stg = [k.sb(f"mstg{i}", [128, 4096]) for i in range(2)]
            si = 0
            for blk in range(8):
                st = stg[si % 2]
                si += 1
                k.dma("sp", st[:].rearrange("p (c n) -> p c n", c=8),
                      self.w_in_m[l][:, blk * 512:(blk + 1) * 512].rearrange("(c p) n -> p c n", p=128), (), [st])
                k.cp(wm[:, :, blk * 512:(blk + 1) * 512], st[:].rearrange("p (c n) -> p c n", c=8), [st], [wm], en="pool")
            for m in range(4):
                st = stg[si % 2]
                si += 1
                k.dma("sp", st[:].rearrange("p (c n) -> p c n", c=4),
                      self.w_branch[l, m].rearrange("(c p) n -> p c n", p=128), (), [st])
                k.cp(wbr[:, m * 4:(m + 1) * 4, :], st[:].rearrange("p (c n) -> p c n", c=4), [st], [wbr], en="pool")
        hts = [k.sb(f"mh{i}", [128, 8, 512], BF16) for i in range(2)]
        ots = [k.sb(f"mo{i}", [128, 16, 512], BF16) for i in range(2)]
        sgs = [k.sb(f"msg{i}", [128, 512]) for i in range(2)]
        tmp = [k.sb(f"mtmp{i}", [128, 512]) for i in range(2)]
        acc = [k.sb(f"macc{i}", [128, 512]) for i in range(2)]
        mts = [k.sb(f"mt{i}", [128, 8, 512], BF16) for i in range(2)]
        n = 0
        for tt in range(T // 512):
            tsl = slice(tt * 512, (tt + 1) * 512)
            ht, ot, mt = hts[tt % 2], ots[tt % 2], mts[tt % 2]
            k.dma("sp", ht[:], self.hT_scr[:, tsl].rearrange("(c p) t -> p c t", p=128), [self.hT_scr], [ht])
            for m in range(4):
                k.dma("sp", ot[:, m * 4:(m + 1) * 4, :], self.o_scr[m, :, tsl].rearrange("(c p) t -> p c t", p=128),
                      [self.o_scr], [ot])
            for dt in range(8):
                a = acc[dt % 2]
                for m in range(4):
                    pl = self.next_pf()
                    for c in range(8):
                        k.mm(pl[:, :], wm[:, c, m * D + dt * 128:m * D + (dt + 1) * 128], ht[:, c, :], c == 0, c == 7,
                             [wm, ht], [pl])
                    pp = self.next_pf()
                    for cc in range(4):
                        k.mm(pp[:, :], wbr[:, m * 4 + cc, dt * 128:(dt + 1) * 128], ot[:, m * 4 + cc, :], cc == 0, cc == 3,
                             [wbr, ot], [pp])
                    sg = sgs[n % 2]
                    n += 1
                    k.act(sg[:], pl[:, :], AF.Sigmoid, [pl], [sg])
                    if m == 0:
                        k.tt(a[:], sg[:], pp[:, :], ALU.mult, [sg, pp], [a])
                    else:
                        t_ = tmp[n % 2]
                        k.tt(t_[:], sg[:], pp[:, :], ALU.mult, [sg, pp], [t_])
                        if m < 3:
                            k.tt(a[:], a[:], t_[:], ALU.add, [a, t_], [a], en="pool")
                        else:
                            k.tt(mt[:, dt, :], a[:], t_[:], ALU.add, [a, t_], [mt], en="pool")
            k.dma("pool", self.mT_scr[:, tsl].rearrange("(c p) t -> p c t", p=128), mt[:], [mt], [self.mT_scr])

    def phase_out(self, l):
        cfg, k = self.cfg, self.k
        T = cfg.T
        x_src = self.x_in if l == 0 else self.y1
        y_dst = self.y1 if l < NL - 1 else self.y_all
        wo = k.sb("wo", [128, 8, D], BF16)
        stg = [k.sb(f"ostg{i}", [128, 4096]) for i in range(2)]
        for hb in range(2):
            st = stg[hb]
            k.dma("sp", st[:].rearrange("p (c n) -> p c n", c=8),
                  self.w_out[l][:, hb * 512:(hb + 1) * 512].rearrange("(c p) n -> p c n", p=128), (), [st])
            k.cp(wo[:, :, hb * 512:(hb + 1) * 512], st[:].rearrange("p (c n) -> p c n", c=8), [st], [wo], en="pool")
        mts = [k.sb(f"omt{i}", [128, 8, 128], BF16) for i in range(3)]
        xts = [k.sb(f"oxt{i}", [128, D]) for i in range(3)]
        yts = [k.sb(f"oyt{i}", [128, D]) for i in range(3)]
        for st_ in range(T // 128):
            g = 0 if st_ * 128 < cfg.TP else 1
            tsl = slice(st_ * 128, (st_ + 1) * 128)
            mt, xt, yt = mts[st_ % 3], xts[st_ % 3], yts[st_ % 3]
            k.dma("sp", mt[:], self.mT_scr[:, tsl].rearrange("(c p) t -> p c t", p=128), [self.mT_scr], [mt])
            k.dma("sp", xt[:], x_src[tsl, :], [x_src], [xt])
            for hb in range(2):
                hs = slice(hb * 512, (hb + 1) * 512)
                ps = self.next_pf()
                for c in range(8):
                    k.mm(ps[:, :], mt[:, c, :], wo[:, c, hs], c == 0, c == 7, [mt, wo], [ps])
                k.tt(yt[:, hs], ps[:, :], self.gate_bc[g][:, hs], ALU.mult, [ps, self.gate_bc[g]], [yt])
                k.tt(yt[:, hs], yt[:, hs], xt[:, hs], ALU.add, [yt, xt], [yt], en="pool")
            k.dma("pool", y_dst[tsl, :], yt[:], [yt], [y_dst])


    def rstd_from_sum(self, out, ps, n, reads):
        k = self.k
        tl, pt = reads
        k.ts(out, ps, 1.0 / n, 1e-6, ALU.mult, ALU.add, [pt], [tl])
        k.act(out, out, AF.Sqrt, [tl], [tl])
        k.op("dve", lambda e: e.reciprocal(out=out, in_=out), [tl], [tl])

    def phase_mla(self, l):
        cfg, k = self.cfg, self.k
        T, TP, ls = cfg.T, cfg.TP, cfg.ls
        TK = T + 256
        kidx = lambda t: t if t < TP else t + 256
        V = [self.vecs]
        qup = k.sb("qup", [128, 3, 768], BF16)
        kvk = k.sb("kvk", [128, 2, 512], BF16)
        kvv = k.sb("kvv", [128, 2, 512], BF16)
        pm96 = k.sb("pm96", [96, 96])
        cs = k.sb("ropecs", [96, 2, ls], BF16)
        ckv_all = k.sb("ckv_all", [128, 2, TK], BF16)
        krot = k.sb("krot", [96, TK], BF16)
        ssr = k.sb("ssr", [128, TK // 128])
        qn = k.sb("qn", [128, 3, T], BF16)
        vall = k.sb("vall", [128, TK // 128, 8, 65], BF16)
        with k.phase():
            kr_all = k.sb("kr_all", [96, TK])
            with k.phase():
                stg = k.sb("mlastg", [128, 4096])
                k.dma("sp", stg[:, :3 * 768].rearrange("p (c n) -> p c n", c=3),
                      self.q_up[l].rearrange("(c p) n -> p c n", p=128), (), [stg])
                k.cp(qup[:], stg[:, :3 * 768].rearrange("p (c n) -> p c n", c=3), [stg], [qup])
                k.dma("sp", stg[:, :1024].rearrange("p (c n) -> p c n", c=2),
                      self.kv_up_k[l].rearrange("(c p) n -> p c n", p=128), (), [stg])
                k.cp(kvk[:], stg[:, :1024].rearrange("p (c n) -> p c n", c=2), [stg], [kvk])
                k.dma("sp", stg[:, :1024].rearrange("p (c n) -> p c n", c=2),
                      self.kv_up_v[l].rearrange("(c p) n -> p c n", p=128), (), [stg])
                k.cp(kvv[:], stg[:, :1024].rearrange("p (c n) -> p c n", c=2), [stg], [kvv])
                k.dma("sp", pm96[64:96, :], self.pm96_d[:, :], (), [pm96])
                for j in range(2):
                    k.dma("sp", stg[64:96, :ls], self.rope_cs[j], (), [stg])
                    k.cp(cs[64:96, j, :], stg[64:96, :ls], [stg], [cs])
            k.memset(vall[:, :, :, 64:65], 1.0, [vall])
            if "mla_s0" in cfg.debug:
                return
            ctm = k.sb("ctm", [128, 2, 256])
            krtm = k.sb("krtm", [128, 2, 32])
            k.dma("sp", ctm[:], self.cache_ckv[l].rearrange("(a p) f -> p a f", p=128), (), [ctm])
            k.dma("sp", krtm[:], self.cache_kr[l].rearrange("(a p) f -> p a f", p=128), (), [krtm])
            for a in range(2):
                for c in range(2):
                    ps = self.next_pf()
                    k.tr(ps[:, 0:128], ctm[:, a, c * 128:(c + 1) * 128], self.ident_f[:], [ctm, self.ident_f], [ps])
                    k.cp(ckv_all[:, c, TP + a * 128:TP + (a + 1) * 128], ps[:, 0:128], [ps], [ckv_all])
                ps = self.next_pf()
                kpad = k.sb(f"kpad{a}", [128, 96])
                k.memset(kpad[:], 0.0, [kpad])
                k.cp(kpad[:, 64:96], krtm[:, a, :], [krtm, kpad], [kpad])
                k.tr(ps[0:96, 0:128], kpad[:, :], self.ident_f[:], [kpad, self.ident_f], [ps])
                k.cp(kr_all[64:96, TP + a * 128:TP + (a + 1) * 128], ps[64:96, 0:128], [ps], [kr_all])
            if "mla_s1" in cfg.debug:
                return
            zo_c, _ = ZROWS["mla_ckv"]
            zo_q, _ = ZROWS["mla_q"]
            zo_m, _ = ZROWS["misc"]
            xs = [k.sb(f"mx{i}", [128, 3, 512]) for i in range(2)]
            sq = [k.sb(f"msq{i}", [128, 3, 512]) for i in range(2)]
            rs = [k.sb(f"mrs{i}", [128, 512]) for i in range(2)]
            cn = [k.sb(f"mcn{i}", [128, 2, 512]) for i in range(2)]
            for tt in range(T // 512):
                tsl = slice(tt * 512, (tt + 1) * 512)
                ksl = slice(kidx(tt * 512), kidx(tt * 512) + 512)
                x, q2, r_, c_ = xs[tt % 2], sq[tt % 2], rs[tt % 2], cn[tt % 2]
                for (zo, nch, gname, is_q) in ((zo_c, 2, "mla_kva_g", False), (zo_q, 3, "mla_qa_g", True)):
                    k.dma("sp", x[:, :nch, :], self.zT[zo:zo + nch * 128, tsl].rearrange("(c p) t -> p c t", p=128),
                          [self.zT], [x])
                    k.act(q2[:, :nch, :], x[:, :nch, :], AF.Square, [x], [q2])
                    ps = self.next_pf()
                    for c in range(nch):
                        k.mm(ps[:, :], self.ones_f[:, :], q2[:, c, :], c == 0, c == nch - 1, [self.ones_f, q2], [ps])
                    self.rstd_from_sum(r_[:], ps[:, :], nch * 128, (r_, ps))
                    for c in range(nch):
                        if is_q:
                            k.stt(qn[:, c, tsl], x[:, c, :], self.vcol(gname, c), r_[:], ALU.mult, ALU.mult,
                                  [x, r_] + V, [qn])
                        else:
                            k.stt(c_[:, c, :], x[:, c, :], self.vcol(gname, c), r_[:], ALU.mult, ALU.mult,
                                  [x, r_] + V, [c_])
                    if not is_q:
                        k.cp(ckv_all[:, :, ksl], c_[:, :, :], [c_], [ckv_all], en="pool")
                        if tt * 512 < TP:
                            k.dma("pool", self.o_ckv[l][:, tsl].rearrange("(c p) t -> p c t", p=128), c_[:, :, :],
                                  [c_], [self.o_ckv])
            if "mla_s2" in cfg.debug:
                return
            k.dma("sp", kr_all[64:96, 0:TP], self.zT[zo_m:zo_m + 32, 0:TP], [self.zT], [kr_all])
            k.dma("sp", kr_all[64:96, TP + 256:TK], self.zT[zo_m:zo_m + 32, TP:T], [self.zT], [kr_all])
            k.dma("pool", self.o_kr[l], kr_all[64:96, 0:TP], [kr_all], [self.o_kr])
            krs = k.sb("krs", [96, 512])
            krg = k.sb("krg", [96, 512])
            t1 = k.sb("krt1", [96, 512])
            pss = self.pf[5]
            lat0 = TP + 256
            segs = [(ks, min(512, lat0 - ks)) for ks in range(0, lat0, 512)] + \
                   [(ks, min(512, TK - ks)) for ks in range(lat0, TK, 512)]
            for ks, w in segs:
                k.act(krs[64:96, :w], kr_all[64:96, ks:ks + w], AF.Square, [kr_all], [krs])
                for j in range(w // 128):
                    kt = ks // 128 + j
                    k.mm(pss[:, kt:kt + 1], krs[64:96, j * 128:(j + 1) * 128], self.ones_f[64:96, 0:1], True, True,
                         [krs, self.ones_f], [pss])
                k.ts(krg[64:96, :w], kr_all[64:96, ks:ks + w], self.vecs[64:96, self.vl.cols["mla_kn_g"][0]:self.vl.cols["mla_kn_g"][0] + 1],
                     None, ALU.mult, None, [kr_all] + V, [krg])
                if ks >= lat0:
                    pr = self.next_pf()
                    k.mm(pr[0:96, :w], pm96[64:96, :], krg[64:96, :w], True, True, [pm96, krg], [pr])
                    po = ks - lat0
                    k.tt(t1[64:96, :w], pr[64:96, :w], cs[64:96, 1, po:po + w], ALU.mult, [pr, cs], [t1])
                    k.tt(krg[64:96, :w], krg[64:96, :w], cs[64:96, 0, po:po + w], ALU.mult, [krg, cs], [krg])
                    k.tt(krot[64:96, ks:ks + w], krg[64:96, :w], t1[64:96, :w], ALU.add, [krg, t1], [krot])
                else:
                    k.cp(krot[64:96, ks:ks + w], krg[64:96, :w], [krg], [krot])
            k.cp(ssr[:], pss[:, 0:TK // 128], [pss], [ssr])
            if "mla_s3" in cfg.debug:
                return
            for kt in range(TK // 128):
                ps = self.next_pf()
                for c in range(2):
                    k.mm(ps[:, :], ckv_all[:, c, kt * 128:(kt + 1) * 128], kvv[:, c, :], c == 0, c == 1,
                         [ckv_all, kvv], [ps])
                k.cp(vall[:, kt, :, 0:64], ps[:, :].rearrange("p (h d) -> p h d", h=8), [ps], [vall],
                     en=("act" if kt % 2 else "dve"))
        if "mla_s4" in cfg.debug:
            return
        zo_g, _ = ZROWS["mla_gate"]
        gcol = self.vl.cols["mla_qn_g"][0]
        kcol = self.vl.cols["mla_kn_g"][0]
        kth = k.sb("kth", [96, TK], BF16)
        qth = k.sb("qth", [96, T], BF16)
        rk = k.sb("rk", [128, TK // 128])
        sqt = [k.sb(f"hsq{i}", [96, 512]) for i in range(2)]
        qf = [k.sb(f"hqf{i}", [96, 512]) for i in range(2)]
        rq = [k.sb(f"hrq{i}", [96, 512]) for i in range(2)]
        t1 = k.sb("ht1", [96, 512])
        t2 = k.sb("ht2", [96, 512])
        pts = [k.sb(f"hpt{i}", [128, 512], BF16) for i in range(4)]
        oa = [k.sb(f"hoa{i}", [65, 512]) for i in range(2)]
        gts = [k.sb(f"hgt{i}", [64, 512]) for i in range(2)]
        obs = [k.sb(f"hob{i}", [64, 512], BF16) for i in range(2)]
        npt = 0
        npo = 0
        for h in range(8):
            prk = self.pf[4]
            for ks in range(0, TK, 512):
                w = min(512, TK - ks)
                ps = self.next_pf()
                for c in range(2):
                    k.mm(ps[0:64, :w], kvk[:, c, h * 64:(h + 1) * 64], ckv_all[:, c, ks:ks + w], c == 0, c == 1,
                         [kvk, ckv_all], [ps])
                s_ = sqt[(ks // 512) % 2]
                if "k_noact" not in cfg.debug:
                    k.act(s_[0:64, :w], ps[0:64, :w], AF.Square, [ps], [s_])
                for j in range(w // 128):
                    kt = ks // 128 + j
                    if "mla_k1" in cfg.debug:
                        continue
                    k.mm(prk[:, kt:kt + 1], s_[0:64, j * 128:(j + 1) * 128], self.ones_f[0:64, 0:1], True, True,
                         [s_, self.ones_f], [prk])
                if "k_nots" not in cfg.debug:
                    k.ts(kth[0:64, ks:ks + w], ps[0:64, :w], self.vecs[0:64, kcol:kcol + 1], None, ALU.mult, None,
                         [ps] + V, [kth])
            k.cp(kth[64:96, :], krot[64:96, :], [krot], [kth], en=("dve" if "mla_k2" in cfg.debug else "pool"))
            if "k_nork" in cfg.debug:
                return
            k.tt(rk[:], prk[:, 0:TK // 128], ssr[:], ALU.add, [prk, ssr], [rk])
            k.ts(rk[:], rk[:], 1.0, 96e-6, ALU.mult, ALU.add, [rk], [rk])
            k.act(rk[:], rk[:], AF.Sqrt, [rk], [rk])
            k.op("dve", lambda e: e.reciprocal(out=rk[:], in_=rk[:]), [rk], [rk])
            if "mla_s5" in cfg.debug:
                return
            for tt in range(T // 512):
                tsl = slice(tt * 512, (tt + 1) * 512)
                ps = self.next_pf()
                for c in range(3):
                    k.mm(ps[0:96, :], qup[:, c, h * 96:(h + 1) * 96], qn[:, c, tsl], c == 0, c == 2, [qup, qn], [ps])
                s_, f_, r_ = sqt[tt % 2], qf[tt % 2], rq[tt % 2]
                k.act(s_[:, :], ps[0:96, :], AF.Square, [ps], [s_])
                p2 = self.next_pf()
                k.mm(p2[0:96, :], self.ones_f[0:96, 0:96], s_[:, :], True, True, [self.ones_f, s_], [p2])
                self.rstd_from_sum(r_[:, :], p2[0:96, :], 96, (r_, p2))
                k.stt(f_[:, :], ps[0:96, :], self.vecs[0:96, gcol:gcol + 1], r_[:, :], ALU.mult, ALU.mult,
                      [ps, r_] + V, [f_])
                k.cp(qth[0:64, tsl], f_[0:64, :], [f_], [qth], en="pool")
                if tt * 512 >= TP:
                    po = tt * 512 - TP
                    pr = self.next_pf()
                    k.mm(pr[0:96, :], pm96[64:96, :], f_[64:96, :], True, True, [pm96, f_], [pr])
                    k.tt(t1[64:96, :], pr[64:96, :], cs[64:96, 1, po:po + 512], ALU.mult, [pr, cs], [t1])
                    k.tt(t2[64:96, :], f_[64:96, :], cs[64:96, 0, po:po + 512], ALU.mult, [f_, cs], [t2])
                    k.tt(qth[64:96, tsl], t1[64:96, :], t2[64:96, :], ALU.add, [t1, t2], [qth], en="pool")
                else:
                    k.cp(qth[64:96, tsl], f_[64:96, :], [f_], [qth], en="pool")
            if "mla_s6" in cfg.debug:
                return
            if h == 0:
                self.dump("mla_kth", kth, kth[:, :], [96, TK], BF16)
                self.dump("mla_qth", qth, qth[:, :], [96, T], BF16)
                self.dump("mla_rk", rk, rk[:, :], [128, TK // 128])
            for (s0, ln, is_s) in cfg.seqs:
                k0 = TP if is_s else s0
                nk = ln + 256 if is_s else ln
                qw = min(512, ln)
                for qg in range(ln // qw):
                    q0 = s0 + qg * qw
                    po = self.pf[4 + (npo % 2)]
                    npo += 1
                    nkt = nk // 128
                    pend = []

                    def _pv(item):
                        pt_, kt_, j_ = item
                        k.mm(po[0:65, :qw], vall[:, kt_, h, :], pt_[:, :qw], j_ == 0, j_ == nkt - 1, [vall, pt_], [po])
                    for j in range(nkt):
                        kt = k0 // 128 + j
                        pS = self.next_pf()
                        k.mm(pS[:, :qw], kth[0:96, kt * 128:(kt + 1) * 128], qth[0:96, q0:q0 + qw], True, True,
                             [kth, qth], [pS])
                        pt = pts[npt % 4]
                        npt += 1
                        k.act(pt[:, :qw], pS[:, :qw], AF.Exp, [pS, rk], [pt], scale=rk[:, kt:kt + 1])
                        pend.append((pt, kt, j))
                        if len(pend) > 2:
                            _pv(pend.pop(0))
                    while pend:
                        _pv(pend.pop(0))
                    o_ = oa[qg % 2]
                    k.cp(o_[:, :qw], po[0:65, :qw], [po], [o_])
                    k.op("dve", lambda e: e.reciprocal(out=o_[64:65, :qw], in_=o_[64:65, :qw]), [o_], [o_])
                    pbc = self.next_pf()
                    k.mm(pbc[0:64, :qw], self.ones_f[64:65, 0:64], o_[64:65, :qw], True, True, [self.ones_f, o_], [pbc])
                    g_ = gts[qg % 2]
                    k.dma("sp", g_[:, :qw], self.zT[zo_g + h * 64:zo_g + (h + 1) * 64, q0:q0 + qw], [self.zT], [g_])
                    k.act(g_[:, :qw], g_[:, :qw], AF.Silu, [g_], [g_])
                    k.tt(o_[0:64, :qw], o_[0:64, :qw], pbc[0:64, :qw], ALU.mult, [o_, pbc], [o_])
                    ob = obs[qg % 2]
                    k.tt(ob[:, :qw], o_[0:64, :qw], g_[:, :qw], ALU.mult, [o_, g_], [ob])
                    k.dma("pool", self.o_scr[1, h * 64:(h + 1) * 64, q0:q0 + qw], ob[:, :qw], [ob], [self.o_scr])


    def phase_ssd(self, l):
        cfg, k = self.cfg, self.k
        T, TP = cfg.T, cfg.TP
        NCH = T // 128
        V = [self.vecs]
        zo_x, _ = ZROWS["ssd_xbc"]
        zo_m, _ = ZROWS["misc"]
        x_tm = k.sb("sx_tm", [128, NCH, 512], BF16)
        b_tm = k.sb("sb_tm", [128, NCH, 128], BF16)
        bT = k.sb("s_bT", [128, T], BF16)
        cT = k.sb("s_cT", [128, T], BF16)
        selm = k.sb("s_selm", [64, 16, 128])
        seld = k.sb("s_seld", [64, 2, 8])
        maskneg = k.sb("s_mask", [128, 2, 128])
        bcv = k.sb("s_bcv", [128, 8 + 512])
        k.dma("sp", selm[:], self.selm_d[:, :, :], (), [selm])
        k.dma("sp", seld[:], self.seld_d[:, :, :], (), [seld])
        k.dma("sp", maskneg[:], self.maskneg_d[:, :, :], (), [maskneg])
        k.dma("sp", bcv[:], self.ssd_bc[l], (), [bcv])
        with k.phase():
            xl = [k.sb(f"s_xl{i}", [128, T]) for i in range(2)]
            xc = [k.sb(f"s_xc{i}", [128, T]) for i in range(2)]
            for ft in range(6):
                a, c_ = xl[ft % 2], xc[ft % 2]
                k.dma("sp", a[:], self.zT[zo_x + ft * 128:zo_x + (ft + 1) * 128, :], [self.zT], [a])
                self.dwconv(c_, a, lambda kk: self.vcol("ssd_cw", kk * 6 + ft), self.vcol("ssd_cb", ft), V)
                k.act(c_[:], c_[:], AF.Silu, [c_], [c_])
                if ft == 4:
                    k.cp(bT[:], c_[:], [c_], [bT], en="pool")
                if ft == 5:
                    k.cp(cT[:], c_[:], [c_], [cT], en="pool")
                if ft <= 4:
                    for ch in range(NCH):
                        ps = self.next_pf()
                        k.tr(ps[:, 0:128], c_[:, ch * 128:(ch + 1) * 128], self.ident_f[:], [c_, self.ident_f], [ps])
                        dst = x_tm[:, ch, ft * 128:(ft + 1) * 128] if ft < 4 else b_tm[:, ch, :]
                        k.cp(dst, ps[:, 0:128], [ps], [x_tm if ft < 4 else b_tm], en=("act" if ch % 2 else "dve"))
        cs = k.sb("s_cs", [64, T])
        sc3t = k.sb("s_sc3", [64, 3, T])
        nega = k.sb("s_nega", [64, 1])
        _cm_stack = contextlib.ExitStack()
        k.barrier()
        k.stacks.append(_cm_stack)
        cmask = k.sb("s_cmask", [64, T + 1])

        class _V:
            def __init__(self, tl, j):
                self.tl, self.j, self.res, self.psum = tl, j, tl.res, False

            def __getitem__(self, key):
                if not isinstance(key, tuple):
                    key = (key,)
                return self.tl[(key[0], self.j) + tuple(key[1:])]
        sc3 = sc3t
        dtt = sc3t
        D0 = lambda *a: sc3t[(a[0], 0) + tuple(a[1:])] if a else sc3t[:, 0, :]
        k.memset(sc3t[:, 0, :], 0.0, [sc3t])
        k.dma("sp", sc3t[0:8, 0, :], self.zT[zo_m + 32:zo_m + 40, :], [self.zT], [sc3t])
        k.dma("sp", sc3t[32:40, 0, :], self.zT[zo_m + 40:zo_m + 48, :], [self.zT], [sc3t])
        k.dma("sp", cmask[:], self.cmask_d[:, :], (), [cmask])
        k.act(sc3t[:, 0, :], sc3t[:, 0, :], AF.Exp, [sc3t] + V, [sc3t], bias=self.vecs[0:64, self.vl.cols["ssd_dtb"][0]:self.vl.cols["ssd_dtb"][0] + 1])
        k.act(sc3t[:, 0, :], sc3t[:, 0, :], AF.Ln, [sc3t], [sc3t], bias=1.0)
        k.act(nega[:], self.vecs[0:64, self.vl.cols["ssd_alog"][0]:self.vl.cols["ssd_alog"][0] + 1], AF.Exp, V, [nega])
        k.ts(nega[:], nega[:], -1.0, None, ALU.mult, None, [nega], [nega])
        k.ts(sc3t[:, 2, :], sc3t[:, 0, :], nega[:, 0:1], None, ALU.mult, None, [sc3t, nega], [sc3t])
        k.scan(cs[0:32, :], cmask[0:32, 0:T], sc3t[0:32, 2, :], 0.0, [cmask, sc3t], [cs])
        k.scan(cs[32:64, :][:, ::-1], cmask[32:64, 1:T + 1][:, ::-1], sc3t[32:64, 2, :][:, ::-1], 0.0, [cmask, sc3t], [cs])
        k.barrier()
        k.stacks.pop()
        _cm_stack.close()
        c3 = lambda ap: ap.rearrange("p (c t) -> p c t", t=128)
        k.tt(c3(sc3[0:32, 1, :]), c3(cs[0:32, :])[:, :, 127:128].to_broadcast([32, NCH, 128]), c3(cs[0:32, :]),
             ALU.subtract, [cs], [sc3])
        k.tt(c3(sc3[32:64, 1, :]), c3(cs[32:64, :])[:, :, 0:1].to_broadcast([32, NCH, 128]), c3(cs[32:64, :]),
             ALU.subtract, [cs], [sc3])
        k.act(sc3[:, 1, :], sc3[:, 1, :], AF.Exp, [sc3], [sc3])
        k.tt(sc3[:, 1, :], sc3[:, 1, :], sc3[:, 0, :], ALU.mult, [sc3], [sc3])
        k.act(sc3[:, 2, :], cs[:], AF.Exp, [cs], [sc3])
        self.dump("ssd_cs", cs, cs[:, :], [64, T])
        self.dump("ssd_sc3", sc3t, sc3t[:, :, :], [64, 3, T])
        self.dump("ssd_xtm", x_tm, x_tm[:, :, :], [128, NCH, 512], BF16)
        self.dump("ssd_bT", bT, bT[:, :], [128, T], BF16)
        Hf = k.sb("s_H", [128, 4, 64])
        Hb = k.sb("s_Hb", [128, 4, 64], BF16)
        selg = k.sb("s_selg", [64, 128])
        k.dma("sp", selg[:], self.selg_d[:, :], (), [selg])
        sctm = [k.sb(f"s_sctm{i}", [128, 3, 64]) for i in range(2)]
        cstm = [k.sb(f"s_cstm{i}", [128, 64]) for i in range(2)]
        xd = [k.sb(f"s_xd{i}", [128, 8, 64], BF16) for i in range(2)]
        xdd = [k.sb(f"s_xdd{i}", [128, 8, 64], BF16) for i in range(2)]
        lt = [k.sb(f"s_lt{i}", [128, 4, 128]) for i in range(2)]
        mT = [k.sb(f"s_mT{i}", [128, 4, 128], BF16) for i in range(2)]
        yo = [k.sb(f"s_yo{i}", [128, 512]) for i in range(2)]
        yf = [k.sb("s_yf0", [128, 512])] * 2
        zg = [k.sb("s_zg0", [128, 512])] * 2
        et = k.sb("s_et", [128, 4])
        etr = k.sb("s_etr", [64, 4])
        h0t = k.sb("s_h0t", [64, 8, 64])
        fint = k.sb("s_fint", [64, 8, 64])
        junk = yf[0]
        ssq = k.sb("s_ssq", [128, 1])
        ob = [k.sb(f"s_ob{i}", [128, 4, 128], BF16) for i in range(2)]
        it = 0
        for (s0, ln, is_s) in cfg.seqs:
            si = s0 // ln
            chs = list(range(s0 // 128, (s0 + ln) // 128))
            for d in range(2):
                if is_s:
                    k.dma("sp", h0t[:], self.ssd_h0[l, d], (), [h0t])
                    for h in range(8):
                        ps = self.next_pf()
                        if h < 4:
                            k.tr(ps[0:64, 0:64], h0t[:, h, :], self.ident_f[0:64, 0:64], [h0t, self.ident_f], [ps])
                            k.cp(Hf[0:64, h, :], ps[0:64, 0:64], [ps], [Hf])
                        else:
                            k.tr(ps[:, 0:64], h0t[:, h - 1:h + 1, :].rearrange("p a n -> p (a n)"),
                                 self.ident_f[0:64, 0:64], [h0t, self.ident_f], [ps])
                            k.cp(Hf[64:128, h - 4, :], ps[64:128, 0:64], [ps], [Hf])
                else:
                    k.memset(Hf[:], 0.0, [Hf], en="dve")
                k.cp(Hb[:], Hf[:], [Hf], [Hb])
                for ch in (chs if d == 0 else chs[::-1]):
                    it += 1
                    tsl = slice(ch * 128, (ch + 1) * 128)
                    st_, ct_, xd_, xdd_ = sctm[it % 2], cstm[it % 2], xd[it % 2], xdd[it % 2]
                    pt = self.next_pf()
                    for j in range(3):
                        k.tr(pt[:, j * 64:(j + 1) * 64], sc3[:, j, tsl], self.ident_f[0:64, 0:64], [sc3, self.ident_f], [pt])
                    k.tr(pt[:, 192:256], cs[:, tsl], self.ident_f[0:64, 0:64], [cs, self.ident_f], [pt])
                    k.cp(st_[:], pt[:, 0:192].rearrange("p (j r) -> p j r", j=3), [pt], [st_])
                    k.cp(ct_[:], pt[:, 192:256], [pt], [ct_])
                    r0 = d * 32
                    xv = x_tm[:, ch, :].rearrange("p (h q) -> p h q", h=8)
                    k.tt(xd_[:], xv, st_[:, 0, r0:r0 + 8].unsqueeze(2).to_broadcast([128, 8, 64]), ALU.mult,
                         [x_tm, st_], [xd_])
                    k.tt(xdd_[:], xv, st_[:, 1, r0:r0 + 8].unsqueeze(2).to_broadcast([128, 8, 64]), ALU.mult,
                         [x_tm, st_], [xdd_], en="pool")
                    py = self.pf[4]
                    pyo = self.pf[5]
                    for g in range(2):
                        lt_, m_ = lt[g], mT[g]
                        pb_ = self.next_pf()
                        for hh in range(4):
                            h = g * 4 + hh
                            k.mm(pb_[:, hh * 128:(hh + 1) * 128], selm[:, d * 8 + h, :], cs[:, tsl], True, True,
                                 [selm, cs], [pb_])
                        k.tt(lt_[:], pb_[:, :].rearrange("p (h t) -> p h t", h=4),
                             ct_[:, r0 + g * 4:r0 + g * 4 + 4].unsqueeze(2).to_broadcast([128, 4, 128]), ALU.subtract,
                             [pb_, ct_], [lt_])
                        k.tt(lt_[:], lt_[:], maskneg[:, d:d + 1, :].to_broadcast([128, 4, 128]), ALU.add,
                             [lt_, maskneg], [lt_])
                        k.act(lt_[:], lt_[:], AF.Exp, [lt_], [lt_])
                        pg = self.next_pf()
                        k.mm(pg[:, 0:128], bT[g * 64:(g + 1) * 64, tsl], cT[g * 64:(g + 1) * 64, tsl], True, True,
                             [bT, cT], [pg])
                        k.tt(m_[:], lt_[:], pg[:, 0:128].unsqueeze(1).to_broadcast([128, 4, 128]), ALU.mult,
                             [lt_, pg], [m_])
                        for hh in range(4):
                            h = g * 4 + hh
                            k.mm(py[:, h * 64:(h + 1) * 64], m_[:, hh, :], xd_[:, h, :], True, True, [m_, xd_], [py])
                        k.mm(pyo[:, g * 256:(g + 1) * 256], cT[g * 64:(g + 1) * 64, tsl],
                             Hb[g * 64:(g + 1) * 64, :, :], True, True, [cT, Hb], [pyo])
                    y_ = yo[it % 2]
                    k.tt(y_[:].rearrange("p (h q) -> p h q", h=8), pyo[:, :].rearrange("p (h q) -> p h q", h=8),
                         st_[:, 2, r0:r0 + 8].unsqueeze(2).to_broadcast([128, 8, 64]), ALU.mult, [pyo, st_], [y_])
                    k.tt(y_[:], y_[:], py[:, :], ALU.add, [y_, py], [y_])
                    ps_ = self.next_pf()
                    for g in range(2):
                        k.mm(ps_[g * 64:(g + 1) * 64, 0:256], b_tm[:, ch, g * 64:(g + 1) * 64],
                             xdd_[:, g * 4:(g + 1) * 4, :], True, True, [b_tm, xdd_], [ps_])
                    e_tok = ch * 128 + (127 if d == 0 else 0)
                    k.ts(etr[:], seld[:, d, 0:4], cs[:, e_tok:e_tok + 1], None, ALU.mult, None, [seld, cs], [etr])
                    pe_ = self.next_pf()
                    k.mm(pe_[:, 0:4], selg[:, :], etr[:], True, True, [selg, etr], [pe_])
                    k.act(et[:], pe_[:, 0:4], AF.Exp, [pe_], [et])
                    k.tt(Hf[:], Hf[:], et[:].unsqueeze(2).to_broadcast([128, 4, 64]), ALU.mult, [Hf, et], [Hf])
                    k.tt(Hf[:], Hf[:], ps_[:, 0:256].rearrange("p (h q) -> p h q", h=4), ALU.add, [Hf, ps_], [Hf])
                    k.cp(Hb[:], Hf[:], [Hf], [Hb], en="pool")
                    if it == 1:
                        self.dump("ssd_lt", lt[1], lt[1][:, :, :], [128, 4, 128])
                        self.dump("ssd_mT", mT[1], mT[1][:, :, :], [128, 4, 128], BF16)
                        self.dump("ssd_y0", y_, y_[:, :], [128, 512])
                        self.dump("ssd_H1", Hf, Hf[:, :, :], [128, 4, 64])
                        self.dump("ssd_sctm", st_, st_[:, :, :], [128, 3, 64])
                    if d == 0:
                        k.dma("sp", self.ssd_y[tsl, :], y_[:], [y_], [self.ssd_y])
                    else:
                        f_, z_ = yf[it % 2], zg[it % 2]
                        k.dma("sp", f_[:], self.ssd_y[tsl, :], [self.ssd_y], [f_])
                        k.dma("sp", z_[:], self.zg_tm[tsl, :], [self.zg_tm], [z_])
                        k.tt(y_[:], y_[:], f_[:], ALU.add, [y_, f_], [y_])
                        k.tt(f_[:].rearrange("p (h q) -> p h q", h=8), xv,
                             bcv[:, 0:8].unsqueeze(2).to_broadcast([128, 8, 64]), ALU.mult, [x_tm, bcv], [f_])
                        k.tt(y_[:], y_[:], f_[:], ALU.add, [y_, f_], [y_])
                        k.act(z_[:], z_[:], AF.Silu, [z_], [z_])
                        k.tt(y_[:], y_[:], z_[:], ALU.mult, [y_, z_], [y_])
                        k.act(junk[:], y_[:], AF.Square, [y_], [junk, ssq], accum_out=ssq[:])
                        self.rstd_from_sum(ssq[:], ssq[:], 512, (ssq, ssq))
                        k.stt(y_[:], y_[:], ssq[:, 0:1], bcv[:, 8:8 + 512], ALU.mult, ALU.mult, [y_, ssq, bcv], [y_])
                        o_ = ob[it % 2]
                        for ft in range(4):
                            pt2 = self.next_pf()
                            k.tr(pt2[:, 0:128], y_[:, ft * 128:(ft + 1) * 128], self.ident_f[:], [y_, self.ident_f], [pt2])
                            k.cp(o_[:, ft, :], pt2[:, 0:128], [pt2], [o_], en=("act" if ft % 2 else "dve"))
                        k.dma("sp", self.o_scr[2, :, tsl].rearrange("(c p) t -> p c t", p=128), o_[:], [o_], [self.o_scr])
                if not is_s:
                    for h in range(8):
                        ps = self.next_pf()
                        g_ = h // 4
                        k.tr(ps[0:64, 0:64], Hf[g_ * 64:(g_ + 1) * 64, h % 4, :],
                             self.ident_f[g_ * 64:(g_ + 1) * 64, g_ * 64:(g_ + 1) * 64], [Hf, self.ident_f], [ps])
                        k.cp(fint[:, h, :], ps[0:64, 0:64], [ps], [fint])
                    k.dma("pool", self.o_ssd_fin[l, si, d], fint[:], [fint], [self.o_ssd_fin])


    def phase_rwkv(self, l):
        cfg, k = self.cfg, self.k
        T, TP = cfg.T, cfg.TP
        TS = 512
        V = [self.vecs]
        NC64 = T // 64
        zr = {n: ZROWS[n][0] for n in ("rw_r", "rw_k", "rw_v", "rw_lora", "rw_gate")}
        pc_all = k.sb("rw_pc", [128, 4, 2, NC64])
        blk = k.sb("rw_blk", [128, 128])
        k.dma("sp", blk[:], self.blk64_d[:, :], (), [blk])
        vc = lambda name, j: self.vcol(name, j)
        with k.phase():
            lw = k.sb("rw_lw", [128, 2, 512])
            cm = k.sb("rw_cm", [128, TS + 1])
            omk = k.sb("rw_omk", [128, 4])
            k.dma("sp", lw[:], self.rw_lw[l], (), [lw])
            k.dma("sp", cm[:], self.cmask64_d[:, :], (), [cm])
            k.ts(omk[:], self.vcol("rw_ka", 0, 4), -1.0, 1.0, ALU.mult, ALU.add, V, [omk])
            xes = [k.sb(f"rw_xe{i}", [128, TS + 2]) for i in range(3)]
            shs = [k.sb(f"rw_sh{i}", [128, TS]) for i in range(3)]
            mixn = [0]
            lora = k.sb("rw_lora", [128, TS])
            ostg = [[k.sb(f"rw_ostg{d_}_{j_}", [128, TS]) for j_ in range(6)] for d_ in range(2)]
            vstg = [k.sb(f"rw_vstg{i}", [128, TS]) for i in range(2)]
            bstg = [k.sb(f"rw_bstg{i}", [128, TS]) for i in range(2)]
            names = ("r", "k", "v", "kk", "a", "kd", "be", "lw_", "cl", "e1", "e2", "t1", "t2", "bon")
            tl = {n: k.sb("rw_" + n, [128, TS]) for n in names}

            def mix(dst, zrow, mucol, a, is_s, s0, ln):
                b_ = a + TS
                mixn[0] += 1
                xe, sh = xes[mixn[0] % 3], shs[mixn[0] % 3]
                k.dma("sp", xe[:, 1:TS + 1], self.zT[zrow:zrow + 128, a:b_], [self.zT], [xe])
                if is_s:
                    if a > s0:
                        k.dma("sp", xe[:, 0:1], self.zT[zrow:zrow + 128, a - 1:a], [self.zT], [xe], allow_slow_non_contiguous=True)
                    else:
                        k.memset(xe[:, 0:1], 0.0, [xe])
                    if b_ < s0 + ln:
                        k.dma("sp", xe[:, TS + 1:TS + 2], self.zT[zrow:zrow + 128, b_:b_ + 1], [self.zT], [xe], allow_slow_non_contiguous=True)
                    else:
                        k.memset(xe[:, TS + 1:TS + 2], 0.0, [xe])
                    k.tt(sh[:], xe[:, 0:TS], xe[:, 2:TS + 2], ALU.add, [xe], [sh])
                else:
                    ns = TS // ln
                    k.memset(sh[:], 0.0, [sh], en="dve")
                    v3 = lambda ap: ap.rearrange("p (s t) -> p s t", s=ns)
                    xv = v3(xe[:, 1:TS + 1])
                    k.tt(v3(sh[:])[:, :, 1:ln], v3(sh[:])[:, :, 1:ln], xv[:, :, 0:ln - 1], ALU.add, [sh, xe], [sh])
                    k.tt(v3(sh[:])[:, :, 0:ln - 1], v3(sh[:])[:, :, 0:ln - 1], xv[:, :, 1:ln], ALU.add, [sh, xe], [sh])
                k.stt(sh[:], sh[:], 0.5, xe[:, 1:TS + 1], ALU.mult, ALU.subtract, [sh, xe], [sh])
                k.stt(dst, sh[:], mucol, xe[:, 1:TS + 1], ALU.mult, ALU.add, [sh, xe] + V, [dst_tl[0]])

            dst_tl = [None]
            for a in range(0, T, TS):
                is_s = a >= TP
                s0, ln = (TP, cfg.ls) if is_s else (0, cfg.lp)
                c64 = a // 64
                dst_tl[0] = lora
                mix(lora[:], zr["rw_lora"], vc("rw_mu", 12), a, is_s, s0, ln)
                k.act(lora[0:64, :], lora[0:64, :], AF.Tanh, [lora], [lora])
                for ft in range(4):
                    R, K_, KK = tl["r"], tl["k"], tl["kk"]
                    Vv = vstg[ft % 2]
                    tl["bon"] = bstg[ft % 2]
                    for (dst, nm, mi) in ((R, "rw_r", 0), (K_, "rw_k", 4), (Vv, "rw_v", 8)):
                        dst_tl[0] = dst
                        mix(dst[:], zr[nm] + ft * 128, vc("rw_mu", mi + ft), a, is_s, s0, ln)
                    k.dma("pool", self.rw_v[ft * 128:(ft + 1) * 128, a:a + TS], Vv[:], [Vv], [self.rw_v])
                    t1, t2 = tl["t1"], tl["t2"]
                    k.ts(KK[:], K_[:], vc("rw_kk", ft), None, ALU.mult, None, [K_] + V, [KK])
                    k.tt(t1[:], KK[:], KK[:], ALU.mult, [KK], [t1])
                    ps = self.next_pf()
                    k.mm(ps[:, :], blk[:, :], t1[:], True, True, [blk, t1], [ps])
                    k.ts(t2[:], ps[:, :], 1e-12, None, ALU.add, None, [ps], [t2])
                    k.act(t2[:], t2[:], AF.Sqrt, [t2], [t2])
                    k.op("dve", lambda e: e.reciprocal(out=t2[:], in_=t2[:]), [t2], [t2])
                    k.tt(KK[:], KK[:], t2[:], ALU.mult, [KK, t2], [KK])
                    for d in range(2):
                        A_, KD, BE, LW, CL, E1, E2, BON = (tl[n] for n in ("a", "kd", "be", "lw_", "cl", "e1", "e2", "bon"))
                        ps = self.next_pf()
                        k.mm(ps[:, :], lw[0:64, d, ft * 128:(ft + 1) * 128], lora[0:64, :], True, True, [lw, lora], [ps])
                        k.act(LW[:], ps[:, :], AF.Sigmoid, [ps] + V, [LW], bias=vc("rw_w0", d * 4 + ft))
                        k.ts(LW[:], LW[:], -0.6065306597126334, None, ALU.mult, None, [LW], [LW])
                        ps2 = self.next_pf()
                        k.mm(ps2[:, :], lw[64:128, d, ft * 128:(ft + 1) * 128], lora[64:128, :], True, True, [lw, lora], [ps2])
                        k.act(A_[:], ps2[:, :], AF.Sigmoid, [ps2] + V, [A_], bias=vc("rw_a0", d * 4 + ft))
                        k.ts(KD[:], A_[:], vc("rw_ka", ft), omk[:, ft:ft + 1], ALU.mult, ALU.add, [A_, omk] + V, [KD])
                        k.tt(KD[:], KD[:], K_[:], ALU.mult, [KD, K_], [KD])
                        k.tt(BE[:], KK[:], A_[:], ALU.mult, [KK, A_], [BE])
                        k.stt(t1[:], R[:], vc("rw_rk", ft), KD[:], ALU.mult, ALU.mult, [R, KD] + V, [t1])
                        ps3 = self.next_pf()
                        k.mm(ps3[:, :], blk[:, :], t1[:], True, True, [blk, t1], [ps3])
                        if d == 0:
                            k.tt(BON[:], ps3[:, :], Vv[:], ALU.mult, [ps3, Vv], [BON])
                        else:
                            k.tt(t1[:], ps3[:, :], Vv[:], ALU.mult, [ps3, Vv], [t1])
                            k.tt(BON[:], BON[:], t1[:], ALU.add, [BON, t1], [BON])
                            k.dma("pool", self.rw_bonus[ft * 128:(ft + 1) * 128, a:a + TS], BON[:], [BON], [self.rw_bonus])
                        if d == 0:
                            k.scan(CL[:], cm[:, 0:TS], LW[:], 0.0, [cm, LW], [CL])
                        else:
                            k.scan(CL[:, ::-1], cm[:, 1:TS + 1][:, ::-1], LW[:, ::-1], 0.0, [cm, LW], [CL])
                        c3 = lambda ap: ap.rearrange("p (c t) -> p c t", t=64)
                        e_ = 63 if d == 0 else 0
                        ce = c3(CL[:])[:, :, e_:e_ + 1]
                        k.act(pc_all[:, ft, d, c64:c64 + TS // 64], ce.rearrange("p c o -> p (c o)"), AF.Exp, [CL], [pc_all])
                        O = lambda j: self.rw_ops[d, j, ft * 128:(ft + 1) * 128, a:a + TS]
                        k.act(E1[:], CL[:], AF.Exp, [CL], [E1])
                        k.tt(ostg[d][0][:], R[:], E1[:], ALU.mult, [R, E1], [ostg[d][0]])
                        k.dma("pool", O(0), ostg[d][0][:], [ostg[d][0]], [self.rw_ops])
                        k.act(E2[:], CL[:], AF.Exp, [CL], [E2], scale=-1.0)
                        k.tt(ostg[d][1][:], KD[:], E2[:], ALU.mult, [KD, E2], [ostg[d][1]])
                        k.dma("pool", O(1), ostg[d][1][:], [ostg[d][1]], [self.rw_ops])
                        k.tt(ostg[d][2][:], BE[:], E2[:], ALU.mult, [BE, E2], [ostg[d][2]])
                        k.dma("pool", O(2), ostg[d][2][:], [ostg[d][2]], [self.rw_ops])
                        k.tt(E1[:], CL[:], LW[:], ALU.subtract, [CL, LW], [E1])
                        k.act(E1[:], E1[:], AF.Exp, [E1], [E1])
                        k.tt(ostg[d][3][:], KK[:], E1[:], ALU.mult, [KK, E1], [ostg[d][3]])
                        k.dma("pool", O(3), ostg[d][3][:], [ostg[d][3]], [self.rw_ops])
                        k.tt(c3(E2[:]), ce.to_broadcast([128, TS // 64, 64]), c3(CL[:]), ALU.subtract, [CL], [E2])
                        k.act(E2[:], E2[:], AF.Exp, [E2], [E2])
                        k.tt(ostg[d][4][:], KD[:], E2[:], ALU.mult, [KD, E2], [ostg[d][4]])
                        k.dma("pool", O(4), ostg[d][4][:], [ostg[d][4]], [self.rw_ops])
                        k.tt(ostg[d][5][:], BE[:], E2[:], ALU.mult, [BE, E2], [ostg[d][5]])
                        k.dma("pool", O(5), ostg[d][5][:], [ostg[d][5]], [self.rw_ops])
        if "rw_prep_only" in cfg.debug:
            return
        with k.phase():
            msk = k.sb("rw_msk", [128, 2, 3, 128])
            k.dma("sp", msk[:], self.rw_mask_d[:, :, :, :], (), [msk])
            ops = [[k.sb(f"rw_op{j}_{i}", [128, 4, 128]) for j in range(7)] for i in range(2)]
            pmask = k.sb("rw_pmask", [128, 2])
            k.dma("sp", pmask[:], self.pmask_d[:, :], (), [pmask])
            mk = {nm: [k.sb(f"rw_mk{nm}{p_}", [128, 4, 128]) for p_ in range(2)] for nm in ("A", "B", "R")}
            NDT = BF16 if RW_BF16 else F32
            if RW_BF16:
                mkb = {nm: [k.sb(f"rw_mkb{nm}{p_}", [128, 4, 128], NDT) for p_ in range(2)] for nm in ("A", "B", "R")}
                Lb = {nm: k.sb(f"rw_Lb{nm}", [128, 4, 128], NDT) for nm in ("A", "B", "K")}
                TTb = k.sb("rw_TTb", [128, 8, 128], NDT)
            tms = [k.sb(f"rw_tm{j}", [128, 512]) for j in range(3)]
            NT = [k.sb(f"rw_NT{i}", [128, 8, 128], BF16 if RW_BF16 else F32) for i in range(2)]
            NN = [k.sb(f"rw_NN{i}", [128, 8, 128], BF16 if RW_BF16 else F32) for i in range(2)]
            TT = k.sb("rw_TT", [128, 8, 128])
            AakT = k.sb("rw_AakT", [128, 8, 128])
            ODT = BF16 if RW_OBF16 else F32
            ArkT = k.sb("rw_ArkT", [128, 8, 128], ODT)
            ArbT = k.sb("rw_ArbT", [128, 8, 128], ODT)
            mkRb = [k.sb(f"rw_mkRb{p_}", [128, 4, 128], ODT) for p_ in range(2)]
            Ktb = k.sb("rw_Ktb", [128, 4, 128], ODT)
            Btb = k.sb("rw_Btb", [128, 4, 128], ODT)
            Vtb = k.sb("rw_Vtb", [128, 512], ODT)
            Utb = k.sb("rw_Utb", [128, 512], ODT)
            STb = k.sb("rw_STb", [128, 4, 64], ODT)
            W1 = k.sb("rw_W1", [128, 512])
            Wt = k.sb("rw_Wt", [128, 512])
            Ut = k.sb("rw_Ut", [128, 512])
            ST = k.sb("rw_ST", [128, 4, 64])
            h0t = k.sb("rw_h0t", [64, 8, 64])
            fint = k.sb("rw_fint", [64, 8, 64])
            osb = [k.sb(f"rw_osb{i}", [128, 4, 128]) for i in range(2)]
            it = 0
            for (s0, ln, is_s) in cfg.seqs:
                si = s0 // ln
                wins = list(range(s0 // 128, (s0 + ln) // 128))
                for d in range(2):
                    if is_s:
                        k.dma("sp", h0t[:], self.rw_h0[l, d], (), [h0t])
                        for h in range(8):
                            ps = self.next_pf()
                            if h % 2 == 0:
                                k.tr(ps[0:64, 0:64], h0t[:, h, :], self.ident_f[0:64, 0:64], [h0t, self.ident_f], [ps])
                                k.cp(ST[0:64, h // 2, :], ps[0:64, 0:64], [ps], [ST])
                            else:
                                k.tr(ps[:, 0:64], h0t[:, h - 1:h + 1, :].rearrange("p a n -> p (a n)"),
                                     self.ident_f[0:64, 0:64], [h0t, self.ident_f], [ps])
                                k.cp(ST[64:128, h // 2, :], ps[64:128, 0:64], [ps], [ST])
                    else:
                        k.memset(ST[:], 0.0, [ST], en="dve")
                    wlist = (wins if d == 0 else wins[::-1])

                    def _load(op_, w_):
                        w0_ = w_ * 128
                        for j in range(6):
                            k.dma("sp", op_[j][:], self.rw_ops[d, j, :, w0_:w0_ + 128].rearrange("(f p) t -> p f t", p=128),
                                  [self.rw_ops], [op_[j]])
                        k.dma("sp", op_[6][:], self.rw_v[:, w0_:w0_ + 128].rearrange("(f p) t -> p f t", p=128),
                              [self.rw_v], [op_[6]])
                    _load(ops[(it + 1) % 2], wlist[0])
                    for wi, w in enumerate(wlist):
                        it += 1
                        w0 = w * 128
                        op = ops[it % 2]
                        if wi + 1 < len(wlist):
                            _load(ops[(it + 1) % 2], wlist[wi + 1])
                        Rt, Kt, Bt, At, Kh, Bh, Vf = op
                        for (src, dst, neg) in ((Vf, tms[0], False), (Kh, tms[1], False), (Bh, tms[2], True)):
                            ps = self.next_pf()
                            for ft in range(4):
                                k.tr(ps[:, ft * 128:(ft + 1) * 128], src[:, ft, :], self.ident_f[:], [src, self.ident_f], [ps])
                            if neg:
                                k.ts(dst[:], ps[:, :], -1.0, None, ALU.mult, None, [ps], [dst])
                            else:
                                k.cp(dst[:], ps[:, :], [ps], [dst], en="act")
                        Vtm, Khtm, Bhtm = tms
                        hb = lambda h: (h % 2) * 64
                        for nm, src in (("A", At), ("B", Bt)):
                            for p_ in range(2):
                                k.ts(mk[nm][p_][:], src[:], pmask[:, p_:p_ + 1], None, ALU.mult, None, [src, pmask], [mk[nm][p_]],
                                     en=("pool" if p_ else "dve"))
                        for p_ in range(2):
                            k.ts(mkRb[p_][:], Rt[:], pmask[:, p_:p_ + 1], None, ALU.mult, None, [Rt, pmask], [mkRb[p_]],
                                 en=("pool" if p_ else "dve"))
                        k.cp(Ktb[:], Kt[:], [Kt], [Ktb], en="act")
                        k.cp(Btb[:], Bt[:], [Bt], [Btb], en="pool")
                        k.cp(Vtb[:], tms[0][:], [tms[0]], [Vtb], en="act")
                        if RW_BF16:
                            for nm, src in (("A", At), ("B", Bt), ("R", Rt)):
                                for p_ in range(2):
                                    k.cp(mkb[nm][p_][:], mk[nm][p_][:], [mk[nm][p_]], [mkb[nm][p_]], en=("pool" if p_ else "act"))
                            for nm, src in (("A", At), ("B", Bt), ("K", Kt)):
                                k.cp(Lb[nm][:], src[:], [src], [Lb[nm]], en=("pool" if nm == "B" else "act"))
                        else:
                            mkb = dict(A=mk["A"], B=mk["B"], R=mkRb)
                            Lb = dict(A=At, B=Bt, K=Kt, Kr=Ktb, Br=Btb)
                        specs = ((Lb["B"], "A", NT[0], 0, -1.0), (Lb["A"], "B", NN[0], 1, -1.0), (Lb["K"], "A", AakT, 0, 1.0),
                                 (Lb["Kr"], "R", ArkT, 2, 1.0), (Lb["Br"], "R", ArbT, 2, -1.0))
                        for (L_, rn, dst, mi, sg) in specs:
                            for half in range(2):
                                ps = self.next_pf()
                                for hh in range(4):
                                    h = half * 4 + hh
                                    R_ = mkb[rn][h % 2]
                                    k.mm(ps[:, hh * 128:(hh + 1) * 128], L_[:, h // 2, :], R_[:, h // 2, :],
                                         True, True, [L_, R_], [ps])
                                k.stt(dst[:, half * 4:(half + 1) * 4, :], ps[:, :].rearrange("p (h t) -> p h t", h=4), sg,
                                      msk[:, d, mi:mi + 1, :].to_broadcast([128, 4, 128]), ALU.mult, ALU.mult, [ps, msk], [dst])
                        k.tt(TT[:], NT[0][:], self.ident_f[:, :].unsqueeze(1).to_broadcast([128, 8, 128]), ALU.add,
                             [NT[0], self.ident_f], [TT])
                        if RW_BF16:
                            k.cp(TTb[:], TT[:], [TT], [TTb], en="pool")
                        else:
                            TTb = TT
                        cur = 0
                        for lev in range(1, 6):
                            nxt = 1 - cur
                            for half in range(2):
                                hs = slice(half * 4, (half + 1) * 4)
                                pn = self.next_pf()
                                for hh in range(4):
                                    h = half * 4 + hh
                                    k.mm(pn[:, hh * 128:(hh + 1) * 128], NT[cur][:, h, :], NN[cur][:, h, :], True, True,
                                         [NT[cur], NN[cur]], [pn])
                                k.cp(NN[nxt][:, hs, :], pn[:, :].rearrange("p (h t) -> p h t", h=4), [pn], [NN[nxt]], en="act")
                                if lev < 5:
                                    pt_ = self.next_pf()
                                    for hh in range(4):
                                        h = half * 4 + hh
                                        k.mm(pt_[:, hh * 128:(hh + 1) * 128], NN[cur][:, h, :], NT[cur][:, h, :], True, True,
                                             [NT[cur], NN[cur]], [pt_])
                                    k.cp(NT[nxt][:, hs, :], pt_[:, :].rearrange("p (h t) -> p h t", h=4), [pt_], [NT[nxt]], en="act")
                                pp = self.next_pf()
                                for hh in range(4):
                                    h = half * 4 + hh
                                    k.mm(pp[:, hh * 128:(hh + 1) * 128], NN[nxt][:, h, :], TTb[:, h, :], True, True,
                                         [NN[nxt], TTb], [pp])
                                k.tt(TT[:, hs, :], TT[:, hs, :], pp[:, :].rearrange("p (h t) -> p h t", h=4), ALU.add,
                                     [TT, pp], [TT])
                                if lev < 5 and RW_BF16:
                                    k.cp(TTb[:, hs, :], TT[:, hs, :], [TT], [TTb], en="pool")
                            cur = nxt
                        pw = self.next_pf()
                        for h in range(8):
                            k.mm(pw[:, h * 64:(h + 1) * 64], AakT[:, h, :], Vtm[:, h * 64:(h + 1) * 64], True, True,
                                 [AakT, Vtm], [pw])
                        k.cp(W1[:], pw[:, :], [pw], [W1])
                        po = self.pf[4 + it % 2]
                        o_ = osb[it % 2]
                        for cb in ((0, 64) if d == 0 else (64, 0)):
                            cs_ = slice(cb, cb + 64)
                            c64i = (w0 + cb) // 64
                            px = self.next_pf()
                            for h in range(8):
                                b0 = hb(h)
                                k.mm(px[cs_, h * 64:(h + 1) * 64], mk["A"][h % 2][:, h // 2, cs_], ST[:, h // 2, :],
                                     True, True, [mk["A"][h % 2], ST], [px])
                            k.tt(Wt[cs_, :], W1[cs_, :], px[cs_, :], ALU.add, [W1, px], [Wt])
                            pu = self.next_pf()
                            for h in range(8):
                                k.mm(pu[cs_, h * 64:(h + 1) * 64], TT[cs_, h, cs_], Wt[cs_, h * 64:(h + 1) * 64], True, True,
                                     [TT, Wt], [pu])
                            k.cp(Ut[cs_, :], pu[cs_, :], [pu], [Ut])
                            k.cp(STb[:], ST[:], [ST], [STb], en="pool")
                            k.cp(Utb[cs_, :], Ut[cs_, :], [Ut], [Utb], en="act")
                            pS = self.next_pf()
                            for h in range(8):
                                b0 = hb(h)
                                ft = h // 2
                                sreg = pS[b0:b0 + 64, ft * 64:(ft + 1) * 64]
                                k.mm(sreg, Khtm[cs_, h * 64:(h + 1) * 64], Vtm[cs_, h * 64:(h + 1) * 64], True, False,
                                     [Khtm, Vtm], [pS])
                                k.mm(sreg, Bhtm[cs_, h * 64:(h + 1) * 64], Ut[cs_, h * 64:(h + 1) * 64], False, True,
                                     [Bhtm, Ut], [pS])
                            for h in range(8):
                                b0 = hb(h)
                                ft = h // 2
                                oreg = po[b0:b0 + 64, ft * 128 + cb:ft * 128 + cb + 64]
                                k.mm(oreg, STb[:, ft, :], mkRb[h % 2][:, ft, cs_], True, False, [STb, mkRb[h % 2]], [po])
                                k.mm(oreg, Vtb[cs_, h * 64:(h + 1) * 64], ArkT[cs_, h, cs_], False, False, [Vtb, ArkT], [po])
                                k.mm(oreg, Utb[cs_, h * 64:(h + 1) * 64], ArbT[cs_, h, cs_], False, True, [Utb, ArbT], [po])
                            k.tt(ST[:], ST[:], pc_all[:, :, d, c64i:c64i + 1].to_broadcast([128, 4, 64]), ALU.mult,
                                 [ST, pc_all], [ST])
                            k.tt(ST[:], ST[:], pS[:, 0:256].rearrange("p (f i) -> p f i", f=4), ALU.add, [ST, pS], [ST])
                        k.cp(o_[:], po[:, :].rearrange("p (f t) -> p f t", f=4), [po], [o_], en="act")
                        k.dma("sp", self.rw_o[d, :, w0:w0 + 128].rearrange("(f p) t -> p f t", p=128), o_[:], [o_], [self.rw_o])
                    if not is_s:
                        for h in range(8):
                            b0 = (h % 2) * 64
                            ps = self.next_pf()
                            k.tr(ps[0:64, 0:64], ST[b0:b0 + 64, h // 2, :], self.ident_f[b0:b0 + 64, b0:b0 + 64],
                                 [ST, self.ident_f], [ps])
                            k.cp(fint[:, h, :], ps[0:64, 0:64], [ps], [fint])
                        k.dma("pool", self.o_rw_fin[si, l, d].rearrange("h i j -> i h j"), fint[:], [fint], [self.o_rw_fin])
        with k.phase():
            of = [k.sb(f"rw_of{i}", [128, TS]) for i in range(2)]
            ob_ = [k.sb(f"rw_obb{i}", [128, TS]) for i in range(2)]
            bo = [k.sb(f"rw_bo{i}", [128, TS]) for i in range(2)]
            gt = [k.sb(f"rw_gt{i}", [128, TS]) for i in range(2)]
            xc = [k.sb(f"rw_xc{i}", [128, TS]) for i in range(2)]
            sq = [k.sb(f"rw_sq{i}", [128, TS]) for i in range(2)]
            oo = [k.sb(f"rw_oo{i}", [128, TS], BF16) for i in range(2)]
            n = 0
            for a in range(0, T, TS):
                for ft in range(4):
                    n += 1
                    i = n % 2
                    rs_ = slice(ft * 128, (ft + 1) * 128)
                    k.dma("sp", of[i][:], self.rw_o[0, rs_, a:a + TS], [self.rw_o], [of[i]])
                    k.dma("sp", ob_[i][:], self.rw_o[1, rs_, a:a + TS], [self.rw_o], [ob_[i]])
                    k.dma("sp", bo[i][:], self.rw_bonus[rs_, a:a + TS], [self.rw_bonus], [bo[i]])
                    k.dma("sp", gt[i][:], self.zT[zr["rw_gate"] + ft * 128:zr["rw_gate"] + (ft + 1) * 128, a:a + TS],
                          [self.zT], [gt[i]])
                    k.tt(of[i][:], of[i][:], ob_[i][:], ALU.add, [of[i], ob_[i]], [of[i]])
                    pm = self.next_pf()
                    k.mm(pm[:, :], blk[:, :], of[i][:], True, True, [blk, of[i]], [pm])
                    k.stt(xc[i][:], pm[:, :], -1.0 / 64, of[i][:], ALU.mult, ALU.add, [pm, of[i]], [xc[i]])
                    k.tt(sq[i][:], xc[i][:], xc[i][:], ALU.mult, [xc[i]], [sq[i]])
                    pv = self.next_pf()
                    k.mm(pv[:, :], blk[:, :], sq[i][:], True, True, [blk, sq[i]], [pv])
                    k.ts(sq[i][:], pv[:, :], 1.0 / 64, 64e-5, ALU.mult, ALU.add, [pv], [sq[i]])
                    k.act(sq[i][:], sq[i][:], AF.Sqrt, [sq[i]], [sq[i]])
                    k.op("dve", lambda e: e.reciprocal(out=sq[i][:], in_=sq[i][:]), [sq[i]], [sq[i]])
                    k.tt(xc[i][:], xc[i][:], sq[i][:], ALU.mult, [xc[i], sq[i]], [xc[i]])
                    k.ts(xc[i][:], xc[i][:], vc("rw_ln_g", ft), vc("rw_ln_b", ft), ALU.mult, ALU.add, [xc[i]] + V, [xc[i]])
                    k.tt(xc[i][:], xc[i][:], bo[i][:], ALU.add, [xc[i], bo[i]], [xc[i]])
                    k.act(gt[i][:], gt[i][:], AF.Silu, [gt[i]], [gt[i]])
                    k.tt(oo[i][:], xc[i][:], gt[i][:], ALU.mult, [xc[i], gt[i]], [oo[i]])
                    k.dma("pool", self.o_scr[0, rs_, a:a + TS], oo[i][:], [oo[i]], [self.o_scr])


def host_weights(inp):
    w_in = np.asarray(inp["w_in"], np.float32)
    cols = []
    o = _IW
    cols.append(w_in[:, :, o["rw_pre"]:o["rw_pre"] + 1664])
    cols.append(w_in[:, :, o["rw_gate"]:o["rw_gate"] + 512])
    cols.append(w_in[:, :, o["mla_q"]:o["mla_q"] + 384])
    cols.append(w_in[:, :, o["mla_ckv"]:o["mla_ckv"] + 256])
    misc = np.zeros((NL, D, 128), np.float32)
    misc[:, :, 0:32] = w_in[:, :, o["mla_krope"]:o["mla_krope"] + 32]
    misc[:, :, 32:48] = w_in[:, :, o["ssd_dt"]:o["ssd_dt"] + 16]
    cols.append(misc)
    cols.append(w_in[:, :, o["mla_gate"]:o["mla_gate"] + 512])
    cols.append(w_in[:, :, o["ssd_xbc"]:o["ssd_xbc"] + 768])
    cols.append(w_in[:, :, o["lru_x"]:o["lru_x"] + 512])
    cols.append(w_in[:, :, o["lru_gate"]:o["lru_gate"] + 512])
    cols.append(w_in[:, :, o["ssd_gate"]:o["ssd_gate"] + 512])
    w_in_a = np.ascontiguousarray(np.concatenate(cols, -1))
    assert w_in_a.shape[-1] == NZ_ALL
    sel2 = np.zeros((2, 2, 128), np.float32)
    sel2[0, 0] = 1.0
    sel2[1, 1] = 1.0
    lw = np.zeros((NL, 2, 2, 4, 128, 128), np.float32)
    for d in range(2):
        for gi, nm in enumerate(("lru_wa", "lru_wx")):
            w = np.asarray(inp[nm], np.float32)
            for ft in range(4):
                for kb in range(2):
                    lw[:, d, gi, ft, kb * 64:(kb + 1) * 64, kb * 64:(kb + 1) * 64] = w[:, d, ft * 2 + kb]
    lru_w = np.ascontiguousarray(lw.reshape(NL, 16, 128, 128).transpose(0, 2, 1, 3))
    kvu = np.asarray(inp["mla_kv_up"], np.float32).reshape(NL, 256, 8, 128)
    inv = 10000.0 ** (-np.arange(8, dtype=np.float32) / 8)
    W_pm = np.zeros((32, 96), np.float32)
    for m_ in range(32):
        sw = m_ + 8 if (m_ % 16) < 8 else m_ - 8
        W_pm[sw, 64 + m_] = 1.0
    selm = np.zeros((64, 16, 128), np.float32)
    seld = np.zeros((64, 2, 8), np.float32)
    selg = np.zeros((64, 128), np.float32)
    for d_ in range(2):
        for h_ in range(8):
            selm[d_ * 32 + h_, d_ * 8 + h_, :] = 1.0
            seld[d_ * 32 + h_, d_, h_ % 4] = 1.0
            selg[d_ * 32 + h_, (h_ // 4) * 64:(h_ // 4 + 1) * 64] = 1.0
    ii = np.arange(128)
    maskneg = np.zeros((128, 2, 128), np.float32)
    maskneg[:, 0, :] = np.where(ii[:, None] <= ii[None, :], 0.0, -30000.0)
    maskneg[:, 1, :] = np.where(ii[:, None] >= ii[None, :], 0.0, -30000.0)
    ssd_bc = np.concatenate([np.broadcast_to(np.asarray(inp["ssd_d"], np.float32)[:, None, :], (NL, 128, 8)),
                             np.broadcast_to(np.asarray(inp["ssd_norm_g"], np.float32)[:, None, :], (NL, 128, 512))], -1)
    rw_lw = np.concatenate([np.asarray(inp["rw_w2"], np.float32), np.asarray(inp["rw_a2"], np.float32)], 2)
    rw_lw = np.ascontiguousarray(rw_lw.transpose(0, 2, 1, 3))
    same = (ii[:, None] // 64) == (ii[None, :] // 64)
    rw_mask = np.zeros((128, 2, 3, 128), np.float32)
    rw_mask[:, 0, 0, :] = same & (ii[:, None] < ii[None, :])
    rw_mask[:, 0, 1, :] = same & (ii[:, None] > ii[None, :])
    rw_mask[:, 0, 2, :] = same & (ii[:, None] <= ii[None, :])
    rw_mask[:, 1, 0, :] = same & (ii[:, None] > ii[None, :])
    rw_mask[:, 1, 1, :] = same & (ii[:, None] < ii[None, :])
    rw_mask[:, 1, 2, :] = same & (ii[:, None] >= ii[None, :])
    cm64 = np.ones((128, 513), np.float32)
    cm64[:, ::64] = 0.0
    W = dict(
        pmask=np.ascontiguousarray(np.stack([(ii < 64), (ii >= 64)], 1).astype(np.float32)),
        rw_lw=rw_lw, rw_mask=rw_mask, blk64=same.astype(np.float32), cmask64=cm64,
        selm=selm, seld=seld, selg=selg, maskneg=maskneg, ssd_bc=np.ascontiguousarray(ssd_bc),
        q_up=np.ascontiguousarray(inp["mla_q_up"], np.float32),
        kv_up_k=np.ascontiguousarray(kvu[:, :, :, :64].reshape(NL, 256, 512)),
        kv_up_v=np.ascontiguousarray(kvu[:, :, :, 64:].reshape(NL, 256, 512)),
        pm96=W_pm,
        _inv=inv,
        w_in_m=np.ascontiguousarray(w_in[:, :, o["merge"]:]),
        w_branch=np.ascontiguousarray(inp["w_branch"], np.float32),
        w_out=np.ascontiguousarray(inp["w_out"], np.float32),
        lru_w=lru_w,
        ada_w=np.ascontiguousarray(inp["ada_w"], np.float32),
        ada_bg=np.ascontiguousarray(np.asarray(inp["ada_b"], np.float32)[:, None, 2048:]),
        w_in_a=w_in_a,
        vecs=host_vecs(inp).pack(),
        ident=np.eye(128, dtype=np.float32),
        sel2=sel2,
    )
    return W


def core_inputs(inp, W, core, cfg):
    b = core // 2
    xp = np.asarray(inp["x_prompt"], np.float32)[core * cfg.n_prompt:(core + 1) * cfg.n_prompt, :cfg.lp]
    xs = np.asarray(inp["x_sample"], np.float32)[b, :cfg.ls]
    x_all = np.ascontiguousarray(np.concatenate([xp.reshape(-1, D), xs], 0))
    cond = np.stack([np.asarray(inp["c_ctx"], np.float32), np.asarray(inp["c"], np.float32)[b]], 0)
    condT = np.ascontiguousarray(cond.reshape(2, 8, 128).transpose(2, 1, 0))
    m = dict(W)
    inv = m.pop("_inv")
    t = np.arange(cfg.ls)
    row = (t // 64).astype(np.float32)
    col = (t % 64).astype(np.float32)
    ar, ac = row[None, :] * inv[:, None], col[None, :] * inv[:, None]
    cosT = np.concatenate([np.cos(ar), np.cos(ar), np.cos(ac), np.cos(ac)], 0)
    sinT = np.concatenate([-np.sin(ar), np.sin(ar), -np.sin(ac), np.sin(ac)], 0)
    cm = np.ones((64, cfg.T + 1), np.float32)
    cm[:, ::128] = 0.0
    m["cmask"] = cm
    m["ssd_h0"] = np.ascontiguousarray(np.asarray(inp["state_ssd"], np.float32)[b].transpose(0, 1, 3, 2, 4))
    m["rw_h0"] = np.ascontiguousarray(np.asarray(inp["state_rwkv"], np.float32)[b].transpose(0, 1, 3, 2, 4))
    m["rope_cs"] = np.ascontiguousarray(np.stack([cosT, sinT], 0).astype(np.float32))
    m["cache_ckv"] = np.ascontiguousarray(np.asarray(inp["cache_mla_ckv"], np.float32)[b])
    m["cache_kr"] = np.ascontiguousarray(np.asarray(inp["cache_mla_krope"], np.float32)[b])
    sl = np.asarray(inp["state_lru"], np.float32)[b]
    lru_h0 = np.ascontiguousarray(sl.reshape(NL, 2, 4, 128).transpose(0, 3, 1, 2))
    m.update(x_all=x_all, condT=condT, lru_h0=lru_h0)
    return m


_PROG = {}


def kernel(**inp):
    cfg = Cfg(n_prompt=4, lp=256, ls=4096, debug=DEBUG_FLAGS)
    if "p" not in _PROG:
        prog = Prog(cfg)
        prog.build()
        _PROG["p"] = prog
    prog = _PROG["p"]
    W = host_weights(inp)
    in_maps = []
    for core in range(8):
        m = core_inputs(inp, W, core, cfg)
        in_maps.append({k_: np.ascontiguousarray(v) for k_, v in m.items() if k_ in prog.inputs})
    res = run_bass_kernel_spmd(prog.nc, in_maps, core_ids=list(range(8)))
    R = res.results
    npq = cfg.n_prompt
    y_prompt = np.concatenate([R[c]["y_all"][:cfg.TP].reshape(npq, cfg.lp, D) for c in range(8)], 0)
    y_sample = np.stack([R[2 * b]["y_all"][cfg.TP:] for b in range(4)], 0)
    ckv = np.concatenate([R[c]["o_ckv"].reshape(NL, 256, npq, cfg.lp).transpose(2, 0, 3, 1) for c in range(8)], 0)
    kr = np.concatenate([R[c]["o_kr"].reshape(NL, 32, npq, cfg.lp).transpose(2, 0, 3, 1) for c in range(8)], 0)
    if "o_rw_fin" in R[0]:
        rw = np.concatenate([R[c]["o_rw_fin"] for c in range(8)], 0)
    else:
        rw = np.zeros((32, NL, 2, 8, 64, 64), np.float32)
    if "o_ssd_fin" in R[0]:
        ssd = np.concatenate([R[c]["o_ssd_fin"].transpose(1, 0, 2, 4, 3, 5) for c in range(8)], 0)
    else:
        ssd = np.zeros((32, NL, 2, 8, 64, 64), np.float32)
    lru = np.concatenate([R[c]["o_lru_fin"].transpose(2, 0, 3, 4, 1).reshape(npq, NL, 2, 512) for c in range(8)], 0)
    f = lambda a: np.ascontiguousarray(a, dtype=np.float32)
    return (f(y_prompt), f(y_sample), f(ckv), f(kr), f(rw), f(ssd), f(lru))
```

```python
import contextlib
import numpy as np
import ml_dtypes
import concourse.bass as bass
import concourse.mybir as mybir
from concourse.bass_utils import run_bass_kernel_spmd

F32 = mybir.dt.float32
BF16 = mybir.dt.bfloat16
AF = mybir.ActivationFunctionType
ALU = mybir.AluOpType
AX = mybir.AxisListType

D = 1024
NL = 2
NDMA_SLOTS = 10
RW_BF16 = False
RW_OBF16 = True
DEBUG_FLAGS = ()
SAME_ENGINE_SYNC = ("act", "dve", "pool")


class Res:
    __slots__ = ("w", "r", "name")

    def __init__(self, name=""):
        self.w = []
        self.r = {}
        self.name = name


class Tl:
    def __init__(self, h, name, psum=False, multi=False):
        self.h = h
        self.res = Res(name)
        self.name = name
        self.psum = psum
        self.res_multi = multi

    def __getitem__(self, k):
        return self.h[k]


class Eng:
    def __init__(self, name, h, sem):
        self.name = name
        self.h = h
        self.sem = sem
        self.count = 0
        self.waited = {}
        self.dma_sems = []
        self.dma_vals = []
        self.dma_n = 0


def _res(x):
    return x.res if isinstance(x, Tl) else x


class KB:
    def __init__(self, nc):
        self.nc = nc
        self.es = contextlib.ExitStack()
        self.stacks = [self.es]
        self.eng = {}
        self.uid = 0
        for name, h in (("pe", nc.tensor), ("dve", nc.vector), ("act", nc.scalar),
                        ("pool", nc.gpsimd), ("sp", nc.sync)):
            sem = self.es.enter_context(nc.semaphore("s_" + name))
            self.eng[name] = Eng(name, h, sem)
        for qn in ("sp", "pool", "act"):
            E = self.eng[qn]
            for i in range(NDMA_SLOTS):
                E.dma_sems.append(self.es.enter_context(nc.semaphore(f"d_{qn}{i}")))
                E.dma_vals.append(0)

    def sb(self, name, shape, dtype=F32):
        self.uid += 1
        nm = f"{name}_{self.uid}"
        h = self.stacks[-1].enter_context(self.nc.sbuf_tensor(nm, list(shape), dtype))
        return Tl(h, nm)

    def ps(self, name, shape, dtype=F32):
        self.uid += 1
        nm = f"{name}_{self.uid}"
        h = self.stacks[-1].enter_context(self.nc.psum_tensor(nm, list(shape), dtype))
        return Tl(h, nm, psum=True)

    def dram(self, name, shape, dtype=F32, kind="Internal"):
        h = self.nc.dram_tensor(name, list(shape), dtype, kind=kind)
        return Tl(h.ap(), name, multi=True)

    @contextlib.contextmanager
    def phase(self):
        self.barrier()
        st = contextlib.ExitStack()
        self.stacks.append(st)
        try:
            with st:
                yield
                self.barrier()
        finally:
            self.stacks.pop()

    def _wait(self, E, ev):
        sem, val, src = ev
        if src is E and E.name not in SAME_ENGINE_SYNC:
            return
        k = id(sem)
        if E.waited.get(k, 0) >= val:
            return
        E.h.wait_ge(sem, val)
        E.waited[k] = val

    def _deps(self, E, reads, writes):
        for r in reads:
            for ev in _res(r).w:
                self._wait(E, ev)
        for w in writes:
            rs = _res(w)
            if not (isinstance(w, Tl) and w.res_multi and not rs.r):
                for ev in rs.w:
                    self._wait(E, ev)
            for ev in rs.r.values():
                self._wait(E, ev)

    def _commit(self, ev, key, reads, writes):
        for r in reads:
            _res(r).r[key] = ev
        for w in writes:
            rs = _res(w)
            if isinstance(w, Tl) and w.res_multi and not rs.r:
                d_ = {id(e[0]): e for e in rs.w}
                d_[id(ev[0])] = ev
                rs.w = list(d_.values())
            else:
                rs.w = [ev]
                rs.r = {}

    def op(self, en, fn, reads=(), writes=()):
        E = self.eng[en]
        pr = [r for r in reads if isinstance(r, Tl) and r.psum]
        if pr:
            reads = [r for r in reads if not (isinstance(r, Tl) and r.psum)]
            writes = list(writes) + [r for r in pr if r not in writes]
        self._deps(E, reads, writes)
        ins = fn(E.h)
        E.count += 1
        ins.then_inc(E.sem, 1)
        ev = (E.sem, E.count, E)
        self._commit(ev, en, reads, writes)
        return ins

    def dma(self, qn, out, in_, reads=(), writes=(), **kw):
        E = self.eng[qn]
        self._deps(E, reads, writes)
        slot = E.dma_n % NDMA_SLOTS
        E.dma_n += 1
        sem = E.dma_sems[slot]
        pv = E.dma_vals[slot]
        if pv > 0:
            self._wait(E, (sem, pv, None))
        E.h.dma_start(out=out, in_=in_, **kw).then_inc(sem, 16)
        E.dma_vals[slot] = pv + 16
        ev = (sem, pv + 16, None)
        self._commit(ev, ("dma", qn, slot), reads, writes)

    def barrier(self):
        evs = []
        for E in self.eng.values():
            if E.count:
                evs.append((E.sem, E.count, E))
            for s, v in zip(E.dma_sems, E.dma_vals):
                if v:
                    evs.append((s, v, None))
        for E in self.eng.values():
            for ev in evs:
                self._wait(E, ev)

    def mm(self, out, lhsT, rhs, start, stop, reads, writes):
        return self.op("pe", lambda e: e.matmul(out, lhsT=lhsT, rhs=rhs, start=start, stop=stop),
                       reads, writes)

    def tr(self, out, in_, ident, reads, writes):
        return self.op("pe", lambda e: e.transpose(out, in_, ident), reads, writes)

    def act(self, out, in_, func, reads, writes, bias=None, scale=None, accum_out=None, en="act"):
        kw = {}
        if bias is not None:
            kw["bias"] = bias
        if scale is not None:
            kw["scale"] = scale
        if accum_out is not None:
            kw["accum_out"] = accum_out
        return self.op(en, lambda e: e.activation(out=out, in_=in_, func=func, **kw), reads, writes)

    def tt(self, out, in0, in1, op, reads, writes, en="dve"):
        return self.op(en, lambda e: e.tensor_tensor(out=out, in0=in0, in1=in1, op=op), reads, writes)

    def ts(self, out, in0, s1, s2, op0, op1, reads, writes, en="dve", accum_out=None):
        kw = {}
        if accum_out is not None:
            kw["accum_out"] = accum_out
        if op1 is None:
            return self.op(en, lambda e: e.tensor_scalar(out=out, in0=in0, scalar1=s1, scalar2=None,
                                                         op0=op0, **kw), reads, writes)
        return self.op(en, lambda e: e.tensor_scalar(out=out, in0=in0, scalar1=s1, scalar2=s2,
                                                     op0=op0, op1=op1, **kw), reads, writes)

    def stt(self, out, in0, scalar, in1, op0, op1, reads, writes):
        return self.op("dve", lambda e: e.scalar_tensor_tensor(out=out, in0=in0, scalar=scalar, in1=in1,
                                                               op0=op0, op1=op1), reads, writes)

    def cp(self, out, in_, reads, writes, en="dve"):
        if en == "act":
            return self.op("act", lambda e: e.copy(out=out, in_=in_), reads, writes)
        return self.op(en, lambda e: e.tensor_copy(out=out, in_=in_), reads, writes)

    def scan(self, out, d0, d1, init, reads, writes, op0=ALU.mult, op1=ALU.add):
        return self.op("dve", lambda e: e.tensor_tensor_scan(out=out, data0=d0, data1=d1, initial=init,
                                                             op0=op0, op1=op1), reads, writes)

    def memset(self, ap, val, writes, en="pool"):
        return self.op(en, lambda e: e.memset(ap, val), (), writes)

    def finish(self):
        self.barrier()


ZROWS = {}
_o = 0
for _n, _w in (("rw_r", 512), ("rw_k", 512), ("rw_v", 512), ("rw_lora", 128), ("rw_gate", 512),
               ("mla_q", 384), ("mla_ckv", 256), ("misc", 128), ("mla_gate", 512),
               ("ssd_xbc", 768), ("lru_x", 512), ("lru_gate", 512)):
    ZROWS[_n] = (_o, _w)
    _o += _w
NZ_FM = _o
NZ_ALL = NZ_FM + 512
_IW = dict(rw_pre=0, rw_gate=1664, mla_q=2176, mla_ckv=2560, mla_krope=2816, mla_gate=2848,
           ssd_gate=3360, ssd_xbc=3872, ssd_dt=4640, lru_x=4656, lru_gate=5168, merge=5680)


class VecPack:
    def __init__(self):
        self.cols = {}
        self.n = 0
        self.data = []

    def add(self, name, arr2d):
        k = arr2d.shape[-1]
        self.cols[name] = (self.n, k)
        self.n += k
        self.data.append(np.asarray(arr2d, np.float32))

    def fm(self, name, v):
        v = np.asarray(v, np.float32)
        n = v.shape[-1]
        if n % 128:
            pad = 128 - n % 128
            v = np.concatenate([v, np.zeros(v.shape[:-1] + (pad,), np.float32)], -1)
        k = v.shape[-1] // 128
        self.add(name, v.reshape(v.shape[0], k, 128).transpose(0, 2, 1))

    def pack(self):
        return np.ascontiguousarray(np.concatenate(self.data, -1))


def vec_layout():
    vp = VecPack()
    z = lambda *s: np.zeros(s, np.float32)
    vp.fm("ada_b_ss", z(NL, 2048))
    vp.fm("norm_g", z(NL, 1024))
    vp.fm("mla_qa_g", z(NL, 384))
    vp.fm("mla_kva_g", z(NL, 256))
    vp.fm("mla_qn_g", z(NL, 96))
    vp.fm("mla_kn_g", z(NL, 96))
    vp.fm("rw_mu", z(NL, 1664))
    vp.fm("rw_w0", z(NL, 1024))
    vp.fm("rw_a0", z(NL, 1024))
    vp.fm("rw_kk", z(NL, 512))
    vp.fm("rw_ka", z(NL, 512))
    vp.fm("rw_rk", z(NL, 512))
    vp.fm("rw_ln_g", z(NL, 512))
    vp.fm("rw_ln_b", z(NL, 512))
    vp.fm("ssd_cw", z(NL, 4 * 768))
    vp.fm("ssd_cb", z(NL, 768))
    vp.fm("ssd_dtb", z(NL, 64))
    vp.fm("ssd_alog", z(NL, 64))
    vp.fm("lru_cw", z(NL, 4 * 512))
    vp.fm("lru_cb", z(NL, 512))
    vp.fm("lru_ba", z(NL, 2 * 512))
    vp.fm("lru_bx", z(NL, 2 * 512))
    vp.fm("lru_lam", z(NL, 2 * 512))
    return vp


def host_vecs(inp):
    vp = VecPack()
    vp.fm("ada_b_ss", inp["ada_b"][:, :2048])
    vp.fm("norm_g", inp["norm_g"])
    r2 = lambda a: np.asarray(a, np.float32).reshape(NL, -1)
    vp.fm("mla_qa_g", inp["mla_qa_g"])
    vp.fm("mla_kva_g", inp["mla_kva_g"])
    vp.fm("mla_qn_g", inp["mla_qn_g"])
    vp.fm("mla_kn_g", inp["mla_kn_g"])
    vp.fm("rw_mu", inp["rw_mu"])
    vp.fm("rw_w0", r2(inp["rw_w0"]))
    vp.fm("rw_a0", r2(inp["rw_a0"]))
    vp.fm("rw_kk", inp["rw_kk"])
    vp.fm("rw_ka", inp["rw_ka"])
    vp.fm("rw_rk", r2(inp["rw_rk"]))
    vp.fm("rw_ln_g", inp["rw_ln_g"])
    vp.fm("rw_ln_b", inp["rw_ln_b"])
    vp.fm("ssd_cw", r2(inp["ssd_conv_w"]))
    vp.fm("ssd_cb", inp["ssd_conv_b"])
    def d64(a):
        a = np.asarray(a, np.float32)
        o_ = np.zeros((NL, 64), np.float32)
        o_[:, 0:8] = a[:, 0]
        o_[:, 32:40] = a[:, 1]
        return o_
    vp.fm("ssd_dtb", d64(inp["ssd_dt_bias"]))
    vp.fm("ssd_alog", d64(inp["ssd_a_log"]))
    vp.fm("lru_cw", r2(inp["lru_conv_w"]))
    vp.fm("lru_cb", inp["lru_conv_b"])
    vp.fm("lru_ba", r2(inp["lru_ba"]))
    vp.fm("lru_bx", r2(inp["lru_bx"]))
    vp.fm("lru_lam", r2(inp["lru_lambda"]))
    return vp


class Cfg:
    def __init__(self, n_prompt=4, lp=256, ls=4096, debug=()):
        self.n_prompt = n_prompt
        self.lp = lp
        self.ls = ls
        self.TP = n_prompt * lp
        self.T = self.TP + ls
        self.seqs = [(i * lp, lp, False) for i in range(n_prompt)] + [(self.TP, ls, True)]
        self.debug = set(debug)
        assert self.T % 512 == 0 and lp % 128 == 0 and ls % 512 == 0 and self.TP % 512 == 0


class Prog:
    def __init__(self, cfg):
        self.cfg = cfg
        self.nc = bass.Bass("TRN2", target_bir_lowering=False)
        self.k = KB(self.nc)
        self.inputs = {}
        self.outputs = {}
        self.vl = vec_layout()

    def din(self, name, shape, dtype=F32):
        t = self.k.dram(name, shape, dtype, kind="ExternalInput")
        self.inputs[name] = t
        return t

    def dout(self, name, shape, dtype=F32):
        t = self.k.dram(name, shape, dtype, kind="ExternalOutput")
        self.outputs[name] = t
        return t

    def vcol(self, name, j=0, n=1):
        o, k = self.vl.cols[name]
        assert j + n <= k
        return self.vecs[:, o + j:o + j + n]

    def build(self):
        cfg, k = self.cfg, self.k
        T = cfg.T
        with k.es:
            self.x_in = self.din("x_all", [T, D])
            self.condT = self.din("condT", [128, 8, 2])
            self.ada_w = self.din("ada_w", [NL, D, 3 * D])
            self.ada_bg = self.din("ada_bg", [NL, 1, D])
            self.w_in_a = self.din("w_in_a", [NL, D, NZ_ALL])
            self.vecs_d = self.din("vecs", [NL, 128, self.vl.n])
            self.ident_d = self.din("ident", [128, 128])
            self.sel2_d = self.din("sel2", [2, 2, 128])
            self.lru_w = self.din("lru_w", [NL, 128, 16, 128])
            self.lru_h0 = self.din("lru_h0", [NL, 128, 2, 4])
            self.o_lru_fin = self.dout("o_lru_fin", [NL, 128, cfg.n_prompt, 2, 4])
            self.q_up = self.din("q_up", [NL, 384, 768])
            self.kv_up_k = self.din("kv_up_k", [NL, 256, 512])
            self.kv_up_v = self.din("kv_up_v", [NL, 256, 512])
            self.cache_ckv = self.din("cache_ckv", [NL, 256, 256])
            self.cache_kr = self.din("cache_kr", [NL, 256, 32])
            self.rope_cs = self.din("rope_cs", [2, 32, cfg.ls])
            self.pm96_d = self.din("pm96", [32, 96])
            self.o_ckv = self.dout("o_ckv", [NL, 256, cfg.TP])
            self.o_kr = self.dout("o_kr", [NL, 32, cfg.TP])
            self.ssd_bc = self.din("ssd_bc", [NL, 128, 8 + 512])
            self.ssd_h0 = self.din("ssd_h0", [NL, 2, 64, 8, 64])
            self.selm_d = self.din("selm", [64, 16, 128])
            self.seld_d = self.din("seld", [64, 2, 8])
            self.selg_d = self.din("selg", [64, 128])
            self.maskneg_d = self.din("maskneg", [128, 2, 128])
            self.cmask_d = self.din("cmask", [64, T + 1])
            self.o_ssd_fin = self.dout("o_ssd_fin", [NL, cfg.n_prompt, 2, 64, 8, 64])
            self.ssd_y = k.dram("ssd_y_scr", [T, 512])
            self.rw_lw = self.din("rw_lw", [NL, 128, 2, 512])
            self.rw_h0 = self.din("rw_h0", [NL, 2, 64, 8, 64])
            self.rw_mask_d = self.din("rw_mask", [128, 2, 3, 128])
            self.blk64_d = self.din("blk64", [128, 128])
            self.pmask_d = self.din("pmask", [128, 2])
            self.cmask64_d = self.din("cmask64", [128, 513])
            self.o_rw_fin = self.dout("o_rw_fin", [cfg.n_prompt, NL, 2, 8, 64, 64])
            self.rw_ops = k.dram("rw_ops_scr", [2, 6, 512, T])
            self.rw_v = k.dram("rw_v_scr", [512, T])
            self.rw_bonus = k.dram("rw_bonus_scr", [512, T])
            self.rw_o = k.dram("rw_o_scr", [2, 512, T])
            self.w_in_m = self.din("w_in_m", [NL, D, 4 * D])
            self.w_branch = self.din("w_branch", [NL, 4, 512, D])
            self.w_out = self.din("w_out", [NL, D, D])
            self.y_all = self.dout("y_all", [T, D])
            self.y1 = k.dram("y1_scr", [T, D])
            self.mT_scr = k.dram("mT_scr", [D, T], BF16)
            self.o_scr = k.dram("o_scr", [4, 512, T], BF16)
            if "o" in cfg.debug:
                self.dbg_o = self.dout("dbg_o", [4, 512, T], BF16)
            self.zT = k.dram("zT_scr", [NZ_FM, T])
            self.zg_tm = k.dram("zg_tm_scr", [T, 512])
            self.hT_scr = k.dram("hT_scr", [D, T], BF16)
            if "z" in cfg.debug:
                self.dbg_zT = self.dout("dbg_zT", [NZ_FM, T])
                self.dbg_zg = self.dout("dbg_zg", [T, 512])
            self.vecs = k.sb("vecs", [128, self.vl.n])
            self.ident_f = k.sb("ident_f", [128, 128])
            self.ident_b = k.sb("ident_b", [128, 128], BF16)
            self.sel2 = k.sb("sel2", [2, 2, 128])
            self.sc = k.sb("sc", [128, 8, 2])
            self.gmod = k.sb("gmod", [128, 8, 2])
            self.shiftc = k.sb("shiftc", [128, 8, 2])
            self.gate_bc = [k.sb(f"gate_bc{g}", [128, D]) for g in range(2)]
            self.ones_f = k.sb("ones_f", [128, 128])
            k.memset(self.ones_f[:], 1.0, [self.ones_f])
            self.pf = [k.ps(f"pf{i}", [128, 512]) for i in range(6)]
            self.pb = [k.ps(f"pb{i}", [128, 1024], BF16) for i in range(2)]
            self.pfi = 0
            k.dma("sp", self.ident_f[:], self.ident_d[:, :], (), [self.ident_f])
            k.dma("sp", self.sel2[:], self.sel2_d[:, :, :], (), [self.sel2])
            k.cp(self.ident_b[:], self.ident_f[:], [self.ident_f], [self.ident_b])
            k.dma("sp", self.sc[:], self.condT[:, :, :], (), [self.sc])
            k.act(self.sc[:], self.sc[:], AF.Silu, [self.sc], [self.sc])

            for l in range(NL):
                self.layer(l)
                if l == 0 and "stop0" in cfg.debug:
                    break
            k.finish()
        return self.nc

    def dump(self, name, tl, ap, shape, dtype=F32):
        if name not in self.cfg.debug:
            return
        t = self.dout("dump_" + name, shape, dtype)
        self.k.dma("sp", t[tuple(slice(None) for _ in shape)], ap, [tl], [t])

    def next_pf(self):
        p = self.pf[self.pfi % 4]
        self.pfi += 1
        return p

    def layer(self, l):
        cfg, k = self.cfg, self.k
        with k.phase():
            k.dma("sp", self.vecs[:], self.vecs_d[l], (), [self.vecs])
            self.phase_mod(l)
        if "zero_o" in cfg.debug:
            with k.phase():
                zt = k.sb("zt", [128, cfg.T], BF16)
                k.memset(zt[:], 0.0, [zt])
                for m in range(4):
                    for ft in range(4):
                        k.dma("sp", self.o_scr[m, ft * 128:(ft + 1) * 128, :], zt[:], [zt], [self.o_scr])
        with k.phase():
            self.phase_front(l)
        with k.phase():
            self.phase_lru(l)
        if "norw" not in cfg.debug:
            with k.phase():
                self.phase_rwkv(l)
        if "nossd" not in cfg.debug:
            with k.phase():
                self.phase_ssd(l)
        if "nomla" not in cfg.debug:
            with k.phase():
                self.phase_mla(l)
        with k.phase():
            self.phase_merge(l)
        with k.phase():
            self.phase_out(l)
        if "y1" in cfg.debug and l == 0:
            k.barrier()
            d_ = self.dout("dbg_y1", [cfg.T, D])
            k.dma("sp", d_[:, :], self.y1[:, :], [self.y1], [d_])
            k.barrier()
        if "o" in cfg.debug and l == 0:
            k.barrier()
            k.dma("sp", self.dbg_o[:, :, :], self.o_scr[:, :, :], [self.o_scr], [self.dbg_o])
            k.barrier()

    def phase_mod(self, l):
        k = self.k
        wst = [k.sb(f"adaw{i}", [128, 8, 512]) for i in range(2)]
        modc = k.sb("modc", [128, 16, 2])
        grow = k.sb("grow", [2, D])
        gb = k.sb("gb", [2, D])
        for g in range(2):
            k.dma("pool", gb[g:g + 1, :], self.ada_bg[l], (), [gb])
        for blk in range(6):
            w = wst[blk % 2]
            k.dma("sp", w[:], self.ada_w[l][:, blk * 512:(blk + 1) * 512].rearrange("(c p) n -> p c n", p=128),
                  (), [w])
            if blk < 4:
                for jt in range(4):
                    ps = self.next_pf()
                    for c in range(8):
                        k.mm(ps[:, 0:2], w[:, c, jt * 128:(jt + 1) * 128], self.sc[:, c, :], c == 0, c == 7,
                             [w, self.sc], [ps])
                    k.cp(modc[:, blk * 4 + jt, :], ps[:, 0:2], [ps], [modc])
            else:
                ps = self.next_pf()
                for c in range(8):
                    k.mm(ps[0:2, :], self.sc[:, c, :], w[:, c, :], c == 0, c == 7, [w, self.sc], [ps])
                hs = slice((blk - 4) * 512, (blk - 3) * 512)
                k.tt(grow[:, hs], ps[0:2, :], gb[:, hs], ALU.add, [ps, gb], [grow])
        ab = self.vcol("ada_b_ss", 0, 16)
        for g in range(2):
            k.tt(modc[:, :, g], modc[:, :, g], ab, ALU.add, [modc, self.vecs], [modc])
            k.cp(self.shiftc[:, :, g], modc[:, 0:8, g], [modc], [self.shiftc])
            k.stt(self.gmod[:, :, g], modc[:, 8:16, g], 1.0, self.vcol("norm_g", 0, 8), ALU.add, ALU.mult,
                  [modc, self.vecs], [self.gmod])
            for half in range(2):
                ps = self.next_pf()
                k.mm(ps[:, :], self.sel2[:, g, :], grow[:, half * 512:(half + 1) * 512], True, True,
                     [self.sel2, grow], [ps])
                k.cp(self.gate_bc[g][:, half * 512:(half + 1) * 512], ps[:, :], [ps], [self.gate_bc[g]])

    def phase_front(self, l):
        cfg, k = self.cfg, self.k
        T = cfg.T
        x_src = self.x_in if l == 0 else self.y1
        hT = k.sb("hT", [128, 8, T], BF16)
        with k.phase():
            xt = [k.sb(f"xt{i}", [128, D]) for i in range(3)]
            xn = [k.sb(f"xn{i}", [128, D], BF16) for i in range(2)]
            junk = k.sb("junk", [128, D], BF16)
            ss = [k.sb(f"ss{i}", [128, 1]) for i in range(2)]
            for st in range(T // 128):
                g = 0 if st * 128 < cfg.TP else 1
                x = xt[st % 3]
                xb = xn[st % 2]
                s = ss[st % 2]
                pb = self.pb[st % 2]
                k.dma("sp", x[:], x_src[st * 128:(st + 1) * 128, :], [x_src], [x])
                k.act(junk[:], x[:], AF.Square, [x], [junk, s], accum_out=s[:])
                k.ts(s[:], s[:], 1.0 / D, 1e-6, ALU.mult, ALU.add, [s], [s])
                k.act(s[:], s[:], AF.Sqrt, [s], [s])
                k.op("dve", lambda e: e.reciprocal(out=s[:], in_=s[:]), [s], [s])
                k.act(xb[:], x[:], AF.Copy, [x, s], [xb], scale=s[:])
                for c in range(8):
                    k.tr(pb[:, c * 128:(c + 1) * 128], xb[:, c * 128:(c + 1) * 128], self.ident_b[:],
                         [xb, self.ident_b], [pb])
                ho = hT[:, :, st * 128:(st + 1) * 128]
                pv = pb[:].rearrange("p (c t) -> p c t", c=8)
                k.tt(ho, pv, self.gmod[:, :, g:g + 1].to_broadcast([128, 8, 128]), ALU.mult,
                     [pb, self.gmod], [hT])
                k.tt(ho, ho, self.shiftc[:, :, g:g + 1].to_broadcast([128, 8, 128]), ALU.add,
                     [hT, self.shiftc], [hT])
        for c in range(8):
            k.dma("pool", self.hT_scr[c * 128:(c + 1) * 128, :], hT[:, c, :], [hT], [self.hT_scr])
        wst = [k.sb(f"wst{i}", [128, 8, 512]) for i in range(2)]
        wbf = [k.sb(f"wbf{i}", [128, 8, 512], BF16) for i in range(2)]
        zst = [k.sb(f"zst{i}", [128, 512]) for i in range(4)]
        blocks = [(c0, min(512, NZ_FM - c0), False) for c0 in range(0, NZ_FM, 512)] + [(NZ_FM, 512, True)]
        zi = 0
        def _wload(bi):
            c0_, ncol_, _ = blocks[bi]
            k.dma("sp", wst[bi % 2][:, :, :ncol_],
                  self.w_in_a[l][:, c0_:c0_ + ncol_].rearrange("(c p) n -> p c n", p=128), (), [wst[bi % 2]])

        def _wcast(bi):
            ncol_ = blocks[bi][1]
            k.cp(wbf[bi % 2][:, :, :ncol_], wst[bi % 2][:, :, :ncol_], [wst[bi % 2]], [wbf[bi % 2]], en="pool")
        _wload(0)
        _wcast(0)
        if len(blocks) > 1:
            _wload(1)
        for blk, (c0, ncol, tm_block) in enumerate(blocks):
            ws, wb = wst[blk % 2], wbf[blk % 2]
            if blk + 1 < len(blocks):
                _wcast(blk + 1)
            if blk + 2 < len(blocks):
                _wload(blk + 2)
            for tt in range(T // 512):
                ts_ = slice(tt * 512, (tt + 1) * 512)
                if not tm_block:
                    for jt in range(ncol // 128):
                        ps = self.next_pf()
                        for c in range(8):
                            k.mm(ps[:, :], wb[:, c, jt * 128:(jt + 1) * 128], hT[:, c, ts_], c == 0, c == 7,
                                 [wb, hT], [ps])
                        z = zst[zi % 4]
                        if zi % 2 == 0:
                            k.cp(z[:], ps[:, :], [ps], [z])
                        else:
                            k.cp(z[:], ps[:, :], [ps], [z], en="act")
                        zi += 1
                        r0 = c0 + jt * 128
                        k.dma("pool", self.zT[r0:r0 + 128, ts_], z[:], [z], [self.zT])
                else:
                    for sub in range(4):
                        ps = self.next_pf()
                        t0 = tt * 512 + sub * 128
                        for c in range(8):
                            k.mm(ps[:, :], hT[:, c, t0:t0 + 128], wb[:, c, :], c == 0, c == 7, [wb, hT], [ps])
                        z = zst[zi % 4]
                        if zi % 2 == 0:
                            k.cp(z[:], ps[:, :], [ps], [z])
                        else:
                            k.cp(z[:], ps[:, :], [ps], [z], en="act")
                        zi += 1
                        k.dma("pool", self.zg_tm[t0:t0 + 128, :], z[:], [z], [self.zg_tm])
        if "z" in cfg.debug and l == 0:
            k.barrier()
            k.dma("sp", self.dbg_zT[:, :], self.zT[:, :], [self.zT], [self.dbg_zT])
            k.dma("sp", self.dbg_zg[:, :], self.zg_tm[:, :], [self.zg_tm], [self.dbg_zg])


    def groups(self):
        cfg = self.cfg
        return [(0, cfg.n_prompt, cfg.lp), (cfg.TP, 1, cfg.ls)]

    def gview(self, ap2d, grp, lo, hi):
        s0, ns, ln = grp
        return ap2d[:, s0:s0 + ns * ln].rearrange("p (s t) -> p s t", s=ns)[:, :, lo:hi]

    def dwconv(self, xc, xl, wcol, bcol, reads):
        k = self.k
        T = self.cfg.T
        k.ts(xc[:, :T], xl[:, :T], wcol(2), bcol, ALU.mult, ALU.add, [xl] + reads, [xc])
        for grp in self.groups():
            ln = grp[2]
            for kk, off in ((0, -2), (1, -1), (3, 1)):
                if off < 0:
                    src = self.gview(xl[:, :T], grp, 0, ln + off)
                    dst = self.gview(xc[:, :T], grp, -off, ln)
                else:
                    src = self.gview(xl[:, :T], grp, off, ln)
                    dst = self.gview(xc[:, :T], grp, 0, ln - off)
                k.stt(dst, src, wcol(kk), dst, ALU.mult, ALU.add, [xl, xc] + reads, [xc])

    def phase_lru(self, l):
        cfg, k = self.cfg, self.k
        T = cfg.T
        zo, _ = ZROWS["lru_x"]
        go, _ = ZROWS["lru_gate"]
        wts = k.sb("lru_wts", [128, 16, 128])
        h0 = k.sb("lru_h0", [128, 2, 4])
        clam = k.sb("lru_clam", [128, 8])
        fin = k.sb("lru_fin", [128, cfg.n_prompt, 2, 4])
        k.dma("sp", wts[:], self.lru_w[l], (), [wts])
        k.dma("sp", h0[:], self.lru_h0[l], (), [h0])
        k.act(clam[:], self.vcol("lru_lam", 0, 8), AF.Exp, [self.vecs], [clam], scale=-1.0)
        k.act(clam[:], clam[:], AF.Ln, [clam], [clam], bias=1.0)
        k.ts(clam[:], clam[:], -8.0, None, ALU.mult, None, [clam], [clam])
        xl = k.sb("lru_xl", [128, T])
        gt = k.sb("lru_gt", [128, T])
        xc = k.sb("lru_xc", [128, T])
        ta = k.sb("lru_ta", [128, T])
        tb = k.sb("lru_tb", [128, T])
        hh = [k.sb(f"lru_h{d}", [128, T]) for d in range(2)]
        ob = k.sb("lru_ob", [128, T], BF16)
        V = [self.vecs]
        for ft in range(4):
            k.dma("sp", xl[:], self.zT[zo + ft * 128:zo + (ft + 1) * 128, :], [self.zT], [xl])
            k.dma("sp", gt[:], self.zT[go + ft * 128:go + (ft + 1) * 128, :], [self.zT], [gt])
            self.dwconv(xc, xl, lambda kk: self.vcol("lru_cw", kk * 4 + ft), self.vcol("lru_cb", ft), V)
            k.act(gt[:], gt[:], AF.Silu, [gt], [gt])
            for d in range(2):
                for tt in range(T // 512):
                    sl = slice(tt * 512, (tt + 1) * 512)
                    pa = self.next_pf()
                    k.mm(pa[:, :], wts[:, (d * 2 + 0) * 4 + ft, :], xc[:, sl], True, True, [wts, xc], [pa])
                    px = self.next_pf()
                    k.mm(px[:, :], wts[:, (d * 2 + 1) * 4 + ft, :], xc[:, sl], True, True, [wts, xc], [px])
                    k.act(ta[:, sl], pa[:, :], AF.Sigmoid, [pa] + V, [ta], bias=self.vcol("lru_ba", d * 4 + ft))
                    k.act(tb[:, sl], px[:, :], AF.Sigmoid, [px] + V, [tb], bias=self.vcol("lru_bx", d * 4 + ft))
                k.act(ta[:], ta[:], AF.Exp, [ta, clam], [ta], scale=clam[:, d * 4 + ft:d * 4 + ft + 1])
                if ft == 0 and d == 0:
                    self.dump("lru_clam", clam, clam[:], [128, 8])
                    self.dump("lru_a", ta, ta[:], [128, T])
                    self.dump("lru_gi", tb, tb[:], [128, T])
                    self.dump("lru_xc", xc, xc[:], [128, T])
                k.tt(tb[:], tb[:], xc[:], ALU.mult, [tb, xc], [tb])
                h = hh[d]
                k.tt(h[:], ta[:], ta[:], ALU.mult, [ta], [h])
                k.act(h[:], h[:], AF.Sqrt, [h], [h], scale=-1.0, bias=1.0)
                k.tt(tb[:], tb[:], h[:], ALU.mult, [tb, h], [tb])
                if ft == 0 and d == 0:
                    self.dump("lru_sq", h, h[:], [128, T])
                    self.dump("lru_u", tb, tb[:], [128, T])
                for (s0, ln, is_s) in cfg.seqs:
                    sl = slice(s0, s0 + ln)
                    init = h0[:, d, ft:ft + 1] if is_s else 0.0
                    rd = [ta, tb] + ([h0] if is_s else [])
                    if d == 0:
                        k.scan(h[:, sl], ta[:, sl], tb[:, sl], init, rd, [h])
                    else:
                        rv = lambda t: t[:, s0:s0 + ln][:, ::-1]
                        k.scan(rv(h), rv(ta), rv(tb), init, rd, [h])
                    if not is_s:
                        si = s0 // ln
                        e = s0 + ln - 1 if d == 0 else s0
                        k.cp(fin[:, si, d, ft:ft + 1], h[:, e:e + 1], [h], [fin], en="pool")
            k.tt(hh[0][:], hh[0][:], hh[1][:], ALU.add, [hh[0], hh[1]], [hh[0]])
            k.tt(ob[:], hh[0][:], gt[:], ALU.mult, [hh[0], gt], [ob])
            k.dma("pool", self.o_scr[3, ft * 128:(ft + 1) * 128, :], ob[:], [ob], [self.o_scr])
        k.dma("pool", self.o_lru_fin[l], fin[:], [fin], [self.o_lru_fin])


    def phase_merge(self, l):
        cfg, k = self.cfg, self.k
        T = cfg.T
        wm = k.sb("wm", [128, 8, 4 * D], BF16)
        wbr = k.sb("wbr", [128, 16, D], BF16)
        with k.phase():
            stg = [k.sb(f"mstg{i}", [128, 4096]) for i in range(2)]
            si = 0
            for blk in range(8):
                st = stg[si % 2]
                si += 1
                k.dma("sp", st[:].rearrange("p (c n) -> p c n", c=8),
                      self.w_in_m[l][:, blk * 512:(blk + 1) * 512].rearrange("(c p) n -> p c n", p=128), (), [st])
                k.cp(wm[:, :, blk * 512:(blk + 1) * 512], st[:].rearrange("p (c n) -> p c n", c=8), [st], [wm], en="pool")
            for m in range(4):
                st = stg[si % 2]
                si += 1
                k.dma("sp", st[:].rearrange("p (c n) -> p c n", c=4),
                      self.w_branch[l, m].rearrange("(c p) n -> p c n", p=128), (), [st])
                k.cp(wbr[:, m * 4:(m + 1) * 4, :], st[:].rearrange("p (c n) -> p c n", c=4), [st], [wbr], en="pool")
        hts = [k.sb(f"mh{i}", [128, 8, 512], BF16) for i in range(2)]
        ots = [k.sb(f"mo{i}", [128, 16, 512], BF16) for i in range(2)]
        sgs = [k.sb(f"msg{i}", [128, 512]) for i in range(2)]
        tmp = [k.sb(f"mtmp{i}", [128, 512]) for i in range(2)]
        acc = [k.sb(f"macc{i}", [128, 512]) for i in range(2)]
        mts = [k.sb(f"mt{i}", [128, 8, 512], BF16) for i in range(2)]
        n = 0
        for tt in range(T // 512):
            tsl = slice(tt * 512, (tt + 1) * 512)
            ht, ot, mt = hts[tt % 2], ots[tt % 2], mts[tt % 2]
            k.dma("sp", ht[:], self.hT_scr[:, tsl].rearrange("(c p) t -> p c t", p=128), [self.hT_scr], [ht])
            for m in range(4):
                k.dma("sp", ot[:, m * 4:(m + 1) * 4, :], self.o_scr[m, :, tsl].rearrange("(c p) t -> p c t", p=128),
                      [self.o_scr], [ot])
            for dt in range(8):
                a = acc[dt % 2]
                for m in range(4):
                    pl = self.next_pf()
                    for c in range(8):
                        k.mm(pl[:, :], wm[:, c, m * D + dt * 128:m * D + (dt + 1) * 128], ht[:, c, :], c == 0, c == 7,
                             [wm, ht], [pl])
                    pp = self.next_pf()
                    for cc in range(4):
                        k.mm(pp[:, :], wbr[:, m * 4 + cc, dt * 128:(dt + 1) * 128], ot[:, m * 4 + cc, :], cc == 0, cc == 3,
                             [wbr, ot], [pp])
                    sg = sgs[n % 2]
                    n += 1
                    k.act(sg[:], pl[:, :], AF.Sigmoid, [pl], [sg])
                    if m == 0:
                        k.tt(a[:], sg[:], pp[:, :], ALU.mult, [sg, pp], [a])
                    else:
                        t_ = tmp[n % 2]
                        k.tt(t_[:], sg[:], pp[:, :], ALU.mult, [sg, pp], [t_])
                        if m < 3:
                            k.tt(a[:], a[:], t_[:], ALU.add, [a, t_], [a], en="pool")
                        else:
                            k.tt(mt[:, dt, :], a[:], t_[:], ALU.add, [a, t_], [mt], en="pool")
            k.dma("pool", self.mT_scr[:, tsl].rearrange("(c p) t -> p c t", p=128), mt[:], [mt], [self.mT_scr])

    def phase_out(self, l):
        cfg, k = self.cfg, self.k
        T = cfg.T
        x_src = self.x_in if l == 0 else self.y1
        y_dst = self.y1 if l < NL - 1 else self.y_all
        wo = k.sb("wo", [128, 8, D], BF16)
        stg = [k.sb(f"ostg{i}", [128, 4096]) for i in range(2)]
        for hb in range(2):
            st = stg[hb]
            k.dma("sp", st[:].rearrange("p (c n) -> p c n", c=8),
                  self.w_out[l][:, hb * 512:(hb + 1) * 512].rearrange("(c p) n -> p c n", p=128), (), [st])
            k.cp(wo[:, :, hb * 512:(hb + 1) * 512], st[:].rearrange("p (c n) -> p c n", c=8), [st], [wo], en="pool")
        mts = [k.sb(f"omt{i}", [128, 8, 128], BF16) for i in range(3)]
        xts = [k.sb(f"oxt{i}", [128, D]) for i in range(3)]
        yts = [k.sb(f"oyt{i}", [128, D]) for i in range(3)]
        for st_ in range(T // 128):
            g = 0 if st_ * 128 < cfg.TP else 1
            tsl = slice(st_ * 128, (st_ + 1) * 128)
            mt, xt, yt = mts[st_ % 3], xts[st_ % 3], yts[st_ % 3]
            k.dma("sp", mt[:], self.mT_scr[:, tsl].rearrange("(c p) t -> p c t", p=128), [self.mT_scr], [mt])
            k.dma("sp", xt[:], x_src[tsl, :], [x_src], [xt])
            for hb in range(2):
                hs = slice(hb * 512, (hb + 1) * 512)
                ps = self.next_pf()
                for c in range(8):
                    k.mm(ps[:, :], mt[:, c, :], wo[:, c, hs], c == 0, c == 7, [mt, wo], [ps])
                k.tt(yt[:, hs], ps[:, :], self.gate_bc[g][:, hs], ALU.mult, [ps, self.gate_bc[g]], [yt])
                k.tt(yt[:, hs], yt[:, hs], xt[:, hs], ALU.add, [yt, xt], [yt], en="pool")
            k.dma("pool", y_dst[tsl, :], yt[:], [yt], [y_dst])


    def rstd_from_sum(self, out, ps, n, reads):
        k = self.k
        tl, pt = reads
        k.ts(out, ps, 1.0 / n, 1e-6, ALU.mult, ALU.add, [pt], [tl])
        k.act(out, out, AF.Sqrt, [tl], [tl])
        k.op("dve", lambda e: e.reciprocal(out=out, in_=out), [tl], [tl])

    def phase_mla(self, l):
        cfg, k = self.cfg, self.k
        T, TP, ls = cfg.T, cfg.TP, cfg.ls
        TK = T + 256
        kidx = lambda t: t if t < TP else t + 256
        V = [self.vecs]
        qup = k.sb("qup", [128, 3, 768], BF16)
        kvk = k.sb("kvk", [128, 2, 512], BF16)
        kvv = k.sb("kvv", [128, 2, 512], BF16)
        pm96 = k.sb("pm96", [96, 96])
        cs = k.sb("ropecs", [96, 2, ls], BF16)
        ckv_all = k.sb("ckv_all", [128, 2, TK], BF16)
        krot = k.sb("krot", [96, TK], BF16)
        ssr = k.sb("ssr", [128, TK // 128])
        qn = k.sb("qn", [128, 3, T], BF16)
        vall = k.sb("vall", [128, TK // 128, 8, 65], BF16)
        with k.phase():
            kr_all = k.sb("kr_all", [96, TK])
            with k.phase():
                stg = k.sb("mlastg", [128, 4096])
                k.dma("sp", stg[:, :3 * 768].rearrange("p (c n) -> p c n", c=3),
                      self.q_up[l].rearrange("(c p) n -> p c n", p=128), (), [stg])
                k.cp(qup[:], stg[:, :3 * 768].rearrange("p (c n) -> p c n", c=3), [stg], [qup])
                k.dma("sp", stg[:, :1024].rearrange("p (c n) -> p c n", c=2),
                      self.kv_up_k[l].rearrange("(c p) n -> p c n", p=128), (), [stg])
                k.cp(kvk[:], stg[:, :1024].rearrange("p (c n) -> p c n", c=2), [stg], [kvk])
                k.dma("sp", stg[:, :1024].rearrange("p (c n) -> p c n", c=2),
                      self.kv_up_v[l].rearrange("(c p) n -> p c n", p=128), (), [stg])
                k.cp(kvv[:], stg[:, :1024].rearrange("p (c n) -> p c n", c=2), [stg], [kvv])
                k.dma("sp", pm96[64:96, :], self.pm96_d[:, :], (), [pm96])
                for j in range(2):
                    k.dma("sp", stg[64:96, :ls], self.rope_cs[j], (), [stg])
                    k.cp(cs[64:96, j, :], stg[64:96, :ls], [stg], [cs])
            k.memset(vall[:, :, :, 64:65], 1.0, [vall])
            if "mla_s0" in cfg.debug:
                return
            ctm = k.sb("ctm", [128, 2, 256])
            krtm = k.sb("krtm", [128, 2, 32])
            k.dma("sp", ctm[:], self.cache_ckv[l].rearrange("(a p) f -> p a f", p=128), (), [ctm])
            k.dma("sp", krtm[:], self.cache_kr[l].rearrange("(a p) f -> p a f", p=128), (), [krtm])
            for a in range(2):
                for c in range(2):
                    ps = self.next_pf()
                    k.tr(ps[:, 0:128], ctm[:, a, c * 128:(c + 1) * 128], self.ident_f[:], [ctm, self.ident_f], [ps])
                    k.cp(ckv_all[:, c, TP + a * 128:TP + (a + 1) * 128], ps[:, 0:128], [ps], [ckv_all])
                ps = self.next_pf()
                kpad = k.sb(f"kpad{a}", [128, 96])
                k.memset(kpad[:], 0.0, [kpad])
                k.cp(kpad[:, 64:96], krtm[:, a, :], [krtm, kpad], [kpad])
                k.tr(ps[0:96, 0:128], kpad[:, :], self.ident_f[:], [kpad, self.ident_f], [ps])
                k.cp(kr_all[64:96, TP + a * 128:TP + (a + 1) * 128], ps[64:96, 0:128], [ps], [kr_all])
            if "mla_s1" in cfg.debug:
                return
            zo_c, _ = ZROWS["mla_ckv"]
            zo_q, _ = ZROWS["mla_q"]
            zo_m, _ = ZROWS["misc"]
            xs = [k.sb(f"mx{i}", [128, 3, 512]) for i in range(2)]
            sq = [k.sb(f"msq{i}", [128, 3, 512]) for i in range(2)]
            rs = [k.sb(f"mrs{i}", [128, 512]) for i in range(2)]
            cn = [k.sb(f"mcn{i}", [128, 2, 512]) for i in range(2)]
            for tt in range(T // 512):
                tsl = slice(tt * 512, (tt + 1) * 512)
                ksl = slice(kidx(tt * 512), kidx(tt * 512) + 512)
                x, q2, r_, c_ = xs[tt % 2], sq[tt % 2], rs[tt % 2], cn[tt % 2]
                for (zo, nch, gname, is_q) in ((zo_c, 2, "mla_kva_g", False), (zo_q, 3, "mla_qa_g", True)):
                    k.dma("sp", x[:, :nch, :], self.zT[zo:zo + nch * 128, tsl].rearrange("(c p) t -> p c t", p=128),
                          [self.zT], [x])
                    k.act(q2[:, :nch, :], x[:, :nch, :], AF.Square, [x], [q2])
                    ps = self.next_pf()
                    for c in range(nch):
                        k.mm(ps[:, :], self.ones_f[:, :], q2[:, c, :], c == 0, c == nch - 1, [self.ones_f, q2], [ps])
                    self.rstd_from_sum(r_[:], ps[:, :], nch * 128, (r_, ps))
                    for c in range(nch):
                        if is_q:
                            k.stt(qn[:, c, tsl], x[:, c, :], self.vcol(gname, c), r_[:], ALU.mult, ALU.mult,
                                  [x, r_] + V, [qn])
                        else:
                            k.stt(c_[:, c, :], x[:, c, :], self.vcol(gname, c), r_[:], ALU.mult, ALU.mult,
                                  [x, r_] + V, [c_])
                    if not is_q:
                        k.cp(ckv_all[:, :, ksl], c_[:, :, :], [c_], [ckv_all], en="pool")
                        if tt * 512 < TP:
                            k.dma("pool", self.o_ckv[l][:, tsl].rearrange("(c p) t -> p c t", p=128), c_[:, :, :],
                                  [c_], [self.o_ckv])
            if "mla_s2" in cfg.debug:
                return
            k.dma("sp", kr_all[64:96, 0:TP], self.zT[zo_m:zo_m + 32, 0:TP], [self.zT], [kr_all])
            k.dma("sp", kr_all[64:96, TP + 256:TK], self.zT[zo_m:zo_m + 32, TP:T], [self.zT], [kr_all])
            k.dma("pool", self.o_kr[l], kr_all[64:96, 0:TP], [kr_all], [self.o_kr])
            krs = k.sb("krs", [96, 512])
            krg = k.sb("krg", [96, 512])
            t1 = k.sb("krt1", [96, 512])
            pss = self.pf[5]
            lat0 = TP + 256
            segs = [(ks, min(512, lat0 - ks)) for ks in range(0, lat0, 512)] + \
                   [(ks, min(512, TK - ks)) for ks in range(lat0, TK, 512)]
            for ks, w in segs:
                k.act(krs[64:96, :w], kr_all[64:96, ks:ks + w], AF.Square, [kr_all], [krs])
                for j in range(w // 128):
                    kt = ks // 128 + j
                    k.mm(pss[:, kt:kt + 1], krs[64:96, j * 128:(j + 1) * 128], self.ones_f[64:96, 0:1], True, True,
                         [krs, self.ones_f], [pss])
                k.ts(krg[64:96, :w], kr_all[64:96, ks:ks + w], self.vecs[64:96, self.vl.cols["mla_kn_g"][0]:self.vl.cols["mla_kn_g"][0] + 1],
                     None, ALU.mult, None, [kr_all] + V, [krg])
                if ks >= lat0:
                    pr = self.next_pf()
                    k.mm(pr[0:96, :w], pm96[64:96, :], krg[64:96, :w], True, True, [pm96, krg], [pr])
                    po = ks - lat0
                    k.tt(t1[64:96, :w], pr[64:96, :w], cs[64:96, 1, po:po + w], ALU.mult, [pr, cs], [t1])
                    k.tt(krg[64:96, :w], krg[64:96, :w], cs[64:96, 0, po:po + w], ALU.mult, [krg, cs], [krg])
                    k.tt(krot[64:96, ks:ks + w], krg[64:96, :w], t1[64:96, :w], ALU.add, [krg, t1], [krot])
                else:
                    k.cp(krot[64:96, ks:ks + w], krg[64:96, :w], [krg], [krot])
            k.cp(ssr[:], pss[:, 0:TK // 128], [pss], [ssr])
            if "mla_s3" in cfg.debug:
                return
            for kt in range(TK // 128):
                ps = self.next_pf()
                for c in range(2):
                    k.mm(ps[:, :], ckv_all[:, c, kt * 128:(kt + 1) * 128], kvv[:, c, :], c == 0, c == 1,
                         [ckv_all, kvv], [ps])
                k.cp(vall[:, kt, :, 0:64], ps[:, :].rearrange("p (h d) -> p h d", h=8), [ps], [vall],
                     en=("act" if kt % 2 else "dve"))
        if "mla_s4" in cfg.debug:
            return
        zo_g, _ = ZROWS["mla_gate"]
        gcol = self.vl.cols["mla_qn_g"][0]
        kcol = self.vl.cols["mla_kn_g"][0]
        kth = k.sb("kth", [96, TK], BF16)
        qth = k.sb("qth", [96, T], BF16)
        rk = k.sb("rk", [128, TK // 128])
        sqt = [k.sb(f"hsq{i}", [96, 512]) for i in range(2)]
        qf = [k.sb(f"hqf{i}", [96, 512]) for i in range(2)]
        rq = [k.sb(f"hrq{i}", [96, 512]) for i in range(2)]
        t1 = k.sb("ht1", [96, 512])
        t2 = k.sb("ht2", [96, 512])
        pts = [k.sb(f"hpt{i}", [128, 512], BF16) for i in range(4)]
        oa = [k.sb(f"hoa{i}", [65, 512]) for i in range(2)]
        gts = [k.sb(f"hgt{i}", [64, 512]) for i in range(2)]
        obs = [k.sb(f"hob{i}", [64, 512], BF16) for i in range(2)]
        npt = 0
        npo = 0
        for h in range(8):
            prk = self.pf[4]
            for ks in range(0, TK, 512):
                w = min(512, TK - ks)
                ps = self.next_pf()
                for c in range(2):
                    k.mm(ps[0:64, :w], kvk[:, c, h * 64:(h + 1) * 64], ckv_all[:, c, ks:ks + w], c == 0, c == 1,
                         [kvk, ckv_all], [ps])
                s_ = sqt[(ks // 512) % 2]
                if "k_noact" not in cfg.debug:
                    k.act(s_[0:64, :w], ps[0:64, :w], AF.Square, [ps], [s_])
                for j in range(w // 128):
                    kt = ks // 128 + j
                    if "mla_k1" in cfg.debug:
                        continue
                    k.mm(prk[:, kt:kt + 1], s_[0:64, j * 128:(j + 1) * 128], self.ones_f[0:64, 0:1], True, True,
                         [s_, self.ones_f], [prk])
                if "k_nots" not in cfg.debug:
                    k.ts(kth[0:64, ks:ks + w], ps[0:64, :w], self.vecs[0:64, kcol:kcol + 1], None, ALU.mult, None,
                         [ps] + V, [kth])
            k.cp(kth[64:96, :], krot[64:96, :], [krot], [kth], en=("dve" if "mla_k2" in cfg.debug else "pool"))
            if "k_nork" in cfg.debug:
                return
            k.tt(rk[:], prk[:, 0:TK // 128], ssr[:], ALU.add, [prk, ssr], [rk])
            k.ts(rk[:], rk[:], 1.0, 96e-6, ALU.mult, ALU.add, [rk], [rk])
            k.act(rk[:], rk[:], AF.Sqrt, [rk], [rk])
            k.op("dve", lambda e: e.reciprocal(out=rk[:], in_=rk[:]), [rk], [rk])
            if "mla_s5" in cfg.debug:
                return
            for tt in range(T // 512):
                tsl = slice(tt * 512, (tt + 1) * 512)
                ps = self.next_pf()
                for c in range(3):
                    k.mm(ps[0:96, :], qup[:, c, h * 96:(h + 1) * 96], qn[:, c, tsl], c == 0, c == 2, [qup, qn], [ps])
                s_, f_, r_ = sqt[tt % 2], qf[tt % 2], rq[tt % 2]
                k.act(s_[:, :], ps[0:96, :], AF.Square, [ps], [s_])
                p2 = self.next_pf()
                k.mm(p2[0:96, :], self.ones_f[0:96, 0:96], s_[:, :], True, True, [self.ones_f, s_], [p2])
                self.rstd_from_sum(r_[:, :], p2[0:96, :], 96, (r_, p2))
                k.stt(f_[:, :], ps[0:96, :], self.vecs[0:96, gcol:gcol + 1], r_[:, :], ALU.mult, ALU.mult,
                      [ps, r_] + V, [f_])
                k.cp(qth[0:64, tsl], f_[0:64, :], [f_], [qth], en="pool")
                if tt * 512 >= TP:
                    po = tt * 512 - TP
                    pr = self.next_pf()
                    k.mm(pr[0:96, :], pm96[64:96, :], f_[64:96, :], True, True, [pm96, f_], [pr])
                    k.tt(t1[64:96, :], pr[64:96, :], cs[64:96, 1, po:po + 512], ALU.mult, [pr, cs], [t1])
                    k.tt(t2[64:96, :], f_[64:96, :], cs[64:96, 0, po:po + 512], ALU.mult, [f_, cs], [t2])
                    k.tt(qth[64:96, tsl], t1[64:96, :], t2[64:96, :], ALU.add, [t1, t2], [qth], en="pool")
                else:
                    k.cp(qth[64:96, tsl], f_[64:96, :], [f_], [qth], en="pool")
            if "mla_s6" in cfg.debug:
                return
            if h == 0:
                self.dump("mla_kth", kth, kth[:, :], [96, TK], BF16)
                self.dump("mla_qth", qth, qth[:, :], [96, T], BF16)
                self.dump("mla_rk", rk, rk[:, :], [128, TK // 128])
            for (s0, ln, is_s) in cfg.seqs:
                k0 = TP if is_s else s0
                nk = ln + 256 if is_s else ln
                qw = min(512, ln)
                for qg in range(ln // qw):
                    q0 = s0 + qg * qw
                    po = self.pf[4 + (npo % 2)]
                    npo += 1
                    nkt = nk // 128
                    pend = []

                    def _pv(item):
                        pt_, kt_, j_ = item
                        k.mm(po[0:65, :qw], vall[:, kt_, h, :], pt_[:, :qw], j_ == 0, j_ == nkt - 1, [vall, pt_], [po])
                    for j in range(nkt):
                        kt = k0 // 128 + j
                        pS = self.next_pf()
                        k.mm(pS[:, :qw], kth[0:96, kt * 128:(kt + 1) * 128], qth[0:96, q0:q0 + qw], True, True,
                             [kth, qth], [pS])
                        pt = pts[npt % 4]
                        npt += 1
                        k.act(pt[:, :qw], pS[:, :qw], AF.Exp, [pS, rk], [pt], scale=rk[:, kt:kt + 1])
                        pend.append((pt, kt, j))
                        if len(pend) > 2:
                            _pv(pend.pop(0))
                    while pend:
                        _pv(pend.pop(0))
                    o_ = oa[qg % 2]
                    k.cp(o_[:, :qw], po[0:65, :qw], [po], [o_])
                    k.op("dve", lambda e: e.reciprocal(out=o_[64:65, :qw], in_=o_[64:65, :qw]), [o_], [o_])
                    pbc = self.next_pf()
                    k.mm(pbc[0:64, :qw], self.ones_f[64:65, 0:64], o_[64:65, :qw], True, True, [self.ones_f, o_], [pbc])
                    g_ = gts[qg % 2]
                    k.dma("sp", g_[:, :qw], self.zT[zo_g + h * 64:zo_g + (h + 1) * 64, q0:q0 + qw], [self.zT], [g_])
                    k.act(g_[:, :qw], g_[:, :qw], AF.Silu, [g_], [g_])
                    k.tt(o_[0:64, :qw], o_[0:64, :qw], pbc[0:64, :qw], ALU.mult, [o_, pbc], [o_])
                    ob = obs[qg % 2]
                    k.tt(ob[:, :qw], o_[0:64, :qw], g_[:, :qw], ALU.mult, [o_, g_], [ob])
                    k.dma("pool", self.o_scr[1, h * 64:(h + 1) * 64, q0:q0 + qw], ob[:, :qw], [ob], [self.o_scr])


    def phase_ssd(self, l):
        cfg, k = self.cfg, self.k
        T, TP = cfg.T, cfg.TP
        NCH = T // 128
        V = [self.vecs]
        zo_x, _ = ZROWS["ssd_xbc"]
        zo_m, _ = ZROWS["misc"]
        x_tm = k.sb("sx_tm", [128, NCH, 512], BF16)
        b_tm = k.sb("sb_tm", [128, NCH, 128], BF16)
        bT = k.sb("s_bT", [128, T], BF16)
        cT = k.sb("s_cT", [128, T], BF16)
        selm = k.sb("s_selm", [64, 16, 128])
        seld = k.sb("s_seld", [64, 2, 8])
        maskneg = k.sb("s_mask", [128, 2, 128])
        bcv = k.sb("s_bcv", [128, 8 + 512])
        k.dma("sp", selm[:], self.selm_d[:, :, :], (), [selm])
        k.dma("sp", seld[:], self.seld_d[:, :, :], (), [seld])
        k.dma("sp", maskneg[:], self.maskneg_d[:, :, :], (), [maskneg])
        k.dma("sp", bcv[:], self.ssd_bc[l], (), [bcv])
        with k.phase():
            xl = [k.sb(f"s_xl{i}", [128, T]) for i in range(2)]
            xc = [k.sb(f"s_xc{i}", [128, T]) for i in range(2)]
            for ft in range(6):
                a, c_ = xl[ft % 2], xc[ft % 2]
                k.dma("sp", a[:], self.zT[zo_x + ft * 128:zo_x + (ft + 1) * 128, :], [self.zT], [a])
                self.dwconv(c_, a, lambda kk: self.vcol("ssd_cw", kk * 6 + ft), self.vcol("ssd_cb", ft), V)
                k.act(c_[:], c_[:], AF.Silu, [c_], [c_])
                if ft == 4:
                    k.cp(bT[:], c_[:], [c_], [bT], en="pool")
                if ft == 5:
                    k.cp(cT[:], c_[:], [c_], [cT], en="pool")
                if ft <= 4:
                    for ch in range(NCH):
                        ps = self.next_pf()
                        k.tr(ps[:, 0:128], c_[:, ch * 128:(ch + 1) * 128], self.ident_f[:], [c_, self.ident_f], [ps])
                        dst = x_tm[:, ch, ft * 128:(ft + 1) * 128] if ft < 4 else b_tm[:, ch, :]
                        k.cp(dst, ps[:, 0:128], [ps], [x_tm if ft < 4 else b_tm], en=("act" if ch % 2 else "dve"))
        cs = k.sb("s_cs", [64, T])
        sc3t = k.sb("s_sc3", [64, 3, T])
        nega = k.sb("s_nega", [64, 1])
        _cm_stack = contextlib.ExitStack()
        k.barrier()
        k.stacks.append(_cm_stack)
        cmask = k.sb("s_cmask", [64, T + 1])

        class _V:
            def __init__(self, tl, j):
                self.tl, self.j, self.res, self.psum = tl, j, tl.res, False

            def __getitem__(self, key):
                if not isinstance(key, tuple):
                    key = (key,)
                return self.tl[(key[0], self.j) + tuple(key[1:])]
        sc3 = sc3t
        dtt = sc3t
        D0 = lambda *a: sc3t[(a[0], 0) + tuple(a[1:])] if a else sc3t[:, 0, :]
        k.memset(sc3t[:, 0, :], 0.0, [sc3t])
        k.dma("sp", sc3t[0:8, 0, :], self.zT[zo_m + 32:zo_m + 40, :], [self.zT], [sc3t])
        k.dma("sp", sc3t[32:40, 0, :], self.zT[zo_m + 40:zo_m + 48, :], [self.zT], [sc3t])
        k.dma("sp", cmask[:], self.cmask_d[:, :], (), [cmask])
        k.act(sc3t[:, 0, :], sc3t[:, 0, :], AF.Exp, [sc3t] + V, [sc3t], bias=self.vecs[0:64, self.vl.cols["ssd_dtb"][0]:self.vl.cols["ssd_dtb"][0] + 1])
        k.act(sc3t[:, 0, :], sc3t[:, 0, :], AF.Ln, [sc3t], [sc3t], bias=1.0)
        k.act(nega[:], self.vecs[0:64, self.vl.cols["ssd_alog"][0]:self.vl.cols["ssd_alog"][0] + 1], AF.Exp, V, [nega])
        k.ts(nega[:], nega[:], -1.0, None, ALU.mult, None, [nega], [nega])
        k.ts(sc3t[:, 2, :], sc3t[:, 0, :], nega[:, 0:1], None, ALU.mult, None, [sc3t, nega], [sc3t])
        k.scan(cs[0:32, :], cmask[0:32, 0:T], sc3t[0:32, 2, :], 0.0, [cmask, sc3t], [cs])
        k.scan(cs[32:64, :][:, ::-1], cmask[32:64, 1:T + 1][:, ::-1], sc3t[32:64, 2, :][:, ::-1], 0.0, [cmask, sc3t], [cs])
        k.barrier()
        k.stacks.pop()
        _cm_stack.close()
        c3 = lambda ap: ap.rearrange("p (c t) -> p c t", t=128)
        k.tt(c3(sc3[0:32, 1, :]), c3(cs[0:32, :])[:, :, 127:128].to_broadcast([32, NCH, 128]), c3(cs[0:32, :]),
             ALU.subtract, [cs], [sc3])
        k.tt(c3(sc3[32:64, 1, :]), c3(cs[32:64, :])[:, :, 0:1].to_broadcast([32, NCH, 128]), c3(cs[32:64, :]),
             ALU.subtract, [cs], [sc3])
        k.act(sc3[:, 1, :], sc3[:, 1, :], AF.Exp, [sc3], [sc3])
        k.tt(sc3[:, 1, :], sc3[:, 1, :], sc3[:, 0, :], ALU.mult, [sc3], [sc3])
        k.act(sc3[:, 2, :], cs[:], AF.Exp, [cs], [sc3])
        self.dump("ssd_cs", cs, cs[:, :], [64, T])
        self.dump("ssd_sc3", sc3t, sc3t[:, :, :], [64, 3, T])
        self.dump("ssd_xtm", x_tm, x_tm[:, :, :], [128, NCH, 512], BF16)
        self.dump("ssd_bT", bT, bT[:, :], [128, T], BF16)
        Hf = k.sb("s_H", [128, 4, 64])
        Hb = k.sb("s_Hb", [128, 4, 64], BF16)
        selg = k.sb("s_selg", [64, 128])
        k.dma("sp", selg[:], self.selg_d[:, :], (), [selg])
        sctm = [k.sb(f"s_sctm{i}", [128, 3, 64]) for i in range(2)]
        cstm = [k.sb(f"s_cstm{i}", [128, 64]) for i in range(2)]
        xd = [k.sb(f"s_xd{i}", [128, 8, 64], BF16) for i in range(2)]
        xdd = [k.sb(f"s_xdd{i}", [128, 8, 64], BF16) for i in range(2)]
        lt = [k.sb(f"s_lt{i}", [128, 4, 128]) for i in range(2)]
        mT = [k.sb(f"s_mT{i}", [128, 4, 128], BF16) for i in range(2)]
        yo = [k.sb(f"s_yo{i}", [128, 512]) for i in range(2)]
        yf = [k.sb("s_yf0", [128, 512])] * 2
        zg = [k.sb("s_zg0", [128, 512])] * 2
        et = k.sb("s_et", [128, 4])
        etr = k.sb("s_etr", [64, 4])
        h0t = k.sb("s_h0t", [64, 8, 64])
        fint = k.sb("s_fint", [64, 8, 64])
        junk = yf[0]
        ssq = k.sb("s_ssq", [128, 1])
        ob = [k.sb(f"s_ob{i}", [128, 4, 128], BF16) for i in range(2)]
        it = 0
        for (s0, ln, is_s) in cfg.seqs:
            si = s0 // ln
            chs = list(range(s0 // 128, (s0 + ln) // 128))
            for d in range(2):
                if is_s:
                    k.dma("sp", h0t[:], self.ssd_h0[l, d], (), [h0t])
                    for h in range(8):
                        ps = self.next_pf()
                        if h < 4:
                            k.tr(ps[0:64, 0:64], h0t[:, h, :], self.ident_f[0:64, 0:64], [h0t, self.ident_f], [ps])
                            k.cp(Hf[0:64, h, :], ps[0:64, 0:64], [ps], [Hf])
                        else:
                            k.tr(ps[:, 0:64], h0t[:, h - 1:h + 1, :].rearrange("p a n -> p (a n)"),
                                 self.ident_f[0:64, 0:64], [h0t, self.ident_f], [ps])
                            k.cp(Hf[64:128, h - 4, :], ps[64:128, 0:64], [ps], [Hf])
                else:
                    k.memset(Hf[:], 0.0, [Hf], en="dve")
                k.cp(Hb[:], Hf[:], [Hf], [Hb])
                for ch in (chs if d == 0 else chs[::-1]):
                    it += 1
                    tsl = slice(ch * 128, (ch + 1) * 128)
                    st_, ct_, xd_, xdd_ = sctm[it % 2], cstm[it % 2], xd[it % 2], xdd[it % 2]
                    pt = self.next_pf()
                    for j in range(3):
                        k.tr(pt[:, j * 64:(j + 1) * 64], sc3[:, j, tsl], self.ident_f[0:64, 0:64], [sc3, self.ident_f], [pt])
                    k.tr(pt[:, 192:256], cs[:, tsl], self.ident_f[0:64, 0:64], [cs, self.ident_f], [pt])
                    k.cp(st_[:], pt[:, 0:192].rearrange("p (j r) -> p j r", j=3), [pt], [st_])
                    k.cp(ct_[:], pt[:, 192:256], [pt], [ct_])
                    r0 = d * 32
                    xv = x_tm[:, ch, :].rearrange("p (h q) -> p h q", h=8)
                    k.tt(xd_[:], xv, st_[:, 0, r0:r0 + 8].unsqueeze(2).to_broadcast([128, 8, 64]), ALU.mult,
                         [x_tm, st_], [xd_])
                    k.tt(xdd_[:], xv, st_[:, 1, r0:r0 + 8].unsqueeze(2).to_broadcast([128, 8, 64]), ALU.mult,
                         [x_tm, st_], [xdd_], en="pool")
                    py = self.pf[4]
                    pyo = self.pf[5]
                    for g in range(2):
                        lt_, m_ = lt[g], mT[g]
                        pb_ = self.next_pf()
                        for hh in range(4):
                            h = g * 4 + hh
                            k.mm(pb_[:, hh * 128:(hh + 1) * 128], selm[:, d * 8 + h, :], cs[:, tsl], True, True,
                                 [selm, cs], [pb_])
                        k.tt(lt_[:], pb_[:, :].rearrange("p (h t) -> p h t", h=4),
                             ct_[:, r0 + g * 4:r0 + g * 4 + 4].unsqueeze(2).to_broadcast([128, 4, 128]), ALU.subtract,
                             [pb_, ct_], [lt_])
                        k.tt(lt_[:], lt_[:], maskneg[:, d:d + 1, :].to_broadcast([128, 4, 128]), ALU.add,
                             [lt_, maskneg], [lt_])
                        k.act(lt_[:], lt_[:], AF.Exp, [lt_], [lt_])
                        pg = self.next_pf()
                        k.mm(pg[:, 0:128], bT[g * 64:(g + 1) * 64, tsl], cT[g * 64:(g + 1) * 64, tsl], True, True,
                             [bT, cT], [pg])
                        k.tt(m_[:], lt_[:], pg[:, 0:128].unsqueeze(1).to_broadcast([128, 4, 128]), ALU.mult,
                             [lt_, pg], [m_])
                        for hh in range(4):
                            h = g * 4 + hh
                            k.mm(py[:, h * 64:(h + 1) * 64], m_[:, hh, :], xd_[:, h, :], True, True, [m_, xd_], [py])
                        k.mm(pyo[:, g * 256:(g + 1) * 256], cT[g * 64:(g + 1) * 64, tsl],
                             Hb[g * 64:(g + 1) * 64, :, :], True, True, [cT, Hb], [pyo])
                    y_ = yo[it % 2]
                    k.tt(y_[:].rearrange("p (h q) -> p h q", h=8), pyo[:, :].rearrange("p (h q) -> p h q", h=8),
                         st_[:, 2, r0:r0 + 8].unsqueeze(2).to_broadcast([128, 8, 64]), ALU.mult, [pyo, st_], [y_])
                    k.tt(y_[:], y_[:], py[:, :], ALU.add, [y_, py], [y_])
                    ps_ = self.next_pf()
                    for g in range(2):
                        k.mm(ps_[g * 64:(g + 1) * 64, 0:256], b_tm[:, ch, g * 64:(g + 1) * 64],
                             xdd_[:, g * 4:(g + 1) * 4, :], True, True, [b_tm, xdd_], [ps_])
                    e_tok = ch * 128 + (127 if d == 0 else 0)
                    k.ts(etr[:], seld[:, d, 0:4], cs[:, e_tok:e_tok + 1], None, ALU.mult, None, [seld, cs], [etr])
                    pe_ = self.next_pf()
                    k.mm(pe_[:, 0:4], selg[:, :], etr[:], True, True, [selg, etr], [pe_])
                    k.act(et[:], pe_[:, 0:4], AF.Exp, [pe_], [et])
                    k.tt(Hf[:], Hf[:], et[:].unsqueeze(2).to_broadcast([128, 4, 64]), ALU.mult, [Hf, et], [Hf])
                    k.tt(Hf[:], Hf[:], ps_[:, 0:256].rearrange("p (h q) -> p h q", h=4), ALU.add, [Hf, ps_], [Hf])
                    k.cp(Hb[:], Hf[:], [Hf], [Hb], en="pool")
                    if it == 1:
                        self.dump("ssd_lt", lt[1], lt[1][:, :, :], [128, 4, 128])
                        self.dump("ssd_mT", mT[1], mT[1][:, :, :], [128, 4, 128], BF16)
                        self.dump("ssd_y0", y_, y_[:, :], [128, 512])
                        self.dump("ssd_H1", Hf, Hf[:, :, :], [128, 4, 64])
                        self.dump("ssd_sctm", st_, st_[:, :, :], [128, 3, 64])
                    if d == 0:
                        k.dma("sp", self.ssd_y[tsl, :], y_[:], [y_], [self.ssd_y])
                    else:
                        f_, z_ = yf[it % 2], zg[it % 2]
                        k.dma("sp", f_[:], self.ssd_y[tsl, :], [self.ssd_y], [f_])
                        k.dma("sp", z_[:], self.zg_tm[tsl, :], [self.zg_tm], [z_])
                        k.tt(y_[:], y_[:], f_[:], ALU.add, [y_, f_], [y_])
                        k.tt(f_[:].rearrange("p (h q) -> p h q", h=8), xv,
                             bcv[:, 0:8].unsqueeze(2).to_broadcast([128, 8, 64]), ALU.mult, [x_tm, bcv], [f_])
                        k.tt(y_[:], y_[:], f_[:], ALU.add, [y_, f_], [y_])
                        k.act(z_[:], z_[:], AF.Silu, [z_], [z_])
                        k.tt(y_[:], y_[:], z_[:], ALU.mult, [y_, z_], [y_])
                        k.act(junk[:], y_[:], AF.Square, [y_], [junk, ssq], accum_out=ssq[:])
                        self.rstd_from_sum(ssq[:], ssq[:], 512, (ssq, ssq))
                        k.stt(y_[:], y_[:], ssq[:, 0:1], bcv[:, 8:8 + 512], ALU.mult, ALU.mult, [y_, ssq, bcv], [y_])
                        o_ = ob[it % 2]
                        for ft in range(4):
                            pt2 = self.next_pf()
                            k.tr(pt2[:, 0:128], y_[:, ft * 128:(ft + 1) * 128], self.ident_f[:], [y_, self.ident_f], [pt2])
                            k.cp(o_[:, ft, :], pt2[:, 0:128], [pt2], [o_], en=("act" if ft % 2 else "dve"))
                        k.dma("sp", self.o_scr[2, :, tsl].rearrange("(c p) t -> p c t", p=128), o_[:], [o_], [self.o_scr])
                if not is_s:
                    for h in range(8):
                        ps = self.next_pf()
                        g_ = h // 4
                        k.tr(ps[0:64, 0:64], Hf[g_ * 64:(g_ + 1) * 64, h % 4, :],
                             self.ident_f[g_ * 64:(g_ + 1) * 64, g_ * 64:(g_ + 1) * 64], [Hf, self.ident_f], [ps])
                        k.cp(fint[:, h, :], ps[0:64, 0:64], [ps], [fint])
                    k.dma("pool", self.o_ssd_fin[l, si, d], fint[:], [fint], [self.o_ssd_fin])


    def phase_rwkv(self, l):
        cfg, k = self.cfg, self.k
        T, TP = cfg.T, cfg.TP
        TS = 512
        V = [self.vecs]
        NC64 = T // 64
        zr = {n: ZROWS[n][0] for n in ("rw_r", "rw_k", "rw_v", "rw_lora", "rw_gate")}
        pc_all = k.sb("rw_pc", [128, 4, 2, NC64])
        blk = k.sb("rw_blk", [128, 128])
        k.dma("sp", blk[:], self.blk64_d[:, :], (), [blk])
        vc = lambda name, j: self.vcol(name, j)
        with k.phase():
            lw = k.sb("rw_lw", [128, 2, 512])
            cm = k.sb("rw_cm", [128, TS + 1])
            omk = k.sb("rw_omk", [128, 4])
            k.dma("sp", lw[:], self.rw_lw[l], (), [lw])
            k.dma("sp", cm[:], self.cmask64_d[:, :], (), [cm])
            k.ts(omk[:], self.vcol("rw_ka", 0, 4), -1.0, 1.0, ALU.mult, ALU.add, V, [omk])
            xes = [k.sb(f"rw_xe{i}", [128, TS + 2]) for i in range(3)]
            shs = [k.sb(f"rw_sh{i}", [128, TS]) for i in range(3)]
            mixn = [0]
            lora = k.sb("rw_lora", [128, TS])
            ostg = [[k.sb(f"rw_ostg{d_}_{j_}", [128, TS]) for j_ in range(6)] for d_ in range(2)]
            vstg = [k.sb(f"rw_vstg{i}", [128, TS]) for i in range(2)]
            bstg = [k.sb(f"rw_bstg{i}", [128, TS]) for i in range(2)]
            names = ("r", "k", "v", "kk", "a", "kd", "be", "lw_", "cl", "e1", "e2", "t1", "t2", "bon")
            tl = {n: k.sb("rw_" + n, [128, TS]) for n in names}

            def mix(dst, zrow, mucol, a, is_s, s0, ln):
                b_ = a + TS
                mixn[0] += 1
                xe, sh = xes[mixn[0] % 3], shs[mixn[0] % 3]
                k.dma("sp", xe[:, 1:TS + 1], self.zT[zrow:zrow + 128, a:b_], [self.zT], [xe])
                if is_s:
                    if a > s0:
                        k.dma("sp", xe[:, 0:1], self.zT[zrow:zrow + 128, a - 1:a], [self.zT], [xe], allow_slow_non_contiguous=True)
                    else:
                        k.memset(xe[:, 0:1], 0.0, [xe])
                    if b_ < s0 + ln:
                        k.dma("sp", xe[:, TS + 1:TS + 2], self.zT[zrow:zrow + 128, b_:b_ + 1], [self.zT], [xe], allow_slow_non_contiguous=True)
                    else:
                        k.memset(xe[:, TS + 1:TS + 2], 0.0, [xe])
                    k.tt(sh[:], xe[:, 0:TS], xe[:, 2:TS + 2], ALU.add, [xe], [sh])
                else:
                    ns = TS // ln
                    k.memset(sh[:], 0.0, [sh], en="dve")
                    v3 = lambda ap: ap.rearrange("p (s t) -> p s t", s=ns)
                    xv = v3(xe[:, 1:TS + 1])
                    k.tt(v3(sh[:])[:, :, 1:ln], v3(sh[:])[:, :, 1:ln], xv[:, :, 0:ln - 1], ALU.add, [sh, xe], [sh])
                    k.tt(v3(sh[:])[:, :, 0:ln - 1], v3(sh[:])[:, :, 0:ln - 1], xv[:, :, 1:ln], ALU.add, [sh, xe], [sh])
                k.stt(sh[:], sh[:], 0.5, xe[:, 1:TS + 1], ALU.mult, ALU.subtract, [sh, xe], [sh])
                k.stt(dst, sh[:], mucol, xe[:, 1:TS + 1], ALU.mult, ALU.add, [sh, xe] + V, [dst_tl[0]])

            dst_tl = [None]
            for a in range(0, T, TS):
                is_s = a >= TP
                s0, ln = (TP, cfg.ls) if is_s else (0, cfg.lp)
                c64 = a // 64
                dst_tl[0] = lora
                mix(lora[:], zr["rw_lora"], vc("rw_mu", 12), a, is_s, s0, ln)
                k.act(lora[0:64, :], lora[0:64, :], AF.Tanh, [lora], [lora])
                for ft in range(4):
                    R, K_, KK = tl["r"], tl["k"], tl["kk"]
                    Vv = vstg[ft % 2]
                    tl["bon"] = bstg[ft % 2]
                    for (dst, nm, mi) in ((R, "rw_r", 0), (K_, "rw_k", 4), (Vv, "rw_v", 8)):
                        dst_tl[0] = dst
                        mix(dst[:], zr[nm] + ft * 128, vc("rw_mu", mi + ft), a, is_s, s0, ln)
                    k.dma("pool", self.rw_v[ft * 128:(ft + 1) * 128, a:a + TS], Vv[:], [Vv], [self.rw_v])
                    t1, t2 = tl["t1"], tl["t2"]
                    k.ts(KK[:], K_[:], vc("rw_kk", ft), None, ALU.mult, None, [K_] + V, [KK])
                    k.tt(t1[:], KK[:], KK[:], ALU.mult, [KK], [t1])
                    ps = self.next_pf()
                    k.mm(ps[:, :], blk[:, :], t1[:], True, True, [blk, t1], [ps])
                    k.ts(t2[:], ps[:, :], 1e-12, None, ALU.add, None, [ps], [t2])
                    k.act(t2[:], t2[:], AF.Sqrt, [t2], [t2])
                    k.op("dve", lambda e: e.reciprocal(out=t2[:], in_=t2[:]), [t2], [t2])
                    k.tt(KK[:], KK[:], t2[:], ALU.mult, [KK, t2], [KK])
                    for d in range(2):
                        A_, KD, BE, LW, CL, E1, E2, BON = (tl[n] for n in ("a", "kd", "be", "lw_", "cl", "e1", "e2", "bon"))
                        ps = self.next_pf()
                        k.mm(ps[:, :], lw[0:64, d, ft * 128:(ft + 1) * 128], lora[0:64, :], True, True, [lw, lora], [ps])
                        k.act(LW[:], ps[:, :], AF.Sigmoid, [ps] + V, [LW], bias=vc("rw_w0", d * 4 + ft))
                        k.ts(LW[:], LW[:], -0.6065306597126334, None, ALU.mult, None, [LW], [LW])
                        ps2 = self.next_pf()
                        k.mm(ps2[:, :], lw[64:128, d, ft * 128:(ft + 1) * 128], lora[64:128, :], True, True, [lw, lora], [ps2])
                        k.act(A_[:], ps2[:, :], AF.Sigmoid, [ps2] + V, [A_], bias=vc("rw_a0", d * 4 + ft))
                        k.ts(KD[:], A_[:], vc("rw_ka", ft), omk[:, ft:ft + 1], ALU.mult, ALU.add, [A_, omk] + V, [KD])
                        k.tt(KD[:], KD[:], K_[:], ALU.mult, [KD, K_], [KD])
                        k.tt(BE[:], KK[:], A_[:], ALU.mult, [KK, A_], [BE])
                        k.stt(t1[:], R[:], vc("rw_rk", ft), KD[:], ALU.mult, ALU.mult, [R, KD] + V, [t1])
                        ps3 = self.next_pf()
                        k.mm(ps3[:, :], blk[:, :], t1[:], True, True, [blk, t1], [ps3])
                        if d == 0:
                            k.tt(BON[:], ps3[:, :], Vv[:], ALU.mult, [ps3, Vv], [BON])
                        else:
                            k.tt(t1[:], ps3[:, :], Vv[:], ALU.mult, [ps3, Vv], [t1])
                            k.tt(BON[:], BON[:], t1[:], ALU.add, [BON, t1], [BON])
                            k.dma("pool", self.rw_bonus[ft * 128:(ft + 1) * 128, a:a + TS], BON[:], [BON], [self.rw_bonus])
                        if d == 0:
                            k.scan(CL[:], cm[:, 0:TS], LW[:], 0.0, [cm, LW], [CL])
                        else:
                            k.scan(CL[:, ::-1], cm[:, 1:TS + 1][:, ::-1], LW[:, ::-1], 0.0, [cm, LW], [CL])
                        c3 = lambda ap: ap.rearrange("p (c t) -> p c t", t=64)
                        e_ = 63 if d == 0 else 0
                        ce = c3(CL[:])[:, :, e_:e_ + 1]
                        k.act(pc_all[:, ft, d, c64:c64 + TS // 64], ce.rearrange("p c o -> p (c o)"), AF.Exp, [CL], [pc_all])
                        O = lambda j: self.rw_ops[d, j, ft * 128:(ft + 1) * 128, a:a + TS]
                        k.act(E1[:], CL[:], AF.Exp, [CL], [E1])
                        k.tt(ostg[d][0][:], R[:], E1[:], ALU.mult, [R, E1], [ostg[d][0]])
                        k.dma("pool", O(0), ostg[d][0][:], [ostg[d][0]], [self.rw_ops])
                        k.act(E2[:], CL[:], AF.Exp, [CL], [E2], scale=-1.0)
                        k.tt(ostg[d][1][:], KD[:], E2[:], ALU.mult, [KD, E2], [ostg[d][1]])
                        k.dma("pool", O(1), ostg[d][1][:], [ostg[d][1]], [self.rw_ops])
                        k.tt(ostg[d][2][:], BE[:], E2[:], ALU.mult, [BE, E2], [ostg[d][2]])
                        k.dma("pool", O(2), ostg[d][2][:], [ostg[d][2]], [self.rw_ops])
                        k.tt(E1[:], CL[:], LW[:], ALU.subtract, [CL, LW], [E1])
                        k.act(E1[:], E1[:], AF.Exp, [E1], [E1])
                        k.tt(ostg[d][3][:], KK[:], E1[:], ALU.mult, [KK, E1], [ostg[d][3]])
                        k.dma("pool", O(3), ostg[d][3][:], [ostg[d][3]], [self.rw_ops])
                        k.tt(c3(E2[:]), ce.to_broadcast([128, TS // 64, 64]), c3(CL[:]), ALU.subtract, [CL], [E2])
                        k.act(E2[:], E2[:], AF.Exp, [E2], [E2])
                        k.tt(ostg[d][4][:], KD[:], E2[:], ALU.mult, [KD, E2], [ostg[d][4]])
                        k.dma("pool", O(4), ostg[d][4][:], [ostg[d][4]], [self.rw_ops])
                        k.tt(ostg[d][5][:], BE[:], E2[:], ALU.mult, [BE, E2], [ostg[d][5]])
                        k.dma("pool", O(5), ostg[d][5][:], [ostg[d][5]], [self.rw_ops])
        if "rw_prep_only" in cfg.debug:
            return
        with k.phase():
            msk = k.sb("rw_msk", [128, 2, 3, 128])
            k.dma("sp", msk[:], self.rw_mask_d[:, :, :, :], (), [msk])
            ops = [[k.sb(f"rw_op{j}_{i}", [128, 4, 128]) for j in range(7)] for i in range(2)]
            pmask = k.sb("rw_pmask", [128, 2])
            k.dma("sp", pmask[:], self.pmask_d[:, :], (), [pmask])
            mk = {nm: [k.sb(f"rw_mk{nm}{p_}", [128, 4, 128]) for p_ in range(2)] for nm in ("A", "B", "R")}
            NDT = BF16 if RW_BF16 else F32
            if RW_BF16:
                mkb = {nm: [k.sb(f"rw_mkb{nm}{p_}", [128, 4, 128], NDT) for p_ in range(2)] for nm in ("A", "B", "R")}
                Lb = {nm: k.sb(f"rw_Lb{nm}", [128, 4, 128], NDT) for nm in ("A", "B", "K")}
                TTb = k.sb("rw_TTb", [128, 8, 128], NDT)
            tms = [k.sb(f"rw_tm{j}", [128, 512]) for j in range(3)]
            NT = [k.sb(f"rw_NT{i}", [128, 8, 128], BF16 if RW_BF16 else F32) for i in range(2)]
            NN = [k.sb(f"rw_NN{i}", [128, 8, 128], BF16 if RW_BF16 else F32) for i in range(2)]
            TT = k.sb("rw_TT", [128, 8, 128])
            AakT = k.sb("rw_AakT", [128, 8, 128])
            ODT = BF16 if RW_OBF16 else F32
            ArkT = k.sb("rw_ArkT", [128, 8, 128], ODT)
            ArbT = k.sb("rw_ArbT", [128, 8, 128], ODT)
            mkRb = [k.sb(f"rw_mkRb{p_}", [128, 4, 128], ODT) for p_ in range(2)]
            Ktb = k.sb("rw_Ktb", [128, 4, 128], ODT)
            Btb = k.sb("rw_Btb", [128, 4, 128], ODT)
            Vtb = k.sb("rw_Vtb", [128, 512], ODT)
            Utb = k.sb("rw_Utb", [128, 512], ODT)
            STb = k.sb("rw_STb", [128, 4, 64], ODT)
            W1 = k.sb("rw_W1", [128, 512])
            Wt = k.sb("rw_Wt", [128, 512])
            Ut = k.sb("rw_Ut", [128, 512])
            ST = k.sb("rw_ST", [128, 4, 64])
            h0t = k.sb("rw_h0t", [64, 8, 64])
            fint = k.sb("rw_fint", [64, 8, 64])
            osb = [k.sb(f"rw_osb{i}", [128, 4, 128]) for i in range(2)]
            it = 0
            for (s0, ln, is_s) in cfg.seqs:
                si = s0 // ln
                wins = list(range(s0 // 128, (s0 + ln) // 128))
                for d in range(2):
                    if is_s:
                        k.dma("sp", h0t[:], self.rw_h0[l, d], (), [h0t])
                        for h in range(8):
                            ps = self.next_pf()
                            if h % 2 == 0:
                                k.tr(ps[0:64, 0:64], h0t[:, h, :], self.ident_f[0:64, 0:64], [h0t, self.ident_f], [ps])
                                k.cp(ST[0:64, h // 2, :], ps[0:64, 0:64], [ps], [ST])
                            else:
                                k.tr(ps[:, 0:64], h0t[:, h - 1:h + 1, :].rearrange("p a n -> p (a n)"),
                                     self.ident_f[0:64, 0:64], [h0t, self.ident_f], [ps])
                                k.cp(ST[64:128, h // 2, :], ps[64:128, 0:64], [ps], [ST])
                    else:
                        k.memset(ST[:], 0.0, [ST], en="dve")
                    wlist = (wins if d == 0 else wins[::-1])

                    def _load(op_, w_):
                        w0_ = w_ * 128
                        for j in range(6):
                            k.dma("sp", op_[j][:], self.rw_ops[d, j, :, w0_:w0_ + 128].rearrange("(f p) t -> p f t", p=128),
                                  [self.rw_ops], [op_[j]])
                        k.dma("sp", op_[6][:], self.rw_v[:, w0_:w0_ + 128].rearrange("(f p) t -> p f t", p=128),
                              [self.rw_v], [op_[6]])
                    _load(ops[(it + 1) % 2], wlist[0])
                    for wi, w in enumerate(wlist):
                        it += 1
                        w0 = w * 128
                        op = ops[it % 2]
                        if wi + 1 < len(wlist):
                            _load(ops[(it + 1) % 2], wlist[wi + 1])
                        Rt, Kt, Bt, At, Kh, Bh, Vf = op
                        for (src, dst, neg) in ((Vf, tms[0], False), (Kh, tms[1], False), (Bh, tms[2], True)):
                            ps = self.next_pf()
                            for ft in range(4):
                                k.tr(ps[:, ft * 128:(ft + 1) * 128], src[:, ft, :], self.ident_f[:], [src, self.ident_f], [ps])
                            if neg:
                                k.ts(dst[:], ps[:, :], -1.0, None, ALU.mult, None, [ps], [dst])
                            else:
                                k.cp(dst[:], ps[:, :], [ps], [dst], en="act")
                        Vtm, Khtm, Bhtm = tms
                        hb = lambda h: (h % 2) * 64
                        for nm, src in (("A", At), ("B", Bt)):
                            for p_ in range(2):
                                k.ts(mk[nm][p_][:], src[:], pmask[:, p_:p_ + 1], None, ALU.mult, None, [src, pmask], [mk[nm][p_]],
                                     en=("pool" if p_ else "dve"))
                        for p_ in range(2):
                            k.ts(mkRb[p_][:], Rt[:], pmask[:, p_:p_ + 1], None, ALU.mult, None, [Rt, pmask], [mkRb[p_]],
                                 en=("pool" if p_ else "dve"))
                        k.cp(Ktb[:], Kt[:], [Kt], [Ktb], en="act")
                        k.cp(Btb[:], Bt[:], [Bt], [Btb], en="pool")
                        k.cp(Vtb[:], tms[0][:], [tms[0]], [Vtb], en="act")
                        if RW_BF16:
                            for nm, src in (("A", At), ("B", Bt), ("R", Rt)):
                                for p_ in range(2):
                                    k.cp(mkb[nm][p_][:], mk[nm][p_][:], [mk[nm][p_]], [mkb[nm][p_]], en=("pool" if p_ else "act"))
                            for nm, src in (("A", At), ("B", Bt), ("K", Kt)):
                                k.cp(Lb[nm][:], src[:], [src], [Lb[nm]], en=("pool" if nm == "B" else "act"))
                        else:
                            mkb = dict(A=mk["A"], B=mk["B"], R=mkRb)
                            Lb = dict(A=At, B=Bt, K=Kt, Kr=Ktb, Br=Btb)
                        specs = ((Lb["B"], "A", NT[0], 0, -1.0), (Lb["A"], "B", NN[0], 1, -1.0), (Lb["K"], "A", AakT, 0, 1.0),
                                 (Lb["Kr"], "R", ArkT, 2, 1.0), (Lb["Br"], "R", ArbT, 2, -1.0))
                        for (L_, rn, dst, mi, sg) in specs:
                            for half in range(2):
                                ps = self.next_pf()
                                for hh in range(4):
                                    h = half * 4 + hh
                                    R_ = mkb[rn][h % 2]
                                    k.mm(ps[:, hh * 128:(hh + 1) * 128], L_[:, h // 2, :], R_[:, h // 2, :],
                                         True, True, [L_, R_], [ps])
                                k.stt(dst[:, half * 4:(half + 1) * 4, :], ps[:, :].rearrange("p (h t) -> p h t", h=4), sg,
                                      msk[:, d, mi:mi + 1, :].to_broadcast([128, 4, 128]), ALU.mult, ALU.mult, [ps, msk], [dst])
                        k.tt(TT[:], NT[0][:], self.ident_f[:, :].unsqueeze(1).to_broadcast([128, 8, 128]), ALU.add,
                             [NT[0], self.ident_f], [TT])
                        if RW_BF16:
                            k.cp(TTb[:], TT[:], [TT], [TTb], en="pool")
                        else:
                            TTb = TT
                        cur = 0
                        for lev in range(1, 6):
                            nxt = 1 - cur
                            for half in range(2):
                                hs = slice(half * 4, (half + 1) * 4)
                                pn = self.next_pf()
                                for hh in range(4):
                                    h = half * 4 + hh
                                    k.mm(pn[:, hh * 128:(hh + 1) * 128], NT[cur][:, h, :], NN[cur][:, h, :], True, True,
                                         [NT[cur], NN[cur]], [pn])
                                k.cp(NN[nxt][:, hs, :], pn[:, :].rearrange("p (h t) -> p h t", h=4), [pn], [NN[nxt]], en="act")
                                if lev < 5:
                                    pt_ = self.next_pf()
                                    for hh in range(4):
                                        h = half * 4 + hh
                                        k.mm(pt_[:, hh * 128:(hh + 1) * 128], NN[cur][:, h, :], NT[cur][:, h, :], True, True,
                                             [NT[cur], NN[cur]], [pt_])
                                    k.cp(NT[nxt][:, hs, :], pt_[:, :].rearrange("p (h t) -> p h t", h=4), [pt_], [NT[nxt]], en="act")
                                pp = self.next_pf()
                                for hh in range(4):
                                    h = half * 4 + hh
                                    k.mm(pp[:, hh * 128:(hh + 1) * 128], NN[nxt][:, h, :], TTb[:, h, :], True, True,
                                         [NN[nxt], TTb], [pp])
                                k.tt(TT[:, hs, :], TT[:, hs, :], pp[:, :].rearrange("p (h t) -> p h t", h=4), ALU.add,
                                     [TT, pp], [TT])
                                if lev < 5 and RW_BF16:
                                    k.cp(TTb[:, hs, :], TT[:, hs, :], [TT], [TTb], en="pool")
                            cur = nxt
                        pw = self.next_pf()
                        for h in range(8):
                            k.mm(pw[:, h * 64:(h + 1) * 64], AakT[:, h, :], Vtm[:, h * 64:(h + 1) * 64], True, True,
                                 [AakT, Vtm], [pw])
                        k.cp(W1[:], pw[:, :], [pw], [W1])
                        po = self.pf[4 + it % 2]
                        o_ = osb[it % 2]
                        for cb in ((0, 64) if d == 0 else (64, 0)):
                            cs_ = slice(cb, cb + 64)
                            c64i = (w0 + cb) // 64
                            px = self.next_pf()
                            for h in range(8):
                                b0 = hb(h)
                                k.mm(px[cs_, h * 64:(h + 1) * 64], mk["A"][h % 2][:, h // 2, cs_], ST[:, h // 2, :],
                                     True, True, [mk["A"][h % 2], ST], [px])
                            k.tt(Wt[cs_, :], W1[cs_, :], px[cs_, :], ALU.add, [W1, px], [Wt])
                            pu = self.next_pf()
                            for h in range(8):
                                k.mm(pu[cs_, h * 64:(h + 1) * 64], TT[cs_, h, cs_], Wt[cs_, h * 64:(h + 1) * 64], True, True,
                                     [TT, Wt], [pu])
                            k.cp(Ut[cs_, :], pu[cs_, :], [pu], [Ut])
                            k.cp(STb[:], ST[:], [ST], [STb], en="pool")
                            k.cp(Utb[cs_, :], Ut[cs_, :], [Ut], [Utb], en="act")
                            pS = self.next_pf()
                            for h in range(8):
                                b0 = hb(h)
                                ft = h // 2
                                sreg = pS[b0:b0 + 64, ft * 64:(ft + 1) * 64]
                                k.mm(sreg, Khtm[cs_, h * 64:(h + 1) * 64], Vtm[cs_, h * 64:(h + 1) * 64], True, False,
                                     [Khtm, Vtm], [pS])
                                k.mm(sreg, Bhtm[cs_, h * 64:(h + 1) * 64], Ut[cs_, h * 64:(h + 1) * 64], False, True,
                                     [Bhtm, Ut], [pS])
                            for h in range(8):
                                b0 = hb(h)
                                ft = h // 2
                                oreg = po[b0:b0 + 64, ft * 128 + cb:ft * 128 + cb + 64]
                                k.mm(oreg, STb[:, ft, :], mkRb[h % 2][:, ft, cs_], True, False, [STb, mkRb[h % 2]], [po])
                                k.mm(oreg, Vtb[cs_, h * 64:(h + 1) * 64], ArkT[cs_, h, cs_], False, False, [Vtb, ArkT], [po])
                                k.mm(oreg, Utb[cs_, h * 64:(h + 1) * 64], ArbT[cs_, h, cs_], False, True, [Utb, ArbT], [po])
                            k.tt(ST[:], ST[:], pc_all[:, :, d, c64i:c64i + 1].to_broadcast([128, 4, 64]), ALU.mult,
                                 [ST, pc_all], [ST])
                            k.tt(ST[:], ST[:], pS[:, 0:256].rearrange("p (f i) -> p f i", f=4), ALU.add, [ST, pS], [ST])
                        k.cp(o_[:], po[:, :].rearrange("p (f t) -> p f t", f=4), [po], [o_], en="act")
                        k.dma("sp", self.rw_o[d, :, w0:w0 + 128].rearrange("(f p) t -> p f t", p=128), o_[:], [o_], [self.rw_o])
                    if not is_s:
                        for h in range(8):
                            b0 = (h % 2) * 64
                            ps = self.next_pf()
                            k.tr(ps[0:64, 0:64], ST[b0:b0 + 64, h // 2, :], self.ident_f[b0:b0 + 64, b0:b0 + 64],
                                 [ST, self.ident_f], [ps])
                            k.cp(fint[:, h, :], ps[0:64, 0:64], [ps], [fint])
                        k.dma("pool", self.o_rw_fin[si, l, d].rearrange("h i j -> i h j"), fint[:], [fint], [self.o_rw_fin])
        with k.phase():
            of = [k.sb(f"rw_of{i}", [128, TS]) for i in range(2)]
            ob_ = [k.sb(f"rw_obb{i}", [128, TS]) for i in range(2)]
            bo = [k.sb(f"rw_bo{i}", [128, TS]) for i in range(2)]
            gt = [k.sb(f"rw_gt{i}", [128, TS]) for i in range(2)]
            xc = [k.sb(f"rw_xc{i}", [128, TS]) for i in range(2)]
            sq = [k.sb(f"rw_sq{i}", [128, TS]) for i in range(2)]
            oo = [k.sb(f"rw_oo{i}", [128, TS], BF16) for i in range(2)]
            n = 0
            for a in range(0, T, TS):
                for ft in range(4):
                    n += 1
                    i = n % 2
                    rs_ = slice(ft * 128, (ft + 1) * 128)
                    k.dma("sp", of[i][:], self.rw_o[0, rs_, a:a + TS], [self.rw_o], [of[i]])
                    k.dma("sp", ob_[i][:], self.rw_o[1, rs_, a:a + TS], [self.rw_o], [ob_[i]])
                    k.dma("sp", bo[i][:], self.rw_bonus[rs_, a:a + TS], [self.rw_bonus], [bo[i]])
                    k.dma("sp", gt[i][:], self.zT[zr["rw_gate"] + ft * 128:zr["rw_gate"] + (ft + 1) * 128, a:a + TS],
                          [self.zT], [gt[i]])
                    k.tt(of[i][:], of[i][:], ob_[i][:], ALU.add, [of[i], ob_[i]], [of[i]])
                    pm = self.next_pf()
                    k.mm(pm[:, :], blk[:, :], of[i][:], True, True, [blk, of[i]], [pm])
                    k.stt(xc[i][:], pm[:, :], -1.0 / 64, of[i][:], ALU.mult, ALU.add, [pm, of[i]], [xc[i]])
                    k.tt(sq[i][:], xc[i][:], xc[i][:], ALU.mult, [xc[i]], [sq[i]])
                    pv = self.next_pf()
                    k.mm(pv[:, :], blk[:, :], sq[i][:], True, True, [blk, sq[i]], [pv])
                    k.ts(sq[i][:], pv[:, :], 1.0 / 64, 64e-5, ALU.mult, ALU.add, [pv], [sq[i]])
                    k.act(sq[i][:], sq[i][:], AF.Sqrt, [sq[i]], [sq[i]])
                    k.op("dve", lambda e: e.reciprocal(out=sq[i][:], in_=sq[i][:]), [sq[i]], [sq[i]])
                    k.tt(xc[i][:], xc[i][:], sq[i][:], ALU.mult, [xc[i], sq[i]], [xc[i]])
                    k.ts(xc[i][:], xc[i][:], vc("rw_ln_g", ft), vc("rw_ln_b", ft), ALU.mult, ALU.add, [xc[i]] + V, [xc[i]])
                    k.tt(xc[i][:], xc[i][:], bo[i][:], ALU.add, [xc[i], bo[i]], [xc[i]])
                    k.act(gt[i][:], gt[i][:], AF.Silu, [gt[i]], [gt[i]])
                    k.tt(oo[i][:], xc[i][:], gt[i][:], ALU.mult, [xc[i], gt[i]], [oo[i]])
                    k.dma("pool", self.o_scr[0, rs_, a:a + TS], oo[i][:], [oo[i]], [self.o_scr])


def host_weights(inp):
    w_in = np.asarray(inp["w_in"], np.float32)
    cols = []
    o = _IW
    cols.append(w_in[:, :, o["rw_pre"]:o["rw_pre"] + 1664])
    cols.append(w_in[:, :, o["rw_gate"]:o["rw_gate"] + 512])
    cols.append(w_in[:, :, o["mla_q"]:o["mla_q"] + 384])
    cols.append(w_in[:, :, o["mla_ckv"]:o["mla_ckv"] + 256])
    misc = np.zeros((NL, D, 128), np.float32)
    misc[:, :, 0:32] = w_in[:, :, o["mla_krope"]:o["mla_krope"] + 32]
    misc[:, :, 32:48] = w_in[:, :, o["ssd_dt"]:o["ssd_dt"] + 16]
    cols.append(misc)
    cols.append(w_in[:, :, o["mla_gate"]:o["mla_gate"] + 512])
    cols.append(w_in[:, :, o["ssd_xbc"]:o["ssd_xbc"] + 768])
    cols.append(w_in[:, :, o["lru_x"]:o["lru_x"] + 512])
    cols.append(w_in[:, :, o["lru_gate"]:o["lru_gate"] + 512])
    cols.append(w_in[:, :, o["ssd_gate"]:o["ssd_gate"] + 512])
    w_in_a = np.ascontiguousarray(np.concatenate(cols, -1))
    assert w_in_a.shape[-1] == NZ_ALL
    sel2 = np.zeros((2, 2, 128), np.float32)
    sel2[0, 0] = 1.0
    sel2[1, 1] = 1.0
    lw = np.zeros((NL, 2, 2, 4, 128, 128), np.float32)
    for d in range(2):
        for gi, nm in enumerate(("lru_wa", "lru_wx")):
            w = np.asarray(inp[nm], np.float32)
            for ft in range(4):
                for kb in range(2):
                    lw[:, d, gi, ft, kb * 64:(kb + 1) * 64, kb * 64:(kb + 1) * 64] = w[:, d, ft * 2 + kb]
    lru_w = np.ascontiguousarray(lw.reshape(NL, 16, 128, 128).transpose(0, 2, 1, 3))
    kvu = np.asarray(inp["mla_kv_up"], np.float32).reshape(NL, 256, 8, 128)
    inv = 10000.0 ** (-np.arange(8, dtype=np.float32) / 8)
    W_pm = np.zeros((32, 96), np.float32)
    for m_ in range(32):
        sw = m_ + 8 if (m_ % 16) < 8 else m_ - 8
        W_pm[sw, 64 + m_] = 1.0
    selm = np.zeros((64, 16, 128), np.float32)
    seld = np.zeros((64, 2, 8), np.float32)
    selg = np.zeros((64, 128), np.float32)
    for d_ in range(2):
        for h_ in range(8):
            selm[d_ * 32 + h_, d_ * 8 + h_, :] = 1.0
            seld[d_ * 32 + h_, d_, h_ % 4] = 1.0
            selg[d_ * 32 + h_, (h_ // 4) * 64:(h_ // 4 + 1) * 64] = 1.0
    ii = np.arange(128)
    maskneg = np.zeros((128, 2, 128), np.float32)
    maskneg[:, 0, :] = np.where(ii[:, None] <= ii[None, :], 0.0, -30000.0)
    maskneg[:, 1, :] = np.where(ii[:, None] >= ii[None, :], 0.0, -30000.0)
    ssd_bc = np.concatenate([np.broadcast_to(np.asarray(inp["ssd_d"], np.float32)[:, None, :], (NL, 128, 8)),
                             np.broadcast_to(np.asarray(inp["ssd_norm_g"], np.float32)[:, None, :], (NL, 128, 512))], -1)
    rw_lw = np.concatenate([np.asarray(inp["rw_w2"], np.float32), np.asarray(inp["rw_a2"], np.float32)], 2)
    rw_lw = np.ascontiguousarray(rw_lw.transpose(0, 2, 1, 3))
    same = (ii[:, None] // 64) == (ii[None, :] // 64)
    rw_mask = np.zeros((128, 2, 3, 128), np.float32)
    rw_mask[:, 0, 0, :] = same & (ii[:, None] < ii[None, :])
    rw_mask[:, 0, 1, :] = same & (ii[:, None] > ii[None, :])
    rw_mask[:, 0, 2, :] = same & (ii[:, None] <= ii[None, :])
    rw_mask[:, 1, 0, :] = same & (ii[:, None] > ii[None, :])
    rw_mask[:, 1, 1, :] = same & (ii[:, None] < ii[None, :])
    rw_mask[:, 1, 2, :] = same & (ii[:, None] >= ii[None, :])
    cm64 = np.ones((128, 513), np.float32)
    cm64[:, ::64] = 0.0
    W = dict(
        pmask=np.ascontiguousarray(np.stack([(ii < 64), (ii >= 64)], 1).astype(np.float32)),
        rw_lw=rw_lw, rw_mask=rw_mask, blk64=same.astype(np.float32), cmask64=cm64,
        selm=selm, seld=seld, selg=selg, maskneg=maskneg, ssd_bc=np.ascontiguousarray(ssd_bc),
        q_up=np.ascontiguousarray(inp["mla_q_up"], np.float32),
        kv_up_k=np.ascontiguousarray(kvu[:, :, :, :64].reshape(NL, 256, 512)),
        kv_up_v=np.ascontiguousarray(kvu[:, :, :, 64:].reshape(NL, 256, 512)),
        pm96=W_pm,
        _inv=inv,
        w_in_m=np.ascontiguousarray(w_in[:, :, o["merge"]:]),
        w_branch=np.ascontiguousarray(inp["w_branch"], np.float32),
        w_out=np.ascontiguousarray(inp["w_out"], np.float32),
        lru_w=lru_w,
        ada_w=np.ascontiguousarray(inp["ada_w"], np.float32),
        ada_bg=np.ascontiguousarray(np.asarray(inp["ada_b"], np.float32)[:, None, 2048:]),
        w_in_a=w_in_a,
        vecs=host_vecs(inp).pack(),
        ident=np.eye(128, dtype=np.float32),
        sel2=sel2,
    )
    return W


def core_inputs(inp, W, core, cfg):
    b = core // 2
    xp = np.asarray(inp["x_prompt"], np.float32)[core * cfg.n_prompt:(core + 1) * cfg.n_prompt, :cfg.lp]
    xs = np.asarray(inp["x_sample"], np.float32)[b, :cfg.ls]
    x_all = np.ascontiguousarray(np.concatenate([xp.reshape(-1, D), xs], 0))
    cond = np.stack([np.asarray(inp["c_ctx"], np.float32), np.asarray(inp["c"], np.float32)[b]], 0)
    condT = np.ascontiguousarray(cond.reshape(2, 8, 128).transpose(2, 1, 0))
    m = dict(W)
    inv = m.pop("_inv")
    t = np.arange(cfg.ls)
    row = (t // 64).astype(np.float32)
    col = (t % 64).astype(np.float32)
    ar, ac = row[None, :] * inv[:, None], col[None, :] * inv[:, None]
    cosT = np.concatenate([np.cos(ar), np.cos(ar), np.cos(ac), np.cos(ac)], 0)
    sinT = np.concatenate([-np.sin(ar), np.sin(ar), -np.sin(ac), np.sin(ac)], 0)
    cm = np.ones((64, cfg.T + 1), np.float32)
    cm[:, ::128] = 0.0
    m["cmask"] = cm
    m["ssd_h0"] = np.ascontiguousarray(np.asarray(inp["state_ssd"], np.float32)[b].transpose(0, 1, 3, 2, 4))
    m["rw_h0"] = np.ascontiguousarray(np.asarray(inp["state_rwkv"], np.float32)[b].transpose(0, 1, 3, 2, 4))
    m["rope_cs"] = np.ascontiguousarray(np.stack([cosT, sinT], 0).astype(np.float32))
    m["cache_ckv"] = np.ascontiguousarray(np.asarray(inp["cache_mla_ckv"], np.float32)[b])
    m["cache_kr"] = np.ascontiguousarray(np.asarray(inp["cache_mla_krope"], np.float32)[b])
    sl = np.asarray(inp["state_lru"], np.float32)[b]
    lru_h0 = np.ascontiguousarray(sl.reshape(NL, 2, 4, 128).transpose(0, 3, 1, 2))
    m.update(x_all=x_all, condT=condT, lru_h0=lru_h0)
    return m


_PROG = {}


def kernel(**inp):
    cfg = Cfg(n_prompt=4, lp=256, ls=4096, debug=DEBUG_FLAGS)
    if "p" not in _PROG:
        prog = Prog(cfg)
        prog.build()
        _PROG["p"] = prog
    prog = _PROG["p"]
    W = host_weights(inp)
    in_maps = []
    for core in range(8):
        m = core_inputs(inp, W, core, cfg)
        in_maps.append({k_: np.ascontiguousarray(v) for k_, v in m.items() if k_ in prog.inputs})
    res = run_bass_kernel_spmd(prog.nc, in_maps, core_ids=list(range(8)))
    R = res.results
    npq = cfg.n_prompt
    y_prompt = np.concatenate([R[c]["y_all"][:cfg.TP].reshape(npq, cfg.lp, D) for c in range(8)], 0)
    y_sample = np.stack([R[2 * b]["y_all"][cfg.TP:] for b in range(4)], 0)
    ckv = np.concatenate([R[c]["o_ckv"].reshape(NL, 256, npq, cfg.lp).transpose(2, 0, 3, 1) for c in range(8)], 0)
    kr = np.concatenate([R[c]["o_kr"].reshape(NL, 32, npq, cfg.lp).transpose(2, 0, 3, 1) for c in range(8)], 0)
    if "o_rw_fin" in R[0]:
        rw = np.concatenate([R[c]["o_rw_fin"] for c in range(8)], 0)
    else:
        rw = np.zeros((32, NL, 2, 8, 64, 64), np.float32)
    if "o_ssd_fin" in R[0]:
        ssd = np.concatenate([R[c]["o_ssd_fin"].transpose(1, 0, 2, 4, 3, 5) for c in range(8)], 0)
    else:
        ssd = np.zeros((32, NL, 2, 8, 64, 64), np.float32)
    lru = np.concatenate([R[c]["o_lru_fin"].transpose(2, 0, 3, 4, 1).reshape(npq, NL, 2, 512) for c in range(8)], 0)
    f = lambda a: np.ascontiguousarray(a, dtype=np.float32)
    return (f(y_prompt), f(y_sample), f(ckv), f(kr), f(rw), f(ssd), f(lru))
```
